# Optimizing a Trainium2 kernel written in Bass

```python
import jax
import jax.numpy as jnp
from jax import lax

D_MODEL = 1024
BATCH = 32
SEQ = 2048
DEPTH = 4

CTX_LEN = 256
GRID_W = 64
N_MIXERS = 3
N_CONV_LAYERS = (DEPTH + 2) // N_MIXERS
N_RWKV_LAYERS = (DEPTH + 1) // N_MIXERS
N_ATTN_LAYERS = DEPTH // N_MIXERS
CONV_WIDTH = 31
RWKV_HEAD = 64
RWKV_HEADS = D_MODEL // RWKV_HEAD
DECAY_LORA = 64
ICLR_LORA = 64
ATTN_HEAD = 128
ATTN_Q_HEADS = D_MODEL // 64
ATTN_KV_HEADS = ATTN_Q_HEADS // 2
ATTN_GROUP = ATTN_Q_HEADS // ATTN_KV_HEADS
ATTN_QW = ATTN_Q_HEADS * ATTN_HEAD
ATTN_KVW = ATTN_KV_HEADS * ATTN_HEAD
Q_BLOCK = 128
ROPE_THETA = 10000.0
AXIS_DIM = ATTN_HEAD // 2
NORM_EPS = 1e-6
LN_EPS = 1e-5
GN_EPS = 64e-5

kernel_name = "hybrid_conv_rwkv7_gqa_prefix_dit"


def rmsnorm(x, g):
    x32 = x.astype(jnp.float32)
    y = x32 * lax.rsqrt(jnp.mean(x32 * x32, axis=-1, keepdims=True) + NORM_EPS)
    return (y * g.astype(jnp.float32)).astype(x.dtype)


def layernorm(x, g, b):
    x32 = x.astype(jnp.float32)
    mean = jnp.mean(x32, axis=-1, keepdims=True)
    var = jnp.mean(jnp.square(x32 - mean), axis=-1, keepdims=True)
    y = (x32 - mean) * lax.rsqrt(var + LN_EPS)
    return (y * g.astype(jnp.float32) + b.astype(jnp.float32)).astype(x.dtype)


def modulation(cvec, w, b):
    m = jax.nn.silu(cvec) @ w + b
    return jnp.split(m, 3, axis=-1)


def modulate(n, shift, scale):
    return n * (1.0 + scale) + shift


def rope_tables(T):
    rows = T // GRID_W
    row = jnp.repeat(jnp.arange(rows), GRID_W).astype(jnp.float32)
    col = (jnp.arange(rows * GRID_W) % GRID_W).astype(jnp.float32)
    inv = 1.0 / (ROPE_THETA ** (jnp.arange(0, AXIS_DIM, 2, dtype=jnp.float32) / AXIS_DIM))
    ang = jnp.stack([row[:, None] * inv, col[:, None] * inv])
    return jnp.cos(ang), jnp.sin(ang)


def rope_axis(x, cos, sin):
    x1, x2 = jnp.split(x, 2, axis=-1)
    return jnp.concatenate([x1 * cos - x2 * sin, x2 * cos + x1 * sin], axis=-1)


def rope2d(x, cos, sin):
    y = jnp.concatenate([rope_axis(x[..., :AXIS_DIM], cos[0], sin[0]),
                         rope_axis(x[..., AXIS_DIM:], cos[1], sin[1])], axis=-1)
    return y.astype(x.dtype)


def depthwise_conv(y, w, b):
    C = y.shape[-1]
    out = lax.conv_general_dilated(y, w[:, None, :].astype(y.dtype), window_strides=(1,),
                                   padding=((CONV_WIDTH // 2, CONV_WIDTH // 2),),
                                   dimension_numbers=("NWC", "WIO", "NWC"),
                                   feature_group_count=C)
    return out + b


def conv_branch(h, w_in, dw, db, ln_g, ln_b, w_out):
    a, b, gate = jnp.split(h @ w_in, 3, axis=-1)
    y = a * jax.nn.sigmoid(b)
    y = depthwise_conv(y, dw, db)
    y = jax.nn.silu(layernorm(y, ln_g, ln_b))
    return (y * jax.nn.silu(gate)) @ w_out


def centred_shift(h):
    hp = jnp.pad(h, ((0, 0), (1, 1), (0, 0)))
    return 0.5 * (hp[:, :-2] + hp[:, 2:])


def rwkv_features(h, proj, with_out):
    mu, w_r, w_k, w_v, w_g, w0, w1, w2, a0, a1, a2, k_k, k_a = proj
    B, T, _ = h.shape
    heads = lambda t: t.reshape(B, T, RWKV_HEADS, RWKV_HEAD)
    xx = centred_shift(h) - h
    lerp = lambda n: h + xx * mu[n]
    k = lerp(2) @ w_k
    v = heads(lerp(3) @ w_v)
    kk = heads(k * k_k).astype(jnp.float32)
    kk = kk * lax.rsqrt(jnp.sum(kk * kk, axis=-1, keepdims=True) + 1e-12)
    xw, xa = lerp(1), lerp(4)
    dirs = []
    for d in range(2):
        w_log = -jax.nn.softplus(-(w0[d] + jnp.tanh(xw @ w1[d]) @ w2[d])) - 0.5
        dec = jnp.exp(-jnp.exp(w_log.astype(jnp.float32)))
        a = jax.nn.sigmoid(a0[d] + (xa @ a1[d]) @ a2[d])
        dirs.append((heads(dec), heads(k * (1.0 + (a - 1.0) * k_a)), heads(a)))
    r = heads(lerp(0) @ w_r) if with_out else None
    g = (lerp(5) @ w_g) if with_out else None
    return r, g, v, kk, dirs


def wkv_scan(state0, r, dec, k, v, kk, a, reverse):
    emit = r is not None

    def step(S, inp):
        d_t, k_t, v_t, kk_t, a_t = inp[:5]
        sa = jnp.einsum("bhvk,bhk->bhv", S, -kk_t)
        S = (S * d_t[:, :, None, :] + sa[..., None] * (kk_t * a_t)[:, :, None, :]
             + v_t[..., None] * k_t[:, :, None, :])
        out = jnp.einsum("bhvk,bhk->bhv", S, inp[5]) if emit else None
        return S, out

    xs = (dec, k, v, kk, a, r) if emit else (dec, k, v, kk, a)
    return lax.scan(step, state0, xs, reverse=reverse)


def rwkv_readout(o, r, v, k_sum, g, r_k, ln_g, ln_b, w_o):
    o = jnp.moveaxis(o, 0, 1)
    B, T = o.shape[:2]
    mean = jnp.mean(o, axis=-1, keepdims=True)
    var = jnp.mean(jnp.square(o - mean), axis=-1, keepdims=True)
    o = (o - mean) * lax.rsqrt(var + GN_EPS)
    o = o * ln_g.reshape(RWKV_HEADS, RWKV_HEAD) + ln_b.reshape(RWKV_HEADS, RWKV_HEAD)
    bonus = jnp.sum(r * k_sum * r_k, axis=-1, keepdims=True) * v
    y = (o + bonus).astype(g.dtype).reshape(B, T, D_MODEL)
    return (y * jax.nn.silu(g)) @ w_o


def rwkv_mixer(h, hc, mu, w_r, w_k, w_v, w_g, w0, w1, w2, a0, a1, a2, k_k, k_a,
               r_k, ln_g, ln_b, w_o, ctx_out):
    proj = (mu, w_r, w_k, w_v, w_g, w0, w1, w2, a0, a1, a2, k_k, k_a)
    r, g, v, kk, dirs = rwkv_features(h, proj, True)
    rc, gc, vc, kkc, dirs_c = rwkv_features(hc, proj, ctx_out)
    tm = lambda t: None if t is None else jnp.moveaxis(t, 1, 0)
    state0 = jnp.zeros((h.shape[0], RWKV_HEADS, RWKV_HEAD, RWKV_HEAD), jnp.float32)
    o, oc = 0.0, 0.0
    for d, rev in ((0, False), (1, True)):
        dec_c, k_c, a_c = dirs_c[d]
        state_c, o_c_d = wkv_scan(state0, tm(rc), tm(dec_c), tm(k_c), tm(vc), tm(kkc), tm(a_c), rev)
        dec_l, k_l, a_l = dirs[d]
        _, o_d = wkv_scan(state_c, tm(r), tm(dec_l), tm(k_l), tm(v), tm(kk), tm(a_l), rev)
        o = o + o_d
        if ctx_out:
            oc = oc + o_c_d
    y = rwkv_readout(o, r, v, dirs[0][1] + dirs[1][1], g, r_k, ln_g, ln_b, w_o)
    yc = None
    if ctx_out:
        yc = rwkv_readout(oc, rc, vc, dirs_c[0][1] + dirs_c[1][1], gc, r_k, ln_g, ln_b, w_o)
    return y, yc


def split_heads(t, n):
    B, T, _ = t.shape
    return t.reshape(B, T, n, ATTN_HEAD).transpose(0, 2, 1, 3)


def merge_heads(o):
    B, _, _, Q, _ = o.shape
    return o.transpose(0, 3, 1, 2, 4).reshape(B, Q, ATTN_QW)


def gqa_attend(q, k, v):
    s = jnp.einsum("bhgqd,bhkd->bhgqk", q, k) * (ATTN_HEAD ** -0.5)
    return jnp.einsum("bhgqk,bhkd->bhgqd", jax.nn.softmax(s, axis=-1), v)


def blocked_attention(q, k, v):
    B, _, T, _ = q.shape
    nb = T // Q_BLOCK
    qb = q.astype(jnp.float32).reshape(B, ATTN_KV_HEADS, ATTN_GROUP, nb, Q_BLOCK, ATTN_HEAD)
    qb = qb.transpose(3, 0, 1, 2, 4, 5)
    ob = lax.map(lambda qq: gqa_attend(qq, k, v), qb)
    return ob.transpose(1, 0, 4, 2, 3, 5).reshape(B, T, ATTN_QW)


def attn_mixer(h, hc, w_in, q_g, k_g, w_out, cos, sin, ctx_out):
    cuts = [ATTN_QW, ATTN_QW + ATTN_KVW, ATTN_QW + 2 * ATTN_KVW]
    q, k, v, g = jnp.split(h @ w_in, cuts, axis=-1)
    q = rope2d(rmsnorm(split_heads(q, ATTN_Q_HEADS), q_g), cos, sin)
    k = rope2d(rmsnorm(split_heads(k, ATTN_KV_HEADS), k_g), cos, sin)
    v = split_heads(v, ATTN_KV_HEADS)
    if ctx_out:
        qc, kc, vc, gc = jnp.split(hc @ w_in, cuts, axis=-1)
    else:
        kc, vc = jnp.split(hc @ w_in[:, ATTN_QW:ATTN_QW + 2 * ATTN_KVW], 2, axis=-1)
    kc = rmsnorm(split_heads(kc, ATTN_KV_HEADS), k_g).astype(jnp.float32)
    vc = split_heads(vc, ATTN_KV_HEADS).astype(jnp.float32)
    k_all = jnp.concatenate([kc, k.astype(jnp.float32)], axis=2)
    v_all = jnp.concatenate([vc, v.astype(jnp.float32)], axis=2)
    o = blocked_attention(q, k_all, v_all).astype(g.dtype)
    y = (o * jax.nn.silu(g)) @ w_out
    yc = None
    if ctx_out:
        qc = rmsnorm(split_heads(qc, ATTN_Q_HEADS), q_g).astype(jnp.float32)
        B, _, L, _ = qc.shape
        oc = merge_heads(gqa_attend(qc.reshape(B, ATTN_KV_HEADS, ATTN_GROUP, L, ATTN_HEAD), kc, vc))
        yc = (oc.astype(gc.dtype) * jax.nn.silu(gc)) @ w_out
    return y, yc


def setup_inputs(seed: int = 0) -> dict:
    key = jax.random.key(seed)
    keys = iter(jax.random.split(key, 48))
    D, NC, NR, NA = D_MODEL, N_CONV_LAYERS, N_RWKV_LAYERS, N_ATTN_LAYERS

    def nrm(shape, s):
        return jax.random.normal(next(keys), shape, jnp.float32) * s

    def unif(shape, lo, hi):
        return jax.random.uniform(next(keys), shape, jnp.float32, lo, hi)

    return {
        "x": nrm((BATCH, SEQ, D), 1.0),
        "c": nrm((BATCH, D), 1.0),
        "ctx": nrm((BATCH, CTX_LEN, D), 1.0),
        "c_ctx": nrm((D,), 1.0),
        "norm_g": 1.0 + nrm((DEPTH, D), 0.02),
        "mod_w": nrm((DEPTH, D, 3 * D), D ** -0.5),
        "mod_b": nrm((DEPTH, 3 * D), 0.02),
        "conv_w_in": nrm((NC, D, 3 * D), D ** -0.5),
        "conv_dw": nrm((NC, CONV_WIDTH, D), CONV_WIDTH ** -0.5),
        "conv_db": nrm((NC, D), 0.02),
        "conv_ln_g": 1.0 + nrm((NC, D), 0.02),
        "conv_ln_b": nrm((NC, D), 0.02),
        "conv_w_out": nrm((NC, D, D), D ** -0.5),
        "rwkv_mu": unif((NR, 6, D), 0.0, 1.0),
        "rwkv_w_r": nrm((NR, D, D), D ** -0.5),
        "rwkv_w_k": nrm((NR, D, D), D ** -0.5),
        "rwkv_w_v": nrm((NR, D, D), D ** -0.5),
        "rwkv_w_g": nrm((NR, D, D), D ** -0.5),
        "rwkv_w0": unif((NR, 2, D), -5.0, 0.5),
        "rwkv_w1": nrm((NR, 2, D, DECAY_LORA), D ** -0.5),
        "rwkv_w2": nrm((NR, 2, DECAY_LORA, D), 0.1 * DECAY_LORA ** -0.5),
        "rwkv_a0": nrm((NR, 2, D), 0.1),
        "rwkv_a1": nrm((NR, 2, D, ICLR_LORA), D ** -0.5),
        "rwkv_a2": nrm((NR, 2, ICLR_LORA, D), 0.1 * ICLR_LORA ** -0.5),
        "rwkv_k_k": 0.85 + nrm((NR, D), 0.02),
        "rwkv_k_a": 1.0 + nrm((NR, D), 0.02),
        "rwkv_r_k": nrm((NR, RWKV_HEADS, RWKV_HEAD), 0.1),
        "rwkv_ln_g": 1.0 + nrm((NR, D), 0.02),
        "rwkv_ln_b": nrm((NR, D), 0.02),
        "rwkv_w_o": nrm((NR, D, D), D ** -0.5),
        "attn_w_in": nrm((NA, D, 2 * ATTN_QW + 2 * ATTN_KVW), D ** -0.5),
        "attn_q_g": 1.0 + nrm((NA, ATTN_HEAD), 0.02),
        "attn_k_g": 1.0 + nrm((NA, ATTN_HEAD), 0.02),
        "attn_w_out": nrm((NA, ATTN_QW, D), ATTN_QW ** -0.5),
        "final_g": 1.0 + nrm((D,), 0.02),
    }


def reference(x, c, ctx, c_ctx, norm_g, mod_w, mod_b,
              conv_w_in, conv_dw, conv_db, conv_ln_g, conv_ln_b, conv_w_out,
              rwkv_mu, rwkv_w_r, rwkv_w_k, rwkv_w_v, rwkv_w_g,
              rwkv_w0, rwkv_w1, rwkv_w2, rwkv_a0, rwkv_a1, rwkv_a2,
              rwkv_k_k, rwkv_k_a, rwkv_r_k, rwkv_ln_g, rwkv_ln_b, rwkv_w_o,
              attn_w_in, attn_q_g, attn_k_g, attn_w_out, final_g):
    cos, sin = rope_tables(x.shape[1])
    xc = ctx
    for i in range(DEPTH):
        kind, j = i % N_MIXERS, i // N_MIXERS
        ctx_out = any(l % N_MIXERS != 0 for l in range(i + 1, DEPTH))
        ctx_in = ctx_out or kind != 0
        shift, scale, gate = modulation(c, mod_w[i], mod_b[i])
        h = modulate(rmsnorm(x, norm_g[i]), shift[:, None], scale[:, None])
        hc = None
        if ctx_in:
            shift_c, scale_c, gate_c = modulation(c_ctx, mod_w[i], mod_b[i])
            hc = modulate(rmsnorm(xc, norm_g[i]), shift_c, scale_c)
        if kind == 0:
            conv_p = (conv_w_in[j], conv_dw[j], conv_db[j], conv_ln_g[j], conv_ln_b[j], conv_w_out[j])
            y = conv_branch(h, *conv_p)
            yc = conv_branch(hc, *conv_p) if ctx_out else None
        elif kind == 1:
            y, yc = rwkv_mixer(h, hc, rwkv_mu[j], rwkv_w_r[j], rwkv_w_k[j], rwkv_w_v[j], rwkv_w_g[j],
                               rwkv_w0[j], rwkv_w1[j], rwkv_w2[j], rwkv_a0[j], rwkv_a1[j], rwkv_a2[j],
                               rwkv_k_k[j], rwkv_k_a[j], rwkv_r_k[j], rwkv_ln_g[j], rwkv_ln_b[j],
                               rwkv_w_o[j], ctx_out)
        else:
            y, yc = attn_mixer(h, hc, attn_w_in[j], attn_q_g[j], attn_k_g[j], attn_w_out[j],
                               cos, sin, ctx_out)
        x = x + gate[:, None] * y
        if ctx_out:
            xc = xc + gate_c * yc
    return rmsnorm(x, final_g)
```

```python
import contextlib
import math
import numpy as np
import ml_dtypes
import concourse.bass as bass
import concourse.mybir as mybir
from concourse.bass_utils import run_bass_kernel_spmd
from concourse.alu_op_type import AluOpType as ALU

F32 = mybir.dt.float32
BF16 = mybir.dt.bfloat16
AF = mybir.ActivationFunctionType
AX = mybir.AxisListType

D = 1024
KC = 8
GRID_W = 64
CONV_W = 31
HALO = 15
NORM_EPS = 1e-6
LN_EPS = 1e-5
GN_EPS = 64e-5

ENG = ("pe", "act", "dve", "pool", "sp")
NDMASEM = 12


class Op:
    __slots__ = ("eng", "fn", "dma", "deps", "needs_inc", "cnt", "slot", "val", "idx")


class Prog:
    def __init__(self, nc):
        self.nc = nc
        self.ops = []
        self.by_eng = {e: [] for e in ENG}
        self.last_w = {}
        self.readers = {}
        self.ndma = {e: 0 for e in ENG}
        self.fence_deps = []
        self.fence_pending = set()

    def fence(self):
        deps = []
        for e in ENG:
            lst = self.by_eng[e]
            for op in reversed(lst):
                if not op.dma:
                    deps.append(op.idx)
                    break
            cnt = 0
            for op in reversed(lst):
                if op.dma:
                    deps.append(op.idx)
                    cnt += 1
                    if cnt >= NDMASEM:
                        break
        self.fence_deps = deps
        self.fence_pending = set(ENG)
        self.last_w.clear()
        self.readers.clear()

    def add(self, eng, fn, r=(), w=(), dma=False):
        op = Op()
        op.eng, op.fn, op.dma = eng, fn, dma
        op.needs_inc = False
        op.cnt = op.slot = op.val = None
        op.idx = len(self.ops)
        deps = set()
        lw = self.last_w
        for k in r:
            y = lw.get(k)
            if y is not None:
                deps.add(y)
        for k in w:
            y = lw.get(k)
            if y is not None:
                deps.add(y)
            ys = self.readers.get(k)
            if ys:
                deps.update(ys)
        keep = []
        for yi in deps:
            y = self.ops[yi]
            if (not y.dma) and (not dma) and y.eng == eng:
                if eng == "pe":
                    continue
                raw = False
                for k in r:
                    if lw.get(k) == yi:
                        raw = True
                        break
                if not raw:
                    continue
            keep.append(yi)
        if eng in self.fence_pending:
            self.fence_pending.discard(eng)
            keep = list(set(keep) | set(self.fence_deps))
        op.deps = keep
        for yi in keep:
            self.ops[yi].needs_inc = True
        if dma:
            n = self.ndma[eng]
            self.ndma[eng] = n + 1
            op.slot = n % NDMASEM
            op.val = 16 * (n // NDMASEM + 1)
        for k in r:
            self.readers.setdefault(k, []).append(op.idx)
        for k in w:
            lw[k] = op.idx
            self.readers[k] = []
        self.ops.append(op)
        self.by_eng[eng].append(op)
        return op

    def emit(self):
        nc = self.nc
        for e in ENG:
            c = 0
            for op in self.by_eng[e]:
                if (not op.dma) and op.needs_inc:
                    c += 1
                    op.cnt = c
        with contextlib.ExitStack() as st:
            csem = {e: st.enter_context(nc.semaphore("c_" + e)) for e in ENG}
            dsem = {e: [st.enter_context(nc.semaphore("d_%s_%d" % (e, i))) for i in range(NDMASEM)]
                    for e in ENG if self.ndma[e] > 0}
            block = st.enter_context(nc.Block())
            handles = {"pe": block.tensor, "act": block.scalar, "dve": block.vector,
                       "pool": block.gpsimd, "sp": block.sync}
            ops = self.ops

            def make(e):
                def body(eng):
                    waited = {}
                    for op in self.by_eng[e]:
                        need = {}
                        for yi in op.deps:
                            y = ops[yi]
                            if y.dma:
                                key = ("d", y.eng, y.slot)
                                v = y.val
                            else:
                                key = ("c", y.eng)
                                v = y.cnt
                            if need.get(key, 0) < v:
                                need[key] = v
                        if op.dma and op.val > 16:
                            key = ("d", e, op.slot)
                            if need.get(key, 0) < op.val - 16:
                                need[key] = op.val - 16
                        for key, v in need.items():
                            if waited.get(key, 0) >= v:
                                continue
                            waited[key] = v
                            sem = csem[key[1]] if key[0] == "c" else dsem[key[1]][key[2]]
                            eng.wait_ge(sem, v)
                        ins = op.fn(eng)
                        if op.dma:
                            ins.then_inc(dsem[e][op.slot], 16)
                        elif op.needs_inc:
                            ins.then_inc(csem[e], 1)
                    if e == "sp":
                        for q in ENG:
                            n = self.ndma[q]
                            for s in range(min(n, NDMASEM)):
                                cnt = (n - 1 - s) // NDMASEM + 1
                                if waited.get(("d", q, s), 0) < 16 * cnt:
                                    eng.wait_ge(dsem[q][s], 16 * cnt)
                        for q in ENG:
                            last = None
                            for op in self.by_eng[q]:
                                if op.cnt is not None:
                                    last = op.cnt
                            if last is not None and q != "sp":
                                eng.wait_ge(csem[q], last)
                return body

            for e in ENG:
                if self.by_eng[e] or e == "sp":
                    handles[e](make(e))


def _tiles(n, t):
    out = []
    o = 0
    while o < n:
        m = min(t, n - o)
        out.append((o, m))
        o += m
    return out


class MK:
    def __init__(self, NB, TL, TC, layers, dbg=False):
        self.NB, self.TL, self.TC, self.layers = NB, TL, TC, layers
        self.TT = TL + TC
        self.NV = NB + 1
        self.nc = bass.Bass("TRN2", target_bir_lowering=False)
        self.p = Prog(self.nc)
        self.st = contextlib.ExitStack()
        self.uid = 0
        self.dbg = dbg

    def dram_in(self, name, shape, dt=F32):
        return self.nc.dram_tensor(name, list(shape), dt, kind="ExternalInput").ap()

    def dram(self, name, shape, dt=F32):
        if self.dbg:
            return self.nc.dram_tensor(name, list(shape), dt, kind="ExternalOutput").ap()
        return self.nc.dram_tensor(name, list(shape), dt).ap()

    def alloc(self, free, dt=F32):
        n = 1
        for f in free:
            n *= f
        units = n * 2 if dt == F32 else n
        self.top = (self.top + 15) // 16 * 16
        assert self.top + units <= self.AREN, ("arena overflow", self.top, units, self.AREN)
        v = self.arena[:, self.top:self.top + units]
        self.top += units
        if dt == F32:
            v = v.bitcast(F32)
        if len(free) == 2:
            v = v.rearrange("p (a b) -> p a b", a=free[0])
        elif len(free) == 3:
            v = v.rearrange("p (a b c) -> p a b c", a=free[0], b=free[1])
        return v

    def key(self, s):
        self.uid += 1
        return "%s#%d" % (s, self.uid)

    def ring(self, name, n, free, dt=F32):
        return [(self.alloc(free, dt), self.key(name)) for _ in range(n)]

    def dma(self, out, in_, r=(), w=(), q="sp", **kw):
        return self.p.add(q, lambda e: e.dma_start(out=out, in_=in_, **kw), r=r, w=w, dma=True)

    def mm(self, out, lhsT, rhs, start, stop, r, w):
        return self.p.add("pe", lambda e: e.matmul(out, lhsT=lhsT, rhs=rhs, start=start, stop=stop), r=r, w=w)

    def tr(self, out, in_, r, w):
        P = in_.shape[0]
        ident = self.ident_f[0:P, 0:P]
        return self.p.add("pe", lambda e: e.transpose(out=out, in_=in_, identity=ident), r=r, w=w)

    def act(self, out, in_, func, r, w, scale=None, bias=None):
        kw = {}
        if scale is not None:
            kw["scale"] = scale
        if bias is not None:
            kw["bias"] = bias
        return self.p.add("act", lambda e: e.activation(out=out, in_=in_, func=func, **kw), r=r, w=w)

    def tt(self, eng, out, in0, in1, op, r, w):
        return self.p.add(eng, lambda e: e.tensor_tensor(out=out, in0=in0, in1=in1, op=op), r=r, w=w)

    def ts(self, eng, out, in0, s1, s2, op0, op1, r, w):
        if op1 is None:
            return self.p.add(eng, lambda e: e.tensor_scalar(out=out, in0=in0, scalar1=s1, scalar2=None, op0=op0), r=r, w=w)
        return self.p.add(eng, lambda e: e.tensor_scalar(out=out, in0=in0, scalar1=s1, scalar2=s2, op0=op0, op1=op1), r=r, w=w)

    def stt(self, out, in0, scalar, in1, op0, op1, r, w):
        return self.p.add("dve", lambda e: e.scalar_tensor_tensor(out=out, in0=in0, scalar=scalar, in1=in1, op0=op0, op1=op1), r=r, w=w)

    def rsqrt(self, out, in_, eps, mul, r, w):
        kt = self.key("rs")
        self.ts("dve", out, in_, mul, eps, ALU.mult, ALU.add, r=r, w=[kt])
        self.p.add("act", lambda e: e.activation(out=out, in_=out, func=AF.Sqrt), r=[kt], w=[kt])
        self.p.add("dve", lambda e: e.reciprocal(out=out, in_=out), r=[kt], w=w)

    def bcast_row(self, dram_ap_row, n):
        return bass.AP(dram_ap_row.tensor, dram_ap_row.offset, [[0, 128], [1, n]])

    def build(self):
        nc, NB, TL, TC, TT = self.nc, self.NB, self.TL, self.TC, self.TT
        NLAY = 4
        i = {}
        i["x"] = self.dram_in("x", [NB, TL, D])
        i["c"] = self.dram_in("c", [NB, D])
        i["ctx"] = self.dram_in("ctx", [NB, TC, D])
        i["c_ctx"] = self.dram_in("c_ctx", [1, D])
        i["norm_g"] = self.dram_in("norm_g", [NLAY, D])
        i["mod_w"] = self.dram_in("mod_w", [NLAY, D, 3 * D])
        i["mod_b"] = self.dram_in("mod_b", [NLAY, 3 * D])
        i["conv_w_in"] = self.dram_in("conv_w_in", [2, D, 3 * D])
        i["conv_dw"] = self.dram_in("conv_dw", [2, CONV_W, D])
        i["conv_db"] = self.dram_in("conv_db", [2, D])
        i["conv_ln_g"] = self.dram_in("conv_ln_g", [2, D])
        i["conv_ln_b"] = self.dram_in("conv_ln_b", [2, D])
        i["conv_w_out"] = self.dram_in("conv_w_out", [2, D, D])
        i["rwkv_mu"] = self.dram_in("rwkv_mu", [1, 6, D])
        for nm in ("rwkv_w_r", "rwkv_w_k", "rwkv_w_v", "rwkv_w_g", "rwkv_w_o"):
            i[nm] = self.dram_in(nm, [1, D, D])
        i["rwkv_w0"] = self.dram_in("rwkv_w0", [1, 2, D])
        i["rwkv_w1"] = self.dram_in("rwkv_w1", [1, 2, D, 64])
        i["rwkv_w2"] = self.dram_in("rwkv_w2", [1, 2, 64, D])
        i["rwkv_a0"] = self.dram_in("rwkv_a0", [1, 2, D])
        i["rwkv_a1"] = self.dram_in("rwkv_a1", [1, 2, D, 64])
        i["rwkv_a2"] = self.dram_in("rwkv_a2", [1, 2, 64, D])
        i["rwkv_k_k"] = self.dram_in("rwkv_k_k", [1, D])
        i["rwkv_k_a"] = self.dram_in("rwkv_k_a", [1, D])
        i["rwkv_r_k"] = self.dram_in("rwkv_r_k", [1, D])
        i["rwkv_ln_g"] = self.dram_in("rwkv_ln_g", [1, D])
        i["rwkv_ln_b"] = self.dram_in("rwkv_ln_b", [1, D])
        i["attn_w_in"] = self.dram_in("attn_w_in", [1, D, 6 * D])
        i["attn_q_g"] = self.dram_in("attn_q_g", [1, 128])
        i["attn_k_g"] = self.dram_in("attn_k_g", [1, 128])
        i["attn_w_out"] = self.dram_in("attn_w_out", [1, 2 * D, D])
        i["final_g"] = self.dram_in("final_g", [1, D])
        i["k_ident"] = self.dram_in("k_ident", [128, 128])
        i["k_perm"] = self.dram_in("k_perm", [128, 128])
        i["k_cos"] = self.dram_in("k_cos", [128, TL])
        i["k_sin"] = self.dram_in("k_sin", [128, TL])
        self.i = i
        self.out = nc.dram_tensor("out", [NB, TL, D], F32, kind="ExternalOutput").ap()
        self.xs = self.dram("xs", [NB, TL, D])
        if self.dbg:
            self.xcs = nc.dram_tensor("xcs", [NB, TC, D], F32, kind="ExternalOutput").ap()
        else:
            self.xcs = self.dram("xcs", [NB, TC, D])
        self.m_d = self.dram("m_d", [NLAY, self.NV, 3 * D])

        st = self.st
        with st:
            self.AREN = 106000
            self.arena = st.enter_context(nc.sbuf_tensor("arena", [128, self.AREN], BF16))
            self.top = 0
            self.ps = [st.enter_context(nc.psum_tensor("ps%d" % k, [128, 512], F32)) for k in range(8)]
            self.psk = ["ps%d" % k for k in range(8)]
            self.prologue()
            self.persist_top = self.top
            used = set(l[0] for l in self.layers)
            self.convert_weights(used)
            self.modulation()
            nl = len(self.layers)
            for li, (kind, j, ctx_in, ctx_out, midx) in enumerate(self.layers):
                self.p.fence()
                self.top = self.persist_top
                last = li == nl - 1
                if kind == 0:
                    self.conv_layer(li, j, ctx_out, midx, last)
                elif kind == 1:
                    self.rwkv_layer(li, j, ctx_out, midx, last)
                else:
                    self.attn_layer(li, j, ctx_out, midx, last)
            if nl == 0:
                self.p.fence()
                self.top = self.persist_top
                self.only_final()
            self.p.emit()
        return nc

    def prologue(self):
        i = self.i
        self.ident_f = self.alloc([128], F32)
        self.dma(self.ident_f, i["k_ident"], w=["ident_f"])
        self.ident_b = self.alloc([128], BF16)
        self.p.add("dve", lambda e: e.tensor_copy(out=self.ident_b, in_=self.ident_f), r=["ident_f"], w=["ident_b"])
        permf = self.alloc([128], F32)
        self.dma(permf, i["k_perm"], w=["permf"])
        self.perm_b = self.alloc([128], BF16)
        self.p.add("dve", lambda e: e.tensor_copy(out=self.perm_b, in_=permf), r=["permf"], w=["perm_b"])
        self.ones_d = self.alloc([128], BF16)
        self.ones_h = self.alloc([128], BF16)
        self.ones_1 = self.alloc([128], BF16)
        self.p.add("pool", lambda e: e.memset(self.ones_d, 1.0 / D), w=["ones_d"])
        self.p.add("pool", lambda e: e.memset(self.ones_h, 1.0 / 128), w=["ones_h"])
        self.p.add("pool", lambda e: e.memset(self.ones_1, 1.0), w=["ones_1"])
        self.fg_b = self.alloc([D], F32)
        self.dma(self.fg_b, self.bcast_row(i["final_g"][0, :], D), w=["fg_b"])
        self.constkeys = ["ident_f", "ident_b", "perm_b", "ones_d", "ones_h", "ones_1", "fg_b"]

    def after_fence_consts(self):
        return

    def convert_weights(self, used):
        i = self.i
        self.wb = {}

        def conv(name, src, rows, cols):
            dst = self.dram(name, [rows, cols], BF16)
            for r0 in range(0, rows, 256):
                rr = min(256, rows - r0)
                self.dma(dst[r0:r0 + rr, :], src[r0:r0 + rr, :], w=[name], q="pool")
            self.wb[name] = dst

        if 0 in used:
            for j in sorted(set(l[1] for l in self.layers if l[0] == 0)):
                conv("cwi%d" % j, i["conv_w_in"][j], D, 3 * D)
                conv("cwo%d" % j, i["conv_w_out"][j], D, D)
        if 1 in used:
            for nm in ("w_r", "w_k", "w_v", "w_g", "w_o"):
                conv("r" + nm, i["rwkv_" + nm][0], D, D)
            for dd in range(2):
                conv("rw1_%d" % dd, i["rwkv_w1"][0, dd], D, 64)
                conv("ra1_%d" % dd, i["rwkv_a1"][0, dd], D, 64)
                conv("rw2_%d" % dd, i["rwkv_w2"][0, dd], 64, D)
                conv("ra2_%d" % dd, i["rwkv_a2"][0, dd], 64, D)
        if 2 in used:
            conv("awi", i["attn_w_in"][0], D, 6 * D)
            conv("awo", i["attn_w_out"][0], 2 * D, D)

    def modulation(self):
        i, NV, NB = self.i, self.NV, self.NB
        top0 = self.top
        crow = self.alloc([D], F32)
        self.dma(crow[0:NB, :], i["c"], w=["crow"])
        self.dma(crow[NB:NV, :], i["c_ctx"], w=["crow"])
        self.act(crow[0:NV, :], crow[0:NV, :], AF.Silu, r=["crow"], w=["crow"])
        scT = self.alloc([KC, NV], F32)
        for kc in range(KC):
            self.tr(self.ps[0][:, kc * NV:(kc + 1) * NV], crow[0:NV, kc * 128:(kc + 1) * 128],
                    r=["crow", "ident_f"], w=["ps0"])
        self.p.add("dve", lambda e: e.tensor_copy(out=scT.rearrange("p a b -> p (a b)"), in_=self.ps[0][:, 0:KC * NV]),
                   r=["ps0"], w=["scT"])
        wring = self.ring("modw", 2, [KC, 512], F32)
        mrow = self.alloc([3 * D], F32)
        brow = self.alloc([3 * D], F32)
        used_mod = sorted(set(l[4] for l in self.layers))
        cnt = 0
        for l in used_mod:
            self.dma(brow[0:NV, :], bass.AP(i["mod_b"].tensor, i["mod_b"][l, :].offset, [[0, NV], [1, 3 * D]]),
                     r=[], w=["brow"])
            for pn in range(6):
                wt, wk = wring[cnt % 2]
                bank = 1 + cnt % 2
                cnt += 1
                self.dma(wt, i["mod_w"][l][:, pn * 512:(pn + 1) * 512].rearrange("(kc p) n -> p kc n", p=128), w=[wk])
                for kc in range(KC):
                    self.mm(self.ps[bank][0:NV, :], scT[:, kc, :], wt[:, kc, :], kc == 0, kc == KC - 1,
                            r=[wk, "scT"], w=[self.psk[bank]])
                self.tt("dve", mrow[0:NV, pn * 512:(pn + 1) * 512], self.ps[bank][0:NV, :],
                        brow[0:NV, pn * 512:(pn + 1) * 512], ALU.add, r=[self.psk[bank], "brow"], w=["mrow"])
            self.dma(self.m_d[l], mrow[0:NV, :], r=["mrow"], w=["m_d"], q="pool")
        self.top = top0

    def rows_to_cols(self, rows_aps, name):
        R = len(rows_aps)
        rt = self.alloc([D], F32)
        kr = self.key(name + "_rows")
        for r_, ap in enumerate(rows_aps):
            self.dma(rt[r_:r_ + 1, :], bass.AP(ap.tensor, ap.offset, [[0, 1], [1, D]]), r=["m_d"], w=[kr])
        colsT = self.alloc([KC, R], F32)
        kc_ = self.key(name + "_cols")
        assert KC * R <= 512
        for kc in range(KC):
            self.tr(self.ps[7][:, kc * R:(kc + 1) * R], rt[0:R, kc * 128:(kc + 1) * 128], r=[kr], w=["ps7"])
        self.p.add("dve", lambda e: e.tensor_copy(out=colsT.rearrange("p a b -> p (a b)"), in_=self.ps[7][:, 0:KC * R]),
                   r=["ps7"], w=[kc_])
        return colsT, kc_

    def layer_vectors(self, midx):
        i, NV = self.i, self.NV
        rows = [i["norm_g"][midx, :]]
        for v in range(NV):
            rows.append(self.m_d[midx, v, 0:D])
        for v in range(NV):
            rows.append(self.m_d[midx, v, D:2 * D])
        cols, ck = self.rows_to_cols(rows, "lv")
        gsT = self.alloc([KC, NV], F32)
        kg = self.key("gsT")
        for v in range(NV):
            self.p.add("dve", lambda e, v=v: e.scalar_tensor_tensor(
                out=gsT[:, :, v], in0=cols[:, :, 1 + NV + v], scalar=1.0, in1=cols[:, :, 0],
                op0=ALU.add, op1=ALU.mult), r=[ck], w=[kg])
        self.gsT, self.gsk = gsT, kg
        self.shT, self.shk = cols, ck
        self.midx = midx
        self.gate_tiles = {}

    def gate_b(self, v):
        gt = self.gate_tiles
        if v in gt:
            ent = gt.pop(v)
            gt[v] = ent
            return ent
        if len(gt) < 2:
            ent = (self.alloc([D], F32), self.key("gate_b"))
        else:
            old_v = next(iter(gt))
            ent = gt.pop(old_v)
        self.dma(ent[0], self.bcast_row(self.m_d[self.midx, v, 2 * D:3 * D], D), r=["m_d"], w=[ent[1]])
        gt[v] = ent
        return ent

    def stage1_setup(self, share=None):
        if share is None:
            self.xt_ring = self.ring("xt", 2, [D], F32)
            self.xn = self.alloc([D], F32)
            self.xnk = self.key("xn")
            self.junk = self.alloc([D], F32)
            self.junkk = self.key("junk")
        else:
            self.xt_ring = [share[0], share[1]]
            self.xn, self.xnk = share[2]
            self.junk, self.junkk = share[3]
        self.ss = self.alloc([4], F32)
        self.ssk = self.key("ss")
        self.xt_cnt = 0
        self.coff, self.loff = 0, self.TC

    def sumsq(self, x, ss, rk):
        junk = self.junk
        self.p.add("act", lambda e: e.activation(out=junk, in_=x, func=AF.Square, accum_out=ss),
                   r=rk, w=[self.junkk, self.ssk])

    def src_tile(self, li, b, is_ctx, t0, n=128):
        if li == 0:
            return (self.i["ctx"] if is_ctx else self.i["x"])[b, t0:t0 + n, :]
        return (self.xcs if is_ctx else self.xs)[b, t0:t0 + n, :]

    def xkey(self, b, is_ctx, t0):
        return "x_%d_%d_%d" % (b, int(is_ctx), t0 // 128)

    def stage1(self, li, b, do_ctx, hT, hk):
        TC, TL, NB = self.TC, self.TL, self.NB
        segs = []
        if do_ctx:
            segs += [(True, t0) for t0 in range(0, TC, 128)]
        segs += [(False, t0) for t0 in range(0, TL, 128)]
        for (is_ctx, t0) in segs:
            v = NB if is_ctx else b
            tok = self.coff + t0 if is_ctx else self.loff + t0
            xt, xk = self.xt_ring[self.xt_cnt % 2]
            self.xt_cnt += 1
            self.dma(xt, self.src_tile(li, b, is_ctx, t0), r=[self.xkey(b, is_ctx, t0)], w=[xk])
            ss = self.ss[:, 0:1]
            self.sumsq(xt, ss, [xk])
            self.rsqrt(ss, ss, NORM_EPS, 1.0 / D, r=[self.ssk], w=[self.ssk])
            xn = self.xn
            self.p.add("act", lambda e, xt=xt, ss=ss, xn=xn: e.activation(out=xn, in_=xt, func=AF.Identity, scale=ss),
                       r=[xk, self.ssk], w=[self.xnk])
            for g in range(2):
                bank = 6 + g
                for q in range(4):
                    kc = g * 4 + q
                    self.tr(self.ps[bank][:, q * 128:(q + 1) * 128], self.xn[:, kc * 128:(kc + 1) * 128],
                            r=[self.xnk], w=[self.psk[bank]])
                for q in range(4):
                    kc = g * 4 + q
                    self.ts("dve", hT[:, kc, tok:tok + 128], self.ps[bank][:, q * 128:(q + 1) * 128],
                            self.gsT[:, kc, v:v + 1], self.shT[:, kc, 1 + v:2 + v], ALU.mult, ALU.add,
                            r=[self.psk[bank], self.gsk, self.shk], w=[hk])

    def epi_setup(self):
        self.tmp = self.alloc([D], F32)
        self.tmpk = self.key("tmp")
        self.xnew_ring = self.ring("xnew", 2, [D], F32)
        self.epi_cnt = 0

    def epilogue_tile(self, li, b, is_ctx, t0, yT_fn, nkc, w_out, wok, ykeys, last, banks=(4, 5)):
        v = self.NB if is_ctx else b
        gb, gk = self.gate_b(v)
        xt, xk = self.xt_ring[self.xt_cnt % 2]
        self.xt_cnt += 1
        xkey = self.xkey(b, is_ctx, t0)
        self.dma(xt, self.src_tile(li, b, is_ctx, t0), r=[xkey], w=[xk])
        xn_, xnk_ = self.xnew_ring[self.epi_cnt % 2]
        self.epi_cnt += 1
        for half in range(2):
            bank = banks[half]
            for kc in range(nkc):
                self.mm(self.ps[bank][:, :], yT_fn(kc), w_out[:, kc, half * 512:(half + 1) * 512],
                        kc == 0, kc == nkc - 1, r=ykeys + [wok], w=[self.psk[bank]])
            hs = slice(half * 512, (half + 1) * 512)
            self.tt("dve", self.tmp[:, hs], self.ps[bank][:, :], gb[:, hs], ALU.mult,
                    r=[self.psk[bank], gk], w=[self.tmpk])
            self.tt("pool", xn_[:, hs], self.tmp[:, hs], xt[:, hs], ALU.add, r=[self.tmpk, xk], w=[xnk_])
        if last and not is_ctx:
            ss = self.ss[:, 1:2]
            self.sumsq(xn_, ss, [xnk_])
            self.rsqrt(ss, ss, NORM_EPS, 1.0 / D, r=[self.ssk], w=[self.ssk])
            self.stt(self.tmp, xn_, ss, self.fg_b, ALU.mult, ALU.mult, r=[xnk_, self.ssk], w=[self.tmpk])
            self.dma(self.out[b, t0:t0 + 128, :], self.tmp, r=[self.tmpk], w=["out"], q="pool")
        else:
            dst = (self.xcs if is_ctx else self.xs)[b, t0:t0 + 128, :]
            self.dma(dst, xn_, r=[xnk_], w=[xkey], q="pool")

    def only_final(self):
        self.stage1_setup()
        tmp = self.alloc([D], F32)
        for b in range(self.NB):
            for t0 in range(0, self.TL, 128):
                xt, xk = self.xt_ring[self.xt_cnt % 2]
                self.xt_cnt += 1
                self.dma(xt, self.i["x"][b, t0:t0 + 128, :], w=[xk])
                ss = self.ss[:, 1:2]
                self.sumsq(xt, ss, [xk])
                self.rsqrt(ss, ss, NORM_EPS, 1.0 / D, r=[self.ssk], w=[self.ssk])
                self.stt(tmp, xt, ss, self.fg_b, ALU.mult, ALU.mult, r=[xk, self.ssk], w=["tmpf"])
                self.dma(self.out[b, t0:t0 + 128, :], tmp, r=["tmpf"], w=["out"], q="pool")

    def conv_layer(self, li, j, ctx_out, midx, last):
        NB, TL, TC, TT = self.NB, self.TL, self.TC, self.TT
        i = self.i
        T2 = 256
        self.layer_vectors(midx)
        rows = [i["conv_dw"][j, t, :] for t in range(CONV_W)]
        rows += [i["conv_db"][j, :], i["conv_ln_g"][j, :], i["conv_ln_b"][j, :]]
        cv, cvk = self.rows_to_cols(rows, "cv")
        w_in = self.wb["cwi%d" % j]
        w_in_v = w_in.rearrange("(kc p) n -> p kc n", p=128)
        w_out = self.alloc([KC, D], BF16)
        wok = self.key("cwo")
        self.dma(w_out, self.wb["cwo%d" % j].rearrange("(kc p) n -> p kc n", p=128), r=["cwo%d" % j], w=[wok])
        w_g = self.alloc([KC, D], BF16)
        wgk = self.key("cwg")
        self.dma(w_g, w_in_v[:, :, 2 * D:3 * D], r=["cwi%d" % j], w=[wgk])
        hT = self.alloc([KC, TT], BF16)
        hk = self.key("hT")
        self.stage1_setup()
        self.epi_setup()
        seqs = []
        if ctx_out:
            seqs.append((True, 0, TC))
        seqs.append((False, TC, TL))
        GLW = sum(n + 2 * HALO for (_, _, n) in seqs)
        glu_d = self.dram("glu_d%d" % li, [KC, 128, GLW], BF16)
        wp_ring = self.ring("wp", 2, [KC, 2, 128], BF16)
        sig_ring = self.ring("sig", 2, [512], F32)
        gl_ring = self.ring("gl", 2, [GLW], BF16)
        for (t, k) in gl_ring:
            self.p.add("pool", lambda e, t=t: e.memset(t, 0.0), w=[k])
        glt_ring = self.ring("glt", 2, [KC, T2 + 2 * HALO], BF16)
        dg_ring = self.ring("dg", 2, [CONV_W, 128], BF16)
        sgt_ring = self.ring("sgt", 2, [T2], BF16)
        yb = self.alloc([KC, T2], BF16)
        ybk = self.key("yb")
        ysq = self.alloc([KC, T2], BF16)
        ysqk = self.key("ysq")
        y2_ring = self.ring("y2", 2, [KC, T2], BF16)
        mean = self.alloc([T2], F32)
        meank = self.key("mean")
        rstd = self.alloc([T2], F32)
        rstdk = self.key("rstd")
        t1_ring = self.ring("t1", 2, [T2], F32)
        s_ring = self.ring("s", 2, [T2], F32)
        cnt1 = 0
        cnt2 = 0
        for b in range(NB):
            self.stage1(li, b, ctx_out, hT, hk)
            for fc in range(KC):
                wp, wpk = wp_ring[fc % 2]
                for ab in range(2):
                    self.dma(wp[:, :, ab, :], w_in_v[:, :, ab * D + fc * 128: ab * D + (fc + 1) * 128],
                             r=["cwi%d" % j], w=[wpk])
                gl, glk = gl_ring[fc % 2]
                col = 0
                for (is_ctx, tok0, n) in seqs:
                    for (o, m) in _tiles(n, 512):
                        bA = (cnt1 % 2) * 2
                        bB = bA + 1
                        sg, sgk = sig_ring[cnt1 % 2]
                        cnt1 += 1
                        for ab, bank in ((0, bA), (1, bB)):
                            for kc in range(KC):
                                self.mm(self.ps[bank][:, 0:m], wp[:, kc, ab, :], hT[:, kc, tok0 + o: tok0 + o + m],
                                        kc == 0, kc == KC - 1, r=[wpk, hk], w=[self.psk[bank]])
                        self.act(sg[:, 0:m], self.ps[bB][:, 0:m], AF.Sigmoid, r=[self.psk[bB]], w=[sgk])
                        c0 = col + HALO + o
                        self.tt("dve", gl[:, c0:c0 + m], self.ps[bA][:, 0:m], sg[:, 0:m], ALU.mult,
                                r=[self.psk[bA], sgk], w=[glk])
                    col += n + 2 * HALO
                self.dma(glu_d[fc], gl, r=[glk], w=["glu_d"], q="pool")
            col = 0
            for (is_ctx, tok0, n) in seqs:
                for (o, m) in _tiles(n, T2):
                    glt, gltk = glt_ring[cnt2 % 2]
                    y2, y2k = y2_ring[cnt2 % 2]
                    cnt2 += 1
                    c0 = col + o
                    self.dma(glt[:, :, 0:m + 2 * HALO], glu_d[:, :, c0:c0 + m + 2 * HALO].rearrange("f p c -> p f c"),
                             r=["glu_d"], w=[gltk])
                    for fc in range(KC):
                        dg, dgk = dg_ring[fc % 2]
                        for t in range(CONV_W):
                            self.ts("pool", dg[:, t, :], self.ident_b, cv[:, fc, t:t + 1], None, ALU.mult, None,
                                    r=[cvk], w=[dgk])
                        bank = fc % 2
                        for t in range(CONV_W):
                            self.mm(self.ps[bank][:, 0:m], dg[:, t, :], glt[:, fc, t:t + m], t == 0, t == CONV_W - 1,
                                    r=[dgk, gltk], w=[self.psk[bank]])
                        self.act(yb[:, fc, 0:m], self.ps[bank][:, 0:m], AF.Identity, r=[self.psk[bank], cvk], w=[ybk],
                                 bias=cv[:, fc, CONV_W:CONV_W + 1])
                        self.act(ysq[:, fc, 0:m], self.ps[bank][:, 0:m], AF.Square, r=[self.psk[bank], cvk], w=[ysqk],
                                 bias=cv[:, fc, CONV_W:CONV_W + 1])
                    for fc in range(KC):
                        self.mm(self.ps[2][:, 0:m], self.ones_d, yb[:, fc, 0:m], fc == 0, fc == KC - 1,
                                r=[ybk], w=[self.psk[2]])
                    for fc in range(KC):
                        self.mm(self.ps[3][:, 0:m], self.ones_d, ysq[:, fc, 0:m], fc == 0, fc == KC - 1,
                                r=[ysqk], w=[self.psk[3]])
                    self.p.add("act", lambda e, m=m: e.copy(out=mean[:, 0:m], in_=self.ps[2][:, 0:m]),
                               r=[self.psk[2]], w=[meank])
                    self.tt("dve", rstd[:, 0:m], mean[:, 0:m], mean[:, 0:m], ALU.mult, r=[meank], w=[rstdk])
                    self.tt("dve", rstd[:, 0:m], self.ps[3][:, 0:m], rstd[:, 0:m], ALU.subtract,
                            r=[self.psk[3], rstdk], w=[rstdk])
                    self.rsqrt(rstd[:, 0:m], rstd[:, 0:m], LN_EPS, 1.0, r=[rstdk], w=[rstdk])
                    for fc in range(KC):
                        t1, t1k = t1_ring[fc % 2]
                        s_, sk = s_ring[fc % 2]
                        sgt, sgtk = sgt_ring[fc % 2]
                        bank = 6 + fc % 2
                        for kc in range(KC):
                            self.mm(self.ps[bank][:, 0:m], w_g[:, kc, fc * 128:(fc + 1) * 128],
                                    hT[:, kc, tok0 + o: tok0 + o + m], kc == 0, kc == KC - 1,
                                    r=[wgk, hk], w=[self.psk[bank]])
                        self.act(sgt[:, 0:m], self.ps[bank][:, 0:m], AF.Silu, r=[self.psk[bank]], w=[sgtk])
                        self.tt("dve", t1[:, 0:m], yb[:, fc, 0:m], mean[:, 0:m], ALU.subtract, r=[ybk, meank], w=[t1k])
                        self.tt("dve", t1[:, 0:m], t1[:, 0:m], rstd[:, 0:m], ALU.mult, r=[t1k, rstdk], w=[t1k])
                        self.act(s_[:, 0:m], t1[:, 0:m], AF.Silu, r=[t1k, cvk], w=[sk],
                                 scale=cv[:, fc, CONV_W + 1:CONV_W + 2], bias=cv[:, fc, CONV_W + 2:CONV_W + 3])
                        self.tt("pool", y2[:, fc, 0:m], s_[:, 0:m], sgt[:, 0:m], ALU.mult, r=[sk, sgtk], w=[y2k])
                    for (oo, mm_) in _tiles(m, 128):
                        self.epilogue_tile(li, b, is_ctx, o + oo,
                                           lambda kc, y2=y2, oo=oo: y2[:, kc, oo:oo + 128],
                                           KC, w_out, wok, [y2k], last)
                col += n + 2 * HALO

    def bc_free(self, ap2, pattern):
        a = list(ap2.ap)
        if pattern == "k":
            return bass.AP(ap2.tensor, ap2.offset, [list(a[0]), [0, 64], list(a[1])])
        return bass.AP(ap2.tensor, ap2.offset, [list(a[0]), list(a[1]), [0, 64]])

    def rwkv_layer(self, li, j, ctx_out, midx, last):
        NB, TL, TC, TT = self.NB, self.TL, self.TC, self.TT
        i = self.i
        names = ["r", "v", "kk", "sg", "dec0", "dec1", "kd0", "kd1", "bn0", "bn1", "o0", "o1"]
        A = {n: self.dram("rk_%s_%d" % (n, li), [NB, TT, D]) for n in names}
        self.rwkv_phase_a(li, j, midx, A)
        self.p.fence()
        self.top = self.persist_top
        self.rwkv_scan(li, A)
        self.p.fence()
        self.top = self.persist_top
        self.rwkv_phase_c(li, j, ctx_out, midx, last, A)

    def rwkv_phase_a(self, li, j, midx, A):
        NB, TL, TC, TT = self.NB, self.TL, self.TC, self.TT
        i = self.i
        self.layer_vectors(midx)
        muT, muk = self.rows_to_cols([i["rwkv_mu"][j, n, :] for n in range(6)], "mu")
        W = {}
        for nm in ("w_r", "w_k", "w_v", "w_g"):
            t = self.alloc([KC, D], BF16)
            k = self.key(nm)
            self.dma(t, self.wb["r" + nm].rearrange("(kc p) n -> p kc n", p=128), w=[k])
            W[nm] = (t, k)
        w1cat = self.alloc([KC, 128], BF16)
        a1cat = self.alloc([KC, 128], BF16)
        w2cat = self.alloc([D], BF16)
        a2cat = self.alloc([D], BF16)
        lk = self.key("lora")
        for d in range(2):
            self.dma(w1cat[:, :, d * 64:(d + 1) * 64], self.wb["rw1_%d" % d].rearrange("(kc p) n -> p kc n", p=128), w=[lk])
            self.dma(a1cat[:, :, d * 64:(d + 1) * 64], self.wb["ra1_%d" % d].rearrange("(kc p) n -> p kc n", p=128), w=[lk])
            self.dma(w2cat[d * 64:(d + 1) * 64, :], self.wb["rw2_%d" % d], w=[lk])
            self.dma(a2cat[d * 64:(d + 1) * 64, :], self.wb["ra2_%d" % d], w=[lk])
        pk = self.key("params")
        kkb = self.alloc([D], F32)
        kab = self.alloc([D], F32)
        self.dma(kkb, self.bcast_row(i["rwkv_k_k"][j, :], D), w=[pk])
        self.dma(kab, self.bcast_row(i["rwkv_k_a"][j, :], D), w=[pk])
        w0b, a0b = [], []
        for d in range(2):
            t = self.alloc([D], F32)
            self.dma(t, self.bcast_row(i["rwkv_w0"][j, d, :], D), w=[pk])
            w0b.append(t)
            t = self.alloc([D], F32)
            self.dma(t, self.bcast_row(i["rwkv_a0"][j, d, :], D), w=[pk])
            a0b.append(t)
        HW = TT + 4
        hT = self.alloc([KC, HW], BF16)
        hk = self.key("hT")
        self.p.add("pool", lambda e: e.memset(hT, 0.0), w=[hk])
        T = [(self.alloc([D], F32), self.key("T%d" % n)) for n in range(4)]
        self.stage1_setup(share=T)
        self.coff, self.loff = 1, TC + 3
        xx = self.alloc([KC, 128], F32)
        xxk = self.key("xx")
        tl = self.alloc([KC, 128], F32)
        tlk = self.key("tl")
        lerp = [(self.alloc([KC, 128], BF16), self.key("lerp%d" % n)) for n in range(6)]
        thT = self.alloc([128], BF16)
        thk = self.key("thT")
        ahT = self.alloc([128], BF16)
        ahk = self.key("ahT")
        kraw = self.alloc([D], F32)
        krk = self.key("kraw")
        kk = self.alloc([D], F32)
        kkk = self.key("kk")
        ssq = self.alloc([16], F32)
        ssqk = self.key("ssq")
        bankc = [0]

        def nxt():
            b_ = bankc[0] % 6
            bankc[0] += 1
            return b_

        def proj(n_, wname, half):
            wt, wk = W[wname]
            lp, lpk = lerp[n_]
            bank = nxt()
            for kc in range(KC):
                self.mm(self.ps[bank][:, :], lp[:, kc, :], wt[:, kc, half * 512:(half + 1) * 512], kc == 0, kc == KC - 1,
                        r=[lpk, wk], w=[self.psk[bank]])
            return bank

        def store(name, b, tok, tile, tk):
            self.dma(A[name][b, tok:tok + 128, :], tile, r=[tk], w=[self.key("st")], q="pool")

        for b in range(NB):
            self.stage1(li, b, True, hT, hk)
            for (col0, n, tokbase) in ((self.coff, TC, 0), (self.loff, TL, TC)):
                for t0 in range(0, n, 128):
                    c = col0 + t0
                    tok = tokbase + t0
                    hc_ = hT[:, :, c:c + 128]
                    self.tt("pool", xx, hT[:, :, c - 1:c + 127], hT[:, :, c + 1:c + 129], ALU.add, r=[hk], w=[xxk])
                    self.stt(xx, xx, 0.5, hc_, ALU.mult, ALU.subtract, r=[xxk, hk], w=[xxk])
                    for n_ in range(6):
                        mb = bass.AP(muT.tensor, muT[:, :, n_].offset, [list(muT.ap[0]), [6, KC], [0, 128]])
                        self.tt("pool", tl, xx, mb, ALU.mult, r=[xxk, muk], w=[tlk])
                        lp, lpk = lerp[n_]
                        self.tt("dve", lp, tl, hc_, ALU.add, r=[tlk, hk], w=[lpk])
                    for half in range(2):
                        hs = slice(half * 512, (half + 1) * 512)
                        bk = proj(2, "w_k", half)
                        self.p.add("act", lambda e, bk=bk, hs=hs: e.copy(out=kraw[:, hs], in_=self.ps[bk][:, :]),
                                   r=[self.psk[bk]], w=[krk])
                    T1, T1k = T[0]
                    T2, T2k = T[1]
                    T3, T3k = T[2]
                    T4, T4k = T[3]
                    self.tt("dve", T1, kraw, kkb, ALU.mult, r=[krk, pk], w=[T1k])
                    self.tt("pool", T2, T1, T1, ALU.mult, r=[T1k], w=[T2k])
                    self.p.add("dve", lambda e: e.tensor_reduce(out=ssq, in_=T2.rearrange("p (h f) -> p h f", f=64),
                                                                axis=AX.X, op=ALU.add), r=[T2k], w=[ssqk])
                    self.rsqrt(ssq, ssq, 1e-12, 1.0, r=[ssqk], w=[ssqk])
                    self.tt("pool", kk.rearrange("p (h f) -> p h f", f=64), T1.rearrange("p (h f) -> p h f", f=64),
                            self.bc_free(ssq, "v"), ALU.mult, r=[T1k, ssqk], w=[kkk])
                    store("kk", b, tok, kk, kkk)
                    for kc in range(KC):
                        self.mm(self.ps[6][:, 0:128], w1cat[:, kc, :], lerp[1][0][:, kc, :], kc == 0, kc == KC - 1,
                                r=[lk, lerp[1][1]], w=[self.psk[6]])
                    self.act(thT, self.ps[6][:, 0:128], AF.Tanh, r=[self.psk[6]], w=[thk])
                    for kc in range(KC):
                        self.mm(self.ps[7][:, 0:128], a1cat[:, kc, :], lerp[4][0][:, kc, :], kc == 0, kc == KC - 1,
                                r=[lk, lerp[4][1]], w=[self.psk[7]])
                    self.p.add("act", lambda e: e.copy(out=ahT, in_=self.ps[7][:, 0:128]), r=[self.psk[7]], w=[ahk])
                    for d in range(2):
                        ds = slice(d * 64, (d + 1) * 64)
                        for half in range(2):
                            hs = slice(half * 512, (half + 1) * 512)
                            bank = nxt()
                            self.mm(self.ps[bank][:, :], ahT[ds, :], a2cat[ds, hs], True, True, r=[ahk, lk], w=[self.psk[bank]])
                            self.tt("dve", T1[:, hs], self.ps[bank][:, :], a0b[d][:, hs], ALU.add,
                                    r=[self.psk[bank], pk], w=[T1k])
                        self.act(T1, T1, AF.Sigmoid, r=[T1k], w=[T1k])
                        self.stt(T2, T1, -1.0, kab, ALU.add, ALU.mult, r=[T1k, pk], w=[T2k])
                        self.stt(T2, T2, 1.0, kraw, ALU.add, ALU.mult, r=[T2k, krk], w=[T2k])
                        store("kd%d" % d, b, tok, T2, T2k)
                        self.stt(T3, kk, -1.0, T1, ALU.mult, ALU.mult, r=[kkk, T1k], w=[T3k])
                        store("bn%d" % d, b, tok, T3, T3k)
                        for half in range(2):
                            hs = slice(half * 512, (half + 1) * 512)
                            bank = nxt()
                            self.mm(self.ps[bank][:, :], thT[ds, :], w2cat[ds, hs], True, True, r=[thk, lk], w=[self.psk[bank]])
                            self.tt("dve", T4[:, hs], self.ps[bank][:, :], w0b[d][:, hs], ALU.add,
                                    r=[self.psk[bank], pk], w=[T4k])
                        self.act(T4, T4, AF.Sigmoid, r=[T4k], w=[T4k])
                        self.act(T4, T4, AF.Exp, r=[T4k], w=[T4k], scale=-math.exp(-0.5))
                        store("dec%d" % d, b, tok, T4, T4k)
                    for (n_, wname, name, tile, tk, fn) in ((0, "w_r", "r", T1, T1k, AF.Identity), (3, "w_v", "v", T2, T2k, AF.Identity),
                                                            (5, "w_g", "sg", T3, T3k, AF.Silu)):
                        for half in range(2):
                            hs = slice(half * 512, (half + 1) * 512)
                            bk = proj(n_, wname, half)
                            self.act(tile[:, hs], self.ps[bk][:, :], fn, r=[self.psk[bk]], w=[tk])
                        store(name, b, tok, tile, tk)

    def rwkv_scan(self, li, A):
        NB, TL, TC, TT = self.NB, self.TL, self.TC, self.TT
        P = 2 * NB * 16
        SB = 32
        S = self.alloc([64, 64], F32)
        Sk = self.key("S")
        tmp = self.alloc([64, 64], F32)
        tmpk = self.key("tmp")
        vk_ring = self.ring("vk", 2, [64, 64], F32)
        sa = self.alloc([64], F32)
        sak = self.key("sa")
        arrs = ["kk", "bn", "dec", "kd", "v", "r"]
        blk = [{a: self.alloc([SB, 64], F32) for a in arrs} for _ in range(2)]
        oblk = self.ring("oblk", 2, [SB, 64], F32)
        self.p.add("dve", lambda e: e.memset(S[0:P], 0.0), w=[Sk])
        cb = 0
        for (soff, n) in ((0, TC), (TC, TL)):
            for i0 in range(0, n, SB):
                slot = cb % 2
                cb += 1
                bkeys = []
                for d in range(2):
                    for b in range(NB):
                        p0 = d * NB * 16 + b * 16
                        if d == 0:
                            tok0, step = soff + i0, D
                        else:
                            tok0, step = soff + n - 1 - i0, -D
                        for a in arrs:
                            name = a + str(d) if a in ("bn", "dec", "kd") else a
                            src_t = A[name]
                            src = bass.AP(src_t.tensor, src_t[b, tok0, :].offset, [[64, 16], [step, SB], [1, 64]])
                            k = "blk_%d_%d_%d_%s" % (slot, d, b, a)
                            self.dma(blk[slot][a][p0:p0 + 16, :, :], src, w=[k])
                            bkeys.append(k)
                ob, obk = oblk[slot]
                for s_ in range(SB):
                    g = lambda a: blk[slot][a][0:P, s_, :]
                    vkt, vkk = vk_ring[s_ % 2]
                    Sv = S[0:P]
                    tv = tmp[0:P]
                    self.tt("pool", vkt[0:P], self.bc_free(g("v"), "v"), self.bc_free(g("kd"), "k"), ALU.mult,
                            r=bkeys, w=[vkk])
                    self.tt("dve", tv, Sv, self.bc_free(g("kk"), "k"), ALU.mult, r=[Sk] + bkeys, w=[tmpk])
                    self.p.add("dve", lambda e, tv=tv: e.tensor_reduce(out=sa[0:P], in_=tv, axis=AX.X, op=ALU.add),
                               r=[tmpk], w=[sak])
                    self.tt("dve", Sv, Sv, self.bc_free(g("dec"), "k"), ALU.mult, r=[Sk] + bkeys, w=[Sk])
                    self.tt("dve", tv, self.bc_free(sa[0:P], "v"), self.bc_free(g("bn"), "k"), ALU.mult,
                            r=[sak] + bkeys, w=[tmpk])
                    self.tt("dve", Sv, Sv, tv, ALU.add, r=[Sk, tmpk], w=[Sk])
                    self.tt("dve", Sv, Sv, vkt[0:P], ALU.add, r=[Sk, vkk], w=[Sk])
                    self.tt("dve", tv, Sv, self.bc_free(g("r"), "k"), ALU.mult, r=[Sk] + bkeys, w=[tmpk])
                    self.p.add("dve", lambda e, tv=tv, ob=ob, s_=s_: e.tensor_reduce(out=ob[0:P, s_, :], in_=tv, axis=AX.X, op=ALU.add),
                               r=[tmpk], w=[obk])
                for d in range(2):
                    for b in range(NB):
                        p0 = d * NB * 16 + b * 16
                        if d == 0:
                            tok0, step = soff + i0, D
                        else:
                            tok0, step = soff + n - 1 - i0, -D
                        dst_t = A["o%d" % d]
                        dst = bass.AP(dst_t.tensor, dst_t[b, tok0, :].offset, [[64, 16], [step, SB], [1, 64]])
                        self.dma(dst, ob[p0:p0 + 16, :, :], r=[obk], w=[self.key("ost")], q="pool")

    def rwkv_phase_c(self, li, j, ctx_out, midx, last, A):
        NB, TL, TC, TT = self.NB, self.TL, self.TC, self.TT
        i = self.i
        self.layer_vectors(midx)
        w_o = self.alloc([KC, D], BF16)
        wok = self.key("rwo")
        self.dma(w_o, self.wb["rw_o"].rearrange("(kc p) n -> p kc n", p=128), w=[wok])
        pk = self.key("cparams")
        lngb = self.alloc([D], F32)
        lnbb = self.alloc([D], F32)
        rkb = self.alloc([D], F32)
        self.dma(lngb, self.bcast_row(i["rwkv_ln_g"][j, :], D), w=[pk])
        self.dma(lnbb, self.bcast_row(i["rwkv_ln_b"][j, :], D), w=[pk])
        self.dma(rkb, self.bcast_row(i["rwkv_r_k"][j, :], D), w=[pk])
        self.stage1_setup()
        self.epi_setup()
        L = {n: self.ring("ld_" + n, 2, [D], F32) for n in ("o0", "o1", "r", "kd0", "kd1", "v", "sg")}
        o = self.alloc([D], F32)
        ok_ = self.key("o")
        sq = self.alloc([D], F32)
        sqk = self.key("sq")
        st = self.alloc([64], F32)
        stk = self.key("st")
        y2T_ring = self.ring("y2T", 2, [KC, 128], BF16)
        cnt = 0
        h3 = lambda t: t.rearrange("p (h f) -> p h f", f=64)
        for b in range(NB):
            segs = []
            if ctx_out:
                segs += [(True, t0, t0) for t0 in range(0, TC, 128)]
            segs += [(False, t0, TC + t0) for t0 in range(0, TL, 128)]
            for (is_ctx, t0, tok) in segs:
                ld = {}
                for n in L:
                    t, k = L[n][cnt % 2]
                    self.dma(t, A[n][b, tok:tok + 128, :], w=[k])
                    ld[n] = (t, k)
                y2T, y2Tk = y2T_ring[cnt % 2]
                cnt += 1
                self.tt("pool", o, ld["o0"][0], ld["o1"][0], ALU.add, r=[ld["o0"][1], ld["o1"][1]], w=[ok_])
                mean, ex2, bon, tmp16 = st[:, 0:16], st[:, 16:32], st[:, 32:48], st[:, 48:64]
                self.p.add("dve", lambda e: e.tensor_reduce(out=mean, in_=h3(o), axis=AX.X, op=ALU.add), r=[ok_], w=[stk])
                self.tt("pool", sq, o, o, ALU.mult, r=[ok_], w=[sqk])
                self.p.add("dve", lambda e: e.tensor_reduce(out=ex2, in_=h3(sq), axis=AX.X, op=ALU.add), r=[sqk], w=[stk])
                self.ts("dve", mean, mean, 1.0 / 64, None, ALU.mult, None, r=[stk], w=[stk])
                self.tt("dve", tmp16, mean, mean, ALU.mult, r=[stk], w=[stk])
                self.stt(ex2, ex2, 1.0 / 64, tmp16, ALU.mult, ALU.subtract, r=[stk], w=[stk])
                self.rsqrt(ex2, ex2, GN_EPS, 1.0, r=[stk], w=[stk])
                self.tt("dve", h3(o), h3(o), self.bc_free(mean, "v"), ALU.subtract, r=[ok_, stk], w=[ok_])
                self.tt("pool", h3(o), h3(o), self.bc_free(ex2, "v"), ALU.mult, r=[ok_, stk], w=[ok_])
                self.tt("dve", o, o, lngb, ALU.mult, r=[ok_, pk], w=[ok_])
                self.tt("pool", o, o, lnbb, ALU.add, r=[ok_, pk], w=[ok_])
                self.tt("pool", sq, ld["kd0"][0], ld["kd1"][0], ALU.add, r=[ld["kd0"][1], ld["kd1"][1]], w=[sqk])
                self.tt("dve", sq, sq, ld["r"][0], ALU.mult, r=[sqk, ld["r"][1]], w=[sqk])
                self.tt("pool", sq, sq, rkb, ALU.mult, r=[sqk, pk], w=[sqk])
                self.p.add("dve", lambda e: e.tensor_reduce(out=bon, in_=h3(sq), axis=AX.X, op=ALU.add), r=[sqk], w=[stk])
                self.tt("pool", h3(sq), h3(ld["v"][0]), self.bc_free(bon, "v"), ALU.mult, r=[ld["v"][1], stk, sqk], w=[sqk])
                self.tt("dve", o, o, sq, ALU.add, r=[ok_, sqk], w=[ok_])
                self.tt("pool", o, o, ld["sg"][0], ALU.mult, r=[ok_, ld["sg"][1]], w=[ok_])
                for g in range(2):
                    bank = 6 + g
                    for q in range(4):
                        kc = g * 4 + q
                        self.tr(self.ps[bank][:, q * 128:(q + 1) * 128], o[:, kc * 128:(kc + 1) * 128], r=[ok_], w=[self.psk[bank]])
                    self.p.add("act", lambda e, bank=bank, g=g, y2T=y2T: e.copy(
                        out=y2T[:, g * 4:(g + 1) * 4, :].rearrange("p a b -> p (a b)"), in_=self.ps[bank][:, :]),
                        r=[self.psk[bank]], w=[y2Tk])
                self.epilogue_tile(li, b, is_ctx, t0, lambda kc, y2T=y2T: y2T[:, kc, :], KC, w_o, wok, [y2Tk], last)

    def attn_layer(self, li, j, ctx_out, midx, last):
        assert not ctx_out
        NB, TL, TC, TT = self.NB, self.TL, self.TC, self.TT
        i = self.i
        NKT = TT // 128
        SC = 1.0 / math.sqrt(128.0)
        self.layer_vectors(midx)
        qg = self.alloc([1], F32)
        kg = self.alloc([1], F32)
        SK = globals().get("ATTN_SKIP", "")
        if "q" not in SK:
            self.dma(qg, bass.AP(i["attn_q_g"].tensor, i["attn_q_g"][j, :].offset, [[1, 128], [1, 1]]), w=["qg"])
            self.dma(kg, bass.AP(i["attn_k_g"].tensor, i["attn_k_g"][j, :].offset, [[1, 128], [1, 1]]), w=["kg"])
        cosT = self.alloc([TL], F32)
        sinT = self.alloc([TL], F32)
        if "c" not in SK:
            self.dma(cosT, i["k_cos"], w=["cosT"])
            self.dma(sinT, i["k_sin"], w=["sinT"])
        w_in_v = self.wb["awi"].rearrange("(kc p) n -> p kc n", p=128)
        w_out = self.alloc([16, D], BF16)
        wok = self.key("awo")
        if "w" not in SK:
            self.dma(w_out, self.wb["awo"].rearrange("(h p) n -> p h n", p=128), r=["awo"], w=[wok])
        og_d = self.dram("og_d%d" % li, [16, 128, TL], BF16)
        hT = self.alloc([KC, TT], BF16)
        hk = self.key("hT")
        self.stage1_setup()
        self.epi_setup()
        wring = self.ring("awp", 2, [KC, 768], BF16)
        kT = self.alloc([TT], BF16)
        kTk = self.key("kT")
        vT = self.alloc([NKT, 128], BF16)
        vTk = self.key("vT")
        qT = self.alloc([2, TL], BF16)
        qTk = self.key("qT")
        sgT = self.alloc([2, TL], BF16)
        sgTk = self.key("sgT")
        sq_ring = self.ring("sq", 1, [512], BF16) * 2
        rs_ring = self.ring("rs", 1, [512], F32) * 2
        kn_ring = self.ring("kn", 1, [512], BF16) * 2
        t1_ring = self.ring("rt1", 1, [512], F32) * 2
        t2_ring = self.ring("rt2", 1, [512], F32) * 2
        pT_ring = self.ring("pT", 3, [512], BF16)
        rz = self.alloc([512], F32)
        rzk = self.key("rz")
        o1 = self.alloc([512], F32)
        o1k = self.key("o1")
        og_ring = self.ring("og", 2, [512], BF16)
        ogt_ring = self.ring("ogt", 2, [16, 128], BF16)
        cn = [0]

        def normrope(ps_ap, psk, n, gvec, gk, dest, destk, pos0):
            c = cn[0]
            cn[0] += 1
            sq, sqk = sq_ring[c % 2]
            rs, rsk = rs_ring[c % 2]
            self.act(sq[:, 0:n], ps_ap, AF.Square, r=[psk], w=[sqk])
            self.mm(self.ps[0][:, 0:n], self.ones_h, sq[:, 0:n], True, True, r=[sqk], w=[self.psk[0]])
            self.rsqrt(rs[:, 0:n], self.ps[0][:, 0:n], NORM_EPS, 1.0, r=[self.psk[0]], w=[rsk])
            if pos0 is None:
                self.stt(dest, ps_ap, gvec, rs[:, 0:n], ALU.mult, ALU.mult, r=[psk, rsk, gk], w=[destk])
                return
            kn, knk = kn_ring[c % 2]
            t1, t1k = t1_ring[c % 2]
            t2, t2k = t2_ring[c % 2]
            self.stt(kn[:, 0:n], ps_ap, gvec, rs[:, 0:n], ALU.mult, ALU.mult, r=[psk, rsk, gk], w=[knk])
            self.mm(self.ps[1][:, 0:n], self.perm_b, kn[:, 0:n], True, True, r=[knk], w=[self.psk[1]])
            self.tt("dve", t1[:, 0:n], kn[:, 0:n], cosT[:, pos0:pos0 + n], ALU.mult, r=[knk, "cosT"], w=[t1k])
            self.tt("dve", t2[:, 0:n], self.ps[1][:, 0:n], sinT[:, pos0:pos0 + n], ALU.mult,
                    r=[self.psk[1], "sinT"], w=[t2k])
            self.tt("pool", dest, t1[:, 0:n], t2[:, 0:n], ALU.add, r=[t1k, t2k], w=[destk])

        cw = 0
        ca = 0
        STOP = globals().get("ATTN_STOP", 99)
        if STOP <= 2:
            return
        for b in range(NB):
            self.stage1(li, b, True, hT, hk)
            if STOP <= 3:
                return
            for g in range(8 if STOP > 4 else 1):
                wt, wtk = wring[cw % 2]
                cw += 1
                self.dma(wt[:, :, 0:256], w_in_v[:, :, 256 * g:256 * g + 256], r=["awi"], w=[wtk])
                self.dma(wt[:, :, 256:384], w_in_v[:, :, 2048 + 128 * g:2048 + 128 * g + 128], r=["awi"], w=[wtk])
                self.dma(wt[:, :, 384:512], w_in_v[:, :, 3072 + 128 * g:3072 + 128 * g + 128], r=["awi"], w=[wtk])
                self.dma(wt[:, :, 512:768], w_in_v[:, :, 4096 + 256 * g:4096 + 256 * g + 256], r=["awi"], w=[wtk])
                ktiles = [(o, m, None) for (o, m) in _tiles(TC, 512)] + [(TC + o, m, o) for (o, m) in _tiles(TL, 512)]
                for (tok0, n, pos0) in ktiles:
                    for kc in range(KC):
                        self.mm(self.ps[7][:, 0:n], wt[:, kc, 256:384], hT[:, kc, tok0:tok0 + n], kc == 0, kc == KC - 1,
                                r=[wtk, hk], w=[self.psk[7]])
                    normrope(self.ps[7][:, 0:n], self.psk[7], n, kg, "kg", kT[:, tok0:tok0 + n], kTk, pos0)
                for k0 in range(0, NKT, 4):
                    nk = min(4, NKT - k0)
                    for q in range(nk):
                        kt = k0 + q
                        for kc in range(KC):
                            self.mm(self.ps[2][:, q * 128:(q + 1) * 128], hT[:, kc, kt * 128:(kt + 1) * 128],
                                    wt[:, kc, 384:512], kc == 0, kc == KC - 1, r=[wtk, hk], w=[self.psk[2]])
                    self.p.add("act", lambda e, k0=k0, nk=nk: e.copy(
                        out=vT[:, k0:k0 + nk, :].rearrange("p a b -> p (a b)"), in_=self.ps[2][:, 0:nk * 128]),
                        r=[self.psk[2]], w=[vTk])
                for hq in range(2):
                    for (o, m) in _tiles(TL, 512):
                        for kc in range(KC):
                            self.mm(self.ps[7][:, 0:m], wt[:, kc, hq * 128:(hq + 1) * 128], hT[:, kc, TC + o:TC + o + m],
                                    kc == 0, kc == KC - 1, r=[wtk, hk], w=[self.psk[7]])
                        normrope(self.ps[7][:, 0:m], self.psk[7], m, qg, "qg", qT[:, hq, o:o + m], qTk, o)
                        for kc in range(KC):
                            self.mm(self.ps[2][:, 0:m], wt[:, kc, 512 + hq * 128:512 + (hq + 1) * 128],
                                    hT[:, kc, TC + o:TC + o + m], kc == 0, kc == KC - 1, r=[wtk, hk], w=[self.psk[2]])
                        self.act(sgT[:, hq, o:o + m], self.ps[2][:, 0:m], AF.Silu, r=[self.psk[2]], w=[sgTk])
                if STOP <= 5:
                    continue
                for hq in range(2):
                    for (o, m) in _tiles(TL, 512):
                        bO = 3 + 2 * (ca % 2)
                        bZ = bO + 1
                        og, ogk = og_ring[ca % 2]
                        ca += 1

                        def S(kt):
                            bank = kt % 3
                            self.mm(self.ps[bank][:, 0:m], kT[:, kt * 128:(kt + 1) * 128], qT[:, hq, o:o + m], True, True,
                                    r=[kTk, qTk], w=[self.psk[bank]])
                        S(0)
                        if NKT > 1:
                            S(1)
                        for kt in range(NKT):
                            bank = kt % 3
                            pT, pTk = pT_ring[kt % 3]
                            self.act(pT[:, 0:m], self.ps[bank][:, 0:m], AF.Exp, r=[self.psk[bank]], w=[pTk], scale=SC)
                            self.mm(self.ps[bO][:, 0:m], vT[:, kt, :], pT[:, 0:m], kt == 0, kt == NKT - 1,
                                    r=[vTk, pTk], w=[self.psk[bO]])
                            self.mm(self.ps[bZ][:, 0:m], self.ones_1, pT[:, 0:m], kt == 0, kt == NKT - 1,
                                    r=[pTk], w=[self.psk[bZ]])
                            if kt + 2 < NKT:
                                S(kt + 2)
                        self.p.add("dve", lambda e, bZ=bZ, m=m: e.reciprocal(out=rz[:, 0:m], in_=self.ps[bZ][:, 0:m]),
                                   r=[self.psk[bZ]], w=[rzk])
                        self.tt("dve", o1[:, 0:m], self.ps[bO][:, 0:m], rz[:, 0:m], ALU.mult, r=[self.psk[bO], rzk], w=[o1k])
                        self.tt("pool", og[:, 0:m], o1[:, 0:m], sgT[:, hq, o:o + m], ALU.mult, r=[o1k, sgTk], w=[ogk])
                        self.dma(og_d[2 * g + hq, :, o:o + m], og[:, 0:m], r=[ogk], w=["og_d"], q="pool")
            if STOP <= 6:
                return
            ce = 0
            for t0 in range(0, TL, 128):
                ogt, ogtk = ogt_ring[ce % 2]
                ce += 1
                self.dma(ogt, og_d[:, :, t0:t0 + 128].rearrange("h p t -> p h t"), r=["og_d"], w=[ogtk])
                self.epilogue_tile(li, b, False, t0, lambda kc, ogt=ogt: ogt[:, kc, :], 16, w_out, wok, [ogtk], last)


FULL_LAYERS = [(0, 0, True, True, 0), (1, 0, True, True, 1), (2, 0, True, False, 2), (0, 1, False, False, 3)]


def host_consts(TL):
    ident = np.eye(128, dtype=np.float32)
    perm = np.zeros((128, 128), np.float32)
    for k in range(128):
        blk = k // 32
        if blk % 2 == 0:
            perm[k, k + 32] = 1.0
        else:
            perm[k, k - 32] = -1.0
    rows = TL // GRID_W
    t = np.arange(TL)
    row = (t // GRID_W).astype(np.float32)
    col = (t % GRID_W).astype(np.float32)
    inv = (1.0 / (10000.0 ** (np.arange(0, 64, 2, dtype=np.float32) / np.float32(64)))).astype(np.float32)
    cosT = np.zeros((128, TL), np.float32)
    sinT = np.zeros((128, TL), np.float32)
    for p in range(128):
        pos = row if p < 64 else col
        ang = (pos * inv[p % 32]).astype(np.float32)
        cosT[p] = np.cos(ang)
        sinT[p] = np.sin(ang)
    return {"k_ident": ident, "k_perm": perm, "k_cos": cosT, "k_sin": sinT}


def make_in_maps(inputs, n_cores, NB, TL):
    consts = host_consts(TL)
    maps = []
    for cid in range(n_cores):
        m = {}
        for k, v in inputs.items():
            v = np.asarray(v)
            if k in ("x", "c", "ctx"):
                m[k] = np.ascontiguousarray(v[cid * NB:(cid + 1) * NB])
            elif k in ("c_ctx", "final_g"):
                m[k] = np.ascontiguousarray(v.reshape(1, -1))
            elif k == "rwkv_r_k":
                m[k] = np.ascontiguousarray(v.reshape(v.shape[0], -1))
            else:
                m[k] = np.ascontiguousarray(v)
        m.update(consts)
        maps.append(m)
    return maps


_CACHE = {}


def run(inputs, layers, n_cores=8, trace=False, dbg=False):
    x = np.asarray(inputs["x"])
    B, TL, _ = x.shape
    TC = np.asarray(inputs["ctx"]).shape[1]
    NB = B // n_cores
    mk = MK(NB, TL, TC, layers, dbg=dbg)
    nc = mk.build()
    maps = make_in_maps(inputs, n_cores, NB, TL)
    res = run_bass_kernel_spmd(nc, maps, core_ids=list(range(n_cores)), trace=trace)
    out = np.concatenate([np.asarray(r["out"]) for r in res.results], axis=0)
    return out.astype(np.float32), res


def kernel(**inputs):
    out, _ = run(inputs, FULL_LAYERS, n_cores=8)
    return out
```

```python
import contextlib
import math
import numpy as np
import ml_dtypes
import concourse.bass as bass
import concourse.mybir as mybir
from concourse.bass_utils import run_bass_kernel_spmd
from concourse.alu_op_type import AluOpType as ALU

F32 = mybir.dt.float32
BF16 = mybir.dt.bfloat16
AF = mybir.ActivationFunctionType
AX = mybir.AxisListType

D = 1024
KC = 8
GRID_W = 64
CONV_W = 31
HALO = 15
NORM_EPS = 1e-6
LN_EPS = 1e-5
GN_EPS = 64e-5

ENG = ("pe", "act", "dve", "pool", "sp")
NDMASEM = 12


class Op:
    __slots__ = ("eng", "fn", "dma", "deps", "needs_inc", "cnt", "slot", "val", "idx")


class Prog:
    def __init__(self, nc):
        self.nc = nc
        self.ops = []
        self.by_eng = {e: [] for e in ENG}
        self.last_w = {}
        self.readers = {}
        self.ndma = {e: 0 for e in ENG}
        self.fence_deps = []
        self.fence_pending = set()

    def fence(self):
        deps = []
        for e in ENG:
            lst = self.by_eng[e]
            for op in reversed(lst):
                if not op.dma:
                    deps.append(op.idx)
                    break
            cnt = 0
            for op in reversed(lst):
                if op.dma:
                    deps.append(op.idx)
                    cnt += 1
                    if cnt >= NDMASEM:
                        break
        self.fence_deps = deps
        self.fence_pending = set(ENG)
        self.last_w.clear()
        self.readers.clear()

    def add(self, eng, fn, r=(), w=(), dma=False):
        op = Op()
        op.eng, op.fn, op.dma = eng, fn, dma
        op.needs_inc = False
        op.cnt = op.slot = op.val = None
        op.idx = len(self.ops)
        deps = set()
        lw = self.last_w
        for k in r:
            y = lw.get(k)
            if y is not None:
                deps.add(y)
        for k in w:
            y = lw.get(k)
            if y is not None:
                deps.add(y)
            ys = self.readers.get(k)
            if ys:
                deps.update(ys)
        keep = []
        for yi in deps:
            y = self.ops[yi]
            if (not y.dma) and (not dma) and y.eng == eng:
                if eng == "pe":
                    continue
                raw = False
                for k in r:
                    if lw.get(k) == yi:
                        raw = True
                        break
                if not raw:
                    continue
            keep.append(yi)
        if eng in self.fence_pending:
            self.fence_pending.discard(eng)
            keep = list(set(keep) | set(self.fence_deps))
        op.deps = keep
        for yi in keep:
            self.ops[yi].needs_inc = True
        if dma:
            n = self.ndma[eng]
            self.ndma[eng] = n + 1
            op.slot = n % NDMASEM
            op.val = 16 * (n // NDMASEM + 1)
        for k in r:
            self.readers.setdefault(k, []).append(op.idx)
        for k in w:
            lw[k] = op.idx
            self.readers[k] = []
        self.ops.append(op)
        self.by_eng[eng].append(op)
        return op

    def emit(self):
        nc = self.nc
        for e in ENG:
            c = 0
            for op in self.by_eng[e]:
                if (not op.dma) and op.needs_inc:
                    c += 1
                    op.cnt = c
        with contextlib.ExitStack() as st:
            csem = {e: st.enter_context(nc.semaphore("c_" + e)) for e in ENG}
            dsem = {e: [st.enter_context(nc.semaphore("d_%s_%d" % (e, i))) for i in range(NDMASEM)]
                    for e in ENG if self.ndma[e] > 0}
            block = st.enter_context(nc.Block())
            handles = {"pe": block.tensor, "act": block.scalar, "dve": block.vector,
                       "pool": block.gpsimd, "sp": block.sync}
            ops = self.ops

            def make(e):
                def body(eng):
                    waited = {}
                    for op in self.by_eng[e]:
                        need = {}
                        for yi in op.deps:
                            y = ops[yi]
                            if y.dma:
                                key = ("d", y.eng, y.slot)
                                v = y.val
                            else:
                                key = ("c", y.eng)
                                v = y.cnt
                            if need.get(key, 0) < v:
                                need[key] = v
                        if op.dma and op.val > 16:
                            key = ("d", e, op.slot)
                            if need.get(key, 0) < op.val - 16:
                                need[key] = op.val - 16
                        for key, v in need.items():
                            if waited.get(key, 0) >= v:
                                continue
                            waited[key] = v
                            sem = csem[key[1]] if key[0] == "c" else dsem[key[1]][key[2]]
                            eng.wait_ge(sem, v)
                        ins = op.fn(eng)
                        if op.dma:
                            ins.then_inc(dsem[e][op.slot], 16)
                        elif op.needs_inc:
                            ins.then_inc(csem[e], 1)
                    if e == "sp":
                        for q in ENG:
                            n = self.ndma[q]
                            for s in range(min(n, NDMASEM)):
                                cnt = (n - 1 - s) // NDMASEM + 1
                                if waited.get(("d", q, s), 0) < 16 * cnt:
                                    eng.wait_ge(dsem[q][s], 16 * cnt)
                        for q in ENG:
                            last = None
                            for op in self.by_eng[q]:
                                if op.cnt is not None:
                                    last = op.cnt
                            if last is not None and q != "sp":
                                eng.wait_ge(csem[q], last)
                return body

            for e in ENG:
                if self.by_eng[e] or e == "sp":
                    handles[e](make(e))


def _tiles(n, t):
    out = []
    o = 0
    while o < n:
        m = min(t, n - o)
        out.append((o, m))
        o += m
    return out


class MK:
    def __init__(self, NB, TL, TC, layers, dbg=False):
        self.NB, self.TL, self.TC, self.layers = NB, TL, TC, layers
        self.TT = TL + TC
        self.NV = NB + 1
        self.nc = bass.Bass("TRN2", target_bir_lowering=False)
        self.p = Prog(self.nc)
        self.st = contextlib.ExitStack()
        self.uid = 0
        self.dbg = dbg

    def dram_in(self, name, shape, dt=F32):
        return self.nc.dram_tensor(name, list(shape), dt, kind="ExternalInput").ap()

    def dram(self, name, shape, dt=F32):
        if self.dbg:
            return self.nc.dram_tensor(name, list(shape), dt, kind="ExternalOutput").ap()
        return self.nc.dram_tensor(name, list(shape), dt).ap()

    def alloc(self, free, dt=F32):
        n = 1
        for f in free:
            n *= f
        units = n * 2 if dt == F32 else n
        self.top = (self.top + 15) // 16 * 16
        assert self.top + units <= self.AREN, ("arena overflow", self.top, units, self.AREN)
        v = self.arena[:, self.top:self.top + units]
        self.top += units
        if dt == F32:
            v = v.bitcast(F32)
        if len(free) == 2:
            v = v.rearrange("p (a b) -> p a b", a=free[0])
        elif len(free) == 3:
            v = v.rearrange("p (a b c) -> p a b c", a=free[0], b=free[1])
        return v

    def key(self, s):
        self.uid += 1
        return "%s#%d" % (s, self.uid)

    def ring(self, name, n, free, dt=F32):
        return [(self.alloc(free, dt), self.key(name)) for _ in range(n)]

    def dma(self, out, in_, r=(), w=(), q="sp", **kw):
        return self.p.add(q, lambda e: e.dma_start(out=out, in_=in_, **kw), r=r, w=w, dma=True)

    def mm(self, out, lhsT, rhs, start, stop, r, w):
        return self.p.add("pe", lambda e: e.matmul(out, lhsT=lhsT, rhs=rhs, start=start, stop=stop), r=r, w=w)

    def tr(self, out, in_, r, w):
        P = in_.shape[0]
        ident = self.ident_f[0:P, 0:P]
        return self.p.add("pe", lambda e: e.transpose(out=out, in_=in_, identity=ident), r=r, w=w)

    def act(self, out, in_, func, r, w, scale=None, bias=None):
        kw = {}
        if scale is not None:
            kw["scale"] = scale
        if bias is not None:
            kw["bias"] = bias
        return self.p.add("act", lambda e: e.activation(out=out, in_=in_, func=func, **kw), r=r, w=w)

    def tt(self, eng, out, in0, in1, op, r, w):
        return self.p.add(eng, lambda e: e.tensor_tensor(out=out, in0=in0, in1=in1, op=op), r=r, w=w)

    def ts(self, eng, out, in0, s1, s2, op0, op1, r, w):
        if op1 is None:
            return self.p.add(eng, lambda e: e.tensor_scalar(out=out, in0=in0, scalar1=s1, scalar2=None, op0=op0), r=r, w=w)
        return self.p.add(eng, lambda e: e.tensor_scalar(out=out, in0=in0, scalar1=s1, scalar2=s2, op0=op0, op1=op1), r=r, w=w)

    def stt(self, out, in0, scalar, in1, op0, op1, r, w):
        return self.p.add("dve", lambda e: e.scalar_tensor_tensor(out=out, in0=in0, scalar=scalar, in1=in1, op0=op0, op1=op1), r=r, w=w)

    def rsqrt(self, out, in_, eps, mul, r, w):
        kt = self.key("rs")
        self.ts("dve", out, in_, mul, eps, ALU.mult, ALU.add, r=r, w=[kt])
        self.p.add("act", lambda e: e.activation(out=out, in_=out, func=AF.Sqrt), r=[kt], w=[kt])
        self.p.add("dve", lambda e: e.reciprocal(out=out, in_=out), r=[kt], w=w)

    def bcast_row(self, dram_ap_row, n):
        return bass.AP(dram_ap_row.tensor, dram_ap_row.offset, [[0, 128], [1, n]])

    def build(self):
        nc, NB, TL, TC, TT = self.nc, self.NB, self.TL, self.TC, self.TT
        NLAY = 4
        i = {}
        i["x"] = self.dram_in("x", [NB, TL, D])
        i["c"] = self.dram_in("c", [NB, D])
        i["ctx"] = self.dram_in("ctx", [NB, TC, D])
        i["c_ctx"] = self.dram_in("c_ctx", [1, D])
        i["norm_g"] = self.dram_in("norm_g", [NLAY, D])
        i["mod_w"] = self.dram_in("mod_w", [NLAY, D, 3 * D])
        i["mod_b"] = self.dram_in("mod_b", [NLAY, 3 * D])
        i["conv_w_in"] = self.dram_in("conv_w_in", [2, D, 3 * D])
        i["conv_dw"] = self.dram_in("conv_dw", [2, CONV_W, D])
        i["conv_db"] = self.dram_in("conv_db", [2, D])
        i["conv_ln_g"] = self.dram_in("conv_ln_g", [2, D])
        i["conv_ln_b"] = self.dram_in("conv_ln_b", [2, D])
        i["conv_w_out"] = self.dram_in("conv_w_out", [2, D, D])
        i["rwkv_mu"] = self.dram_in("rwkv_mu", [1, 6, D])
        for nm in ("rwkv_w_r", "rwkv_w_k", "rwkv_w_v", "rwkv_w_g", "rwkv_w_o"):
            i[nm] = self.dram_in(nm, [1, D, D])
        i["rwkv_w0"] = self.dram_in("rwkv_w0", [1, 2, D])
        i["rwkv_w1"] = self.dram_in("rwkv_w1", [1, 2, D, 64])
        i["rwkv_w2"] = self.dram_in("rwkv_w2", [1, 2, 64, D])
        i["rwkv_a0"] = self.dram_in("rwkv_a0", [1, 2, D])
        i["rwkv_a1"] = self.dram_in("rwkv_a1", [1, 2, D, 64])
        i["rwkv_a2"] = self.dram_in("rwkv_a2", [1, 2, 64, D])
        i["rwkv_k_k"] = self.dram_in("rwkv_k_k", [1, D])
        i["rwkv_k_a"] = self.dram_in("rwkv_k_a", [1, D])
        i["rwkv_r_k"] = self.dram_in("rwkv_r_k", [1, D])
        i["rwkv_ln_g"] = self.dram_in("rwkv_ln_g", [1, D])
        i["rwkv_ln_b"] = self.dram_in("rwkv_ln_b", [1, D])
        i["attn_w_in"] = self.dram_in("attn_w_in", [1, D, 6 * D])
        i["attn_q_g"] = self.dram_in("attn_q_g", [1, 128])
        i["attn_k_g"] = self.dram_in("attn_k_g", [1, 128])
        i["attn_w_out"] = self.dram_in("attn_w_out", [1, 2 * D, D])
        i["final_g"] = self.dram_in("final_g", [1, D])
        i["k_ident"] = self.dram_in("k_ident", [128, 128])
        i["k_perm"] = self.dram_in("k_perm", [128, 128])
        i["k_cos"] = self.dram_in("k_cos", [128, TL])
        i["k_sin"] = self.dram_in("k_sin", [128, TL])
        self.i = i
        self.out = nc.dram_tensor("out", [NB, TL, D], F32, kind="ExternalOutput").ap()
        self.xs = self.dram("xs", [NB, TL, D])
        if self.dbg:
            self.xcs = nc.dram_tensor("xcs", [NB, TC, D], F32, kind="ExternalOutput").ap()
        else:
            self.xcs = self.dram("xcs", [NB, TC, D])
        self.m_d = self.dram("m_d", [NLAY, self.NV, 3 * D])

        st = self.st
        with st:
            self.AREN = 106000
            self.arena = st.enter_context(nc.sbuf_tensor("arena", [128, self.AREN], BF16))
            self.top = 0
            self.ps = [st.enter_context(nc.psum_tensor("ps%d" % k, [128, 512], F32)) for k in range(8)]
            self.psk = ["ps%d" % k for k in range(8)]
            self.prologue()
            self.persist_top = self.top
            used = set(l[0] for l in self.layers)
            self.convert_weights(used)
            self.modulation()
            nl = len(self.layers)
            for li, (kind, j, ctx_in, ctx_out, midx) in enumerate(self.layers):
                self.p.fence()
                self.top = self.persist_top
                last = li == nl - 1
                if kind == 0:
                    self.conv_layer(li, j, ctx_out, midx, last)
                elif kind == 1:
                    self.rwkv_layer(li, j, ctx_out, midx, last)
                else:
                    self.attn_layer(li, j, ctx_out, midx, last)
            if nl == 0:
                self.p.fence()
                self.top = self.persist_top
                self.only_final()
            self.p.emit()
        return nc

    def prologue(self):
        i = self.i
        self.ident_f = self.alloc([128], F32)
        self.dma(self.ident_f, i["k_ident"], w=["ident_f"])
        self.ident_b = self.alloc([128], BF16)
        self.p.add("dve", lambda e: e.tensor_copy(out=self.ident_b, in_=self.ident_f), r=["ident_f"], w=["ident_b"])
        permf = self.alloc([128], F32)
        self.dma(permf, i["k_perm"], w=["permf"])
        self.perm_b = self.alloc([128], BF16)
        self.p.add("dve", lambda e: e.tensor_copy(out=self.perm_b, in_=permf), r=["permf"], w=["perm_b"])
        self.ones_d = self.alloc([128], BF16)
        self.ones_h = self.alloc([128], BF16)
        self.ones_1 = self.alloc([128], BF16)
        self.p.add("pool", lambda e: e.memset(self.ones_d, 1.0 / D), w=["ones_d"])
        self.p.add("pool", lambda e: e.memset(self.ones_h, 1.0 / 128), w=["ones_h"])
        self.p.add("pool", lambda e: e.memset(self.ones_1, 1.0), w=["ones_1"])
        self.fg_b = self.alloc([D], F32)
        self.dma(self.fg_b, self.bcast_row(i["final_g"][0, :], D), w=["fg_b"])
        self.constkeys = ["ident_f", "ident_b", "perm_b", "ones_d", "ones_h", "ones_1", "fg_b"]

    def after_fence_consts(self):
        return

    def convert_weights(self, used):
        i = self.i
        self.wb = {}

        def conv(name, src, rows, cols):
            dst = self.dram(name, [rows, cols], BF16)
            for r0 in range(0, rows, 256):
                rr = min(256, rows - r0)
                self.dma(dst[r0:r0 + rr, :], src[r0:r0 + rr, :], w=[name], q="pool")
            self.wb[name] = dst

        if 0 in used:
            for j in sorted(set(l[1] for l in self.layers if l[0] == 0)):
                conv("cwi%d" % j, i["conv_w_in"][j], D, 3 * D)
                conv("cwo%d" % j, i["conv_w_out"][j], D, D)
        if 1 in used:
            for nm in ("w_r", "w_k", "w_v", "w_g", "w_o"):
                conv("r" + nm, i["rwkv_" + nm][0], D, D)
            for dd in range(2):
                conv("rw1_%d" % dd, i["rwkv_w1"][0, dd], D, 64)
                conv("ra1_%d" % dd, i["rwkv_a1"][0, dd], D, 64)
                conv("rw2_%d" % dd, i["rwkv_w2"][0, dd], 64, D)
                conv("ra2_%d" % dd, i["rwkv_a2"][0, dd], 64, D)
        if 2 in used:
            conv("awi", i["attn_w_in"][0], D, 6 * D)
            conv("awo", i["attn_w_out"][0], 2 * D, D)

    def modulation(self):
        i, NV, NB = self.i, self.NV, self.NB
        top0 = self.top
        crow = self.alloc([D], F32)
        self.dma(crow[0:NB, :], i["c"], w=["crow"])
        self.dma(crow[NB:NV, :], i["c_ctx"], w=["crow"])
        self.act(crow[0:NV, :], crow[0:NV, :], AF.Silu, r=["crow"], w=["crow"])
        scT = self.alloc([KC, NV], F32)
        for kc in range(KC):
            self.tr(self.ps[0][:, kc * NV:(kc + 1) * NV], crow[0:NV, kc * 128:(kc + 1) * 128],
                    r=["crow", "ident_f"], w=["ps0"])
        self.p.add("dve", lambda e: e.tensor_copy(out=scT.rearrange("p a b -> p (a b)"), in_=self.ps[0][:, 0:KC * NV]),
                   r=["ps0"], w=["scT"])
        wring = self.ring("modw", 2, [KC, 512], F32)
        mrow = self.alloc([3 * D], F32)
        brow = self.alloc([3 * D], F32)
        used_mod = sorted(set(l[4] for l in self.layers))
        cnt = 0
        for l in used_mod:
            self.dma(brow[0:NV, :], bass.AP(i["mod_b"].tensor, i["mod_b"][l, :].offset, [[0, NV], [1, 3 * D]]),
                     r=[], w=["brow"])
            for pn in range(6):
                wt, wk = wring[cnt % 2]
                bank = 1 + cnt % 2
                cnt += 1
                self.dma(wt, i["mod_w"][l][:, pn * 512:(pn + 1) * 512].rearrange("(kc p) n -> p kc n", p=128), w=[wk])
                for kc in range(KC):
                    self.mm(self.ps[bank][0:NV, :], scT[:, kc, :], wt[:, kc, :], kc == 0, kc == KC - 1,
                            r=[wk, "scT"], w=[self.psk[bank]])
                self.tt("dve", mrow[0:NV, pn * 512:(pn + 1) * 512], self.ps[bank][0:NV, :],
                        brow[0:NV, pn * 512:(pn + 1) * 512], ALU.add, r=[self.psk[bank], "brow"], w=["mrow"])
            self.dma(self.m_d[l], mrow[0:NV, :], r=["mrow"], w=["m_d"], q="pool")
        self.top = top0

    def rows_to_cols(self, rows_aps, name):
        R = len(rows_aps)
        rt = self.alloc([D], F32)
        kr = self.key(name + "_rows")
        for r_, ap in enumerate(rows_aps):
            self.dma(rt[r_:r_ + 1, :], bass.AP(ap.tensor, ap.offset, [[0, 1], [1, D]]), r=["m_d"], w=[kr])
        colsT = self.alloc([KC, R], F32)
        kc_ = self.key(name + "_cols")
        assert KC * R <= 512
        for kc in range(KC):
            self.tr(self.ps[7][:, kc * R:(kc + 1) * R], rt[0:R, kc * 128:(kc + 1) * 128], r=[kr], w=["ps7"])
        self.p.add("dve", lambda e: e.tensor_copy(out=colsT.rearrange("p a b -> p (a b)"), in_=self.ps[7][:, 0:KC * R]),
                   r=["ps7"], w=[kc_])
        return colsT, kc_

    def layer_vectors(self, midx):
        i, NV = self.i, self.NV
        rows = [i["norm_g"][midx, :]]
        for v in range(NV):
            rows.append(self.m_d[midx, v, 0:D])
        for v in range(NV):
            rows.append(self.m_d[midx, v, D:2 * D])
        cols, ck = self.rows_to_cols(rows, "lv")
        gsT = self.alloc([KC, NV], F32)
        kg = self.key("gsT")
        for v in range(NV):
            self.p.add("dve", lambda e, v=v: e.scalar_tensor_tensor(
                out=gsT[:, :, v], in0=cols[:, :, 1 + NV + v], scalar=1.0, in1=cols[:, :, 0],
                op0=ALU.add, op1=ALU.mult), r=[ck], w=[kg])
        self.gsT, self.gsk = gsT, kg
        self.shT, self.shk = cols, ck
        self.midx = midx
        self.gate_tiles = {}

    def gate_b(self, v):
        gt = self.gate_tiles
        if v in gt:
            ent = gt.pop(v)
            gt[v] = ent
            return ent
        if len(gt) < 2:
            ent = (self.alloc([D], F32), self.key("gate_b"))
        else:
            old_v = next(iter(gt))
            ent = gt.pop(old_v)
        self.dma(ent[0], self.bcast_row(self.m_d[self.midx, v, 2 * D:3 * D], D), r=["m_d"], w=[ent[1]])
        gt[v] = ent
        return ent

    def stage1_setup(self, share=None):
        if share is None:
            self.xt_ring = self.ring("xt", 2, [D], F32)
            self.xn = self.alloc([D], F32)
            self.xnk = self.key("xn")
            self.junk = self.alloc([D], F32)
            self.junkk = self.key("junk")
        else:
            self.xt_ring = [share[0], share[1]]
            self.xn, self.xnk = share[2]
            self.junk, self.junkk = share[3]
        self.ss = self.alloc([4], F32)
        self.ssk = self.key("ss")
        self.xt_cnt = 0
        self.coff, self.loff = 0, self.TC

    def sumsq(self, x, ss, rk):
        junk = self.junk
        self.p.add("act", lambda e: e.activation(out=junk, in_=x, func=AF.Square, accum_out=ss),
                   r=rk, w=[self.junkk, self.ssk])

    def src_tile(self, li, b, is_ctx, t0, n=128):
        if li == 0:
            return (self.i["ctx"] if is_ctx else self.i["x"])[b, t0:t0 + n, :]
        return (self.xcs if is_ctx else self.xs)[b, t0:t0 + n, :]

    def xkey(self, b, is_ctx, t0):
        return "x_%d_%d_%d" % (b, int(is_ctx), t0 // 128)

    def stage1(self, li, b, do_ctx, hT, hk):
        TC, TL, NB = self.TC, self.TL, self.NB
        segs = []
        if do_ctx:
            segs += [(True, t0) for t0 in range(0, TC, 128)]
        segs += [(False, t0) for t0 in range(0, TL, 128)]
        for (is_ctx, t0) in segs:
            v = NB if is_ctx else b
            tok = self.coff + t0 if is_ctx else self.loff + t0
            xt, xk = self.xt_ring[self.xt_cnt % 2]
            self.xt_cnt += 1
            self.dma(xt, self.src_tile(li, b, is_ctx, t0), r=[self.xkey(b, is_ctx, t0)], w=[xk])
            ss = self.ss[:, 0:1]
            self.sumsq(xt, ss, [xk])
            self.rsqrt(ss, ss, NORM_EPS, 1.0 / D, r=[self.ssk], w=[self.ssk])
            xn = self.xn
            self.p.add("act", lambda e, xt=xt, ss=ss, xn=xn: e.activation(out=xn, in_=xt, func=AF.Identity, scale=ss),
                       r=[xk, self.ssk], w=[self.xnk])
            for g in range(2):
                bank = 6 + g
                for q in range(4):
                    kc = g * 4 + q
                    self.tr(self.ps[bank][:, q * 128:(q + 1) * 128], self.xn[:, kc * 128:(kc + 1) * 128],
                            r=[self.xnk], w=[self.psk[bank]])
                for q in range(4):
                    kc = g * 4 + q
                    self.ts("dve", hT[:, kc, tok:tok + 128], self.ps[bank][:, q * 128:(q + 1) * 128],
                            self.gsT[:, kc, v:v + 1], self.shT[:, kc, 1 + v:2 + v], ALU.mult, ALU.add,
                            r=[self.psk[bank], self.gsk, self.shk], w=[hk])

    def epi_setup(self):
        self.tmp = self.alloc([D], F32)
        self.tmpk = self.key("tmp")
        self.xnew_ring = self.ring("xnew", 2, [D], F32)
        self.epi_cnt = 0

    def epilogue_tile(self, li, b, is_ctx, t0, yT_fn, nkc, w_out, wok, ykeys, last, banks=(4, 5)):
        v = self.NB if is_ctx else b
        gb, gk = self.gate_b(v)
        xt, xk = self.xt_ring[self.xt_cnt % 2]
        self.xt_cnt += 1
        xkey = self.xkey(b, is_ctx, t0)
        self.dma(xt, self.src_tile(li, b, is_ctx, t0), r=[xkey], w=[xk])
        xn_, xnk_ = self.xnew_ring[self.epi_cnt % 2]
        self.epi_cnt += 1
        for half in range(2):
            bank = banks[half]
            for kc in range(nkc):
                self.mm(self.ps[bank][:, :], yT_fn(kc), w_out[:, kc, half * 512:(half + 1) * 512],
                        kc == 0, kc == nkc - 1, r=ykeys + [wok], w=[self.psk[bank]])
            hs = slice(half * 512, (half + 1) * 512)
            self.tt("dve", self.tmp[:, hs], self.ps[bank][:, :], gb[:, hs], ALU.mult,
                    r=[self.psk[bank], gk], w=[self.tmpk])
            self.tt("pool", xn_[:, hs], self.tmp[:, hs], xt[:, hs], ALU.add, r=[self.tmpk, xk], w=[xnk_])
        if last and not is_ctx:
            ss = self.ss[:, 1:2]
            self.sumsq(xn_, ss, [xnk_])
            self.rsqrt(ss, ss, NORM_EPS, 1.0 / D, r=[self.ssk], w=[self.ssk])
            self.stt(self.tmp, xn_, ss, self.fg_b, ALU.mult, ALU.mult, r=[xnk_, self.ssk], w=[self.tmpk])
            self.dma(self.out[b, t0:t0 + 128, :], self.tmp, r=[self.tmpk], w=["out"], q="pool")
        else:
            dst = (self.xcs if is_ctx else self.xs)[b, t0:t0 + 128, :]
            self.dma(dst, xn_, r=[xnk_], w=[xkey], q="pool")

    def only_final(self):
        self.stage1_setup()
        tmp = self.alloc([D], F32)
        for b in range(self.NB):
            for t0 in range(0, self.TL, 128):
                xt, xk = self.xt_ring[self.xt_cnt % 2]
                self.xt_cnt += 1
                self.dma(xt, self.i["x"][b, t0:t0 + 128, :], w=[xk])
                ss = self.ss[:, 1:2]
                self.sumsq(xt, ss, [xk])
                self.rsqrt(ss, ss, NORM_EPS, 1.0 / D, r=[self.ssk], w=[self.ssk])
                self.stt(tmp, xt, ss, self.fg_b, ALU.mult, ALU.mult, r=[xk, self.ssk], w=["tmpf"])
                self.dma(self.out[b, t0:t0 + 128, :], tmp, r=["tmpf"], w=["out"], q="pool")

    def conv_layer(self, li, j, ctx_out, midx, last):
        NB, TL, TC, TT = self.NB, self.TL, self.TC, self.TT
        i = self.i
        T2 = 256
        self.layer_vectors(midx)
        rows = [i["conv_dw"][j, t, :] for t in range(CONV_W)]
        rows += [i["conv_db"][j, :], i["conv_ln_g"][j, :], i["conv_ln_b"][j, :]]
        cv, cvk = self.rows_to_cols(rows, "cv")
        w_in = self.wb["cwi%d" % j]
        w_in_v = w_in.rearrange("(kc p) n -> p kc n", p=128)
        w_out = self.alloc([KC, D], BF16)
        wok = self.key("cwo")
        self.dma(w_out, self.wb["cwo%d" % j].rearrange("(kc p) n -> p kc n", p=128), r=["cwo%d" % j], w=[wok])
        w_g = self.alloc([KC, D], BF16)
        wgk = self.key("cwg")
        self.dma(w_g, w_in_v[:, :, 2 * D:3 * D], r=["cwi%d" % j], w=[wgk])
        hT = self.alloc([KC, TT], BF16)
        hk = self.key("hT")
        self.stage1_setup()
        self.epi_setup()
        seqs = []
        if ctx_out:
            seqs.append((True, 0, TC))
        seqs.append((False, TC, TL))
        GLW = sum(n + 2 * HALO for (_, _, n) in seqs)
        glu_d = self.dram("glu_d%d" % li, [KC, 128, GLW], BF16)
        wp_ring = self.ring("wp", 2, [KC, 2, 128], BF16)
        sig_ring = self.ring("sig", 2, [512], F32)
        gl_ring = self.ring("gl", 2, [GLW], BF16)
        for (t, k) in gl_ring:
            self.p.add("pool", lambda e, t=t: e.memset(t, 0.0), w=[k])
        glt_ring = self.ring("glt", 2, [KC, T2 + 2 * HALO], BF16)
        dg_ring = self.ring("dg", 2, [CONV_W, 128], BF16)
        sgt_ring = self.ring("sgt", 2, [T2], BF16)
        yb = self.alloc([KC, T2], BF16)
        ybk = self.key("yb")
        ysq = self.alloc([KC, T2], BF16)
        ysqk = self.key("ysq")
        y2_ring = self.ring("y2", 2, [KC, T2], BF16)
        mean = self.alloc([T2], F32)
        meank = self.key("mean")
        rstd = self.alloc([T2], F32)
        rstdk = self.key("rstd")
        t1_ring = self.ring("t1", 2, [T2], F32)
        s_ring = self.ring("s", 2, [T2], F32)
        cnt1 = 0
        cnt2 = 0
        for b in range(NB):
            self.stage1(li, b, ctx_out, hT, hk)
            for fc in range(KC):
                wp, wpk = wp_ring[fc % 2]
                for ab in range(2):
                    self.dma(wp[:, :, ab, :], w_in_v[:, :, ab * D + fc * 128: ab * D + (fc + 1) * 128],
                             r=["cwi%d" % j], w=[wpk])
                gl, glk = gl_ring[fc % 2]
                col = 0
                for (is_ctx, tok0, n) in seqs:
                    for (o, m) in _tiles(n, 512):
                        bA = (cnt1 % 2) * 2
                        bB = bA + 1
                        sg, sgk = sig_ring[cnt1 % 2]
                        cnt1 += 1
                        for ab, bank in ((0, bA), (1, bB)):
                            for kc in range(KC):
                                self.mm(self.ps[bank][:, 0:m], wp[:, kc, ab, :], hT[:, kc, tok0 + o: tok0 + o + m],
                                        kc == 0, kc == KC - 1, r=[wpk, hk], w=[self.psk[bank]])
                        self.act(sg[:, 0:m], self.ps[bB][:, 0:m], AF.Sigmoid, r=[self.psk[bB]], w=[sgk])
                        c0 = col + HALO + o
                        self.tt("dve", gl[:, c0:c0 + m], self.ps[bA][:, 0:m], sg[:, 0:m], ALU.mult,
                                r=[self.psk[bA], sgk], w=[glk])
                    col += n + 2 * HALO
                self.dma(glu_d[fc], gl, r=[glk], w=["glu_d"], q="pool")
            col = 0
            for (is_ctx, tok0, n) in seqs:
                for (o, m) in _tiles(n, T2):
                    glt, gltk = glt_ring[cnt2 % 2]
                    y2, y2k = y2_ring[cnt2 % 2]
                    cnt2 += 1
                    c0 = col + o
                    self.dma(glt[:, :, 0:m + 2 * HALO], glu_d[:, :, c0:c0 + m + 2 * HALO].rearrange("f p c -> p f c"),
                             r=["glu_d"], w=[gltk])
                    for fc in range(KC):
                        dg, dgk = dg_ring[fc % 2]
                        for t in range(CONV_W):
                            if t % 2 == 0:
                                self.act(dg[:, t, :], self.ident_b, AF.Copy, r=[cvk], w=[dgk], scale=cv[:, fc, t:t + 1])
                            else:
                                self.ts("dve", dg[:, t, :], self.ident_b, cv[:, fc, t:t + 1], None, ALU.mult, None,
                                        r=[cvk], w=[dgk])
                        bank = fc % 2
                        for t in range(CONV_W):
                            self.mm(self.ps[bank][:, 0:m], dg[:, t, :], glt[:, fc, t:t + m], t == 0, t == CONV_W - 1,
                                    r=[dgk, gltk], w=[self.psk[bank]])
                        self.act(yb[:, fc, 0:m], self.ps[bank][:, 0:m], AF.Identity, r=[self.psk[bank], cvk], w=[ybk],
                                 bias=cv[:, fc, CONV_W:CONV_W + 1])
                        self.act(ysq[:, fc, 0:m], self.ps[bank][:, 0:m], AF.Square, r=[self.psk[bank], cvk], w=[ysqk],
                                 bias=cv[:, fc, CONV_W:CONV_W + 1])
                    for fc in range(KC):
                        self.mm(self.ps[2][:, 0:m], self.ones_d, yb[:, fc, 0:m], fc == 0, fc == KC - 1,
                                r=[ybk], w=[self.psk[2]])
                    for fc in range(KC):
                        self.mm(self.ps[3][:, 0:m], self.ones_d, ysq[:, fc, 0:m], fc == 0, fc == KC - 1,
                                r=[ysqk], w=[self.psk[3]])
                    self.p.add("act", lambda e, m=m: e.copy(out=mean[:, 0:m], in_=self.ps[2][:, 0:m]),
                               r=[self.psk[2]], w=[meank])
                    self.tt("dve", rstd[:, 0:m], mean[:, 0:m], mean[:, 0:m], ALU.mult, r=[meank], w=[rstdk])
                    self.tt("dve", rstd[:, 0:m], self.ps[3][:, 0:m], rstd[:, 0:m], ALU.subtract,
                            r=[self.psk[3], rstdk], w=[rstdk])
                    self.rsqrt(rstd[:, 0:m], rstd[:, 0:m], LN_EPS, 1.0, r=[rstdk], w=[rstdk])
                    for fc in range(KC):
                        t1, t1k = t1_ring[fc % 2]
                        s_, sk = s_ring[fc % 2]
                        sgt, sgtk = sgt_ring[fc % 2]
                        bank = 6 + fc % 2
                        for kc in range(KC):
                            self.mm(self.ps[bank][:, 0:m], w_g[:, kc, fc * 128:(fc + 1) * 128],
                                    hT[:, kc, tok0 + o: tok0 + o + m], kc == 0, kc == KC - 1,
                                    r=[wgk, hk], w=[self.psk[bank]])
                        self.act(sgt[:, 0:m], self.ps[bank][:, 0:m], AF.Silu, r=[self.psk[bank]], w=[sgtk])
                        self.tt("dve", t1[:, 0:m], yb[:, fc, 0:m], mean[:, 0:m], ALU.subtract, r=[ybk, meank], w=[t1k])
                        self.tt("dve", t1[:, 0:m], t1[:, 0:m], rstd[:, 0:m], ALU.mult, r=[t1k, rstdk], w=[t1k])
                        self.act(s_[:, 0:m], t1[:, 0:m], AF.Silu, r=[t1k, cvk], w=[sk],
                                 scale=cv[:, fc, CONV_W + 1:CONV_W + 2], bias=cv[:, fc, CONV_W + 2:CONV_W + 3])
                        self.tt("pool", y2[:, fc, 0:m], s_[:, 0:m], sgt[:, 0:m], ALU.mult, r=[sk, sgtk], w=[y2k])
                    for (oo, mm_) in _tiles(m, 128):
                        self.epilogue_tile(li, b, is_ctx, o + oo,
                                           lambda kc, y2=y2, oo=oo: y2[:, kc, oo:oo + 128],
                                           KC, w_out, wok, [y2k], last)
                col += n + 2 * HALO

    def bc_free(self, ap2, pattern):
        a = list(ap2.ap)
        if pattern == "k":
            return bass.AP(ap2.tensor, ap2.offset, [list(a[0]), [0, 64], list(a[1])])
        return bass.AP(ap2.tensor, ap2.offset, [list(a[0]), list(a[1]), [0, 64]])

    def rwkv_layer(self, li, j, ctx_out, midx, last):
        NB, TL, TC, TT = self.NB, self.TL, self.TC, self.TT
        i = self.i
        names = ["r", "v", "kk", "sg", "dec0", "dec1", "kd0", "kd1", "bn0", "bn1", "o0", "o1"]
        A = {n: self.dram("rk_%s_%d" % (n, li), [NB, TT, D]) for n in names}
        self.rwkv_phase_a(li, j, midx, A)
        self.p.fence()
        self.top = self.persist_top
        self.rwkv_scan(li, A)
        self.p.fence()
        self.top = self.persist_top
        self.rwkv_phase_c(li, j, ctx_out, midx, last, A)

    def rwkv_phase_a(self, li, j, midx, A):
        NB, TL, TC, TT = self.NB, self.TL, self.TC, self.TT
        i = self.i
        self.layer_vectors(midx)
        muT, muk = self.rows_to_cols([i["rwkv_mu"][j, n, :] for n in range(6)], "mu")
        W = {}
        for nm in ("w_r", "w_k", "w_v", "w_g"):
            t = self.alloc([KC, D], BF16)
            k = self.key(nm)
            self.dma(t, self.wb["r" + nm].rearrange("(kc p) n -> p kc n", p=128), w=[k])
            W[nm] = (t, k)
        w1cat = self.alloc([KC, 128], BF16)
        a1cat = self.alloc([KC, 128], BF16)
        w2cat = self.alloc([D], BF16)
        a2cat = self.alloc([D], BF16)
        lk = self.key("lora")
        for d in range(2):
            self.dma(w1cat[:, :, d * 64:(d + 1) * 64], self.wb["rw1_%d" % d].rearrange("(kc p) n -> p kc n", p=128), w=[lk])
            self.dma(a1cat[:, :, d * 64:(d + 1) * 64], self.wb["ra1_%d" % d].rearrange("(kc p) n -> p kc n", p=128), w=[lk])
            self.dma(w2cat[d * 64:(d + 1) * 64, :], self.wb["rw2_%d" % d], w=[lk])
            self.dma(a2cat[d * 64:(d + 1) * 64, :], self.wb["ra2_%d" % d], w=[lk])
        pk = self.key("params")
        kkb = self.alloc([D], F32)
        kab = self.alloc([D], F32)
        self.dma(kkb, self.bcast_row(i["rwkv_k_k"][j, :], D), w=[pk])
        self.dma(kab, self.bcast_row(i["rwkv_k_a"][j, :], D), w=[pk])
        w0b, a0b = [], []
        for d in range(2):
            t = self.alloc([D], F32)
            self.dma(t, self.bcast_row(i["rwkv_w0"][j, d, :], D), w=[pk])
            w0b.append(t)
            t = self.alloc([D], F32)
            self.dma(t, self.bcast_row(i["rwkv_a0"][j, d, :], D), w=[pk])
            a0b.append(t)
        HW = TT + 4
        hT = self.alloc([KC, HW], BF16)
        hk = self.key("hT")
        self.p.add("pool", lambda e: e.memset(hT, 0.0), w=[hk])
        T = [(self.alloc([D], F32), self.key("T%d" % n)) for n in range(4)]
        self.stage1_setup(share=T)
        self.coff, self.loff = 1, TC + 3
        xx = self.alloc([KC, 128], F32)
        xxk = self.key("xx")
        tl = self.alloc([KC, 128], F32)
        tlk = self.key("tl")
        lerp = [(self.alloc([KC, 128], BF16), self.key("lerp%d" % n)) for n in range(6)]
        thT = self.alloc([128], BF16)
        thk = self.key("thT")
        ahT = self.alloc([128], BF16)
        ahk = self.key("ahT")
        kraw = self.alloc([D], F32)
        krk = self.key("kraw")
        kk = self.alloc([D], F32)
        kkk = self.key("kk")
        ssq = self.alloc([16], F32)
        ssqk = self.key("ssq")
        bankc = [0]

        def nxt():
            b_ = bankc[0] % 6
            bankc[0] += 1
            return b_

        def proj(n_, wname, half):
            wt, wk = W[wname]
            lp, lpk = lerp[n_]
            bank = nxt()
            for kc in range(KC):
                self.mm(self.ps[bank][:, :], lp[:, kc, :], wt[:, kc, half * 512:(half + 1) * 512], kc == 0, kc == KC - 1,
                        r=[lpk, wk], w=[self.psk[bank]])
            return bank

        def store(name, b, tok, tile, tk):
            self.dma(A[name][b, tok:tok + 128, :], tile, r=[tk], w=[self.key("st")], q="pool")

        for b in range(NB):
            self.stage1(li, b, True, hT, hk)
            for (col0, n, tokbase) in ((self.coff, TC, 0), (self.loff, TL, TC)):
                for t0 in range(0, n, 128):
                    c = col0 + t0
                    tok = tokbase + t0
                    hc_ = hT[:, :, c:c + 128]
                    self.tt("pool", xx, hT[:, :, c - 1:c + 127], hT[:, :, c + 1:c + 129], ALU.add, r=[hk], w=[xxk])
                    self.stt(xx, xx, 0.5, hc_, ALU.mult, ALU.subtract, r=[xxk, hk], w=[xxk])
                    for n_ in range(6):
                        mb = bass.AP(muT.tensor, muT[:, :, n_].offset, [list(muT.ap[0]), [6, KC], [0, 128]])
                        self.tt("pool", tl, xx, mb, ALU.mult, r=[xxk, muk], w=[tlk])
                        lp, lpk = lerp[n_]
                        self.tt("dve", lp, tl, hc_, ALU.add, r=[tlk, hk], w=[lpk])
                    for half in range(2):
                        hs = slice(half * 512, (half + 1) * 512)
                        bk = proj(2, "w_k", half)
                        self.p.add("act", lambda e, bk=bk, hs=hs: e.copy(out=kraw[:, hs], in_=self.ps[bk][:, :]),
                                   r=[self.psk[bk]], w=[krk])
                    T1, T1k = T[0]
                    T2, T2k = T[1]
                    T3, T3k = T[2]
                    T4, T4k = T[3]
                    self.tt("dve", T1, kraw, kkb, ALU.mult, r=[krk, pk], w=[T1k])
                    self.tt("pool", T2, T1, T1, ALU.mult, r=[T1k], w=[T2k])
                    self.p.add("dve", lambda e: e.tensor_reduce(out=ssq, in_=T2.rearrange("p (h f) -> p h f", f=64),
                                                                axis=AX.X, op=ALU.add), r=[T2k], w=[ssqk])
                    self.rsqrt(ssq, ssq, 1e-12, 1.0, r=[ssqk], w=[ssqk])
                    self.tt("pool", kk.rearrange("p (h f) -> p h f", f=64), T1.rearrange("p (h f) -> p h f", f=64),
                            self.bc_free(ssq, "v"), ALU.mult, r=[T1k, ssqk], w=[kkk])
                    store("kk", b, tok, kk, kkk)
                    for kc in range(KC):
                        self.mm(self.ps[6][:, 0:128], w1cat[:, kc, :], lerp[1][0][:, kc, :], kc == 0, kc == KC - 1,
                                r=[lk, lerp[1][1]], w=[self.psk[6]])
                    self.act(thT, self.ps[6][:, 0:128], AF.Tanh, r=[self.psk[6]], w=[thk])
                    for kc in range(KC):
                        self.mm(self.ps[7][:, 0:128], a1cat[:, kc, :], lerp[4][0][:, kc, :], kc == 0, kc == KC - 1,
                                r=[lk, lerp[4][1]], w=[self.psk[7]])
                    self.p.add("act", lambda e: e.copy(out=ahT, in_=self.ps[7][:, 0:128]), r=[self.psk[7]], w=[ahk])
                    for d in range(2):
                        ds = slice(d * 64, (d + 1) * 64)
                        for half in range(2):
                            hs = slice(half * 512, (half + 1) * 512)
                            bank = nxt()
                            self.mm(self.ps[bank][:, :], ahT[ds, :], a2cat[ds, hs], True, True, r=[ahk, lk], w=[self.psk[bank]])
                            self.tt("dve", T1[:, hs], self.ps[bank][:, :], a0b[d][:, hs], ALU.add,
                                    r=[self.psk[bank], pk], w=[T1k])
                        self.act(T1, T1, AF.Sigmoid, r=[T1k], w=[T1k])
                        self.stt(T2, T1, -1.0, kab, ALU.add, ALU.mult, r=[T1k, pk], w=[T2k])
                        self.stt(T2, T2, 1.0, kraw, ALU.add, ALU.mult, r=[T2k, krk], w=[T2k])
                        store("kd%d" % d, b, tok, T2, T2k)
                        self.stt(T3, kk, -1.0, T1, ALU.mult, ALU.mult, r=[kkk, T1k], w=[T3k])
                        store("bn%d" % d, b, tok, T3, T3k)
                        for half in range(2):
                            hs = slice(half * 512, (half + 1) * 512)
                            bank = nxt()
                            self.mm(self.ps[bank][:, :], thT[ds, :], w2cat[ds, hs], True, True, r=[thk, lk], w=[self.psk[bank]])
                            self.tt("dve", T4[:, hs], self.ps[bank][:, :], w0b[d][:, hs], ALU.add,
                                    r=[self.psk[bank], pk], w=[T4k])
                        self.act(T4, T4, AF.Sigmoid, r=[T4k], w=[T4k])
                        self.act(T4, T4, AF.Exp, r=[T4k], w=[T4k], scale=-math.exp(-0.5))
                        store("dec%d" % d, b, tok, T4, T4k)
                    for (n_, wname, name, tile, tk, fn) in ((0, "w_r", "r", T1, T1k, AF.Identity), (3, "w_v", "v", T2, T2k, AF.Identity),
                                                            (5, "w_g", "sg", T3, T3k, AF.Silu)):
                        for half in range(2):
                            hs = slice(half * 512, (half + 1) * 512)
                            bk = proj(n_, wname, half)
                            self.act(tile[:, hs], self.ps[bk][:, :], fn, r=[self.psk[bk]], w=[tk])
                        store(name, b, tok, tile, tk)

    def rwkv_scan(self, li, A):
        NB, TL, TC, TT = self.NB, self.TL, self.TC, self.TT
        P = 2 * NB * 16
        SB = 32
        S = self.alloc([64, 64], F32)
        Sk = self.key("S")
        tmp = self.alloc([64, 64], F32)
        tmpk = self.key("tmp")
        vk_ring = self.ring("vk", 2, [64, 64], F32)
        sa = self.alloc([64], F32)
        sak = self.key("sa")
        arrs = ["kk", "bn", "dec", "kd", "v", "r"]
        blk = [{a: self.alloc([SB, 64], F32) for a in arrs} for _ in range(2)]
        oblk = self.ring("oblk", 2, [SB, 64], F32)
        self.p.add("dve", lambda e: e.memset(S[0:P], 0.0), w=[Sk])
        cb = 0
        for (soff, n) in ((0, TC), (TC, TL)):
            for i0 in range(0, n, SB):
                slot = cb % 2
                cb += 1
                bkeys = []
                for d in range(2):
                    for b in range(NB):
                        p0 = d * NB * 16 + b * 16
                        if d == 0:
                            tok0, step = soff + i0, D
                        else:
                            tok0, step = soff + n - 1 - i0, -D
                        for a in arrs:
                            name = a + str(d) if a in ("bn", "dec", "kd") else a
                            src_t = A[name]
                            src = bass.AP(src_t.tensor, src_t[b, tok0, :].offset, [[64, 16], [step, SB], [1, 64]])
                            k = "blk_%d_%d_%d_%s" % (slot, d, b, a)
                            self.dma(blk[slot][a][p0:p0 + 16, :, :], src, w=[k])
                            bkeys.append(k)
                ob, obk = oblk[slot]
                for s_ in range(SB):
                    g = lambda a: blk[slot][a][0:P, s_, :]
                    vkt, vkk = vk_ring[s_ % 2]
                    Sv = S[0:P]
                    tv = tmp[0:P]
                    self.tt("pool", vkt[0:P], self.bc_free(g("v"), "v"), self.bc_free(g("kd"), "k"), ALU.mult,
                            r=bkeys, w=[vkk])
                    self.tt("dve", tv, Sv, self.bc_free(g("kk"), "k"), ALU.mult, r=[Sk] + bkeys, w=[tmpk])
                    self.p.add("dve", lambda e, tv=tv: e.tensor_reduce(out=sa[0:P], in_=tv, axis=AX.X, op=ALU.add),
                               r=[tmpk], w=[sak])
                    self.tt("dve", Sv, Sv, self.bc_free(g("dec"), "k"), ALU.mult, r=[Sk] + bkeys, w=[Sk])
                    self.tt("dve", tv, self.bc_free(sa[0:P], "v"), self.bc_free(g("bn"), "k"), ALU.mult,
                            r=[sak] + bkeys, w=[tmpk])
                    self.tt("dve", Sv, Sv, tv, ALU.add, r=[Sk, tmpk], w=[Sk])
                    self.tt("dve", Sv, Sv, vkt[0:P], ALU.add, r=[Sk, vkk], w=[Sk])
                    self.tt("dve", tv, Sv, self.bc_free(g("r"), "k"), ALU.mult, r=[Sk] + bkeys, w=[tmpk])
                    self.p.add("dve", lambda e, tv=tv, ob=ob, s_=s_: e.tensor_reduce(out=ob[0:P, s_, :], in_=tv, axis=AX.X, op=ALU.add),
                               r=[tmpk], w=[obk])
                for d in range(2):
                    for b in range(NB):
                        p0 = d * NB * 16 + b * 16
                        if d == 0:
                            tok0, step = soff + i0, D
                        else:
                            tok0, step = soff + n - 1 - i0, -D
                        dst_t = A["o%d" % d]
                        dst = bass.AP(dst_t.tensor, dst_t[b, tok0, :].offset, [[64, 16], [step, SB], [1, 64]])
                        self.dma(dst, ob[p0:p0 + 16, :, :], r=[obk], w=[self.key("ost")], q="pool")

    def rwkv_phase_c(self, li, j, ctx_out, midx, last, A):
        NB, TL, TC, TT = self.NB, self.TL, self.TC, self.TT
        i = self.i
        self.layer_vectors(midx)
        w_o = self.alloc([KC, D], BF16)
        wok = self.key("rwo")
        self.dma(w_o, self.wb["rw_o"].rearrange("(kc p) n -> p kc n", p=128), w=[wok])
        pk = self.key("cparams")
        lngb = self.alloc([D], F32)
        lnbb = self.alloc([D], F32)
        rkb = self.alloc([D], F32)
        self.dma(lngb, self.bcast_row(i["rwkv_ln_g"][j, :], D), w=[pk])
        self.dma(lnbb, self.bcast_row(i["rwkv_ln_b"][j, :], D), w=[pk])
        self.dma(rkb, self.bcast_row(i["rwkv_r_k"][j, :], D), w=[pk])
        self.stage1_setup()
        self.epi_setup()
        L = {n: self.ring("ld_" + n, 2, [D], F32) for n in ("o0", "o1", "r", "kd0", "kd1", "v", "sg")}
        o = self.alloc([D], F32)
        ok_ = self.key("o")
        sq = self.alloc([D], F32)
        sqk = self.key("sq")
        st = self.alloc([64], F32)
        stk = self.key("st")
        y2T_ring = self.ring("y2T", 2, [KC, 128], BF16)
        cnt = 0
        h3 = lambda t: t.rearrange("p (h f) -> p h f", f=64)
        for b in range(NB):
            segs = []
            if ctx_out:
                segs += [(True, t0, t0) for t0 in range(0, TC, 128)]
            segs += [(False, t0, TC + t0) for t0 in range(0, TL, 128)]
            for (is_ctx, t0, tok) in segs:
                ld = {}
                for n in L:
                    t, k = L[n][cnt % 2]
                    self.dma(t, A[n][b, tok:tok + 128, :], w=[k])
                    ld[n] = (t, k)
                y2T, y2Tk = y2T_ring[cnt % 2]
                cnt += 1
                self.tt("pool", o, ld["o0"][0], ld["o1"][0], ALU.add, r=[ld["o0"][1], ld["o1"][1]], w=[ok_])
                mean, ex2, bon, tmp16 = st[:, 0:16], st[:, 16:32], st[:, 32:48], st[:, 48:64]
                self.p.add("dve", lambda e: e.tensor_reduce(out=mean, in_=h3(o), axis=AX.X, op=ALU.add), r=[ok_], w=[stk])
                self.tt("pool", sq, o, o, ALU.mult, r=[ok_], w=[sqk])
                self.p.add("dve", lambda e: e.tensor_reduce(out=ex2, in_=h3(sq), axis=AX.X, op=ALU.add), r=[sqk], w=[stk])
                self.ts("dve", mean, mean, 1.0 / 64, None, ALU.mult, None, r=[stk], w=[stk])
                self.tt("dve", tmp16, mean, mean, ALU.mult, r=[stk], w=[stk])
                self.stt(ex2, ex2, 1.0 / 64, tmp16, ALU.mult, ALU.subtract, r=[stk], w=[stk])
                self.rsqrt(ex2, ex2, GN_EPS, 1.0, r=[stk], w=[stk])
                self.tt("dve", h3(o), h3(o), self.bc_free(mean, "v"), ALU.subtract, r=[ok_, stk], w=[ok_])
                self.tt("pool", h3(o), h3(o), self.bc_free(ex2, "v"), ALU.mult, r=[ok_, stk], w=[ok_])
                self.tt("dve", o, o, lngb, ALU.mult, r=[ok_, pk], w=[ok_])
                self.tt("pool", o, o, lnbb, ALU.add, r=[ok_, pk], w=[ok_])
                self.tt("pool", sq, ld["kd0"][0], ld["kd1"][0], ALU.add, r=[ld["kd0"][1], ld["kd1"][1]], w=[sqk])
                self.tt("dve", sq, sq, ld["r"][0], ALU.mult, r=[sqk, ld["r"][1]], w=[sqk])
                self.tt("pool", sq, sq, rkb, ALU.mult, r=[sqk, pk], w=[sqk])
                self.p.add("dve", lambda e: e.tensor_reduce(out=bon, in_=h3(sq), axis=AX.X, op=ALU.add), r=[sqk], w=[stk])
                self.tt("pool", h3(sq), h3(ld["v"][0]), self.bc_free(bon, "v"), ALU.mult, r=[ld["v"][1], stk, sqk], w=[sqk])
                self.tt("dve", o, o, sq, ALU.add, r=[ok_, sqk], w=[ok_])
                self.tt("pool", o, o, ld["sg"][0], ALU.mult, r=[ok_, ld["sg"][1]], w=[ok_])
                for g in range(2):
                    bank = 6 + g
                    for q in range(4):
                        kc = g * 4 + q
                        self.tr(self.ps[bank][:, q * 128:(q + 1) * 128], o[:, kc * 128:(kc + 1) * 128], r=[ok_], w=[self.psk[bank]])
                    self.p.add("act", lambda e, bank=bank, g=g, y2T=y2T: e.copy(
                        out=y2T[:, g * 4:(g + 1) * 4, :].rearrange("p a b -> p (a b)"), in_=self.ps[bank][:, :]),
                        r=[self.psk[bank]], w=[y2Tk])
                self.epilogue_tile(li, b, is_ctx, t0, lambda kc, y2T=y2T: y2T[:, kc, :], KC, w_o, wok, [y2Tk], last)

    def attn_layer(self, li, j, ctx_out, midx, last):
        assert not ctx_out
        NB, TL, TC, TT = self.NB, self.TL, self.TC, self.TT
        i = self.i
        NKT = TT // 128
        SC = 1.0 / math.sqrt(128.0)
        self.layer_vectors(midx)
        qg = self.alloc([1], F32)
        kg = self.alloc([1], F32)
        SK = globals().get("ATTN_SKIP", "")
        if "q" not in SK:
            self.dma(qg, bass.AP(i["attn_q_g"].tensor, i["attn_q_g"][j, :].offset, [[1, 128], [1, 1]]), w=["qg"])
            self.dma(kg, bass.AP(i["attn_k_g"].tensor, i["attn_k_g"][j, :].offset, [[1, 128], [1, 1]]), w=["kg"])
        cosT = self.alloc([TL], F32)
        sinT = self.alloc([TL], F32)
        if "c" not in SK:
            self.dma(cosT, i["k_cos"], w=["cosT"])
            self.dma(sinT, i["k_sin"], w=["sinT"])
        w_in_v = self.wb["awi"].rearrange("(kc p) n -> p kc n", p=128)
        w_out = self.alloc([16, D], BF16)
        wok = self.key("awo")
        if "w" not in SK:
            self.dma(w_out, self.wb["awo"].rearrange("(h p) n -> p h n", p=128), r=["awo"], w=[wok])
        og_d = self.dram("og_d%d" % li, [16, 128, TL], BF16)
        hT = self.alloc([KC, TT], BF16)
        hk = self.key("hT")
        self.stage1_setup()
        self.epi_setup()
        wring = self.ring("awp", 2, [KC, 768], BF16)
        kT = self.alloc([TT], BF16)
        kTk = self.key("kT")
        vT = self.alloc([NKT, 128], BF16)
        vTk = self.key("vT")
        qT = self.alloc([2, TL], BF16)
        qTk = self.key("qT")
        sgT = self.alloc([2, TL], BF16)
        sgTk = self.key("sgT")
        sq_ring = self.ring("sq", 1, [512], BF16) * 2
        rs_ring = self.ring("rs", 1, [512], F32) * 2
        kn_ring = self.ring("kn", 1, [512], BF16) * 2
        t1_ring = self.ring("rt1", 1, [512], F32) * 2
        t2_ring = self.ring("rt2", 1, [512], F32) * 2
        pT_ring = self.ring("pT", 3, [512], BF16)
        rz = self.alloc([512], F32)
        rzk = self.key("rz")
        o1 = self.alloc([512], F32)
        o1k = self.key("o1")
        og_ring = self.ring("og", 2, [512], BF16)
        ogt_ring = self.ring("ogt", 2, [16, 128], BF16)
        cn = [0]

        def normrope(ps_ap, psk, n, gvec, gk, dest, destk, pos0):
            c = cn[0]
            cn[0] += 1
            sq, sqk = sq_ring[c % 2]
            rs, rsk = rs_ring[c % 2]
            self.act(sq[:, 0:n], ps_ap, AF.Square, r=[psk], w=[sqk])
            self.mm(self.ps[0][:, 0:n], self.ones_h, sq[:, 0:n], True, True, r=[sqk], w=[self.psk[0]])
            self.rsqrt(rs[:, 0:n], self.ps[0][:, 0:n], NORM_EPS, 1.0, r=[self.psk[0]], w=[rsk])
            if pos0 is None:
                self.stt(dest, ps_ap, gvec, rs[:, 0:n], ALU.mult, ALU.mult, r=[psk, rsk, gk], w=[destk])
                return
            kn, knk = kn_ring[c % 2]
            t1, t1k = t1_ring[c % 2]
            t2, t2k = t2_ring[c % 2]
            self.stt(kn[:, 0:n], ps_ap, gvec, rs[:, 0:n], ALU.mult, ALU.mult, r=[psk, rsk, gk], w=[knk])
            self.mm(self.ps[1][:, 0:n], self.perm_b, kn[:, 0:n], True, True, r=[knk], w=[self.psk[1]])
            self.tt("dve", t1[:, 0:n], kn[:, 0:n], cosT[:, pos0:pos0 + n], ALU.mult, r=[knk, "cosT"], w=[t1k])
            self.tt("dve", t2[:, 0:n], self.ps[1][:, 0:n], sinT[:, pos0:pos0 + n], ALU.mult,
                    r=[self.psk[1], "sinT"], w=[t2k])
            self.tt("pool", dest, t1[:, 0:n], t2[:, 0:n], ALU.add, r=[t1k, t2k], w=[destk])

        cw = 0
        ca = 0
        STOP = globals().get("ATTN_STOP", 99)
        if STOP <= 2:
            return
        for b in range(NB):
            self.stage1(li, b, True, hT, hk)
            if STOP <= 3:
                return
            for g in range(8 if STOP > 4 else 1):
                wt, wtk = wring[cw % 2]
                cw += 1
                self.dma(wt[:, :, 0:256], w_in_v[:, :, 256 * g:256 * g + 256], r=["awi"], w=[wtk])
                self.dma(wt[:, :, 256:384], w_in_v[:, :, 2048 + 128 * g:2048 + 128 * g + 128], r=["awi"], w=[wtk])
                self.dma(wt[:, :, 384:512], w_in_v[:, :, 3072 + 128 * g:3072 + 128 * g + 128], r=["awi"], w=[wtk])
                self.dma(wt[:, :, 512:768], w_in_v[:, :, 4096 + 256 * g:4096 + 256 * g + 256], r=["awi"], w=[wtk])
                ktiles = [(o, m, None) for (o, m) in _tiles(TC, 512)] + [(TC + o, m, o) for (o, m) in _tiles(TL, 512)]
                for (tok0, n, pos0) in ktiles:
                    for kc in range(KC):
                        self.mm(self.ps[7][:, 0:n], wt[:, kc, 256:384], hT[:, kc, tok0:tok0 + n], kc == 0, kc == KC - 1,
                                r=[wtk, hk], w=[self.psk[7]])
                    normrope(self.ps[7][:, 0:n], self.psk[7], n, kg, "kg", kT[:, tok0:tok0 + n], kTk, pos0)
                for k0 in range(0, NKT, 4):
                    nk = min(4, NKT - k0)
                    for q in range(nk):
                        kt = k0 + q
                        for kc in range(KC):
                            self.mm(self.ps[2][:, q * 128:(q + 1) * 128], hT[:, kc, kt * 128:(kt + 1) * 128],
                                    wt[:, kc, 384:512], kc == 0, kc == KC - 1, r=[wtk, hk], w=[self.psk[2]])
                    self.p.add("act", lambda e, k0=k0, nk=nk: e.copy(
                        out=vT[:, k0:k0 + nk, :].rearrange("p a b -> p (a b)"), in_=self.ps[2][:, 0:nk * 128]),
                        r=[self.psk[2]], w=[vTk])
                for hq in range(2):
                    for (o, m) in _tiles(TL, 512):
                        for kc in range(KC):
                            self.mm(self.ps[7][:, 0:m], wt[:, kc, hq * 128:(hq + 1) * 128], hT[:, kc, TC + o:TC + o + m],
                                    kc == 0, kc == KC - 1, r=[wtk, hk], w=[self.psk[7]])
                        normrope(self.ps[7][:, 0:m], self.psk[7], m, qg, "qg", qT[:, hq, o:o + m], qTk, o)
                        for kc in range(KC):
                            self.mm(self.ps[2][:, 0:m], wt[:, kc, 512 + hq * 128:512 + (hq + 1) * 128],
                                    hT[:, kc, TC + o:TC + o + m], kc == 0, kc == KC - 1, r=[wtk, hk], w=[self.psk[2]])
                        self.act(sgT[:, hq, o:o + m], self.ps[2][:, 0:m], AF.Silu, r=[self.psk[2]], w=[sgTk])
                if STOP <= 5:
                    continue
                for hq in range(2):
                    for (o, m) in _tiles(TL, 512):
                        bO = 3 + 2 * (ca % 2)
                        bZ = bO + 1
                        og, ogk = og_ring[ca % 2]
                        ca += 1

                        def S(kt):
                            bank = kt % 3
                            self.mm(self.ps[bank][:, 0:m], kT[:, kt * 128:(kt + 1) * 128], qT[:, hq, o:o + m], True, True,
                                    r=[kTk, qTk], w=[self.psk[bank]])
                        S(0)
                        if NKT > 1:
                            S(1)
                        for kt in range(NKT):
                            bank = kt % 3
                            pT, pTk = pT_ring[kt % 3]
                            self.act(pT[:, 0:m], self.ps[bank][:, 0:m], AF.Exp, r=[self.psk[bank]], w=[pTk], scale=SC)
                            self.mm(self.ps[bO][:, 0:m], vT[:, kt, :], pT[:, 0:m], kt == 0, kt == NKT - 1,
                                    r=[vTk, pTk], w=[self.psk[bO]])
                            self.mm(self.ps[bZ][:, 0:m], self.ones_1, pT[:, 0:m], kt == 0, kt == NKT - 1,
                                    r=[pTk], w=[self.psk[bZ]])
                            if kt + 2 < NKT:
                                S(kt + 2)
                        self.p.add("dve", lambda e, bZ=bZ, m=m: e.reciprocal(out=rz[:, 0:m], in_=self.ps[bZ][:, 0:m]),
                                   r=[self.psk[bZ]], w=[rzk])
                        self.tt("dve", o1[:, 0:m], self.ps[bO][:, 0:m], rz[:, 0:m], ALU.mult, r=[self.psk[bO], rzk], w=[o1k])
                        self.tt("pool", og[:, 0:m], o1[:, 0:m], sgT[:, hq, o:o + m], ALU.mult, r=[o1k, sgTk], w=[ogk])
                        self.dma(og_d[2 * g + hq, :, o:o + m], og[:, 0:m], r=[ogk], w=["og_d"], q="pool")
            if STOP <= 6:
                return
            ce = 0
            for t0 in range(0, TL, 128):
                ogt, ogtk = ogt_ring[ce % 2]
                ce += 1
                self.dma(ogt, og_d[:, :, t0:t0 + 128].rearrange("h p t -> p h t"), r=["og_d"], w=[ogtk])
                self.epilogue_tile(li, b, False, t0, lambda kc, ogt=ogt: ogt[:, kc, :], 16, w_out, wok, [ogtk], last)


FULL_LAYERS = [(0, 0, True, True, 0), (1, 0, True, True, 1), (2, 0, True, False, 2), (0, 1, False, False, 3)]


def host_consts(TL):
    ident = np.eye(128, dtype=np.float32)
    perm = np.zeros((128, 128), np.float32)
    for k in range(128):
        blk = k // 32
        if blk % 2 == 0:
            perm[k, k + 32] = 1.0
        else:
            perm[k, k - 32] = -1.0
    rows = TL // GRID_W
    t = np.arange(TL)
    row = (t // GRID_W).astype(np.float32)
    col = (t % GRID_W).astype(np.float32)
    inv = (1.0 / (10000.0 ** (np.arange(0, 64, 2, dtype=np.float32) / np.float32(64)))).astype(np.float32)
    cosT = np.zeros((128, TL), np.float32)
    sinT = np.zeros((128, TL), np.float32)
    for p in range(128):
        pos = row if p < 64 else col
        ang = (pos * inv[p % 32]).astype(np.float32)
        cosT[p] = np.cos(ang)
        sinT[p] = np.sin(ang)
    return {"k_ident": ident, "k_perm": perm, "k_cos": cosT, "k_sin": sinT}


def make_in_maps(inputs, n_cores, NB, TL):
    consts = host_consts(TL)
    maps = []
    for cid in range(n_cores):
        m = {}
        for k, v in inputs.items():
            v = np.asarray(v)
            if k in ("x", "c", "ctx"):
                m[k] = np.ascontiguousarray(v[cid * NB:(cid + 1) * NB])
            elif k in ("c_ctx", "final_g"):
                m[k] = np.ascontiguousarray(v.reshape(1, -1))
            elif k == "rwkv_r_k":
                m[k] = np.ascontiguousarray(v.reshape(v.shape[0], -1))
            else:
                m[k] = np.ascontiguousarray(v)
        m.update(consts)
        maps.append(m)
    return maps


_CACHE = {}


def run(inputs, layers, n_cores=8, trace=False, dbg=False):
    x = np.asarray(inputs["x"])
    B, TL, _ = x.shape
    TC = np.asarray(inputs["ctx"]).shape[1]
    NB = B // n_cores
    mk = MK(NB, TL, TC, layers, dbg=dbg)
    nc = mk.build()
    maps = make_in_maps(inputs, n_cores, NB, TL)
    res = run_bass_kernel_spmd(nc, maps, core_ids=list(range(n_cores)), trace=trace)
    out = np.concatenate([np.asarray(r["out"]) for r in res.results], axis=0)
    return out.astype(np.float32), res


def kernel(**inputs):
    out, _ = run(inputs, FULL_LAYERS, n_cores=8)
    return out
```

```python
import contextlib
import math
import numpy as np
import ml_dtypes
import concourse.bass as bass
import concourse.mybir as mybir
from concourse.bass_utils import run_bass_kernel_spmd
from concourse.alu_op_type import AluOpType as ALU

F32 = mybir.dt.float32
BF16 = mybir.dt.bfloat16
AF = mybir.ActivationFunctionType
AX = mybir.AxisListType

D = 1024
KC = 8
GRID_W = 64
CONV_W = 31
HALO = 15
NORM_EPS = 1e-6
LN_EPS = 1e-5
GN_EPS = 64e-5

ENG = ("pe", "act", "dve", "pool", "sp")
NDMASEM = 12


class Op:
    __slots__ = ("eng", "fn", "dma", "deps", "needs_inc", "cnt", "slot", "val", "idx")


class Prog:
    def __init__(self, nc):
        self.nc = nc
        self.ops = []
        self.by_eng = {e: [] for e in ENG}
        self.last_w = {}
        self.readers = {}
        self.ndma = {e: 0 for e in ENG}
        self.fence_deps = []
        self.fence_pending = set()

    def fence(self):
        deps = []
        for e in ENG:
            lst = self.by_eng[e]
            for op in reversed(lst):
                if not op.dma:
                    deps.append(op.idx)
                    break
            cnt = 0
            for op in reversed(lst):
                if op.dma:
                    deps.append(op.idx)
                    cnt += 1
                    if cnt >= NDMASEM:
                        break
        self.fence_deps = deps
        self.fence_pending = set(ENG)
        self.last_w.clear()
        self.readers.clear()

    def add(self, eng, fn, r=(), w=(), dma=False):
        op = Op()
        op.eng, op.fn, op.dma = eng, fn, dma
        op.needs_inc = False
        op.cnt = op.slot = op.val = None
        op.idx = len(self.ops)
        deps = set()
        lw = self.last_w
        for k in r:
            y = lw.get(k)
            if y is not None:
                deps.add(y)
        for k in w:
            y = lw.get(k)
            if y is not None:
                deps.add(y)
            ys = self.readers.get(k)
            if ys:
                deps.update(ys)
        keep = []
        for yi in deps:
            y = self.ops[yi]
            if (not y.dma) and (not dma) and y.eng == eng:
                if eng == "pe":
                    continue
                raw = False
                for k in r:
                    if lw.get(k) == yi:
                        raw = True
                        break
                if not raw:
                    continue
            keep.append(yi)
        if eng in self.fence_pending:
            self.fence_pending.discard(eng)
            keep = list(set(keep) | set(self.fence_deps))
        op.deps = keep
        for yi in keep:
            self.ops[yi].needs_inc = True
        if dma:
            n = self.ndma[eng]
            self.ndma[eng] = n + 1
            op.slot = n % NDMASEM
            op.val = 16 * (n // NDMASEM + 1)
        for k in r:
            self.readers.setdefault(k, []).append(op.idx)
        for k in w:
            lw[k] = op.idx
            self.readers[k] = []
        self.ops.append(op)
        self.by_eng[eng].append(op)
        return op

    def emit(self):
        nc = self.nc
        for e in ENG:
            c = 0
            for op in self.by_eng[e]:
                if (not op.dma) and op.needs_inc:
                    c += 1
                    op.cnt = c
        with contextlib.ExitStack() as st:
            csem = {e: st.enter_context(nc.semaphore("c_" + e)) for e in ENG}
            dsem = {e: [st.enter_context(nc.semaphore("d_%s_%d" % (e, i))) for i in range(NDMASEM)]
                    for e in ENG if self.ndma[e] > 0}
            block = st.enter_context(nc.Block())
            handles = {"pe": block.tensor, "act": block.scalar, "dve": block.vector,
                       "pool": block.gpsimd, "sp": block.sync}
            ops = self.ops

            def make(e):
                def body(eng):
                    waited = {}
                    for op in self.by_eng[e]:
                        need = {}
                        for yi in op.deps:
                            y = ops[yi]
                            if y.dma:
                                key = ("d", y.eng, y.slot)
                                v = y.val
                            else:
                                key = ("c", y.eng)
                                v = y.cnt
                            if need.get(key, 0) < v:
                                need[key] = v
                        if op.dma and op.val > 16:
                            key = ("d", e, op.slot)
                            if need.get(key, 0) < op.val - 16:
                                need[key] = op.val - 16
                        for key, v in need.items():
                            if waited.get(key, 0) >= v:
                                continue
                            waited[key] = v
                            sem = csem[key[1]] if key[0] == "c" else dsem[key[1]][key[2]]
                            eng.wait_ge(sem, v)
                        ins = op.fn(eng)
                        if op.dma:
                            ins.then_inc(dsem[e][op.slot], 16)
                        elif op.needs_inc:
                            ins.then_inc(csem[e], 1)
                    if e == "sp":
                        for q in ENG:
                            n = self.ndma[q]
                            for s in range(min(n, NDMASEM)):
                                cnt = (n - 1 - s) // NDMASEM + 1
                                if waited.get(("d", q, s), 0) < 16 * cnt:
                                    eng.wait_ge(dsem[q][s], 16 * cnt)
                        for q in ENG:
                            last = None
                            for op in self.by_eng[q]:
                                if op.cnt is not None:
                                    last = op.cnt
                            if last is not None and q != "sp":
                                eng.wait_ge(csem[q], last)
                return body

            for e in ENG:
                if self.by_eng[e] or e == "sp":
                    handles[e](make(e))


def _tiles(n, t):
    out = []
    o = 0
    while o < n:
        m = min(t, n - o)
        out.append((o, m))
        o += m
    return out


class MK:
    def __init__(self, NB, TL, TC, layers, dbg=False):
        self.NB, self.TL, self.TC, self.layers = NB, TL, TC, layers
        self.TT = TL + TC
        self.NV = NB + 1
        self.nc = bass.Bass("TRN2", target_bir_lowering=False)
        self.p = Prog(self.nc)
        self.st = contextlib.ExitStack()
        self.uid = 0
        self.dbg = dbg

    def dram_in(self, name, shape, dt=F32):
        return self.nc.dram_tensor(name, list(shape), dt, kind="ExternalInput").ap()

    def dram(self, name, shape, dt=F32):
        if self.dbg:
            return self.nc.dram_tensor(name, list(shape), dt, kind="ExternalOutput").ap()
        return self.nc.dram_tensor(name, list(shape), dt).ap()

    def alloc(self, free, dt=F32):
        n = 1
        for f in free:
            n *= f
        units = n * 2 if dt == F32 else n
        self.top = (self.top + 15) // 16 * 16
        assert self.top + units <= self.AREN, ("arena overflow", self.top, units, self.AREN)
        v = self.arena[:, self.top:self.top + units]
        self.top += units
        if dt == F32:
            v = v.bitcast(F32)
        if len(free) == 2:
            v = v.rearrange("p (a b) -> p a b", a=free[0])
        elif len(free) == 3:
            v = v.rearrange("p (a b c) -> p a b c", a=free[0], b=free[1])
        return v

    def key(self, s):
        self.uid += 1
        return "%s#%d" % (s, self.uid)

    def ring(self, name, n, free, dt=F32):
        return [(self.alloc(free, dt), self.key(name)) for _ in range(n)]

    def dma(self, out, in_, r=(), w=(), q="sp", **kw):
        return self.p.add(q, lambda e: e.dma_start(out=out, in_=in_, **kw), r=r, w=w, dma=True)

    def mm(self, out, lhsT, rhs, start, stop, r, w):
        return self.p.add("pe", lambda e: e.matmul(out, lhsT=lhsT, rhs=rhs, start=start, stop=stop), r=r, w=w)

    def tr(self, out, in_, r, w):
        P = in_.shape[0]
        ident = self.ident_f[0:P, 0:P]
        return self.p.add("pe", lambda e: e.transpose(out=out, in_=in_, identity=ident), r=r, w=w)

    def act(self, out, in_, func, r, w, scale=None, bias=None):
        kw = {}
        if scale is not None:
            kw["scale"] = scale
        if bias is not None:
            kw["bias"] = bias
        return self.p.add("act", lambda e: e.activation(out=out, in_=in_, func=func, **kw), r=r, w=w)

    def tt(self, eng, out, in0, in1, op, r, w):
        return self.p.add(eng, lambda e: e.tensor_tensor(out=out, in0=in0, in1=in1, op=op), r=r, w=w)

    def ts(self, eng, out, in0, s1, s2, op0, op1, r, w):
        if op1 is None:
            return self.p.add(eng, lambda e: e.tensor_scalar(out=out, in0=in0, scalar1=s1, scalar2=None, op0=op0), r=r, w=w)
        return self.p.add(eng, lambda e: e.tensor_scalar(out=out, in0=in0, scalar1=s1, scalar2=s2, op0=op0, op1=op1), r=r, w=w)

    def stt(self, out, in0, scalar, in1, op0, op1, r, w):
        return self.p.add("dve", lambda e: e.scalar_tensor_tensor(out=out, in0=in0, scalar=scalar, in1=in1, op0=op0, op1=op1), r=r, w=w)

    def rsqrt(self, out, in_, eps, mul, r, w):
        kt = self.key("rs")
        self.ts("dve", out, in_, mul, eps, ALU.mult, ALU.add, r=r, w=[kt])
        self.p.add("act", lambda e: e.activation(out=out, in_=out, func=AF.Sqrt), r=[kt], w=[kt])
        self.p.add("dve", lambda e: e.reciprocal(out=out, in_=out), r=[kt], w=w)

    def bcast_row(self, dram_ap_row, n):
        return bass.AP(dram_ap_row.tensor, dram_ap_row.offset, [[0, 128], [1, n]])

    def build(self):
        nc, NB, TL, TC, TT = self.nc, self.NB, self.TL, self.TC, self.TT
        NLAY = 4
        i = {}
        i["x"] = self.dram_in("x", [NB, TL, D])
        i["c"] = self.dram_in("c", [NB, D])
        i["ctx"] = self.dram_in("ctx", [NB, TC, D])
        i["c_ctx"] = self.dram_in("c_ctx", [1, D])
        i["norm_g"] = self.dram_in("norm_g", [NLAY, D])
        i["mod_w"] = self.dram_in("mod_w", [NLAY, D, 3 * D])
        i["mod_b"] = self.dram_in("mod_b", [NLAY, 3 * D])
        i["conv_w_in"] = self.dram_in("conv_w_in", [2, D, 3 * D])
        i["conv_dw"] = self.dram_in("conv_dw", [2, CONV_W, D])
        i["conv_db"] = self.dram_in("conv_db", [2, D])
        i["conv_ln_g"] = self.dram_in("conv_ln_g", [2, D])
        i["conv_ln_b"] = self.dram_in("conv_ln_b", [2, D])
        i["conv_w_out"] = self.dram_in("conv_w_out", [2, D, D])
        i["rwkv_mu"] = self.dram_in("rwkv_mu", [1, 6, D])
        for nm in ("rwkv_w_r", "rwkv_w_k", "rwkv_w_v", "rwkv_w_g", "rwkv_w_o"):
            i[nm] = self.dram_in(nm, [1, D, D])
        i["rwkv_w0"] = self.dram_in("rwkv_w0", [1, 2, D])
        i["rwkv_w1"] = self.dram_in("rwkv_w1", [1, 2, D, 64])
        i["rwkv_w2"] = self.dram_in("rwkv_w2", [1, 2, 64, D])
        i["rwkv_a0"] = self.dram_in("rwkv_a0", [1, 2, D])
        i["rwkv_a1"] = self.dram_in("rwkv_a1", [1, 2, D, 64])
        i["rwkv_a2"] = self.dram_in("rwkv_a2", [1, 2, 64, D])
        i["rwkv_k_k"] = self.dram_in("rwkv_k_k", [1, D])
        i["rwkv_k_a"] = self.dram_in("rwkv_k_a", [1, D])
        i["rwkv_r_k"] = self.dram_in("rwkv_r_k", [1, D])
        i["rwkv_ln_g"] = self.dram_in("rwkv_ln_g", [1, D])
        i["rwkv_ln_b"] = self.dram_in("rwkv_ln_b", [1, D])
        i["attn_w_in"] = self.dram_in("attn_w_in", [1, D, 6 * D])
        i["attn_q_g"] = self.dram_in("attn_q_g", [1, 128])
        i["attn_k_g"] = self.dram_in("attn_k_g", [1, 128])
        i["attn_w_out"] = self.dram_in("attn_w_out", [1, 2 * D, D])
        i["final_g"] = self.dram_in("final_g", [1, D])
        i["k_ident"] = self.dram_in("k_ident", [128, 128])
        i["k_perm"] = self.dram_in("k_perm", [128, 128])
        i["k_cos"] = self.dram_in("k_cos", [128, TL])
        i["k_sin"] = self.dram_in("k_sin", [128, TL])
        self.i = i
        self.out = nc.dram_tensor("out", [NB, TL, D], F32, kind="ExternalOutput").ap()
        self.xs = self.dram("xs", [NB, TL, D])
        if self.dbg:
            self.xcs = nc.dram_tensor("xcs", [NB, TC, D], F32, kind="ExternalOutput").ap()
        else:
            self.xcs = self.dram("xcs", [NB, TC, D])
        self.m_d = self.dram("m_d", [NLAY, self.NV, 3 * D])

        st = self.st
        with st:
            self.AREN = 106000
            self.arena = st.enter_context(nc.sbuf_tensor("arena", [128, self.AREN], BF16))
            self.top = 0
            self.ps = [st.enter_context(nc.psum_tensor("ps%d" % k, [128, 512], F32)) for k in range(8)]
            self.psk = ["ps%d" % k for k in range(8)]
            self.prologue()
            self.persist_top = self.top
            used = set(l[0] for l in self.layers)
            self.convert_weights(used)
            self.modulation()
            nl = len(self.layers)
            for li, (kind, j, ctx_in, ctx_out, midx) in enumerate(self.layers):
                self.p.fence()
                self.top = self.persist_top
                last = li == nl - 1
                if kind == 0:
                    self.conv_layer(li, j, ctx_out, midx, last)
                elif kind == 1:
                    self.rwkv_layer(li, j, ctx_out, midx, last)
                else:
                    self.attn_layer(li, j, ctx_out, midx, last)
            if nl == 0:
                self.p.fence()
                self.top = self.persist_top
                self.only_final()
            self.p.emit()
        return nc

    def prologue(self):
        i = self.i
        self.ident_f = self.alloc([128], F32)
        self.dma(self.ident_f, i["k_ident"], w=["ident_f"])
        self.ident_b = self.alloc([128], BF16)
        self.p.add("dve", lambda e: e.tensor_copy(out=self.ident_b, in_=self.ident_f), r=["ident_f"], w=["ident_b"])
        permf = self.alloc([128], F32)
        self.dma(permf, i["k_perm"], w=["permf"])
        self.perm_b = self.alloc([128], BF16)
        self.p.add("dve", lambda e: e.tensor_copy(out=self.perm_b, in_=permf), r=["permf"], w=["perm_b"])
        self.ones_d = self.alloc([128], BF16)
        self.ones_h = self.alloc([128], BF16)
        self.ones_1 = self.alloc([128], BF16)
        self.p.add("pool", lambda e: e.memset(self.ones_d, 1.0 / D), w=["ones_d"])
        self.p.add("pool", lambda e: e.memset(self.ones_h, 1.0 / 128), w=["ones_h"])
        self.p.add("pool", lambda e: e.memset(self.ones_1, 1.0), w=["ones_1"])
        self.fg_b = self.alloc([D], F32)
        self.dma(self.fg_b, self.bcast_row(i["final_g"][0, :], D), w=["fg_b"])
        self.constkeys = ["ident_f", "ident_b", "perm_b", "ones_d", "ones_h", "ones_1", "fg_b"]

    def after_fence_consts(self):
        return

    def convert_weights(self, used):
        i = self.i
        self.wb = {}

        def conv(name, src, rows, cols):
            dst = self.dram(name, [rows, cols], BF16)
            for r0 in range(0, rows, 256):
                rr = min(256, rows - r0)
                self.dma(dst[r0:r0 + rr, :], src[r0:r0 + rr, :], w=[name], q="pool")
            self.wb[name] = dst

        if 0 in used:
            for j in sorted(set(l[1] for l in self.layers if l[0] == 0)):
                conv("cwi%d" % j, i["conv_w_in"][j], D, 3 * D)
                conv("cwo%d" % j, i["conv_w_out"][j], D, D)
        if 1 in used:
            for nm in ("w_r", "w_k", "w_v", "w_g", "w_o"):
                conv("r" + nm, i["rwkv_" + nm][0], D, D)
            for dd in range(2):
                conv("rw1_%d" % dd, i["rwkv_w1"][0, dd], D, 64)
                conv("ra1_%d" % dd, i["rwkv_a1"][0, dd], D, 64)
                conv("rw2_%d" % dd, i["rwkv_w2"][0, dd], 64, D)
                conv("ra2_%d" % dd, i["rwkv_a2"][0, dd], 64, D)
        if 2 in used:
            conv("awi", i["attn_w_in"][0], D, 6 * D)
            conv("awo", i["attn_w_out"][0], 2 * D, D)

    def modulation(self):
        i, NV, NB = self.i, self.NV, self.NB
        top0 = self.top
        crow = self.alloc([D], F32)
        self.dma(crow[0:NB, :], i["c"], w=["crow"])
        self.dma(crow[NB:NV, :], i["c_ctx"], w=["crow"])
        self.act(crow[0:NV, :], crow[0:NV, :], AF.Silu, r=["crow"], w=["crow"])
        scT = self.alloc([KC, NV], F32)
        for kc in range(KC):
            self.tr(self.ps[0][:, kc * NV:(kc + 1) * NV], crow[0:NV, kc * 128:(kc + 1) * 128],
                    r=["crow", "ident_f"], w=["ps0"])
        self.p.add("dve", lambda e: e.tensor_copy(out=scT.rearrange("p a b -> p (a b)"), in_=self.ps[0][:, 0:KC * NV]),
                   r=["ps0"], w=["scT"])
        wring = self.ring("modw", 2, [KC, 512], F32)
        mrow = self.alloc([3 * D], F32)
        brow = self.alloc([3 * D], F32)
        used_mod = sorted(set(l[4] for l in self.layers))
        cnt = 0
        for l in used_mod:
            self.dma(brow[0:NV, :], bass.AP(i["mod_b"].tensor, i["mod_b"][l, :].offset, [[0, NV], [1, 3 * D]]),
                     r=[], w=["brow"])
            for pn in range(6):
                wt, wk = wring[cnt % 2]
                bank = 1 + cnt % 2
                cnt += 1
                self.dma(wt, i["mod_w"][l][:, pn * 512:(pn + 1) * 512].rearrange("(kc p) n -> p kc n", p=128), w=[wk])
                for kc in range(KC):
                    self.mm(self.ps[bank][0:NV, :], scT[:, kc, :], wt[:, kc, :], kc == 0, kc == KC - 1,
                            r=[wk, "scT"], w=[self.psk[bank]])
                self.tt("dve", mrow[0:NV, pn * 512:(pn + 1) * 512], self.ps[bank][0:NV, :],
                        brow[0:NV, pn * 512:(pn + 1) * 512], ALU.add, r=[self.psk[bank], "brow"], w=["mrow"])
            self.dma(self.m_d[l], mrow[0:NV, :], r=["mrow"], w=["m_d"], q="pool")
        self.top = top0

    def rows_to_cols(self, rows_aps, name):
        R = len(rows_aps)
        rt = self.alloc([D], F32)
        kr = self.key(name + "_rows")
        for r_, ap in enumerate(rows_aps):
            self.dma(rt[r_:r_ + 1, :], bass.AP(ap.tensor, ap.offset, [[0, 1], [1, D]]), r=["m_d"], w=[kr])
        colsT = self.alloc([KC, R], F32)
        kc_ = self.key(name + "_cols")
        assert KC * R <= 512
        for kc in range(KC):
            self.tr(self.ps[7][:, kc * R:(kc + 1) * R], rt[0:R, kc * 128:(kc + 1) * 128], r=[kr], w=["ps7"])
        self.p.add("dve", lambda e: e.tensor_copy(out=colsT.rearrange("p a b -> p (a b)"), in_=self.ps[7][:, 0:KC * R]),
                   r=["ps7"], w=[kc_])
        return colsT, kc_

    def layer_vectors(self, midx):
        i, NV = self.i, self.NV
        rows = [i["norm_g"][midx, :]]
        for v in range(NV):
            rows.append(self.m_d[midx, v, 0:D])
        for v in range(NV):
            rows.append(self.m_d[midx, v, D:2 * D])
        cols, ck = self.rows_to_cols(rows, "lv")
        gsT = self.alloc([KC, NV], F32)
        kg = self.key("gsT")
        for v in range(NV):
            self.p.add("dve", lambda e, v=v: e.scalar_tensor_tensor(
                out=gsT[:, :, v], in0=cols[:, :, 1 + NV + v], scalar=1.0, in1=cols[:, :, 0],
                op0=ALU.add, op1=ALU.mult), r=[ck], w=[kg])
        self.gsT, self.gsk = gsT, kg
        self.shT, self.shk = cols, ck
        self.midx = midx
        self.gate_tiles = {}

    def gate_b(self, v):
        gt = self.gate_tiles
        if v in gt:
            ent = gt.pop(v)
            gt[v] = ent
            return ent
        if len(gt) < 2:
            ent = (self.alloc([D], F32), self.key("gate_b"))
        else:
            old_v = next(iter(gt))
            ent = gt.pop(old_v)
        self.dma(ent[0], self.bcast_row(self.m_d[self.midx, v, 2 * D:3 * D], D), r=["m_d"], w=[ent[1]])
        gt[v] = ent
        return ent

    def stage1_setup(self, share=None):
        if share is None:
            self.xt_ring = self.ring("xt", 2, [D], F32)
            self.xn = self.alloc([D], F32)
            self.xnk = self.key("xn")
            self.junk = self.alloc([D], F32)
            self.junkk = self.key("junk")
        else:
            self.xt_ring = [share[0], share[1]]
            self.xn, self.xnk = share[2]
            self.junk, self.junkk = share[3]
        self.ss = self.alloc([4], F32)
        self.ssk = self.key("ss")
        self.xt_cnt = 0
        self.coff, self.loff = 0, self.TC

    def sumsq(self, x, ss, rk):
        junk = self.junk
        self.p.add("act", lambda e: e.activation(out=junk, in_=x, func=AF.Square, accum_out=ss),
                   r=rk, w=[self.junkk, self.ssk])

    def src_tile(self, li, b, is_ctx, t0, n=128):
        if li == 0:
            return (self.i["ctx"] if is_ctx else self.i["x"])[b, t0:t0 + n, :]
        return (self.xcs if is_ctx else self.xs)[b, t0:t0 + n, :]

    def xkey(self, b, is_ctx, t0):
        return "x_%d_%d_%d" % (b, int(is_ctx), t0 // 128)

    def stage1(self, li, b, do_ctx, hT, hk):
        TC, TL, NB = self.TC, self.TL, self.NB
        segs = []
        if do_ctx:
            segs += [(True, t0) for t0 in range(0, TC, 128)]
        segs += [(False, t0) for t0 in range(0, TL, 128)]
        for (is_ctx, t0) in segs:
            v = NB if is_ctx else b
            tok = self.coff + t0 if is_ctx else self.loff + t0
            xt, xk = self.xt_ring[self.xt_cnt % 2]
            self.xt_cnt += 1
            self.dma(xt, self.src_tile(li, b, is_ctx, t0), r=[self.xkey(b, is_ctx, t0)], w=[xk])
            ss = self.ss[:, 0:1]
            self.sumsq(xt, ss, [xk])
            self.rsqrt(ss, ss, NORM_EPS, 1.0 / D, r=[self.ssk], w=[self.ssk])
            xn = self.xn
            self.p.add("act", lambda e, xt=xt, ss=ss, xn=xn: e.activation(out=xn, in_=xt, func=AF.Identity, scale=ss),
                       r=[xk, self.ssk], w=[self.xnk])
            for g in range(2):
                bank = 6 + g
                for q in range(4):
                    kc = g * 4 + q
                    self.tr(self.ps[bank][:, q * 128:(q + 1) * 128], self.xn[:, kc * 128:(kc + 1) * 128],
                            r=[self.xnk], w=[self.psk[bank]])
                for q in range(4):
                    kc = g * 4 + q
                    self.ts("dve", hT[:, kc, tok:tok + 128], self.ps[bank][:, q * 128:(q + 1) * 128],
                            self.gsT[:, kc, v:v + 1], self.shT[:, kc, 1 + v:2 + v], ALU.mult, ALU.add,
                            r=[self.psk[bank], self.gsk, self.shk], w=[hk])

    def epi_setup(self):
        self.tmp = self.alloc([D], F32)
        self.tmpk = self.key("tmp")
        self.xnew_ring = self.ring("xnew", 2, [D], F32)
        self.epi_cnt = 0

    def epilogue_tile(self, li, b, is_ctx, t0, yT_fn, nkc, w_out, wok, ykeys, last, banks=(4, 5)):
        v = self.NB if is_ctx else b
        gb, gk = self.gate_b(v)
        xt, xk = self.xt_ring[self.xt_cnt % 2]
        self.xt_cnt += 1
        xkey = self.xkey(b, is_ctx, t0)
        self.dma(xt, self.src_tile(li, b, is_ctx, t0), r=[xkey], w=[xk])
        xn_, xnk_ = self.xnew_ring[self.epi_cnt % 2]
        self.epi_cnt += 1
        for half in range(2):
            bank = banks[half]
            for kc in range(nkc):
                self.mm(self.ps[bank][:, :], yT_fn(kc), w_out[:, kc, half * 512:(half + 1) * 512],
                        kc == 0, kc == nkc - 1, r=ykeys + [wok], w=[self.psk[bank]])
            hs = slice(half * 512, (half + 1) * 512)
            self.tt("dve", self.tmp[:, hs], self.ps[bank][:, :], gb[:, hs], ALU.mult,
                    r=[self.psk[bank], gk], w=[self.tmpk])
            self.tt("pool", xn_[:, hs], self.tmp[:, hs], xt[:, hs], ALU.add, r=[self.tmpk, xk], w=[xnk_])
        if last and not is_ctx:
            ss = self.ss[:, 1:2]
            self.sumsq(xn_, ss, [xnk_])
            self.rsqrt(ss, ss, NORM_EPS, 1.0 / D, r=[self.ssk], w=[self.ssk])
            self.stt(self.tmp, xn_, ss, self.fg_b, ALU.mult, ALU.mult, r=[xnk_, self.ssk], w=[self.tmpk])
            self.dma(self.out[b, t0:t0 + 128, :], self.tmp, r=[self.tmpk], w=["out"], q="pool")
        else:
            dst = (self.xcs if is_ctx else self.xs)[b, t0:t0 + 128, :]
            self.dma(dst, xn_, r=[xnk_], w=[xkey], q="pool")

    def only_final(self):
        self.stage1_setup()
        tmp = self.alloc([D], F32)
        for b in range(self.NB):
            for t0 in range(0, self.TL, 128):
                xt, xk = self.xt_ring[self.xt_cnt % 2]
                self.xt_cnt += 1
                self.dma(xt, self.i["x"][b, t0:t0 + 128, :], w=[xk])
                ss = self.ss[:, 1:2]
                self.sumsq(xt, ss, [xk])
                self.rsqrt(ss, ss, NORM_EPS, 1.0 / D, r=[self.ssk], w=[self.ssk])
                self.stt(tmp, xt, ss, self.fg_b, ALU.mult, ALU.mult, r=[xk, self.ssk], w=["tmpf"])
                self.dma(self.out[b, t0:t0 + 128, :], tmp, r=["tmpf"], w=["out"], q="pool")

    def conv_layer(self, li, j, ctx_out, midx, last):
        NB, TL, TC, TT = self.NB, self.TL, self.TC, self.TT
        i = self.i
        T2 = 256
        self.layer_vectors(midx)
        rows = [i["conv_dw"][j, t, :] for t in range(CONV_W)]
        rows += [i["conv_db"][j, :], i["conv_ln_g"][j, :], i["conv_ln_b"][j, :]]
        cv, cvk = self.rows_to_cols(rows, "cv")
        w_in = self.wb["cwi%d" % j]
        w_in_v = w_in.rearrange("(kc p) n -> p kc n", p=128)
        w_out = self.alloc([KC, D], BF16)
        wok = self.key("cwo")
        self.dma(w_out, self.wb["cwo%d" % j].rearrange("(kc p) n -> p kc n", p=128), r=["cwo%d" % j], w=[wok])
        w_g = self.alloc([KC, D], BF16)
        wgk = self.key("cwg")
        self.dma(w_g, w_in_v[:, :, 2 * D:3 * D], r=["cwi%d" % j], w=[wgk])
        hT = self.alloc([KC, TT], BF16)
        hk = self.key("hT")
        self.stage1_setup()
        self.epi_setup()
        seqs = []
        if ctx_out:
            seqs.append((True, 0, TC))
        seqs.append((False, TC, TL))
        GLW = sum(n + 2 * HALO for (_, _, n) in seqs)
        glu_d = self.dram("glu_d%d" % li, [KC, 128, GLW], BF16)
        wp_ring = self.ring("wp", 2, [KC, 2, 128], BF16)
        sig_ring = self.ring("sig", 2, [512], F32)
        gl_ring = self.ring("gl", 2, [GLW], BF16)
        for (t, k) in gl_ring:
            self.p.add("pool", lambda e, t=t: e.memset(t, 0.0), w=[k])
        glt_ring = self.ring("glt", 2, [KC, T2 + 2 * HALO], BF16)
        dg_ring = self.ring("dg", 2, [CONV_W, 128], BF16)
        sgt_ring = self.ring("sgt", 2, [T2], BF16)
        yb = self.alloc([KC, T2], BF16)
        ybk = self.key("yb")
        ysq = self.alloc([KC, T2], BF16)
        ysqk = self.key("ysq")
        y2_ring = self.ring("y2", 2, [KC, T2], BF16)
        mean = self.alloc([T2], F32)
        meank = self.key("mean")
        rstd = self.alloc([T2], F32)
        rstdk = self.key("rstd")
        t1_ring = self.ring("t1", 2, [T2], F32)
        s_ring = self.ring("s", 2, [T2], F32)
        cnt1 = 0
        cnt2 = 0
        for b in range(NB):
            self.stage1(li, b, ctx_out, hT, hk)
            for fc in range(KC):
                wp, wpk = wp_ring[fc % 2]
                for ab in range(2):
                    self.dma(wp[:, :, ab, :], w_in_v[:, :, ab * D + fc * 128: ab * D + (fc + 1) * 128],
                             r=["cwi%d" % j], w=[wpk])
                gl, glk = gl_ring[fc % 2]
                col = 0
                for (is_ctx, tok0, n) in seqs:
                    for (o, m) in _tiles(n, 512):
                        bA = (cnt1 % 2) * 2
                        bB = bA + 1
                        sg, sgk = sig_ring[cnt1 % 2]
                        cnt1 += 1
                        for ab, bank in ((0, bA), (1, bB)):
                            for kc in range(KC):
                                self.mm(self.ps[bank][:, 0:m], wp[:, kc, ab, :], hT[:, kc, tok0 + o: tok0 + o + m],
                                        kc == 0, kc == KC - 1, r=[wpk, hk], w=[self.psk[bank]])
                        self.act(sg[:, 0:m], self.ps[bB][:, 0:m], AF.Sigmoid, r=[self.psk[bB]], w=[sgk])
                        c0 = col + HALO + o
                        self.tt("dve", gl[:, c0:c0 + m], self.ps[bA][:, 0:m], sg[:, 0:m], ALU.mult,
                                r=[self.psk[bA], sgk], w=[glk])
                    col += n + 2 * HALO
                self.dma(glu_d[fc], gl, r=[glk], w=["glu_d"], q="pool")
            col = 0
            for (is_ctx, tok0, n) in seqs:
                for (o, m) in _tiles(n, T2):
                    glt, gltk = glt_ring[cnt2 % 2]
                    y2, y2k = y2_ring[cnt2 % 2]
                    cnt2 += 1
                    c0 = col + o
                    self.dma(glt[:, :, 0:m + 2 * HALO], glu_d[:, :, c0:c0 + m + 2 * HALO].rearrange("f p c -> p f c"),
                             r=["glu_d"], w=[gltk])
                    for fc in range(KC):
                        dg, dgk = dg_ring[fc % 2]
                        for t in range(CONV_W):
                            self.ts("dve", dg[:, t, :], self.ident_b, cv[:, fc, t:t + 1], None, ALU.mult, None,
                                    r=[cvk], w=[dgk])
                        bank = fc % 2
                        for t in range(CONV_W):
                            self.mm(self.ps[bank][:, 0:m], dg[:, t, :], glt[:, fc, t:t + m], t == 0, t == CONV_W - 1,
                                    r=[dgk, gltk], w=[self.psk[bank]])
                        self.act(yb[:, fc, 0:m], self.ps[bank][:, 0:m], AF.Identity, r=[self.psk[bank], cvk], w=[ybk],
                                 bias=cv[:, fc, CONV_W:CONV_W + 1])
                        self.act(ysq[:, fc, 0:m], self.ps[bank][:, 0:m], AF.Square, r=[self.psk[bank], cvk], w=[ysqk],
                                 bias=cv[:, fc, CONV_W:CONV_W + 1])
                    for fc in range(KC):
                        self.mm(self.ps[2][:, 0:m], self.ones_d, yb[:, fc, 0:m], fc == 0, fc == KC - 1,
                                r=[ybk], w=[self.psk[2]])
                    for fc in range(KC):
                        self.mm(self.ps[3][:, 0:m], self.ones_d, ysq[:, fc, 0:m], fc == 0, fc == KC - 1,
                                r=[ysqk], w=[self.psk[3]])
                    self.p.add("act", lambda e, m=m: e.copy(out=mean[:, 0:m], in_=self.ps[2][:, 0:m]),
                               r=[self.psk[2]], w=[meank])
                    self.tt("dve", rstd[:, 0:m], mean[:, 0:m], mean[:, 0:m], ALU.mult, r=[meank], w=[rstdk])
                    self.tt("dve", rstd[:, 0:m], self.ps[3][:, 0:m], rstd[:, 0:m], ALU.subtract,
                            r=[self.psk[3], rstdk], w=[rstdk])
                    self.rsqrt(rstd[:, 0:m], rstd[:, 0:m], LN_EPS, 1.0, r=[rstdk], w=[rstdk])
                    for fc in range(KC):
                        t1, t1k = t1_ring[fc % 2]
                        s_, sk = s_ring[fc % 2]
                        sgt, sgtk = sgt_ring[fc % 2]
                        bank = 6 + fc % 2
                        for kc in range(KC):
                            self.mm(self.ps[bank][:, 0:m], w_g[:, kc, fc * 128:(fc + 1) * 128],
                                    hT[:, kc, tok0 + o: tok0 + o + m], kc == 0, kc == KC - 1,
                                    r=[wgk, hk], w=[self.psk[bank]])
                        self.act(sgt[:, 0:m], self.ps[bank][:, 0:m], AF.Silu, r=[self.psk[bank]], w=[sgtk])
                        self.tt("dve", t1[:, 0:m], yb[:, fc, 0:m], mean[:, 0:m], ALU.subtract, r=[ybk, meank], w=[t1k])
                        self.tt("dve", t1[:, 0:m], t1[:, 0:m], rstd[:, 0:m], ALU.mult, r=[t1k, rstdk], w=[t1k])
                        self.act(s_[:, 0:m], t1[:, 0:m], AF.Silu, r=[t1k, cvk], w=[sk],
                                 scale=cv[:, fc, CONV_W + 1:CONV_W + 2], bias=cv[:, fc, CONV_W + 2:CONV_W + 3])
                        self.tt("pool", y2[:, fc, 0:m], s_[:, 0:m], sgt[:, 0:m], ALU.mult, r=[sk, sgtk], w=[y2k])
                    for (oo, mm_) in _tiles(m, 128):
                        self.epilogue_tile(li, b, is_ctx, o + oo,
                                           lambda kc, y2=y2, oo=oo: y2[:, kc, oo:oo + 128],
                                           KC, w_out, wok, [y2k], last)
                col += n + 2 * HALO

    def bc_free(self, ap2, pattern):
        a = list(ap2.ap)
        if pattern == "k":
            return bass.AP(ap2.tensor, ap2.offset, [list(a[0]), [0, 64], list(a[1])])
        return bass.AP(ap2.tensor, ap2.offset, [list(a[0]), list(a[1]), [0, 64]])

    def rwkv_layer(self, li, j, ctx_out, midx, last):
        NB, TL, TC, TT = self.NB, self.TL, self.TC, self.TT
        i = self.i
        names = ["r", "v", "kk", "sg", "dec0", "dec1", "kd0", "kd1", "bn0", "bn1", "o0", "o1"]
        A = {n: self.dram("rk_%s_%d" % (n, li), [NB, TT, D]) for n in names}
        self.rwkv_phase_a(li, j, midx, A)
        self.p.fence()
        self.top = self.persist_top
        self.rwkv_scan(li, A)
        self.p.fence()
        self.top = self.persist_top
        self.rwkv_phase_c(li, j, ctx_out, midx, last, A)

    def rwkv_phase_a(self, li, j, midx, A):
        NB, TL, TC, TT = self.NB, self.TL, self.TC, self.TT
        i = self.i
        self.layer_vectors(midx)
        muT, muk = self.rows_to_cols([i["rwkv_mu"][j, n, :] for n in range(6)], "mu")
        W = {}
        for nm in ("w_r", "w_k", "w_v", "w_g"):
            t = self.alloc([KC, D], BF16)
            k = self.key(nm)
            self.dma(t, self.wb["r" + nm].rearrange("(kc p) n -> p kc n", p=128), w=[k])
            W[nm] = (t, k)
        w1cat = self.alloc([KC, 128], BF16)
        a1cat = self.alloc([KC, 128], BF16)
        w2cat = self.alloc([D], BF16)
        a2cat = self.alloc([D], BF16)
        lk = self.key("lora")
        for d in range(2):
            self.dma(w1cat[:, :, d * 64:(d + 1) * 64], self.wb["rw1_%d" % d].rearrange("(kc p) n -> p kc n", p=128), w=[lk])
            self.dma(a1cat[:, :, d * 64:(d + 1) * 64], self.wb["ra1_%d" % d].rearrange("(kc p) n -> p kc n", p=128), w=[lk])
            self.dma(w2cat[d * 64:(d + 1) * 64, :], self.wb["rw2_%d" % d], w=[lk])
            self.dma(a2cat[d * 64:(d + 1) * 64, :], self.wb["ra2_%d" % d], w=[lk])
        pk = self.key("params")
        kkb = self.alloc([D], F32)
        kab = self.alloc([D], F32)
        self.dma(kkb, self.bcast_row(i["rwkv_k_k"][j, :], D), w=[pk])
        self.dma(kab, self.bcast_row(i["rwkv_k_a"][j, :], D), w=[pk])
        w0b, a0b = [], []
        for d in range(2):
            t = self.alloc([D], F32)
            self.dma(t, self.bcast_row(i["rwkv_w0"][j, d, :], D), w=[pk])
            w0b.append(t)
            t = self.alloc([D], F32)
            self.dma(t, self.bcast_row(i["rwkv_a0"][j, d, :], D), w=[pk])
            a0b.append(t)
        HW = TT + 4
        hT = self.alloc([KC, HW], BF16)
        hk = self.key("hT")
        self.p.add("pool", lambda e: e.memset(hT, 0.0), w=[hk])
        T = [(self.alloc([D], F32), self.key("T%d" % n)) for n in range(4)]
        self.stage1_setup(share=T)
        self.coff, self.loff = 1, TC + 3
        xx = self.alloc([KC, 128], F32)
        xxk = self.key("xx")
        tl = self.alloc([KC, 128], F32)
        tlk = self.key("tl")
        lerp = [(self.alloc([KC, 128], BF16), self.key("lerp%d" % n)) for n in range(6)]
        thT = self.alloc([128], BF16)
        thk = self.key("thT")
        ahT = self.alloc([128], BF16)
        ahk = self.key("ahT")
        kraw = self.alloc([D], F32)
        krk = self.key("kraw")
        kk = self.alloc([D], F32)
        kkk = self.key("kk")
        ssq = self.alloc([16], F32)
        ssqk = self.key("ssq")
        bankc = [0]

        def nxt():
            b_ = bankc[0] % 6
            bankc[0] += 1
            return b_

        def proj(n_, wname, half):
            wt, wk = W[wname]
            lp, lpk = lerp[n_]
            bank = nxt()
            for kc in range(KC):
                self.mm(self.ps[bank][:, :], lp[:, kc, :], wt[:, kc, half * 512:(half + 1) * 512], kc == 0, kc == KC - 1,
                        r=[lpk, wk], w=[self.psk[bank]])
            return bank

        def store(name, b, tok, tile, tk):
            self.dma(A[name][b, tok:tok + 128, :], tile, r=[tk], w=[self.key("st")], q="pool")

        for b in range(NB):
            self.stage1(li, b, True, hT, hk)
            for (col0, n, tokbase) in ((self.coff, TC, 0), (self.loff, TL, TC)):
                for t0 in range(0, n, 128):
                    c = col0 + t0
                    tok = tokbase + t0
                    hc_ = hT[:, :, c:c + 128]
                    self.tt("pool", xx, hT[:, :, c - 1:c + 127], hT[:, :, c + 1:c + 129], ALU.add, r=[hk], w=[xxk])
                    self.stt(xx, xx, 0.5, hc_, ALU.mult, ALU.subtract, r=[xxk, hk], w=[xxk])
                    for n_ in range(6):
                        mb = bass.AP(muT.tensor, muT[:, :, n_].offset, [list(muT.ap[0]), [6, KC], [0, 128]])
                        self.tt("pool", tl, xx, mb, ALU.mult, r=[xxk, muk], w=[tlk])
                        lp, lpk = lerp[n_]
                        self.tt("dve", lp, tl, hc_, ALU.add, r=[tlk, hk], w=[lpk])
                    for half in range(2):
                        hs = slice(half * 512, (half + 1) * 512)
                        bk = proj(2, "w_k", half)
                        self.p.add("act", lambda e, bk=bk, hs=hs: e.copy(out=kraw[:, hs], in_=self.ps[bk][:, :]),
                                   r=[self.psk[bk]], w=[krk])
                    T1, T1k = T[0]
                    T2, T2k = T[1]
                    T3, T3k = T[2]
                    T4, T4k = T[3]
                    self.tt("dve", T1, kraw, kkb, ALU.mult, r=[krk, pk], w=[T1k])
                    self.tt("pool", T2, T1, T1, ALU.mult, r=[T1k], w=[T2k])
                    self.p.add("dve", lambda e: e.tensor_reduce(out=ssq, in_=T2.rearrange("p (h f) -> p h f", f=64),
                                                                axis=AX.X, op=ALU.add), r=[T2k], w=[ssqk])
                    self.rsqrt(ssq, ssq, 1e-12, 1.0, r=[ssqk], w=[ssqk])
                    self.tt("pool", kk.rearrange("p (h f) -> p h f", f=64), T1.rearrange("p (h f) -> p h f", f=64),
                            self.bc_free(ssq, "v"), ALU.mult, r=[T1k, ssqk], w=[kkk])
                    store("kk", b, tok, kk, kkk)
                    for kc in range(KC):
                        self.mm(self.ps[6][:, 0:128], w1cat[:, kc, :], lerp[1][0][:, kc, :], kc == 0, kc == KC - 1,
                                r=[lk, lerp[1][1]], w=[self.psk[6]])
                    self.act(thT, self.ps[6][:, 0:128], AF.Tanh, r=[self.psk[6]], w=[thk])
                    for kc in range(KC):
                        self.mm(self.ps[7][:, 0:128], a1cat[:, kc, :], lerp[4][0][:, kc, :], kc == 0, kc == KC - 1,
                                r=[lk, lerp[4][1]], w=[self.psk[7]])
                    self.p.add("act", lambda e: e.copy(out=ahT, in_=self.ps[7][:, 0:128]), r=[self.psk[7]], w=[ahk])
                    for d in range(2):
                        ds = slice(d * 64, (d + 1) * 64)
                        for half in range(2):
                            hs = slice(half * 512, (half + 1) * 512)
                            bank = nxt()
                            self.mm(self.ps[bank][:, :], ahT[ds, :], a2cat[ds, hs], True, True, r=[ahk, lk], w=[self.psk[bank]])
                            self.tt("dve", T1[:, hs], self.ps[bank][:, :], a0b[d][:, hs], ALU.add,
                                    r=[self.psk[bank], pk], w=[T1k])
                        self.act(T1, T1, AF.Sigmoid, r=[T1k], w=[T1k])
                        self.stt(T2, T1, -1.0, kab, ALU.add, ALU.mult, r=[T1k, pk], w=[T2k])
                        self.stt(T2, T2, 1.0, kraw, ALU.add, ALU.mult, r=[T2k, krk], w=[T2k])
                        store("kd%d" % d, b, tok, T2, T2k)
                        self.stt(T3, kk, -1.0, T1, ALU.mult, ALU.mult, r=[kkk, T1k], w=[T3k])
                        store("bn%d" % d, b, tok, T3, T3k)
                        for half in range(2):
                            hs = slice(half * 512, (half + 1) * 512)
                            bank = nxt()
                            self.mm(self.ps[bank][:, :], thT[ds, :], w2cat[ds, hs], True, True, r=[thk, lk], w=[self.psk[bank]])
                            self.tt("dve", T4[:, hs], self.ps[bank][:, :], w0b[d][:, hs], ALU.add,
                                    r=[self.psk[bank], pk], w=[T4k])
                        self.act(T4, T4, AF.Sigmoid, r=[T4k], w=[T4k])
                        self.act(T4, T4, AF.Exp, r=[T4k], w=[T4k], scale=-math.exp(-0.5))
                        store("dec%d" % d, b, tok, T4, T4k)
                    for (n_, wname, name, tile, tk, fn) in ((0, "w_r", "r", T1, T1k, AF.Identity), (3, "w_v", "v", T2, T2k, AF.Identity),
                                                            (5, "w_g", "sg", T3, T3k, AF.Silu)):
                        for half in range(2):
                            hs = slice(half * 512, (half + 1) * 512)
                            bk = proj(n_, wname, half)
                            self.act(tile[:, hs], self.ps[bk][:, :], fn, r=[self.psk[bk]], w=[tk])
                        store(name, b, tok, tile, tk)

    def rwkv_scan(self, li, A):
        NB, TL, TC, TT = self.NB, self.TL, self.TC, self.TT
        P = 2 * NB * 16
        SB = 32
        S = self.alloc([64, 64], F32)
        Sk = self.key("S")
        tmp = self.alloc([64, 64], F32)
        tmpk = self.key("tmp")
        vk_ring = self.ring("vk", 2, [64, 64], F32)
        sa = self.alloc([64], F32)
        sak = self.key("sa")
        arrs = ["kk", "bn", "dec", "kd", "v", "r"]
        blk = [{a: self.alloc([SB, 64], F32) for a in arrs} for _ in range(2)]
        oblk = self.ring("oblk", 2, [SB, 64], F32)
        self.p.add("dve", lambda e: e.memset(S[0:P], 0.0), w=[Sk])
        cb = 0
        for (soff, n) in ((0, TC), (TC, TL)):
            for i0 in range(0, n, SB):
                slot = cb % 2
                cb += 1
                bkeys = []
                for d in range(2):
                    for b in range(NB):
                        p0 = d * NB * 16 + b * 16
                        if d == 0:
                            tok0, step = soff + i0, D
                        else:
                            tok0, step = soff + n - 1 - i0, -D
                        for a in arrs:
                            name = a + str(d) if a in ("bn", "dec", "kd") else a
                            src_t = A[name]
                            src = bass.AP(src_t.tensor, src_t[b, tok0, :].offset, [[64, 16], [step, SB], [1, 64]])
                            k = "blk_%d_%d_%d_%s" % (slot, d, b, a)
                            self.dma(blk[slot][a][p0:p0 + 16, :, :], src, w=[k])
                            bkeys.append(k)
                ob, obk = oblk[slot]
                for s_ in range(SB):
                    g = lambda a: blk[slot][a][0:P, s_, :]
                    vkt, vkk = vk_ring[s_ % 2]
                    Sv = S[0:P]
                    tv = tmp[0:P]
                    self.tt("pool", vkt[0:P], self.bc_free(g("v"), "v"), self.bc_free(g("kd"), "k"), ALU.mult,
                            r=bkeys, w=[vkk])
                    self.tt("dve", tv, Sv, self.bc_free(g("kk"), "k"), ALU.mult, r=[Sk] + bkeys, w=[tmpk])
                    self.p.add("dve", lambda e, tv=tv: e.tensor_reduce(out=sa[0:P], in_=tv, axis=AX.X, op=ALU.add),
                               r=[tmpk], w=[sak])
                    self.tt("dve", Sv, Sv, self.bc_free(g("dec"), "k"), ALU.mult, r=[Sk] + bkeys, w=[Sk])
                    self.tt("dve", tv, self.bc_free(sa[0:P], "v"), self.bc_free(g("bn"), "k"), ALU.mult,
                            r=[sak] + bkeys, w=[tmpk])
                    self.tt("dve", Sv, Sv, tv, ALU.add, r=[Sk, tmpk], w=[Sk])
                    self.tt("dve", Sv, Sv, vkt[0:P], ALU.add, r=[Sk, vkk], w=[Sk])
                    self.tt("dve", tv, Sv, self.bc_free(g("r"), "k"), ALU.mult, r=[Sk] + bkeys, w=[tmpk])
                    self.p.add("dve", lambda e, tv=tv, ob=ob, s_=s_: e.tensor_reduce(out=ob[0:P, s_, :], in_=tv, axis=AX.X, op=ALU.add),
                               r=[tmpk], w=[obk])
                for d in range(2):
                    for b in range(NB):
                        p0 = d * NB * 16 + b * 16
                        if d == 0:
                            tok0, step = soff + i0, D
                        else:
                            tok0, step = soff + n - 1 - i0, -D
                        dst_t = A["o%d" % d]
                        dst = bass.AP(dst_t.tensor, dst_t[b, tok0, :].offset, [[64, 16], [step, SB], [1, 64]])
                        self.dma(dst, ob[p0:p0 + 16, :, :], r=[obk], w=[self.key("ost")], q="pool")

    def rwkv_phase_c(self, li, j, ctx_out, midx, last, A):
        NB, TL, TC, TT = self.NB, self.TL, self.TC, self.TT
        i = self.i
        self.layer_vectors(midx)
        w_o = self.alloc([KC, D], BF16)
        wok = self.key("rwo")
        self.dma(w_o, self.wb["rw_o"].rearrange("(kc p) n -> p kc n", p=128), w=[wok])
        pk = self.key("cparams")
        lngb = self.alloc([D], F32)
        lnbb = self.alloc([D], F32)
        rkb = self.alloc([D], F32)
        self.dma(lngb, self.bcast_row(i["rwkv_ln_g"][j, :], D), w=[pk])
        self.dma(lnbb, self.bcast_row(i["rwkv_ln_b"][j, :], D), w=[pk])
        self.dma(rkb, self.bcast_row(i["rwkv_r_k"][j, :], D), w=[pk])
        self.stage1_setup()
        self.epi_setup()
        L = {n: self.ring("ld_" + n, 2, [D], F32) for n in ("o0", "o1", "r", "kd0", "kd1", "v", "sg")}
        o = self.alloc([D], F32)
        ok_ = self.key("o")
        sq = self.alloc([D], F32)
        sqk = self.key("sq")
        st = self.alloc([64], F32)
        stk = self.key("st")
        y2T_ring = self.ring("y2T", 2, [KC, 128], BF16)
        cnt = 0
        h3 = lambda t: t.rearrange("p (h f) -> p h f", f=64)
        for b in range(NB):
            segs = []
            if ctx_out:
                segs += [(True, t0, t0) for t0 in range(0, TC, 128)]
            segs += [(False, t0, TC + t0) for t0 in range(0, TL, 128)]
            for (is_ctx, t0, tok) in segs:
                ld = {}
                for n in L:
                    t, k = L[n][cnt % 2]
                    self.dma(t, A[n][b, tok:tok + 128, :], w=[k])
                    ld[n] = (t, k)
                y2T, y2Tk = y2T_ring[cnt % 2]
                cnt += 1
                self.tt("pool", o, ld["o0"][0], ld["o1"][0], ALU.add, r=[ld["o0"][1], ld["o1"][1]], w=[ok_])
                mean, ex2, bon, tmp16 = st[:, 0:16], st[:, 16:32], st[:, 32:48], st[:, 48:64]
                self.p.add("dve", lambda e: e.tensor_reduce(out=mean, in_=h3(o), axis=AX.X, op=ALU.add), r=[ok_], w=[stk])
                self.tt("pool", sq, o, o, ALU.mult, r=[ok_], w=[sqk])
                self.p.add("dve", lambda e: e.tensor_reduce(out=ex2, in_=h3(sq), axis=AX.X, op=ALU.add), r=[sqk], w=[stk])
                self.ts("dve", mean, mean, 1.0 / 64, None, ALU.mult, None, r=[stk], w=[stk])
                self.tt("dve", tmp16, mean, mean, ALU.mult, r=[stk], w=[stk])
                self.stt(ex2, ex2, 1.0 / 64, tmp16, ALU.mult, ALU.subtract, r=[stk], w=[stk])
                self.rsqrt(ex2, ex2, GN_EPS, 1.0, r=[stk], w=[stk])
                self.tt("dve", h3(o), h3(o), self.bc_free(mean, "v"), ALU.subtract, r=[ok_, stk], w=[ok_])
                self.tt("pool", h3(o), h3(o), self.bc_free(ex2, "v"), ALU.mult, r=[ok_, stk], w=[ok_])
                self.tt("dve", o, o, lngb, ALU.mult, r=[ok_, pk], w=[ok_])
                self.tt("pool", o, o, lnbb, ALU.add, r=[ok_, pk], w=[ok_])
                self.tt("pool", sq, ld["kd0"][0], ld["kd1"][0], ALU.add, r=[ld["kd0"][1], ld["kd1"][1]], w=[sqk])
                self.tt("dve", sq, sq, ld["r"][0], ALU.mult, r=[sqk, ld["r"][1]], w=[sqk])
                self.tt("pool", sq, sq, rkb, ALU.mult, r=[sqk, pk], w=[sqk])
                self.p.add("dve", lambda e: e.tensor_reduce(out=bon, in_=h3(sq), axis=AX.X, op=ALU.add), r=[sqk], w=[stk])
                self.tt("pool", h3(sq), h3(ld["v"][0]), self.bc_free(bon, "v"), ALU.mult, r=[ld["v"][1], stk, sqk], w=[sqk])
                self.tt("dve", o, o, sq, ALU.add, r=[ok_, sqk], w=[ok_])
                self.tt("pool", o, o, ld["sg"][0], ALU.mult, r=[ok_, ld["sg"][1]], w=[ok_])
                for g in range(2):
                    bank = 6 + g
                    for q in range(4):
                        kc = g * 4 + q
                        self.tr(self.ps[bank][:, q * 128:(q + 1) * 128], o[:, kc * 128:(kc + 1) * 128], r=[ok_], w=[self.psk[bank]])
                    self.p.add("act", lambda e, bank=bank, g=g, y2T=y2T: e.copy(
                        out=y2T[:, g * 4:(g + 1) * 4, :].rearrange("p a b -> p (a b)"), in_=self.ps[bank][:, :]),
                        r=[self.psk[bank]], w=[y2Tk])
                self.epilogue_tile(li, b, is_ctx, t0, lambda kc, y2T=y2T: y2T[:, kc, :], KC, w_o, wok, [y2Tk], last)

    def attn_layer(self, li, j, ctx_out, midx, last):
        assert not ctx_out
        NB, TL, TC, TT = self.NB, self.TL, self.TC, self.TT
        i = self.i
        NKT = TT // 128
        SC = 1.0 / math.sqrt(128.0)
        self.layer_vectors(midx)
        qg = self.alloc([1], F32)
        kg = self.alloc([1], F32)
        SK = globals().get("ATTN_SKIP", "")
        if "q" not in SK:
            self.dma(qg, bass.AP(i["attn_q_g"].tensor, i["attn_q_g"][j, :].offset, [[1, 128], [1, 1]]), w=["qg"])
            self.dma(kg, bass.AP(i["attn_k_g"].tensor, i["attn_k_g"][j, :].offset, [[1, 128], [1, 1]]), w=["kg"])
        cosT = self.alloc([TL], F32)
        sinT = self.alloc([TL], F32)
        if "c" not in SK:
            self.dma(cosT, i["k_cos"], w=["cosT"])
            self.dma(sinT, i["k_sin"], w=["sinT"])
        w_in_v = self.wb["awi"].rearrange("(kc p) n -> p kc n", p=128)
        w_out = self.alloc([16, D], BF16)
        wok = self.key("awo")
        if "w" not in SK:
            self.dma(w_out, self.wb["awo"].rearrange("(h p) n -> p h n", p=128), r=["awo"], w=[wok])
        og_d = self.dram("og_d%d" % li, [16, 128, TL], BF16)
        hT = self.alloc([KC, TT], BF16)
        hk = self.key("hT")
        self.stage1_setup()
        self.epi_setup()
        wring = self.ring("awp", 2, [KC, 768], BF16)
        kT = self.alloc([TT], BF16)
        kTk = self.key("kT")
        vT = self.alloc([NKT, 128], BF16)
        vTk = self.key("vT")
        qT = self.alloc([2, TL], BF16)
        qTk = self.key("qT")
        sgT = self.alloc([2, TL], BF16)
        sgTk = self.key("sgT")
        sq_ring = self.ring("sq", 1, [512], BF16) * 2
        rs_ring = self.ring("rs", 1, [512], F32) * 2
        kn_ring = self.ring("kn", 1, [512], BF16) * 2
        t1_ring = self.ring("rt1", 1, [512], F32) * 2
        t2_ring = self.ring("rt2", 1, [512], F32) * 2
        pT_ring = self.ring("pT", 3, [512], BF16)
        rz = self.alloc([512], F32)
        rzk = self.key("rz")
        o1 = self.alloc([512], F32)
        o1k = self.key("o1")
        og_ring = self.ring("og", 2, [512], BF16)
        ogt_ring = self.ring("ogt", 2, [16, 128], BF16)
        cn = [0]

        def normrope(ps_ap, psk, n, gvec, gk, dest, destk, pos0):
            c = cn[0]
            cn[0] += 1
            sq, sqk = sq_ring[c % 2]
            rs, rsk = rs_ring[c % 2]
            self.act(sq[:, 0:n], ps_ap, AF.Square, r=[psk], w=[sqk])
            self.mm(self.ps[0][:, 0:n], self.ones_h, sq[:, 0:n], True, True, r=[sqk], w=[self.psk[0]])
            self.rsqrt(rs[:, 0:n], self.ps[0][:, 0:n], NORM_EPS, 1.0, r=[self.psk[0]], w=[rsk])
            if pos0 is None:
                self.stt(dest, ps_ap, gvec, rs[:, 0:n], ALU.mult, ALU.mult, r=[psk, rsk, gk], w=[destk])
                return
            kn, knk = kn_ring[c % 2]
            t1, t1k = t1_ring[c % 2]
            t2, t2k = t2_ring[c % 2]
            self.stt(kn[:, 0:n], ps_ap, gvec, rs[:, 0:n], ALU.mult, ALU.mult, r=[psk, rsk, gk], w=[knk])
            self.mm(self.ps[1][:, 0:n], self.perm_b, kn[:, 0:n], True, True, r=[knk], w=[self.psk[1]])
            self.tt("dve", t1[:, 0:n], kn[:, 0:n], cosT[:, pos0:pos0 + n], ALU.mult, r=[knk, "cosT"], w=[t1k])
            self.tt("dve", t2[:, 0:n], self.ps[1][:, 0:n], sinT[:, pos0:pos0 + n], ALU.mult,
                    r=[self.psk[1], "sinT"], w=[t2k])
            self.tt("pool", dest, t1[:, 0:n], t2[:, 0:n], ALU.add, r=[t1k, t2k], w=[destk])

        cw = 0
        ca = 0
        STOP = globals().get("ATTN_STOP", 99)
        if STOP <= 2:
            return
        for b in range(NB):
            self.stage1(li, b, True, hT, hk)
            if STOP <= 3:
                return
            for g in range(8 if STOP > 4 else 1):
                wt, wtk = wring[cw % 2]
                cw += 1
                self.dma(wt[:, :, 0:256], w_in_v[:, :, 256 * g:256 * g + 256], r=["awi"], w=[wtk])
                self.dma(wt[:, :, 256:384], w_in_v[:, :, 2048 + 128 * g:2048 + 128 * g + 128], r=["awi"], w=[wtk])
                self.dma(wt[:, :, 384:512], w_in_v[:, :, 3072 + 128 * g:3072 + 128 * g + 128], r=["awi"], w=[wtk])
                self.dma(wt[:, :, 512:768], w_in_v[:, :, 4096 + 256 * g:4096 + 256 * g + 256], r=["awi"], w=[wtk])
                ktiles = [(o, m, None) for (o, m) in _tiles(TC, 512)] + [(TC + o, m, o) for (o, m) in _tiles(TL, 512)]
                for (tok0, n, pos0) in ktiles:
                    for kc in range(KC):
                        self.mm(self.ps[7][:, 0:n], wt[:, kc, 256:384], hT[:, kc, tok0:tok0 + n], kc == 0, kc == KC - 1,
                                r=[wtk, hk], w=[self.psk[7]])
                    normrope(self.ps[7][:, 0:n], self.psk[7], n, kg, "kg", kT[:, tok0:tok0 + n], kTk, pos0)
                for k0 in range(0, NKT, 4):
                    nk = min(4, NKT - k0)
                    for q in range(nk):
                        kt = k0 + q
                        for kc in range(KC):
                            self.mm(self.ps[2][:, q * 128:(q + 1) * 128], hT[:, kc, kt * 128:(kt + 1) * 128],
                                    wt[:, kc, 384:512], kc == 0, kc == KC - 1, r=[wtk, hk], w=[self.psk[2]])
                    self.p.add("act", lambda e, k0=k0, nk=nk: e.copy(
                        out=vT[:, k0:k0 + nk, :].rearrange("p a b -> p (a b)"), in_=self.ps[2][:, 0:nk * 128]),
                        r=[self.psk[2]], w=[vTk])
                for hq in range(2):
                    for (o, m) in _tiles(TL, 512):
                        for kc in range(KC):
                            self.mm(self.ps[7][:, 0:m], wt[:, kc, hq * 128:(hq + 1) * 128], hT[:, kc, TC + o:TC + o + m],
                                    kc == 0, kc == KC - 1, r=[wtk, hk], w=[self.psk[7]])
                        normrope(self.ps[7][:, 0:m], self.psk[7], m, qg, "qg", qT[:, hq, o:o + m], qTk, o)
                        for kc in range(KC):
                            self.mm(self.ps[2][:, 0:m], wt[:, kc, 512 + hq * 128:512 + (hq + 1) * 128],
                                    hT[:, kc, TC + o:TC + o + m], kc == 0, kc == KC - 1, r=[wtk, hk], w=[self.psk[2]])
                        self.act(sgT[:, hq, o:o + m], self.ps[2][:, 0:m], AF.Silu, r=[self.psk[2]], w=[sgTk])
                if STOP <= 5:
                    continue
                for hq in range(2):
                    for (o, m) in _tiles(TL, 512):
                        bO = 3 + 2 * (ca % 2)
                        bZ = bO + 1
                        og, ogk = og_ring[ca % 2]
                        ca += 1

                        def S(kt):
                            bank = kt % 3
                            self.mm(self.ps[bank][:, 0:m], kT[:, kt * 128:(kt + 1) * 128], qT[:, hq, o:o + m], True, True,
                                    r=[kTk, qTk], w=[self.psk[bank]])
                        S(0)
                        if NKT > 1:
                            S(1)
                        for kt in range(NKT):
                            bank = kt % 3
                            pT, pTk = pT_ring[kt % 3]
                            self.act(pT[:, 0:m], self.ps[bank][:, 0:m], AF.Exp, r=[self.psk[bank]], w=[pTk], scale=SC)
                            self.mm(self.ps[bO][:, 0:m], vT[:, kt, :], pT[:, 0:m], kt == 0, kt == NKT - 1,
                                    r=[vTk, pTk], w=[self.psk[bO]])
                            self.mm(self.ps[bZ][:, 0:m], self.ones_1, pT[:, 0:m], kt == 0, kt == NKT - 1,
                                    r=[pTk], w=[self.psk[bZ]])
                            if kt + 2 < NKT:
                                S(kt + 2)
                        self.p.add("dve", lambda e, bZ=bZ, m=m: e.reciprocal(out=rz[:, 0:m], in_=self.ps[bZ][:, 0:m]),
                                   r=[self.psk[bZ]], w=[rzk])
                        self.tt("dve", o1[:, 0:m], self.ps[bO][:, 0:m], rz[:, 0:m], ALU.mult, r=[self.psk[bO], rzk], w=[o1k])
                        self.tt("pool", og[:, 0:m], o1[:, 0:m], sgT[:, hq, o:o + m], ALU.mult, r=[o1k, sgTk], w=[ogk])
                        self.dma(og_d[2 * g + hq, :, o:o + m], og[:, 0:m], r=[ogk], w=["og_d"], q="pool")
            if STOP <= 6:
                return
            ce = 0
            for t0 in range(0, TL, 128):
                ogt, ogtk = ogt_ring[ce % 2]
                ce += 1
                self.dma(ogt, og_d[:, :, t0:t0 + 128].rearrange("h p t -> p h t"), r=["og_d"], w=[ogtk])
                self.epilogue_tile(li, b, False, t0, lambda kc, ogt=ogt: ogt[:, kc, :], 16, w_out, wok, [ogtk], last)


FULL_LAYERS = [(0, 0, True, True, 0), (1, 0, True, True, 1), (2, 0, True, False, 2), (0, 1, False, False, 3)]


def host_consts(TL):
    ident = np.eye(128, dtype=np.float32)
    perm = np.zeros((128, 128), np.float32)
    for k in range(128):
        blk = k // 32
        if blk % 2 == 0:
            perm[k, k + 32] = 1.0
        else:
            perm[k, k - 32] = -1.0
    rows = TL // GRID_W
    t = np.arange(TL)
    row = (t // GRID_W).astype(np.float32)
    col = (t % GRID_W).astype(np.float32)
    inv = (1.0 / (10000.0 ** (np.arange(0, 64, 2, dtype=np.float32) / np.float32(64)))).astype(np.float32)
    cosT = np.zeros((128, TL), np.float32)
    sinT = np.zeros((128, TL), np.float32)
    for p in range(128):
        pos = row if p < 64 else col
        ang = (pos * inv[p % 32]).astype(np.float32)
        cosT[p] = np.cos(ang)
        sinT[p] = np.sin(ang)
    return {"k_ident": ident, "k_perm": perm, "k_cos": cosT, "k_sin": sinT}


def make_in_maps(inputs, n_cores, NB, TL):
    consts = host_consts(TL)
    maps = []
    for cid in range(n_cores):
        m = {}
        for k, v in inputs.items():
            v = np.asarray(v)
            if k in ("x", "c", "ctx"):
                m[k] = np.ascontiguousarray(v[cid * NB:(cid + 1) * NB])
            elif k in ("c_ctx", "final_g"):
                m[k] = np.ascontiguousarray(v.reshape(1, -1))
            elif k == "rwkv_r_k":
                m[k] = np.ascontiguousarray(v.reshape(v.shape[0], -1))
            else:
                m[k] = np.ascontiguousarray(v)
        m.update(consts)
        maps.append(m)
    return maps


_CACHE = {}


def run(inputs, layers, n_cores=8, trace=False, dbg=False):
    x = np.asarray(inputs["x"])
    B, TL, _ = x.shape
    TC = np.asarray(inputs["ctx"]).shape[1]
    NB = B // n_cores
    mk = MK(NB, TL, TC, layers, dbg=dbg)
    nc = mk.build()
    maps = make_in_maps(inputs, n_cores, NB, TL)
    res = run_bass_kernel_spmd(nc, maps, core_ids=list(range(n_cores)), trace=trace)
    out = np.concatenate([np.asarray(r["out"]) for r in res.results], axis=0)
    return out.astype(np.float32), res


def kernel(**inputs):
    out, _ = run(inputs, FULL_LAYERS, n_cores=8)
    return out
```

```python
import contextlib
import math
import numpy as np
import ml_dtypes
import concourse.bass as bass
import concourse.mybir as mybir
from concourse.bass_utils import run_bass_kernel_spmd
from concourse.alu_op_type import AluOpType as ALU

F32 = mybir.dt.float32
BF16 = mybir.dt.bfloat16
AF = mybir.ActivationFunctionType
AX = mybir.AxisListType

D = 1024
KC = 8
GRID_W = 64
CONV_W = 31
HALO = 15
NORM_EPS = 1e-6
LN_EPS = 1e-5
GN_EPS = 64e-5

ENG = ("pe", "act", "dve", "pool", "sp")
NDMASEM = 12
RWKV_CHUNKED = True
SCAN_POOL = 0
RQ = "sp"


class Op:
    __slots__ = ("eng", "fn", "dma", "deps", "needs_inc", "cnt", "slot", "val", "idx")


class Prog:
    def __init__(self, nc):
        self.nc = nc
        self.ops = []
        self.by_eng = {e: [] for e in ENG}
        self.last_w = {}
        self.readers = {}
        self.ndma = {e: 0 for e in ENG}
        self.fence_deps = []
        self.fence_pending = set()

    def fence(self):
        deps = []
        for e in ENG:
            lst = self.by_eng[e]
            for op in reversed(lst):
                if not op.dma:
                    deps.append(op.idx)
                    break
            cnt = 0
            for op in reversed(lst):
                if op.dma:
                    deps.append(op.idx)
                    cnt += 1
                    if cnt >= NDMASEM:
                        break
        self.fence_deps = deps
        self.fence_pending = set(ENG)
        self.last_w.clear()
        self.readers.clear()

    def add(self, eng, fn, r=(), w=(), dma=False):
        op = Op()
        op.eng, op.fn, op.dma = eng, fn, dma
        op.needs_inc = False
        op.cnt = op.slot = op.val = None
        op.idx = len(self.ops)
        deps = set()
        lw = self.last_w
        for k in r:
            y = lw.get(k)
            if y is not None:
                deps.add(y)
        for k in w:
            y = lw.get(k)
            if y is not None:
                deps.add(y)
            ys = self.readers.get(k)
            if ys:
                deps.update(ys)
        keep = []
        for yi in deps:
            y = self.ops[yi]
            if (not y.dma) and (not dma) and y.eng == eng:
                if eng == "pe":
                    continue
                raw = False
                for k in r:
                    if lw.get(k) == yi:
                        raw = True
                        break
                if not raw:
                    continue
            keep.append(yi)
        if eng in self.fence_pending:
            self.fence_pending.discard(eng)
            keep = list(set(keep) | set(self.fence_deps))
        op.deps = keep
        for yi in keep:
            self.ops[yi].needs_inc = True
        if dma:
            n = self.ndma[eng]
            self.ndma[eng] = n + 1
            op.slot = n % NDMASEM
            op.val = 16 * (n // NDMASEM + 1)
        for k in r:
            self.readers.setdefault(k, []).append(op.idx)
        for k in w:
            lw[k] = op.idx
            self.readers[k] = []
        self.ops.append(op)
        self.by_eng[eng].append(op)
        return op

    def emit(self):
        nc = self.nc
        for e in ENG:
            c = 0
            for op in self.by_eng[e]:
                if (not op.dma) and op.needs_inc:
                    c += 1
                    op.cnt = c
        with contextlib.ExitStack() as st:
            csem = {e: st.enter_context(nc.semaphore("c_" + e)) for e in ENG}
            dsem = {e: [st.enter_context(nc.semaphore("d_%s_%d" % (e, i))) for i in range(NDMASEM)]
                    for e in ENG if self.ndma[e] > 0}
            block = st.enter_context(nc.Block())
            handles = {"pe": block.tensor, "act": block.scalar, "dve": block.vector,
                       "pool": block.gpsimd, "sp": block.sync}
            ops = self.ops

            def make(e):
                def body(eng):
                    waited = {}
                    for op in self.by_eng[e]:
                        need = {}
                        for yi in op.deps:
                            y = ops[yi]
                            if y.dma:
                                key = ("d", y.eng, y.slot)
                                v = y.val
                            else:
                                key = ("c", y.eng)
                                v = y.cnt
                            if need.get(key, 0) < v:
                                need[key] = v
                        if op.dma and op.val > 16:
                            key = ("d", e, op.slot)
                            if need.get(key, 0) < op.val - 16:
                                need[key] = op.val - 16
                        for key, v in need.items():
                            if waited.get(key, 0) >= v:
                                continue
                            waited[key] = v
                            sem = csem[key[1]] if key[0] == "c" else dsem[key[1]][key[2]]
                            eng.wait_ge(sem, v)
                        ins = op.fn(eng)
                        if op.dma:
                            ins.then_inc(dsem[e][op.slot], 16)
                        elif op.needs_inc:
                            ins.then_inc(csem[e], 1)
                    if e == "sp":
                        for q in ENG:
                            n = self.ndma[q]
                            for s in range(min(n, NDMASEM)):
                                cnt = (n - 1 - s) // NDMASEM + 1
                                if waited.get(("d", q, s), 0) < 16 * cnt:
                                    eng.wait_ge(dsem[q][s], 16 * cnt)
                        for q in ENG:
                            last = None
                            for op in self.by_eng[q]:
                                if op.cnt is not None:
                                    last = op.cnt
                            if last is not None and q != "sp":
                                eng.wait_ge(csem[q], last)
                return body

            for e in ENG:
                if self.by_eng[e] or e == "sp":
                    handles[e](make(e))


def _tiles(n, t):
    out = []
    o = 0
    while o < n:
        m = min(t, n - o)
        out.append((o, m))
        o += m
    return out


class MK:
    def __init__(self, NB, TL, TC, layers, dbg=False):
        self.NB, self.TL, self.TC, self.layers = NB, TL, TC, layers
        self.TT = TL + TC
        self.NV = NB + 1
        self.nc = bass.Bass("TRN2", target_bir_lowering=False)
        self.p = Prog(self.nc)
        self.st = contextlib.ExitStack()
        self.uid = 0
        self.dbg = dbg

    def dram_in(self, name, shape, dt=F32):
        return self.nc.dram_tensor(name, list(shape), dt, kind="ExternalInput").ap()

    def dram(self, name, shape, dt=F32):
        if self.dbg:
            return self.nc.dram_tensor(name, list(shape), dt, kind="ExternalOutput").ap()
        return self.nc.dram_tensor(name, list(shape), dt).ap()

    def alloc(self, free, dt=F32):
        n = 1
        for f in free:
            n *= f
        units = n * 2 if dt == F32 else n
        self.top = (self.top + 15) // 16 * 16
        assert self.top + units <= self.AREN, ("arena overflow", self.top, units, self.AREN)
        v = self.arena[:, self.top:self.top + units]
        self.top += units
        if dt == F32:
            v = v.bitcast(F32)
        if len(free) == 2:
            v = v.rearrange("p (a b) -> p a b", a=free[0])
        elif len(free) == 3:
            v = v.rearrange("p (a b c) -> p a b c", a=free[0], b=free[1])
        return v

    def key(self, s):
        self.uid += 1
        return "%s#%d" % (s, self.uid)

    def ring(self, name, n, free, dt=F32):
        return [(self.alloc(free, dt), self.key(name)) for _ in range(n)]

    def dma(self, out, in_, r=(), w=(), q="sp", **kw):
        return self.p.add(q, lambda e: e.dma_start(out=out, in_=in_, **kw), r=r, w=w, dma=True)

    def mm(self, out, lhsT, rhs, start, stop, r, w):
        return self.p.add("pe", lambda e: e.matmul(out, lhsT=lhsT, rhs=rhs, start=start, stop=stop), r=r, w=w)

    def tr(self, out, in_, r, w):
        P = in_.shape[0]
        ident = self.ident_f[0:P, 0:P]
        return self.p.add("pe", lambda e: e.transpose(out=out, in_=in_, identity=ident), r=r, w=w)

    def act(self, out, in_, func, r, w, scale=None, bias=None):
        kw = {}
        if scale is not None:
            kw["scale"] = scale
        if bias is not None:
            kw["bias"] = bias
        return self.p.add("act", lambda e: e.activation(out=out, in_=in_, func=func, **kw), r=r, w=w)

    def tt(self, eng, out, in0, in1, op, r, w):
        return self.p.add(eng, lambda e: e.tensor_tensor(out=out, in0=in0, in1=in1, op=op), r=r, w=w)

    def ts(self, eng, out, in0, s1, s2, op0, op1, r, w):
        if op1 is None:
            return self.p.add(eng, lambda e: e.tensor_scalar(out=out, in0=in0, scalar1=s1, scalar2=None, op0=op0), r=r, w=w)
        return self.p.add(eng, lambda e: e.tensor_scalar(out=out, in0=in0, scalar1=s1, scalar2=s2, op0=op0, op1=op1), r=r, w=w)

    def stt(self, out, in0, scalar, in1, op0, op1, r, w):
        return self.p.add("dve", lambda e: e.scalar_tensor_tensor(out=out, in0=in0, scalar=scalar, in1=in1, op0=op0, op1=op1), r=r, w=w)

    def rsqrt(self, out, in_, eps, mul, r, w):
        kt = self.key("rs")
        self.ts("dve", out, in_, mul, eps, ALU.mult, ALU.add, r=r, w=[kt])
        self.p.add("act", lambda e: e.activation(out=out, in_=out, func=AF.Sqrt), r=[kt], w=[kt])
        self.p.add("dve", lambda e: e.reciprocal(out=out, in_=out), r=[kt], w=w)

    def bcast_row(self, dram_ap_row, n):
        return bass.AP(dram_ap_row.tensor, dram_ap_row.offset, [[0, 128], [1, n]])

    def build(self):
        nc, NB, TL, TC, TT = self.nc, self.NB, self.TL, self.TC, self.TT
        NLAY = 4
        i = {}
        i["x"] = self.dram_in("x", [NB, TL, D])
        i["c"] = self.dram_in("c", [NB, D])
        i["ctx"] = self.dram_in("ctx", [NB, TC, D])
        i["c_ctx"] = self.dram_in("c_ctx", [1, D])
        i["norm_g"] = self.dram_in("norm_g", [NLAY, D])
        i["mod_w"] = self.dram_in("mod_w", [NLAY, D, 3 * D])
        i["mod_b"] = self.dram_in("mod_b", [NLAY, 3 * D])
        i["conv_w_in"] = self.dram_in("conv_w_in", [2, D, 3 * D])
        i["conv_dw"] = self.dram_in("conv_dw", [2, CONV_W, D])
        i["conv_db"] = self.dram_in("conv_db", [2, D])
        i["conv_ln_g"] = self.dram_in("conv_ln_g", [2, D])
        i["conv_ln_b"] = self.dram_in("conv_ln_b", [2, D])
        i["conv_w_out"] = self.dram_in("conv_w_out", [2, D, D])
        i["rwkv_mu"] = self.dram_in("rwkv_mu", [1, 6, D])
        for nm in ("rwkv_w_r", "rwkv_w_k", "rwkv_w_v", "rwkv_w_g", "rwkv_w_o"):
            i[nm] = self.dram_in(nm, [1, D, D])
        i["rwkv_w0"] = self.dram_in("rwkv_w0", [1, 2, D])
        i["rwkv_w1"] = self.dram_in("rwkv_w1", [1, 2, D, 64])
        i["rwkv_w2"] = self.dram_in("rwkv_w2", [1, 2, 64, D])
        i["rwkv_a0"] = self.dram_in("rwkv_a0", [1, 2, D])
        i["rwkv_a1"] = self.dram_in("rwkv_a1", [1, 2, D, 64])
        i["rwkv_a2"] = self.dram_in("rwkv_a2", [1, 2, 64, D])
        i["rwkv_k_k"] = self.dram_in("rwkv_k_k", [1, D])
        i["rwkv_k_a"] = self.dram_in("rwkv_k_a", [1, D])
        i["rwkv_r_k"] = self.dram_in("rwkv_r_k", [1, D])
        i["rwkv_ln_g"] = self.dram_in("rwkv_ln_g", [1, D])
        i["rwkv_ln_b"] = self.dram_in("rwkv_ln_b", [1, D])
        i["attn_w_in"] = self.dram_in("attn_w_in", [1, D, 6 * D])
        i["attn_q_g"] = self.dram_in("attn_q_g", [1, 128])
        i["attn_k_g"] = self.dram_in("attn_k_g", [1, 128])
        i["attn_w_out"] = self.dram_in("attn_w_out", [1, 2 * D, D])
        i["final_g"] = self.dram_in("final_g", [1, D])
        i["k_ident"] = self.dram_in("k_ident", [128, 128])
        i["k_perm"] = self.dram_in("k_perm", [128, 128])
        i["k_cos"] = self.dram_in("k_cos", [128, TL])
        i["k_sin"] = self.dram_in("k_sin", [128, TL])
        i["k_tri"] = self.dram_in("k_tri", [2, 128, 128])
        i["k_ind"] = self.dram_in("k_ind", [128, 2])
        i["k_masks"] = self.dram_in("k_masks", [5, 64, 512])
        self.i = i
        self.out = nc.dram_tensor("out", [NB, TL, D], F32, kind="ExternalOutput").ap()
        self.xs = self.dram("xs", [NB, TL, D])
        if self.dbg:
            self.xcs = nc.dram_tensor("xcs", [NB, TC, D], F32, kind="ExternalOutput").ap()
        else:
            self.xcs = self.dram("xcs", [NB, TC, D])
        self.m_d = self.dram("m_d", [NLAY, self.NV, 3 * D])

        st = self.st
        with st:
            self.AREN = 106000
            self.arena = st.enter_context(nc.sbuf_tensor("arena", [128, self.AREN], BF16))
            self.top = 0
            self.ps = [st.enter_context(nc.psum_tensor("ps%d" % k, [128, 512], F32)) for k in range(8)]
            self.psk = ["ps%d" % k for k in range(8)]
            self.prologue()
            self.persist_top = self.top
            used = set(l[0] for l in self.layers)
            self.convert_weights(used)
            self.modulation()
            nl = len(self.layers)
            for li, (kind, j, ctx_in, ctx_out, midx) in enumerate(self.layers):
                self.p.fence()
                self.top = self.persist_top
                last = li == nl - 1
                if kind == 0:
                    self.conv_layer(li, j, ctx_out, midx, last)
                elif kind == 1:
                    self.rwkv_layer(li, j, ctx_out, midx, last)
                else:
                    self.attn_layer(li, j, ctx_out, midx, last)
            if nl == 0:
                self.p.fence()
                self.top = self.persist_top
                self.only_final()
            self.p.emit()
        return nc

    def prologue(self):
        i = self.i
        self.ident_f = self.alloc([128], F32)
        self.dma(self.ident_f, i["k_ident"], w=["ident_f"])
        self.ident_b = self.alloc([128], BF16)
        self.p.add("dve", lambda e: e.tensor_copy(out=self.ident_b, in_=self.ident_f), r=["ident_f"], w=["ident_b"])
        permf = self.alloc([128], F32)
        self.dma(permf, i["k_perm"], w=["permf"])
        self.perm_b = self.alloc([128], BF16)
        self.p.add("dve", lambda e: e.tensor_copy(out=self.perm_b, in_=permf), r=["permf"], w=["perm_b"])
        self.ones_d = self.alloc([128], BF16)
        self.ones_h = self.alloc([128], BF16)
        self.ones_1 = self.alloc([128], BF16)
        self.p.add("pool", lambda e: e.memset(self.ones_d, 1.0 / D), w=["ones_d"])
        self.p.add("pool", lambda e: e.memset(self.ones_h, 1.0 / 128), w=["ones_h"])
        self.p.add("pool", lambda e: e.memset(self.ones_1, 1.0), w=["ones_1"])
        self.fg_b = self.alloc([D], F32)
        self.dma(self.fg_b, self.bcast_row(i["final_g"][0, :], D), w=["fg_b"])
        self.constkeys = ["ident_f", "ident_b", "perm_b", "ones_d", "ones_h", "ones_1", "fg_b"]

    def after_fence_consts(self):
        return

    def convert_weights(self, used):
        i = self.i
        self.wb = {}

        def conv(name, src, rows, cols):
            dst = self.dram(name, [rows, cols], BF16)
            for r0 in range(0, rows, 256):
                rr = min(256, rows - r0)
                self.dma(dst[r0:r0 + rr, :], src[r0:r0 + rr, :], w=[name], q="pool")
            self.wb[name] = dst

        if 0 in used:
            for j in sorted(set(l[1] for l in self.layers if l[0] == 0)):
                conv("cwi%d" % j, i["conv_w_in"][j], D, 3 * D)
                conv("cwo%d" % j, i["conv_w_out"][j], D, D)
        if 1 in used:
            for nm in ("w_r", "w_k", "w_v", "w_g", "w_o"):
                conv("r" + nm, i["rwkv_" + nm][0], D, D)
            for dd in range(2):
                conv("rw1_%d" % dd, i["rwkv_w1"][0, dd], D, 64)
                conv("ra1_%d" % dd, i["rwkv_a1"][0, dd], D, 64)
                conv("rw2_%d" % dd, i["rwkv_w2"][0, dd], 64, D)
                conv("ra2_%d" % dd, i["rwkv_a2"][0, dd], 64, D)
        if 2 in used:
            conv("awi", i["attn_w_in"][0], D, 6 * D)
            conv("awo", i["attn_w_out"][0], 2 * D, D)

    def modulation(self):
        i, NV, NB = self.i, self.NV, self.NB
        top0 = self.top
        crow = self.alloc([D], F32)
        self.dma(crow[0:NB, :], i["c"], w=["crow"])
        self.dma(crow[NB:NV, :], i["c_ctx"], w=["crow"])
        self.act(crow[0:NV, :], crow[0:NV, :], AF.Silu, r=["crow"], w=["crow"])
        scT = self.alloc([KC, NV], F32)
        for kc in range(KC):
            self.tr(self.ps[0][:, kc * NV:(kc + 1) * NV], crow[0:NV, kc * 128:(kc + 1) * 128],
                    r=["crow", "ident_f"], w=["ps0"])
        self.p.add("dve", lambda e: e.tensor_copy(out=scT.rearrange("p a b -> p (a b)"), in_=self.ps[0][:, 0:KC * NV]),
                   r=["ps0"], w=["scT"])
        wring = self.ring("modw", 2, [KC, 512], F32)
        mrow = self.alloc([3 * D], F32)
        brow = self.alloc([3 * D], F32)
        used_mod = sorted(set(l[4] for l in self.layers))
        cnt = 0
        for l in used_mod:
            self.dma(brow[0:NV, :], bass.AP(i["mod_b"].tensor, i["mod_b"][l, :].offset, [[0, NV], [1, 3 * D]]),
                     r=[], w=["brow"])
            for pn in range(6):
                wt, wk = wring[cnt % 2]
                bank = 1 + cnt % 2
                cnt += 1
                self.dma(wt, i["mod_w"][l][:, pn * 512:(pn + 1) * 512].rearrange("(kc p) n -> p kc n", p=128), w=[wk])
                for kc in range(KC):
                    self.mm(self.ps[bank][0:NV, :], scT[:, kc, :], wt[:, kc, :], kc == 0, kc == KC - 1,
                            r=[wk, "scT"], w=[self.psk[bank]])
                self.tt("dve", mrow[0:NV, pn * 512:(pn + 1) * 512], self.ps[bank][0:NV, :],
                        brow[0:NV, pn * 512:(pn + 1) * 512], ALU.add, r=[self.psk[bank], "brow"], w=["mrow"])
            self.dma(self.m_d[l], mrow[0:NV, :], r=["mrow"], w=["m_d"], q="pool")
        self.top = top0

    def rows_to_cols(self, rows_aps, name):
        R = len(rows_aps)
        rt = self.alloc([D], F32)
        kr = self.key(name + "_rows")
        for r_, ap in enumerate(rows_aps):
            self.dma(rt[r_:r_ + 1, :], bass.AP(ap.tensor, ap.offset, [[0, 1], [1, D]]), r=["m_d"], w=[kr])
        colsT = self.alloc([KC, R], F32)
        kc_ = self.key(name + "_cols")
        assert KC * R <= 512
        for kc in range(KC):
            self.tr(self.ps[7][:, kc * R:(kc + 1) * R], rt[0:R, kc * 128:(kc + 1) * 128], r=[kr], w=["ps7"])
        self.p.add("dve", lambda e: e.tensor_copy(out=colsT.rearrange("p a b -> p (a b)"), in_=self.ps[7][:, 0:KC * R]),
                   r=["ps7"], w=[kc_])
        return colsT, kc_

    def layer_vectors(self, midx):
        i, NV = self.i, self.NV
        rows = [i["norm_g"][midx, :]]
        for v in range(NV):
            rows.append(self.m_d[midx, v, 0:D])
        for v in range(NV):
            rows.append(self.m_d[midx, v, D:2 * D])
        cols, ck = self.rows_to_cols(rows, "lv")
        gsT = self.alloc([KC, NV], F32)
        kg = self.key("gsT")
        for v in range(NV):
            self.p.add("dve", lambda e, v=v: e.scalar_tensor_tensor(
                out=gsT[:, :, v], in0=cols[:, :, 1 + NV + v], scalar=1.0, in1=cols[:, :, 0],
                op0=ALU.add, op1=ALU.mult), r=[ck], w=[kg])
        self.gsT, self.gsk = gsT, kg
        self.shT, self.shk = cols, ck
        self.midx = midx
        self.gate_tiles = {}

    def gate_b(self, v):
        gt = self.gate_tiles
        if v in gt:
            ent = gt.pop(v)
            gt[v] = ent
            return ent
        if len(gt) < 2:
            ent = (self.alloc([D], F32), self.key("gate_b"))
        else:
            old_v = next(iter(gt))
            ent = gt.pop(old_v)
        self.dma(ent[0], self.bcast_row(self.m_d[self.midx, v, 2 * D:3 * D], D), r=["m_d"], w=[ent[1]])
        gt[v] = ent
        return ent

    def stage1_setup(self, share=None):
        if share is None:
            self.xt_ring = self.ring("xt", 2, [D], F32)
            self.xn = self.alloc([D], F32)
            self.xnk = self.key("xn")
            self.junk = self.alloc([D], F32)
            self.junkk = self.key("junk")
        else:
            self.xt_ring = [share[0], share[1]]
            self.xn, self.xnk = share[2]
            self.junk, self.junkk = share[3]
        self.ss = self.alloc([4], F32)
        self.ssk = self.key("ss")
        self.xt_cnt = 0
        self.coff, self.loff = 0, self.TC

    def sumsq(self, x, ss, rk):
        junk = self.junk
        self.p.add("act", lambda e: e.activation(out=junk, in_=x, func=AF.Square, accum_out=ss),
                   r=rk, w=[self.junkk, self.ssk])

    def src_tile(self, li, b, is_ctx, t0, n=128):
        if li == 0:
            return (self.i["ctx"] if is_ctx else self.i["x"])[b, t0:t0 + n, :]
        return (self.xcs if is_ctx else self.xs)[b, t0:t0 + n, :]

    def xkey(self, b, is_ctx, t0):
        return "x_%d_%d_%d" % (b, int(is_ctx), t0 // 128)

    def stage1(self, li, b, do_ctx, hT, hk):
        TC, TL, NB = self.TC, self.TL, self.NB
        segs = []
        if do_ctx:
            segs += [(True, t0) for t0 in range(0, TC, 128)]
        segs += [(False, t0) for t0 in range(0, TL, 128)]
        for (is_ctx, t0) in segs:
            v = NB if is_ctx else b
            tok = self.coff + t0 if is_ctx else self.loff + t0
            xt, xk = self.xt_ring[self.xt_cnt % 2]
            self.xt_cnt += 1
            self.dma(xt, self.src_tile(li, b, is_ctx, t0), r=[self.xkey(b, is_ctx, t0)], w=[xk])
            ss = self.ss[:, 0:1]
            self.sumsq(xt, ss, [xk])
            self.rsqrt(ss, ss, NORM_EPS, 1.0 / D, r=[self.ssk], w=[self.ssk])
            xn = self.xn
            self.p.add("act", lambda e, xt=xt, ss=ss, xn=xn: e.activation(out=xn, in_=xt, func=AF.Identity, scale=ss),
                       r=[xk, self.ssk], w=[self.xnk])
            for g in range(2):
                bank = 6 + g
                for q in range(4):
                    kc = g * 4 + q
                    self.tr(self.ps[bank][:, q * 128:(q + 1) * 128], self.xn[:, kc * 128:(kc + 1) * 128],
                            r=[self.xnk], w=[self.psk[bank]])
                for q in range(4):
                    kc = g * 4 + q
                    self.ts("dve", hT[:, kc, tok:tok + 128], self.ps[bank][:, q * 128:(q + 1) * 128],
                            self.gsT[:, kc, v:v + 1], self.shT[:, kc, 1 + v:2 + v], ALU.mult, ALU.add,
                            r=[self.psk[bank], self.gsk, self.shk], w=[hk])

    def epi_setup(self):
        self.tmp = self.alloc([D], F32)
        self.tmpk = self.key("tmp")
        self.xnew_ring = self.ring("xnew", 2, [D], F32)
        self.epi_cnt = 0

    def epilogue_tile(self, li, b, is_ctx, t0, yT_fn, nkc, w_out, wok, ykeys, last, banks=(4, 5)):
        v = self.NB if is_ctx else b
        gb, gk = self.gate_b(v)
        xt, xk = self.xt_ring[self.xt_cnt % 2]
        self.xt_cnt += 1
        xkey = self.xkey(b, is_ctx, t0)
        self.dma(xt, self.src_tile(li, b, is_ctx, t0), r=[xkey], w=[xk])
        xn_, xnk_ = self.xnew_ring[self.epi_cnt % 2]
        self.epi_cnt += 1
        for half in range(2):
            bank = banks[half]
            for kc in range(nkc):
                self.mm(self.ps[bank][:, :], yT_fn(kc), w_out[:, kc, half * 512:(half + 1) * 512],
                        kc == 0, kc == nkc - 1, r=ykeys + [wok], w=[self.psk[bank]])
            hs = slice(half * 512, (half + 1) * 512)
            self.tt("dve", self.tmp[:, hs], self.ps[bank][:, :], gb[:, hs], ALU.mult,
                    r=[self.psk[bank], gk], w=[self.tmpk])
            self.tt("pool", xn_[:, hs], self.tmp[:, hs], xt[:, hs], ALU.add, r=[self.tmpk, xk], w=[xnk_])
        if last and not is_ctx:
            ss = self.ss[:, 1:2]
            self.sumsq(xn_, ss, [xnk_])
            self.rsqrt(ss, ss, NORM_EPS, 1.0 / D, r=[self.ssk], w=[self.ssk])
            self.stt(self.tmp, xn_, ss, self.fg_b, ALU.mult, ALU.mult, r=[xnk_, self.ssk], w=[self.tmpk])
            self.dma(self.out[b, t0:t0 + 128, :], self.tmp, r=[self.tmpk], w=["out"], q="pool")
        else:
            dst = (self.xcs if is_ctx else self.xs)[b, t0:t0 + 128, :]
            self.dma(dst, xn_, r=[xnk_], w=[xkey], q="pool")

    def only_final(self):
        self.stage1_setup()
        tmp = self.alloc([D], F32)
        for b in range(self.NB):
            for t0 in range(0, self.TL, 128):
                xt, xk = self.xt_ring[self.xt_cnt % 2]
                self.xt_cnt += 1
                self.dma(xt, self.i["x"][b, t0:t0 + 128, :], w=[xk])
                ss = self.ss[:, 1:2]
                self.sumsq(xt, ss, [xk])
                self.rsqrt(ss, ss, NORM_EPS, 1.0 / D, r=[self.ssk], w=[self.ssk])
                self.stt(tmp, xt, ss, self.fg_b, ALU.mult, ALU.mult, r=[xk, self.ssk], w=["tmpf"])
                self.dma(self.out[b, t0:t0 + 128, :], tmp, r=["tmpf"], w=["out"], q="pool")

    def conv_layer(self, li, j, ctx_out, midx, last):
        NB, TL, TC, TT = self.NB, self.TL, self.TC, self.TT
        i = self.i
        T2 = 256
        self.layer_vectors(midx)
        rows = [i["conv_dw"][j, t, :] for t in range(CONV_W)]
        rows += [i["conv_db"][j, :], i["conv_ln_g"][j, :], i["conv_ln_b"][j, :]]
        cv, cvk = self.rows_to_cols(rows, "cv")
        w_in = self.wb["cwi%d" % j]
        w_in_v = w_in.rearrange("(kc p) n -> p kc n", p=128)
        w_out = self.alloc([KC, D], BF16)
        wok = self.key("cwo")
        self.dma(w_out, self.wb["cwo%d" % j].rearrange("(kc p) n -> p kc n", p=128), r=["cwo%d" % j], w=[wok])
        w_g = self.alloc([KC, D], BF16)
        wgk = self.key("cwg")
        self.dma(w_g, w_in_v[:, :, 2 * D:3 * D], r=["cwi%d" % j], w=[wgk])
        hT = self.alloc([KC, TT], BF16)
        hk = self.key("hT")
        self.stage1_setup()
        self.epi_setup()
        seqs = []
        if ctx_out:
            seqs.append((True, 0, TC))
        seqs.append((False, TC, TL))
        GLW = sum(n + 2 * HALO for (_, _, n) in seqs)
        glu_d = self.dram("glu_d%d" % li, [KC, 128, GLW], BF16)
        wp_ring = self.ring("wp", 2, [KC, 2, 128], BF16)
        sig_ring = self.ring("sig", 2, [512], F32)
        gl_ring = self.ring("gl", 2, [GLW], BF16)
        for (t, k) in gl_ring:
            self.p.add("pool", lambda e, t=t: e.memset(t, 0.0), w=[k])
        glt_ring = self.ring("glt", 2, [KC, T2 + 2 * HALO], BF16)
        dg_ring = self.ring("dg", 2, [CONV_W, 128], BF16)
        sgt_ring = self.ring("sgt", 2, [T2], BF16)
        yb = self.alloc([KC, T2], BF16)
        ybk = self.key("yb")
        ysq = self.alloc([KC, T2], BF16)
        ysqk = self.key("ysq")
        y2_ring = self.ring("y2", 2, [KC, T2], BF16)
        mean = self.alloc([T2], F32)
        meank = self.key("mean")
        rstd = self.alloc([T2], F32)
        rstdk = self.key("rstd")
        t1_ring = self.ring("t1", 2, [T2], F32)
        s_ring = self.ring("s", 2, [T2], F32)
        cnt1 = 0
        cnt2 = 0
        for b in range(NB):
            self.stage1(li, b, ctx_out, hT, hk)
            for fc in range(KC):
                wp, wpk = wp_ring[fc % 2]
                for ab in range(2):
                    self.dma(wp[:, :, ab, :], w_in_v[:, :, ab * D + fc * 128: ab * D + (fc + 1) * 128],
                             r=["cwi%d" % j], w=[wpk])
                gl, glk = gl_ring[fc % 2]
                col = 0
                for (is_ctx, tok0, n) in seqs:
                    for (o, m) in _tiles(n, 512):
                        bA = (cnt1 % 2) * 2
                        bB = bA + 1
                        sg, sgk = sig_ring[cnt1 % 2]
                        cnt1 += 1
                        for ab, bank in ((0, bA), (1, bB)):
                            for kc in range(KC):
                                self.mm(self.ps[bank][:, 0:m], wp[:, kc, ab, :], hT[:, kc, tok0 + o: tok0 + o + m],
                                        kc == 0, kc == KC - 1, r=[wpk, hk], w=[self.psk[bank]])
                        self.act(sg[:, 0:m], self.ps[bB][:, 0:m], AF.Sigmoid, r=[self.psk[bB]], w=[sgk])
                        c0 = col + HALO + o
                        self.tt("dve", gl[:, c0:c0 + m], self.ps[bA][:, 0:m], sg[:, 0:m], ALU.mult,
                                r=[self.psk[bA], sgk], w=[glk])
                    col += n + 2 * HALO
                self.dma(glu_d[fc], gl, r=[glk], w=["glu_d"], q="pool")
            col = 0
            for (is_ctx, tok0, n) in seqs:
                for (o, m) in _tiles(n, T2):
                    glt, gltk = glt_ring[cnt2 % 2]
                    y2, y2k = y2_ring[cnt2 % 2]
                    cnt2 += 1
                    c0 = col + o
                    self.dma(glt[:, :, 0:m + 2 * HALO], glu_d[:, :, c0:c0 + m + 2 * HALO].rearrange("f p c -> p f c"),
                             r=["glu_d"], w=[gltk])
                    for fc in range(KC):
                        dg, dgk = dg_ring[fc % 2]
                        for t in range(CONV_W):
                            self.ts("dve", dg[:, t, :], self.ident_b, cv[:, fc, t:t + 1], None, ALU.mult, None,
                                    r=[cvk], w=[dgk])
                        bank = fc % 2
                        for t in range(CONV_W):
                            self.mm(self.ps[bank][:, 0:m], dg[:, t, :], glt[:, fc, t:t + m], t == 0, t == CONV_W - 1,
                                    r=[dgk, gltk], w=[self.psk[bank]])
                        self.act(yb[:, fc, 0:m], self.ps[bank][:, 0:m], AF.Identity, r=[self.psk[bank], cvk], w=[ybk],
                                 bias=cv[:, fc, CONV_W:CONV_W + 1])
                        self.act(ysq[:, fc, 0:m], self.ps[bank][:, 0:m], AF.Square, r=[self.psk[bank], cvk], w=[ysqk],
                                 bias=cv[:, fc, CONV_W:CONV_W + 1])
                    for fc in range(KC):
                        self.mm(self.ps[2][:, 0:m], self.ones_d, yb[:, fc, 0:m], fc == 0, fc == KC - 1,
                                r=[ybk], w=[self.psk[2]])
                    for fc in range(KC):
                        self.mm(self.ps[3][:, 0:m], self.ones_d, ysq[:, fc, 0:m], fc == 0, fc == KC - 1,
                                r=[ysqk], w=[self.psk[3]])
                    self.p.add("act", lambda e, m=m: e.copy(out=mean[:, 0:m], in_=self.ps[2][:, 0:m]),
                               r=[self.psk[2]], w=[meank])
                    self.tt("dve", rstd[:, 0:m], mean[:, 0:m], mean[:, 0:m], ALU.mult, r=[meank], w=[rstdk])
                    self.tt("dve", rstd[:, 0:m], self.ps[3][:, 0:m], rstd[:, 0:m], ALU.subtract,
                            r=[self.psk[3], rstdk], w=[rstdk])
                    self.rsqrt(rstd[:, 0:m], rstd[:, 0:m], LN_EPS, 1.0, r=[rstdk], w=[rstdk])
                    for fc in range(KC):
                        t1, t1k = t1_ring[fc % 2]
                        s_, sk = s_ring[fc % 2]
                        sgt, sgtk = sgt_ring[fc % 2]
                        bank = 6 + fc % 2
                        for kc in range(KC):
                            self.mm(self.ps[bank][:, 0:m], w_g[:, kc, fc * 128:(fc + 1) * 128],
                                    hT[:, kc, tok0 + o: tok0 + o + m], kc == 0, kc == KC - 1,
                                    r=[wgk, hk], w=[self.psk[bank]])
                        self.act(sgt[:, 0:m], self.ps[bank][:, 0:m], AF.Silu, r=[self.psk[bank]], w=[sgtk])
                        self.tt("dve", t1[:, 0:m], yb[:, fc, 0:m], mean[:, 0:m], ALU.subtract, r=[ybk, meank], w=[t1k])
                        self.tt("dve", t1[:, 0:m], t1[:, 0:m], rstd[:, 0:m], ALU.mult, r=[t1k, rstdk], w=[t1k])
                        self.act(s_[:, 0:m], t1[:, 0:m], AF.Silu, r=[t1k, cvk], w=[sk],
                                 scale=cv[:, fc, CONV_W + 1:CONV_W + 2], bias=cv[:, fc, CONV_W + 2:CONV_W + 3])
                        self.tt("pool", y2[:, fc, 0:m], s_[:, 0:m], sgt[:, 0:m], ALU.mult, r=[sk, sgtk], w=[y2k])
                    for (oo, mm_) in _tiles(m, 128):
                        self.epilogue_tile(li, b, is_ctx, o + oo,
                                           lambda kc, y2=y2, oo=oo: y2[:, kc, oo:oo + 128],
                                           KC, w_out, wok, [y2k], last)
                col += n + 2 * HALO

    def bc_free(self, ap2, pattern):
        a = list(ap2.ap)
        if pattern == "k":
            return bass.AP(ap2.tensor, ap2.offset, [list(a[0]), [0, 64], list(a[1])])
        return bass.AP(ap2.tensor, ap2.offset, [list(a[0]), list(a[1]), [0, 64]])

    def rwkv_layer(self, li, j, ctx_out, midx, last):
        NB, TL, TC, TT = self.NB, self.TL, self.TC, self.TT
        i = self.i
        names = ["r", "v", "kk", "sg", "dec0", "dec1", "kd0", "kd1", "bn0", "bn1", "o0", "o1"]
        A = {n: self.dram("rk_%s_%d" % (n, li), [NB, TT, D]) for n in names}
        if RWKV_CHUNKED:
            NCH = TT // 64
            F = {}
            for d in range(2):
                for nm in ("KT", "RT", "KS", "NB"):
                    F[nm, d] = self.dram("rf_%s%d_%d" % (nm, d, li), [NB, 16, 64, TT], BF16)
                for nm in ("KSt", "NBt"):
                    F[nm, d] = self.dram("rf_%s%d_%d" % (nm, d, li), [NB, TT, D], BF16)
                F["pc", d] = self.dram("rf_pc%d_%d" % (d, li), [NB, NCH, 64, 16])
            F["Vt"] = self.dram("rf_Vt_%d" % li, [NB, TT, D], BF16)
            self.rwkv_phase_a(li, j, midx, A, F)
            self.p.fence()
            self.top = self.persist_top
            self.rwkv_scan2(li, A, F)
        else:
            self.rwkv_phase_a(li, j, midx, A, None)
            self.p.fence()
            self.top = self.persist_top
            self.rwkv_scan(li, A)
        self.p.fence()
        self.top = self.persist_top
        self.rwkv_phase_c(li, j, ctx_out, midx, last, A)

    def rwkv_phase_a(self, li, j, midx, A, F):
        NB, TL, TC, TT = self.NB, self.TL, self.TC, self.TT
        i = self.i
        self.layer_vectors(midx)
        muT, muk = self.rows_to_cols([i["rwkv_mu"][j, n, :] for n in range(6)], "mu")
        W = {}
        for nm in ("w_r", "w_k", "w_v", "w_g"):
            t = self.alloc([KC, D], BF16)
            k = self.key(nm)
            self.dma(t, self.wb["r" + nm].rearrange("(kc p) n -> p kc n", p=128), w=[k])
            W[nm] = (t, k)
        w1cat = self.alloc([KC, 128], BF16)
        a1cat = self.alloc([KC, 128], BF16)
        w2cat = self.alloc([D], BF16)
        a2cat = self.alloc([D], BF16)
        lk = self.key("lora")
        for d in range(2):
            self.dma(w1cat[:, :, d * 64:(d + 1) * 64], self.wb["rw1_%d" % d].rearrange("(kc p) n -> p kc n", p=128), w=[lk])
            self.dma(a1cat[:, :, d * 64:(d + 1) * 64], self.wb["ra1_%d" % d].rearrange("(kc p) n -> p kc n", p=128), w=[lk])
            self.dma(w2cat[d * 64:(d + 1) * 64, :], self.wb["rw2_%d" % d], w=[lk])
            self.dma(a2cat[d * 64:(d + 1) * 64, :], self.wb["ra2_%d" % d], w=[lk])
        pk = self.key("params")
        kkb = self.alloc([D], F32)
        kab = self.alloc([D], F32)
        self.dma(kkb, self.bcast_row(i["rwkv_k_k"][j, :], D), w=[pk])
        self.dma(kab, self.bcast_row(i["rwkv_k_a"][j, :], D), w=[pk])
        w0b, a0b = [], []
        for d in range(2):
            t = self.alloc([D], F32)
            self.dma(t, self.bcast_row(i["rwkv_w0"][j, d, :], D), w=[pk])
            w0b.append(t)
            t = self.alloc([D], F32)
            self.dma(t, self.bcast_row(i["rwkv_a0"][j, d, :], D), w=[pk])
            a0b.append(t)
        HW = TT + 4
        hT = self.alloc([KC, HW], BF16)
        hk = self.key("hT")
        self.p.add("pool", lambda e: e.memset(hT, 0.0), w=[hk])
        T = [(self.alloc([D], F32), self.key("T%d" % n)) for n in range(4)]
        self.stage1_setup(share=T)
        self.coff, self.loff = 1, TC + 3
        xx = self.alloc([KC, 128], F32)
        xxk = self.key("xx")
        tl = self.alloc([KC, 128], F32)
        tlk = self.key("tl")
        lerp = [(self.alloc([KC, 128], BF16), self.key("lerp%d" % n)) for n in range(6)]
        thT = self.alloc([128], BF16)
        thk = self.key("thT")
        ahT = self.alloc([128], BF16)
        ahk = self.key("ahT")
        kraw = self.alloc([D], F32)
        krk = self.key("kraw")
        kk = self.alloc([D], F32)
        kkk = self.key("kk")
        ssq = self.alloc([16], F32)
        ssqk = self.key("ssq")
        bankc = [0]
        if F is not None:
            rt = self.alloc([D], F32)
            rtk = self.key("rt")
            tri = self.alloc([2, 128], F32)
            ind = self.alloc([2], F32)
            ck = self.key("rconst")
            for d in range(2):
                self.dma(tri[:, d, :], i["k_tri"][d], w=[ck])
            self.dma(ind, i["k_ind"], w=[ck])
            fm_ring = self.ring("fmt", 2, [16, 128], BF16)
            tm_ring = self.ring("tmt", 1, [D], BF16) * 2
            pct = self.alloc([32], F32)
            pctk = self.key("pct")
            fmc = [0]
            tmc = [0]

            def emit_fm(tile, tk, name, d, b, tok):
                fmt, fmk = fm_ring[fmc[0] % 2]
                fmc[0] += 1
                for q4 in range(4):
                    bank = nxt()
                    for jj in range(4):
                        h = q4 * 4 + jj
                        self.tr(self.ps[bank][0:64, jj * 128:(jj + 1) * 128], tile[:, h * 64:(h + 1) * 64],
                                r=[tk], w=[self.psk[bank]])
                    self.p.add("act", lambda e, bank=bank, q4=q4, fmt=fmt: e.copy(
                        out=fmt[0:64, q4 * 4:(q4 + 1) * 4, :].rearrange("p a b -> p (a b)"), in_=self.ps[bank][0:64, :]),
                        r=[self.psk[bank]], w=[fmk])
                self.dma(F[name, d][b, :, :, tok:tok + 128].rearrange("h f t -> f h t"), fmt[0:64], r=[fmk],
                         w=[self.key("fst")], q=RQ)

            def emit_tm(tile, tk, dst):
                tmt, tmk = tm_ring[tmc[0] % 2]
                tmc[0] += 1
                self.p.add("pool", lambda e, tmt=tmt, tile=tile: e.tensor_copy(out=tmt, in_=tile), r=[tk], w=[tmk])
                self.dma(dst, tmt, r=[tmk], w=[self.key("tst")], q=RQ)

        def nxt():
            b_ = bankc[0] % 6
            bankc[0] += 1
            return b_

        def proj(n_, wname, half):
            wt, wk = W[wname]
            lp, lpk = lerp[n_]
            bank = nxt()
            for kc in range(KC):
                self.mm(self.ps[bank][:, :], lp[:, kc, :], wt[:, kc, half * 512:(half + 1) * 512], kc == 0, kc == KC - 1,
                        r=[lpk, wk], w=[self.psk[bank]])
            return bank

        def store(name, b, tok, tile, tk):
            self.dma(A[name][b, tok:tok + 128, :], tile, r=[tk], w=[self.key("st")], q=(RQ if F is not None else "pool"))

        for b in range(NB):
            self.stage1(li, b, True, hT, hk)
            for (col0, n, tokbase) in ((self.coff, TC, 0), (self.loff, TL, TC)):
                for t0 in range(0, n, 128):
                    c = col0 + t0
                    tok = tokbase + t0
                    hc_ = hT[:, :, c:c + 128]
                    self.tt("pool", xx, hT[:, :, c - 1:c + 127], hT[:, :, c + 1:c + 129], ALU.add, r=[hk], w=[xxk])
                    self.stt(xx, xx, 0.5, hc_, ALU.mult, ALU.subtract, r=[xxk, hk], w=[xxk])
                    for n_ in range(6):
                        mb = bass.AP(muT.tensor, muT[:, :, n_].offset, [list(muT.ap[0]), [6, KC], [0, 128]])
                        self.tt("pool", tl, xx, mb, ALU.mult, r=[xxk, muk], w=[tlk])
                        lp, lpk = lerp[n_]
                        self.tt("dve", lp, tl, hc_, ALU.add, r=[tlk, hk], w=[lpk])
                    for half in range(2):
                        hs = slice(half * 512, (half + 1) * 512)
                        bk = proj(2, "w_k", half)
                        self.p.add("act", lambda e, bk=bk, hs=hs: e.copy(out=kraw[:, hs], in_=self.ps[bk][:, :]),
                                   r=[self.psk[bk]], w=[krk])
                    T1, T1k = T[0]
                    T2, T2k = T[1]
                    T3, T3k = T[2]
                    T4, T4k = T[3]
                    self.tt("dve", T1, kraw, kkb, ALU.mult, r=[krk, pk], w=[T1k])
                    self.tt("pool", T2, T1, T1, ALU.mult, r=[T1k], w=[T2k])
                    self.p.add("dve", lambda e: e.tensor_reduce(out=ssq, in_=T2.rearrange("p (h f) -> p h f", f=64),
                                                                axis=AX.X, op=ALU.add), r=[T2k], w=[ssqk])
                    self.rsqrt(ssq, ssq, 1e-12, 1.0, r=[ssqk], w=[ssqk])
                    self.tt("pool", kk.rearrange("p (h f) -> p h f", f=64), T1.rearrange("p (h f) -> p h f", f=64),
                            self.bc_free(ssq, "v"), ALU.mult, r=[T1k, ssqk], w=[kkk])
                    if F is None:
                        store("kk", b, tok, kk, kkk)
                    else:
                        for half in range(2):
                            hs = slice(half * 512, (half + 1) * 512)
                            bk = proj(0, "w_r", half)
                            self.p.add("act", lambda e, bk=bk, hs=hs: e.copy(out=rt[:, hs], in_=self.ps[bk][:, :]),
                                       r=[self.psk[bk]], w=[rtk])
                        store("r", b, tok, rt, rtk)
                    for kc in range(KC):
                        self.mm(self.ps[6][:, 0:128], w1cat[:, kc, :], lerp[1][0][:, kc, :], kc == 0, kc == KC - 1,
                                r=[lk, lerp[1][1]], w=[self.psk[6]])
                    self.act(thT, self.ps[6][:, 0:128], AF.Tanh, r=[self.psk[6]], w=[thk])
                    for kc in range(KC):
                        self.mm(self.ps[7][:, 0:128], a1cat[:, kc, :], lerp[4][0][:, kc, :], kc == 0, kc == KC - 1,
                                r=[lk, lerp[4][1]], w=[self.psk[7]])
                    self.p.add("act", lambda e: e.copy(out=ahT, in_=self.ps[7][:, 0:128]), r=[self.psk[7]], w=[ahk])
                    for d in range(2):
                        ds = slice(d * 64, (d + 1) * 64)
                        for half in range(2):
                            hs = slice(half * 512, (half + 1) * 512)
                            bank = nxt()
                            self.mm(self.ps[bank][:, :], ahT[ds, :], a2cat[ds, hs], True, True, r=[ahk, lk], w=[self.psk[bank]])
                            self.tt("dve", T1[:, hs], self.ps[bank][:, :], a0b[d][:, hs], ALU.add,
                                    r=[self.psk[bank], pk], w=[T1k])
                        self.act(T1, T1, AF.Sigmoid, r=[T1k], w=[T1k])
                        self.stt(T2, T1, -1.0, kab, ALU.add, ALU.mult, r=[T1k, pk], w=[T2k])
                        self.stt(T2, T2, 1.0, kraw, ALU.add, ALU.mult, r=[T2k, krk], w=[T2k])
                        store("kd%d" % d, b, tok, T2, T2k)
                        self.stt(T3, kk, -1.0, T1, ALU.mult, ALU.mult, r=[kkk, T1k], w=[T3k])
                        if F is None:
                            store("bn%d" % d, b, tok, T3, T3k)
                        for half in range(2):
                            hs = slice(half * 512, (half + 1) * 512)
                            bank = nxt()
                            self.mm(self.ps[bank][:, :], thT[ds, :], w2cat[ds, hs], True, True, r=[thk, lk], w=[self.psk[bank]])
                            self.tt("dve", T4[:, hs], self.ps[bank][:, :], w0b[d][:, hs], ALU.add,
                                    r=[self.psk[bank], pk], w=[T4k])
                        self.act(T4, T4, AF.Sigmoid, r=[T4k], w=[T4k])
                        if F is None:
                            self.act(T4, T4, AF.Exp, r=[T4k], w=[T4k], scale=-math.exp(-0.5))
                            store("dec%d" % d, b, tok, T4, T4k)
                            continue
                        self.ts("dve", T4, T4, -math.exp(-0.5), None, ALU.mult, None, r=[T4k], w=[T4k])
                        for h in range(16):
                            self.mm(self.ps[7][0:64, h:h + 17:16], T4[:, h * 64:(h + 1) * 64], ind, True, True,
                                    r=[T4k, ck], w=[self.psk[7]])
                        self.act(pct[0:64, :], self.ps[7][0:64, 0:32], AF.Exp, r=[self.psk[7]], w=[pctk])
                        for jj in range(2):
                            self.dma(F["pc", d][b, tok // 64 + jj], pct[0:64, jj * 16:(jj + 1) * 16], r=[pctk],
                                     w=[self.key("pst")], q=RQ)
                        cb = []
                        for half in range(2):
                            hs = slice(half * 512, (half + 1) * 512)
                            bank = 6 + half
                            cb.append(bank)
                            self.mm(self.ps[bank][:, :], tri[:, d, :], T4[:, hs], True, True, r=[T4k, ck], w=[self.psk[bank]])
                        for half in range(2):
                            hs = slice(half * 512, (half + 1) * 512)
                            self.tt("dve", T4[:, hs], self.ps[cb[half]][:, :], T4[:, hs], ALU.subtract,
                                    r=[self.psk[cb[half]], T4k], w=[T4k])
                        self.act(T4, T4, AF.Exp, r=[T4k], w=[T4k])
                        self.tt("dve", T1, kk, T4, ALU.mult, r=[kkk, T4k], w=[T1k])
                        emit_fm(T1, T1k, "KT", d, b, tok)
                        for half in range(2):
                            hs = slice(half * 512, (half + 1) * 512)
                            self.act(T4[:, hs], self.ps[cb[half]][:, :], AF.Exp, r=[self.psk[cb[half]], T1k], w=[T4k],
                                     scale=-1.0)
                        self.tt("dve", T2, T2, T4, ALU.mult, r=[T2k, T4k], w=[T2k])
                        emit_fm(T2, T2k, "KS", d, b, tok)
                        emit_tm(T2, T2k, F["KSt", d][b, tok:tok + 128, :])
                        self.tt("dve", T3, T3, T4, ALU.mult, r=[T3k, T4k], w=[T3k])
                        emit_fm(T3, T3k, "NB", d, b, tok)
                        emit_tm(T3, T3k, F["NBt", d][b, tok:tok + 128, :])
                        for half in range(2):
                            hs = slice(half * 512, (half + 1) * 512)
                            self.act(T4[:, hs], self.ps[cb[half]][:, :], AF.Exp, r=[self.psk[cb[half]], T2k, T3k], w=[T4k])
                        self.tt("dve", T1, rt, T4, ALU.mult, r=[rtk, T4k], w=[T1k])
                        emit_fm(T1, T1k, "RT", d, b, tok)
                    plist = ((3, "w_v", "v", T2, T2k, AF.Identity), (5, "w_g", "sg", T3, T3k, AF.Silu))
                    if F is None:
                        plist = ((0, "w_r", "r", T1, T1k, AF.Identity),) + plist
                    for (n_, wname, name, tile, tk, fn) in plist:
                        for half in range(2):
                            hs = slice(half * 512, (half + 1) * 512)
                            bk = proj(n_, wname, half)
                            self.act(tile[:, hs], self.ps[bk][:, :], fn, r=[self.psk[bk]], w=[tk])
                        store(name, b, tok, tile, tk)
                        if F is not None and name == "v":
                            emit_tm(tile, tk, F["Vt"][b, tok:tok + 128, :])

    def rwkv_scan(self, li, A):
        NB, TL, TC, TT = self.NB, self.TL, self.TC, self.TT
        P = 2 * NB * 16
        SB = 32
        S = self.alloc([64, 64], F32)
        Sk = self.key("S")
        tmp = self.alloc([64, 64], F32)
        tmpk = self.key("tmp")
        vk_ring = self.ring("vk", 2, [64, 64], F32)
        sa = self.alloc([64], F32)
        sak = self.key("sa")
        arrs = ["kk", "bn", "dec", "kd", "v", "r"]
        blk = [{a: self.alloc([SB, 64], F32) for a in arrs} for _ in range(2)]
        oblk = self.ring("oblk", 2, [SB, 64], F32)
        self.p.add("dve", lambda e: e.memset(S[0:P], 0.0), w=[Sk])
        cb = 0
        for (soff, n) in ((0, TC), (TC, TL)):
            for i0 in range(0, n, SB):
                slot = cb % 2
                cb += 1
                bkeys = []
                for d in range(2):
                    for b in range(NB):
                        p0 = d * NB * 16 + b * 16
                        if d == 0:
                            tok0, step = soff + i0, D
                        else:
                            tok0, step = soff + n - 1 - i0, -D
                        for a in arrs:
                            name = a + str(d) if a in ("bn", "dec", "kd") else a
                            src_t = A[name]
                            src = bass.AP(src_t.tensor, src_t[b, tok0, :].offset, [[64, 16], [step, SB], [1, 64]])
                            k = "blk_%d_%d_%d_%s" % (slot, d, b, a)
                            self.dma(blk[slot][a][p0:p0 + 16, :, :], src, w=[k])
                            bkeys.append(k)
                ob, obk = oblk[slot]
                for s_ in range(SB):
                    g = lambda a: blk[slot][a][0:P, s_, :]
                    vkt, vkk = vk_ring[s_ % 2]
                    Sv = S[0:P]
                    tv = tmp[0:P]
                    self.tt("pool", vkt[0:P], self.bc_free(g("v"), "v"), self.bc_free(g("kd"), "k"), ALU.mult,
                            r=bkeys, w=[vkk])
                    self.tt("dve", tv, Sv, self.bc_free(g("kk"), "k"), ALU.mult, r=[Sk] + bkeys, w=[tmpk])
                    self.p.add("dve", lambda e, tv=tv: e.tensor_reduce(out=sa[0:P], in_=tv, axis=AX.X, op=ALU.add),
                               r=[tmpk], w=[sak])
                    e3 = "pool" if SCAN_POOL >= 1 else "dve"
                    e7 = "pool" if SCAN_POOL >= 2 else "dve"
                    self.tt(e3, Sv, Sv, self.bc_free(g("dec"), "k"), ALU.mult, r=[Sk] + bkeys, w=[Sk])
                    if SCAN_POOL >= 2:
                        self.tt(e7, Sv, Sv, vkt[0:P], ALU.add, r=[Sk, vkk], w=[Sk])
                    self.tt("dve", tv, self.bc_free(sa[0:P], "v"), self.bc_free(g("bn"), "k"), ALU.mult,
                            r=[sak] + bkeys, w=[tmpk])
                    self.tt("dve", Sv, Sv, tv, ALU.add, r=[Sk, tmpk], w=[Sk])
                    if SCAN_POOL < 2:
                        self.tt(e7, Sv, Sv, vkt[0:P], ALU.add, r=[Sk, vkk], w=[Sk])
                    self.tt("dve", tv, Sv, self.bc_free(g("r"), "k"), ALU.mult, r=[Sk] + bkeys, w=[tmpk])
                    self.p.add("dve", lambda e, tv=tv, ob=ob, s_=s_: e.tensor_reduce(out=ob[0:P, s_, :], in_=tv, axis=AX.X, op=ALU.add),
                               r=[tmpk], w=[obk])
                for d in range(2):
                    for b in range(NB):
                        p0 = d * NB * 16 + b * 16
                        if d == 0:
                            tok0, step = soff + i0, D
                        else:
                            tok0, step = soff + n - 1 - i0, -D
                        dst_t = A["o%d" % d]
                        dst = bass.AP(dst_t.tensor, dst_t[b, tok0, :].offset, [[64, 16], [step, SB], [1, 64]])
                        self.dma(dst, ob[p0:p0 + 16, :, :], r=[obk], w=[self.key("ost")], q="pool")

    def rwkv_scan2(self, li, A, F):
        NB, TL, TC, TT = self.NB, self.TL, self.TC, self.TT
        i = self.i
        C = 64
        masks = self.alloc([5, 512], F32)
        mk = self.key("masks")
        self.dma(masks[0:64], i["k_masks"].rearrange("m p c -> p m c"), w=[mk])
        US, LS, UI, LI, EYE = (masks[0:64, m, :] for m in range(5))
        units = [(d, b, g) for d in range(2) for b in range(NB) for g in range(2)]
        UB = 4
        fm_names = ("KT", "RT", "KS", "NB")
        tm_names = ("KSt", "NBt", "Vt")

        def t512(dt):
            return self.alloc([512], dt)[0:64]

        slots = []
        for s_ in range(UB):
            sl = {"ST": t512(F32), "STk": self.key("ST"), "STb": t512(BF16), "STbk": self.key("STb")}
            sl["op"] = []
            for par in range(2):
                o = {}
                for nm in fm_names:
                    o[nm] = (self.alloc([8, 64], BF16)[0:64], self.key(nm))
                for nm in tm_names:
                    o[nm] = (t512(BF16), self.key(nm))
                o["pc"] = (self.alloc([8], F32)[0:64], self.key("pc"))
                sl["op"].append(o)
            for nm in ("X", "XT"):
                sl[nm] = [(t512(F32), self.key(nm)) for _ in range(2)]
            sl["Tb"] = [(t512(BF16), self.key("Tb"))] * 2
            sl["Tf"] = (t512(F32), self.key("Tf"))
            for nm in ("Akk", "Akr", "nAbr", "RHST", "UT"):
                sl[nm] = (t512(BF16), self.key(nm))
            sl["ot"] = [(t512(F32), self.key("ot")) for _ in range(2)]
            slots.append(sl)
        bankc = [0]

        def nxtb():
            b_ = bankc[0] % 8
            bankc[0] += 1
            return b_

        def hc(ap, h):
            return ap[:, h * 64:(h + 1) * 64]

        def mmg(terms, rkeys):
            bank = nxtb()
            n = len(terms)
            for h in range(8):
                for ti, (lf, rf) in enumerate(terms):
                    self.mm(self.ps[bank][0:64, h * 64:(h + 1) * 64], lf(h), rf(h), ti == 0, ti == n - 1,
                            r=rkeys, w=[self.psk[bank]])
            return bank

        def pcopy(eng, out, bank, rk, wk):
            if eng == "act":
                self.p.add("act", lambda e: e.copy(out=out, in_=self.ps[bank][0:64, :]), r=[self.psk[bank]] + rk, w=wk)
            else:
                self.p.add("dve", lambda e: e.tensor_copy(out=out, in_=self.ps[bank][0:64, :]), r=[self.psk[bank]] + rk, w=wk)

        seqs = ((0, TC), (TC, TL))
        for batch in range(0, len(units), UB):
            ub = units[batch:batch + UB]
            chunks = []
            for (d, b, g) in ub:
                lst = []
                for (soff, n) in seqs:
                    toks = list(range(soff, soff + n, C))
                    if d == 1:
                        toks = toks[::-1]
                    lst += toks
                chunks.append(lst)
            for s_, _u in enumerate(ub):
                sl = slots[s_]
                self.p.add("dve", lambda e, sl=sl: e.memset(sl["ST"], 0.0), w=[sl["STk"]])
                self.p.add("dve", lambda e, sl=sl: e.memset(sl["STb"], 0.0), w=[sl["STbk"]])
            NCH = len(chunks[0])
            for ci in range(NCH):
                par = ci % 2
                cur = []
                for s_, (d, b, g) in enumerate(ub):
                    sl = slots[s_]
                    o = sl["op"][par]
                    tok0 = chunks[s_][ci]
                    for nm in fm_names:
                        t, k = o[nm]
                        self.dma(t, F[nm, d][b, g * 8:(g + 1) * 8, :, tok0:tok0 + C].rearrange("h f t -> f h t"), w=[k])
                    for nm in tm_names:
                        t, k = o[nm]
                        src = F["Vt"] if nm == "Vt" else F[nm, d]
                        self.dma(t, src[b, tok0:tok0 + C, g * 512:(g + 1) * 512], w=[k])
                    t, k = o["pc"]
                    self.dma(t, F["pc", d][b, tok0 // C, :, g * 8:(g + 1) * 8], w=[k])
                    mX, mXT, mS, mI = (US, LS, US, UI) if d == 0 else (LS, US, LS, LI)
                    cur.append((sl, o, d, b, g, tok0, mX, mXT, mS, mI))
                for (sl, o, d, b, g, tok0, mX, mXT, mS, mI) in cur:
                    NBf, NBk = o["NB"]
                    KT, KTk = o["KT"]
                    bk = mmg([(lambda h: NBf[:, h, :], lambda h: KT[:, h, :])], [NBk, KTk])
                    X0, X0k = sl["X"][0]
                    self.tt("dve", X0, self.ps[bk][0:64, :], mX, ALU.mult, r=[self.psk[bk], mk], w=[X0k])
                    bk = mmg([(lambda h: KT[:, h, :], lambda h: NBf[:, h, :])], [NBk, KTk])
                    XT0, XT0k = sl["XT"][0]
                    self.tt("dve", XT0, self.ps[bk][0:64, :], mXT, ALU.mult, r=[self.psk[bk], mk], w=[XT0k])
                for (sl, o, d, b, g, tok0, mX, mXT, mS, mI) in cur:
                    X0, X0k = sl["X"][0]
                    Tf, Tfk = sl["Tf"]
                    self.tt("dve", Tf, X0, EYE, ALU.add, r=[X0k, mk], w=[Tfk])
                for j in range(1, 6):
                    pj, cj = (j - 1) % 2, j % 2
                    for (sl, o, d, b, g, tok0, mX, mXT, mS, mI) in cur:
                        Xp, Xpk = sl["X"][pj]
                        XTp, XTpk = sl["XT"][pj]
                        if j < 5:
                            bk = mmg([(lambda h: hc(XTp, h), lambda h: hc(Xp, h))], [Xpk, XTpk])
                            pcopy("act", sl["X"][cj][0], bk, [], [sl["X"][cj][1]])
                        bk = mmg([(lambda h: hc(Xp, h), lambda h: hc(XTp, h))], [Xpk, XTpk])
                        pcopy("dve", sl["XT"][cj][0], bk, [], [sl["XT"][cj][1]])
                    for (sl, o, d, b, g, tok0, mX, mXT, mS, mI) in cur:
                        XTc, XTck = sl["XT"][cj]
                        Tf, Tfk = sl["Tf"]
                        bk = mmg([(lambda h: hc(XTc, h), lambda h: hc(Tf, h))], [XTck, Tfk])
                        self.tt("dve", Tf, Tf, self.ps[bk][0:64, :], ALU.add, r=[Tfk, self.psk[bk]], w=[Tfk])
                        if j == 5:
                            Tbc, Tbck = sl["Tb"][0]
                            self.p.add("act", lambda e, Tbc=Tbc, Tf=Tf: e.copy(out=Tbc, in_=Tf), r=[Tfk], w=[Tbck])
                for (sl, o, d, b, g, tok0, mX, mXT, mS, mI) in cur:
                    KS, KSk = o["KS"]
                    KT, KTk = o["KT"]
                    RT, RTk = o["RT"]
                    NBf, NBk = o["NB"]
                    for (nm, lf, lk_, rf, rk_, msk) in (("Akk", KS, KSk, KT, KTk, mS), ("Akr", KS, KSk, RT, RTk, mI),
                                                         ("nAbr", NBf, NBk, RT, RTk, mI)):
                        bk = mmg([(lambda h, lf=lf: lf[:, h, :], lambda h, rf=rf: rf[:, h, :])], [lk_, rk_])
                        self.tt("dve", sl[nm][0], self.ps[bk][0:64, :], msk, ALU.mult, r=[self.psk[bk], mk], w=[sl[nm][1]])
                Tfin = 5 % 2
                for (sl, o, d, b, g, tok0, mX, mXT, mS, mI) in cur:
                    KT, KTk = o["KT"]
                    Vt, Vtk = o["Vt"]
                    Akk, Akkk = sl["Akk"]
                    STb, STbk = sl["STb"], sl["STbk"]
                    bk = mmg([(lambda h: KT[:, h, :], lambda h: hc(STb, h)), (lambda h: hc(Akk, h), lambda h: hc(Vt, h))],
                             [KTk, STbk, Akkk, Vtk])
                    pcopy("act", sl["RHST"][0], bk, [], [sl["RHST"][1]])
                for (sl, o, d, b, g, tok0, mX, mXT, mS, mI) in cur:
                    Tb, Tbk = sl["Tb"][Tfin]
                    RH, RHk = sl["RHST"]
                    bk = mmg([(lambda h: hc(Tb, h), lambda h: hc(RH, h))], [Tbk, RHk])
                    pcopy("act", sl["UT"][0], bk, [], [sl["UT"][1]])
                for (sl, o, d, b, g, tok0, mX, mXT, mS, mI) in cur:
                    RT, RTk = o["RT"]
                    Vt, Vtk = o["Vt"]
                    Akr, Akrk = sl["Akr"]
                    nAbr, nAbrk = sl["nAbr"]
                    UT, UTk = sl["UT"]
                    STb, STbk = sl["STb"], sl["STbk"]
                    bk = mmg([(lambda h: RT[:, h, :], lambda h: hc(STb, h)), (lambda h: hc(Akr, h), lambda h: hc(Vt, h)),
                              (lambda h: hc(nAbr, h), lambda h: hc(UT, h))], [RTk, STbk, Akrk, Vtk, nAbrk, UTk])
                    ot, otk = sl["ot"][ci % 2]
                    pcopy("act", ot, bk, [], [otk])
                    self.dma(A["o%d" % d][b, tok0:tok0 + C, g * 512:(g + 1) * 512], ot, r=[otk], w=[self.key("ost")], q=RQ)
                if self.dbg and batch == 0 and ci == 0:
                    sl = cur[0][0]
                    for nm, (t_, k_) in (("X0f", sl["Tf"]), ("Tb", sl["Tb"][Tfin]), ("Akk", sl["Akk"]), ("Akr", sl["Akr"]),
                                         ("nAbr", sl["nAbr"]), ("RHST", sl["RHST"]), ("UT", sl["UT"])):
                        dd = self.nc.dram_tensor("dbg_" + nm, [64, 512], t_.dtype, kind="ExternalOutput").ap()
                        self.dma(dd, t_, r=[k_], w=[self.key("dbg")], q="pool")
                for (sl, o, d, b, g, tok0, mX, mXT, mS, mI) in cur:
                    KSt, KStk = o["KSt"]
                    NBt, NBtk = o["NBt"]
                    Vt, Vtk = o["Vt"]
                    UT, UTk = sl["UT"]
                    pc, pck = o["pc"]
                    ST, STk = sl["ST"], sl["STk"]
                    STb, STbk = sl["STb"], sl["STbk"]
                    bk = mmg([(lambda h: hc(KSt, h), lambda h: hc(Vt, h)), (lambda h: hc(NBt, h), lambda h: hc(UT, h))],
                             [KStk, Vtk, NBtk, UTk])
                    self.tt("dve", ST, ST, self.ps[bk][0:64, :], ALU.add, r=[STk, self.psk[bk]], w=[STk])
                    st3 = ST.rearrange("p (h v) -> p h v", v=64)
                    self.tt("dve", st3, st3, self.bc_free(pc, "v"), ALU.mult, r=[STk, pck], w=[STk])
                    self.p.add("act", lambda e, STb=STb, ST=ST: e.copy(out=STb, in_=ST), r=[STk], w=[STbk])

    def rwkv_phase_c(self, li, j, ctx_out, midx, last, A):
        NB, TL, TC, TT = self.NB, self.TL, self.TC, self.TT
        i = self.i
        self.layer_vectors(midx)
        w_o = self.alloc([KC, D], BF16)
        wok = self.key("rwo")
        self.dma(w_o, self.wb["rw_o"].rearrange("(kc p) n -> p kc n", p=128), w=[wok])
        pk = self.key("cparams")
        lngb = self.alloc([D], F32)
        lnbb = self.alloc([D], F32)
        rkb = self.alloc([D], F32)
        self.dma(lngb, self.bcast_row(i["rwkv_ln_g"][j, :], D), w=[pk])
        self.dma(lnbb, self.bcast_row(i["rwkv_ln_b"][j, :], D), w=[pk])
        self.dma(rkb, self.bcast_row(i["rwkv_r_k"][j, :], D), w=[pk])
        self.stage1_setup()
        self.epi_setup()
        L = {n: self.ring("ld_" + n, 2, [D], F32) for n in ("o0", "o1", "r", "kd0", "kd1", "v", "sg")}
        o = self.alloc([D], F32)
        ok_ = self.key("o")
        sq = self.alloc([D], F32)
        sqk = self.key("sq")
        st = self.alloc([64], F32)
        stk = self.key("st")
        y2T_ring = self.ring("y2T", 2, [KC, 128], BF16)
        cnt = 0
        h3 = lambda t: t.rearrange("p (h f) -> p h f", f=64)
        for b in range(NB):
            segs = []
            if ctx_out:
                segs += [(True, t0, t0) for t0 in range(0, TC, 128)]
            segs += [(False, t0, TC + t0) for t0 in range(0, TL, 128)]
            for (is_ctx, t0, tok) in segs:
                ld = {}
                for n in L:
                    t, k = L[n][cnt % 2]
                    self.dma(t, A[n][b, tok:tok + 128, :], w=[k])
                    ld[n] = (t, k)
                y2T, y2Tk = y2T_ring[cnt % 2]
                cnt += 1
                self.tt("pool", o, ld["o0"][0], ld["o1"][0], ALU.add, r=[ld["o0"][1], ld["o1"][1]], w=[ok_])
                mean, ex2, bon, tmp16 = st[:, 0:16], st[:, 16:32], st[:, 32:48], st[:, 48:64]
                self.p.add("dve", lambda e: e.tensor_reduce(out=mean, in_=h3(o), axis=AX.X, op=ALU.add), r=[ok_], w=[stk])
                self.tt("pool", sq, o, o, ALU.mult, r=[ok_], w=[sqk])
                self.p.add("dve", lambda e: e.tensor_reduce(out=ex2, in_=h3(sq), axis=AX.X, op=ALU.add), r=[sqk], w=[stk])
                self.ts("dve", mean, mean, 1.0 / 64, None, ALU.mult, None, r=[stk], w=[stk])
                self.tt("dve", tmp16, mean, mean, ALU.mult, r=[stk], w=[stk])
                self.stt(ex2, ex2, 1.0 / 64, tmp16, ALU.mult, ALU.subtract, r=[stk], w=[stk])
                self.rsqrt(ex2, ex2, GN_EPS, 1.0, r=[stk], w=[stk])
                self.tt("dve", h3(o), h3(o), self.bc_free(mean, "v"), ALU.subtract, r=[ok_, stk], w=[ok_])
                self.tt("pool", h3(o), h3(o), self.bc_free(ex2, "v"), ALU.mult, r=[ok_, stk], w=[ok_])
                self.tt("dve", o, o, lngb, ALU.mult, r=[ok_, pk], w=[ok_])
                self.tt("pool", o, o, lnbb, ALU.add, r=[ok_, pk], w=[ok_])
                self.tt("pool", sq, ld["kd0"][0], ld["kd1"][0], ALU.add, r=[ld["kd0"][1], ld["kd1"][1]], w=[sqk])
                self.tt("dve", sq, sq, ld["r"][0], ALU.mult, r=[sqk, ld["r"][1]], w=[sqk])
                self.tt("pool", sq, sq, rkb, ALU.mult, r=[sqk, pk], w=[sqk])
                self.p.add("dve", lambda e: e.tensor_reduce(out=bon, in_=h3(sq), axis=AX.X, op=ALU.add), r=[sqk], w=[stk])
                self.tt("pool", h3(sq), h3(ld["v"][0]), self.bc_free(bon, "v"), ALU.mult, r=[ld["v"][1], stk, sqk], w=[sqk])
                self.tt("dve", o, o, sq, ALU.add, r=[ok_, sqk], w=[ok_])
                self.tt("pool", o, o, ld["sg"][0], ALU.mult, r=[ok_, ld["sg"][1]], w=[ok_])
                for g in range(2):
                    bank = 6 + g
                    for q in range(4):
                        kc = g * 4 + q
                        self.tr(self.ps[bank][:, q * 128:(q + 1) * 128], o[:, kc * 128:(kc + 1) * 128], r=[ok_], w=[self.psk[bank]])
                    self.p.add("act", lambda e, bank=bank, g=g, y2T=y2T: e.copy(
                        out=y2T[:, g * 4:(g + 1) * 4, :].rearrange("p a b -> p (a b)"), in_=self.ps[bank][:, :]),
                        r=[self.psk[bank]], w=[y2Tk])
                self.epilogue_tile(li, b, is_ctx, t0, lambda kc, y2T=y2T: y2T[:, kc, :], KC, w_o, wok, [y2Tk], last)

    def attn_layer(self, li, j, ctx_out, midx, last):
        assert not ctx_out
        NB, TL, TC, TT = self.NB, self.TL, self.TC, self.TT
        i = self.i
        NKT = TT // 128
        SC = 1.0 / math.sqrt(128.0)
        self.layer_vectors(midx)
        qg = self.alloc([1], F32)
        kg = self.alloc([1], F32)
        SK = globals().get("ATTN_SKIP", "")
        if "q" not in SK:
            self.dma(qg, bass.AP(i["attn_q_g"].tensor, i["attn_q_g"][j, :].offset, [[1, 128], [1, 1]]), w=["qg"])
            self.dma(kg, bass.AP(i["attn_k_g"].tensor, i["attn_k_g"][j, :].offset, [[1, 128], [1, 1]]), w=["kg"])
        cosT = self.alloc([TL], F32)
        sinT = self.alloc([TL], F32)
        if "c" not in SK:
            self.dma(cosT, i["k_cos"], w=["cosT"])
            self.dma(sinT, i["k_sin"], w=["sinT"])
        w_in_v = self.wb["awi"].rearrange("(kc p) n -> p kc n", p=128)
        w_out = self.alloc([16, D], BF16)
        wok = self.key("awo")
        if "w" not in SK:
            self.dma(w_out, self.wb["awo"].rearrange("(h p) n -> p h n", p=128), r=["awo"], w=[wok])
        og_d = self.dram("og_d%d" % li, [16, 128, TL], BF16)
        hT = self.alloc([KC, TT], BF16)
        hk = self.key("hT")
        self.stage1_setup()
        self.epi_setup()
        wring = self.ring("awp", 2, [KC, 768], BF16)
        kT = self.alloc([TT], BF16)
        kTk = self.key("kT")
        vT = self.alloc([NKT, 128], BF16)
        vTk = self.key("vT")
        qT = self.alloc([2, TL], BF16)
        qTk = self.key("qT")
        sgT = self.alloc([2, TL], BF16)
        sgTk = self.key("sgT")
        sq_ring = self.ring("sq", 1, [512], BF16) * 2
        rs_ring = self.ring("rs", 1, [512], F32) * 2
        kn_ring = self.ring("kn", 1, [512], BF16) * 2
        t1_ring = self.ring("rt1", 1, [512], F32) * 2
        t2_ring = self.ring("rt2", 1, [512], F32) * 2
        pT_ring = self.ring("pT", 3, [512], BF16)
        rz = self.alloc([512], F32)
        rzk = self.key("rz")
        o1 = self.alloc([512], F32)
        o1k = self.key("o1")
        og_ring = self.ring("og", 2, [512], BF16)
        ogt_ring = self.ring("ogt", 2, [16, 128], BF16)
        cn = [0]

        def normrope(ps_ap, psk, n, gvec, gk, dest, destk, pos0):
            c = cn[0]
            cn[0] += 1
            sq, sqk = sq_ring[c % 2]
            rs, rsk = rs_ring[c % 2]
            self.act(sq[:, 0:n], ps_ap, AF.Square, r=[psk], w=[sqk])
            self.mm(self.ps[0][:, 0:n], self.ones_h, sq[:, 0:n], True, True, r=[sqk], w=[self.psk[0]])
            self.rsqrt(rs[:, 0:n], self.ps[0][:, 0:n], NORM_EPS, 1.0, r=[self.psk[0]], w=[rsk])
            if pos0 is None:
                self.stt(dest, ps_ap, gvec, rs[:, 0:n], ALU.mult, ALU.mult, r=[psk, rsk, gk], w=[destk])
                return
            kn, knk = kn_ring[c % 2]
            t1, t1k = t1_ring[c % 2]
            t2, t2k = t2_ring[c % 2]
            self.stt(kn[:, 0:n], ps_ap, gvec, rs[:, 0:n], ALU.mult, ALU.mult, r=[psk, rsk, gk], w=[knk])
            self.mm(self.ps[1][:, 0:n], self.perm_b, kn[:, 0:n], True, True, r=[knk], w=[self.psk[1]])
            self.tt("dve", t1[:, 0:n], kn[:, 0:n], cosT[:, pos0:pos0 + n], ALU.mult, r=[knk, "cosT"], w=[t1k])
            self.tt("dve", t2[:, 0:n], self.ps[1][:, 0:n], sinT[:, pos0:pos0 + n], ALU.mult,
                    r=[self.psk[1], "sinT"], w=[t2k])
            self.tt("pool", dest, t1[:, 0:n], t2[:, 0:n], ALU.add, r=[t1k, t2k], w=[destk])

        cw = 0
        ca = 0
        STOP = globals().get("ATTN_STOP", 99)
        if STOP <= 2:
            return
        for b in range(NB):
            self.stage1(li, b, True, hT, hk)
            if STOP <= 3:
                return
            for g in range(8 if STOP > 4 else 1):
                wt, wtk = wring[cw % 2]
                cw += 1
                self.dma(wt[:, :, 0:256], w_in_v[:, :, 256 * g:256 * g + 256], r=["awi"], w=[wtk])
                self.dma(wt[:, :, 256:384], w_in_v[:, :, 2048 + 128 * g:2048 + 128 * g + 128], r=["awi"], w=[wtk])
                self.dma(wt[:, :, 384:512], w_in_v[:, :, 3072 + 128 * g:3072 + 128 * g + 128], r=["awi"], w=[wtk])
                self.dma(wt[:, :, 512:768], w_in_v[:, :, 4096 + 256 * g:4096 + 256 * g + 256], r=["awi"], w=[wtk])
                ktiles = [(o, m, None) for (o, m) in _tiles(TC, 512)] + [(TC + o, m, o) for (o, m) in _tiles(TL, 512)]
                for (tok0, n, pos0) in ktiles:
                    for kc in range(KC):
                        self.mm(self.ps[7][:, 0:n], wt[:, kc, 256:384], hT[:, kc, tok0:tok0 + n], kc == 0, kc == KC - 1,
                                r=[wtk, hk], w=[self.psk[7]])
                    normrope(self.ps[7][:, 0:n], self.psk[7], n, kg, "kg", kT[:, tok0:tok0 + n], kTk, pos0)
                for k0 in range(0, NKT, 4):
                    nk = min(4, NKT - k0)
                    for q in range(nk):
                        kt = k0 + q
                        for kc in range(KC):
                            self.mm(self.ps[2][:, q * 128:(q + 1) * 128], hT[:, kc, kt * 128:(kt + 1) * 128],
                                    wt[:, kc, 384:512], kc == 0, kc == KC - 1, r=[wtk, hk], w=[self.psk[2]])
                    self.p.add("act", lambda e, k0=k0, nk=nk: e.copy(
                        out=vT[:, k0:k0 + nk, :].rearrange("p a b -> p (a b)"), in_=self.ps[2][:, 0:nk * 128]),
                        r=[self.psk[2]], w=[vTk])
                for hq in range(2):
                    for (o, m) in _tiles(TL, 512):
                        for kc in range(KC):
                            self.mm(self.ps[7][:, 0:m], wt[:, kc, hq * 128:(hq + 1) * 128], hT[:, kc, TC + o:TC + o + m],
                                    kc == 0, kc == KC - 1, r=[wtk, hk], w=[self.psk[7]])
                        normrope(self.ps[7][:, 0:m], self.psk[7], m, qg, "qg", qT[:, hq, o:o + m], qTk, o)
                        for kc in range(KC):
                            self.mm(self.ps[2][:, 0:m], wt[:, kc, 512 + hq * 128:512 + (hq + 1) * 128],
                                    hT[:, kc, TC + o:TC + o + m], kc == 0, kc == KC - 1, r=[wtk, hk], w=[self.psk[2]])
                        self.act(sgT[:, hq, o:o + m], self.ps[2][:, 0:m], AF.Silu, r=[self.psk[2]], w=[sgTk])
                if STOP <= 5:
                    continue
                for hq in range(2):
                    for (o, m) in _tiles(TL, 512):
                        bO = 3 + 2 * (ca % 2)
                        bZ = bO + 1
                        og, ogk = og_ring[ca % 2]
                        ca += 1

                        def S(kt):
                            bank = kt % 3
                            self.mm(self.ps[bank][:, 0:m], kT[:, kt * 128:(kt + 1) * 128], qT[:, hq, o:o + m], True, True,
                                    r=[kTk, qTk], w=[self.psk[bank]])
                        S(0)
                        if NKT > 1:
                            S(1)
                        for kt in range(NKT):
                            bank = kt % 3
                            pT, pTk = pT_ring[kt % 3]
                            self.act(pT[:, 0:m], self.ps[bank][:, 0:m], AF.Exp, r=[self.psk[bank]], w=[pTk], scale=SC)
                            self.mm(self.ps[bO][:, 0:m], vT[:, kt, :], pT[:, 0:m], kt == 0, kt == NKT - 1,
                                    r=[vTk, pTk], w=[self.psk[bO]])
                            self.mm(self.ps[bZ][:, 0:m], self.ones_1, pT[:, 0:m], kt == 0, kt == NKT - 1,
                                    r=[pTk], w=[self.psk[bZ]])
                            if kt + 2 < NKT:
                                S(kt + 2)
                        self.p.add("dve", lambda e, bZ=bZ, m=m: e.reciprocal(out=rz[:, 0:m], in_=self.ps[bZ][:, 0:m]),
                                   r=[self.psk[bZ]], w=[rzk])
                        self.tt("dve", o1[:, 0:m], self.ps[bO][:, 0:m], rz[:, 0:m], ALU.mult, r=[self.psk[bO], rzk], w=[o1k])
                        self.tt("pool", og[:, 0:m], o1[:, 0:m], sgT[:, hq, o:o + m], ALU.mult, r=[o1k, sgTk], w=[ogk])
                        self.dma(og_d[2 * g + hq, :, o:o + m], og[:, 0:m], r=[ogk], w=["og_d"], q="pool")
            if STOP <= 6:
                return
            ce = 0
            for t0 in range(0, TL, 128):
                ogt, ogtk = ogt_ring[ce % 2]
                ce += 1
                self.dma(ogt, og_d[:, :, t0:t0 + 128].rearrange("h p t -> p h t"), r=["og_d"], w=[ogtk])
                self.epilogue_tile(li, b, False, t0, lambda kc, ogt=ogt: ogt[:, kc, :], 16, w_out, wok, [ogtk], last)


FULL_LAYERS = [(0, 0, True, True, 0), (1, 0, True, True, 1), (2, 0, True, False, 2), (0, 1, False, False, 3)]


def host_consts(TL):
    ident = np.eye(128, dtype=np.float32)
    perm = np.zeros((128, 128), np.float32)
    for k in range(128):
        blk = k // 32
        if blk % 2 == 0:
            perm[k, k + 32] = 1.0
        else:
            perm[k, k - 32] = -1.0
    rows = TL // GRID_W
    t = np.arange(TL)
    row = (t // GRID_W).astype(np.float32)
    col = (t % GRID_W).astype(np.float32)
    inv = (1.0 / (10000.0 ** (np.arange(0, 64, 2, dtype=np.float32) / np.float32(64)))).astype(np.float32)
    cosT = np.zeros((128, TL), np.float32)
    sinT = np.zeros((128, TL), np.float32)
    for p in range(128):
        pos = row if p < 64 else col
        ang = (pos * inv[p % 32]).astype(np.float32)
        cosT[p] = np.cos(ang)
        sinT[p] = np.sin(ang)
    idx = np.arange(128)
    same = (idx[:, None] // 64) == (idx[None, :] // 64)
    tri = np.zeros((2, 128, 128), np.float32)
    tri[0] = (same & (idx[:, None] <= idx[None, :])).astype(np.float32)
    tri[1] = (same & (idx[:, None] >= idx[None, :])).astype(np.float32)
    ind = np.zeros((128, 2), np.float32)
    ind[:64, 0] = 1.0
    ind[64:, 1] = 1.0
    i64 = np.arange(64)
    us = (i64[:, None] < i64[None, :]).astype(np.float32)
    ls = (i64[:, None] > i64[None, :]).astype(np.float32)
    ui = (i64[:, None] <= i64[None, :]).astype(np.float32)
    li_ = (i64[:, None] >= i64[None, :]).astype(np.float32)
    eye = np.eye(64, dtype=np.float32)
    masks = np.stack([np.tile(m, (1, 8)) for m in (us, ls, ui, li_, eye)]).astype(np.float32)
    return {"k_ident": ident, "k_perm": perm, "k_cos": cosT, "k_sin": sinT,
            "k_tri": tri, "k_ind": ind, "k_masks": masks}


def make_in_maps(inputs, n_cores, NB, TL):
    consts = host_consts(TL)
    maps = []
    for cid in range(n_cores):
        m = {}
        for k, v in inputs.items():
            v = np.asarray(v)
            if k in ("x", "c", "ctx"):
                m[k] = np.ascontiguousarray(v[cid * NB:(cid + 1) * NB])
            elif k in ("c_ctx", "final_g"):
                m[k] = np.ascontiguousarray(v.reshape(1, -1))
            elif k == "rwkv_r_k":
                m[k] = np.ascontiguousarray(v.reshape(v.shape[0], -1))
            else:
                m[k] = np.ascontiguousarray(v)
        m.update(consts)
        maps.append(m)
    return maps


_CACHE = {}


def run(inputs, layers, n_cores=8, trace=False, dbg=False):
    x = np.asarray(inputs["x"])
    B, TL, _ = x.shape
    TC = np.asarray(inputs["ctx"]).shape[1]
    NB = B // n_cores
    mk = MK(NB, TL, TC, layers, dbg=dbg)
    nc = mk.build()
    maps = make_in_maps(inputs, n_cores, NB, TL)
    res = run_bass_kernel_spmd(nc, maps, core_ids=list(range(n_cores)), trace=trace)
    out = np.concatenate([np.asarray(r["out"]) for r in res.results], axis=0)
    return out.astype(np.float32), res


def kernel(**inputs):
    out, _ = run(inputs, FULL_LAYERS, n_cores=8)
    return out
```

```python
import contextlib
import math
import numpy as np
import ml_dtypes
import concourse.bass as bass
import concourse.mybir as mybir
from concourse.bass_utils import run_bass_kernel_spmd
from concourse.alu_op_type import AluOpType as ALU

F32 = mybir.dt.float32
BF16 = mybir.dt.bfloat16
AF = mybir.ActivationFunctionType
AX = mybir.AxisListType

D = 1024
KC = 8
GRID_W = 64
CONV_W = 31
HALO = 15
NORM_EPS = 1e-6
LN_EPS = 1e-5
GN_EPS = 64e-5

ENG = ("pe", "act", "dve", "pool", "sp")
NDMASEM = 12
RWKV_CHUNKED = True
SCAN_POOL = 0
RQ = "sp"


class Op:
    __slots__ = ("eng", "fn", "dma", "deps", "needs_inc", "cnt", "slot", "val", "idx")


class Prog:
    def __init__(self, nc):
        self.nc = nc
        self.ops = []
        self.by_eng = {e: [] for e in ENG}
        self.last_w = {}
        self.readers = {}
        self.ndma = {e: 0 for e in ENG}
        self.fence_deps = []
        self.fence_pending = set()

    def fence(self):
        deps = []
        for e in ENG:
            lst = self.by_eng[e]
            for op in reversed(lst):
                if not op.dma:
                    deps.append(op.idx)
                    break
            cnt = 0
            for op in reversed(lst):
                if op.dma:
                    deps.append(op.idx)
                    cnt += 1
                    if cnt >= NDMASEM:
                        break
        self.fence_deps = deps
        self.fence_pending = set(ENG)
        self.last_w.clear()
        self.readers.clear()

    def add(self, eng, fn, r=(), w=(), dma=False):
        op = Op()
        op.eng, op.fn, op.dma = eng, fn, dma
        op.needs_inc = False
        op.cnt = op.slot = op.val = None
        op.idx = len(self.ops)
        deps = set()
        lw = self.last_w
        for k in r:
            y = lw.get(k)
            if y is not None:
                deps.add(y)
        for k in w:
            y = lw.get(k)
            if y is not None:
                deps.add(y)
            ys = self.readers.get(k)
            if ys:
                deps.update(ys)
        keep = []
        for yi in deps:
            y = self.ops[yi]
            if (not y.dma) and (not dma) and y.eng == eng:
                if eng == "pe":
                    continue
                raw = False
                for k in r:
                    if lw.get(k) == yi:
                        raw = True
                        break
                if not raw:
                    continue
            keep.append(yi)
        if eng in self.fence_pending:
            self.fence_pending.discard(eng)
            keep = list(set(keep) | set(self.fence_deps))
        op.deps = keep
        for yi in keep:
            self.ops[yi].needs_inc = True
        if dma:
            n = self.ndma[eng]
            self.ndma[eng] = n + 1
            op.slot = n % NDMASEM
            op.val = 16 * (n // NDMASEM + 1)
        for k in r:
            self.readers.setdefault(k, []).append(op.idx)
        for k in w:
            lw[k] = op.idx
            self.readers[k] = []
        self.ops.append(op)
        self.by_eng[eng].append(op)
        return op

    def emit(self):
        nc = self.nc
        for e in ENG:
            c = 0
            for op in self.by_eng[e]:
                if (not op.dma) and op.needs_inc:
                    c += 1
                    op.cnt = c
        with contextlib.ExitStack() as st:
            csem = {e: st.enter_context(nc.semaphore("c_" + e)) for e in ENG}
            dsem = {e: [st.enter_context(nc.semaphore("d_%s_%d" % (e, i))) for i in range(NDMASEM)]
                    for e in ENG if self.ndma[e] > 0}
            block = st.enter_context(nc.Block())
            handles = {"pe": block.tensor, "act": block.scalar, "dve": block.vector,
                       "pool": block.gpsimd, "sp": block.sync}
            ops = self.ops

            def make(e):
                def body(eng):
                    waited = {}
                    for op in self.by_eng[e]:
                        need = {}
                        for yi in op.deps:
                            y = ops[yi]
                            if y.dma:
                                key = ("d", y.eng, y.slot)
                                v = y.val
                            else:
                                key = ("c", y.eng)
                                v = y.cnt
                            if need.get(key, 0) < v:
                                need[key] = v
                        if op.dma and op.val > 16:
                            key = ("d", e, op.slot)
                            if need.get(key, 0) < op.val - 16:
                                need[key] = op.val - 16
                        for key, v in need.items():
                            if waited.get(key, 0) >= v:
                                continue
                            waited[key] = v
                            sem = csem[key[1]] if key[0] == "c" else dsem[key[1]][key[2]]
                            eng.wait_ge(sem, v)
                        ins = op.fn(eng)
                        if op.dma:
                            ins.then_inc(dsem[e][op.slot], 16)
                        elif op.needs_inc:
                            ins.then_inc(csem[e], 1)
                    if e == "sp":
                        for q in ENG:
                            n = self.ndma[q]
                            for s in range(min(n, NDMASEM)):
                                cnt = (n - 1 - s) // NDMASEM + 1
                                if waited.get(("d", q, s), 0) < 16 * cnt:
                                    eng.wait_ge(dsem[q][s], 16 * cnt)
                        for q in ENG:
                            last = None
                            for op in self.by_eng[q]:
                                if op.cnt is not None:
                                    last = op.cnt
                            if last is not None and q != "sp":
                                eng.wait_ge(csem[q], last)
                return body

            for e in ENG:
                if self.by_eng[e] or e == "sp":
                    handles[e](make(e))


def _tiles(n, t):
    out = []
    o = 0
    while o < n:
        m = min(t, n - o)
        out.append((o, m))
        o += m
    return out


class MK:
    def __init__(self, NB, TL, TC, layers, dbg=False):
        self.NB, self.TL, self.TC, self.layers = NB, TL, TC, layers
        self.TT = TL + TC
        self.NV = NB + 1
        self.nc = bass.Bass("TRN2", target_bir_lowering=False)
        self.p = Prog(self.nc)
        self.st = contextlib.ExitStack()
        self.uid = 0
        self.dbg = dbg

    def dram_in(self, name, shape, dt=F32):
        return self.nc.dram_tensor(name, list(shape), dt, kind="ExternalInput").ap()

    def dram(self, name, shape, dt=F32):
        if self.dbg:
            return self.nc.dram_tensor(name, list(shape), dt, kind="ExternalOutput").ap()
        return self.nc.dram_tensor(name, list(shape), dt).ap()

    def alloc(self, free, dt=F32):
        n = 1
        for f in free:
            n *= f
        units = n * 2 if dt == F32 else n
        self.top = (self.top + 15) // 16 * 16
        assert self.top + units <= self.AREN, ("arena overflow", self.top, units, self.AREN)
        v = self.arena[:, self.top:self.top + units]
        self.top += units
        if dt == F32:
            v = v.bitcast(F32)
        if len(free) == 2:
            v = v.rearrange("p (a b) -> p a b", a=free[0])
        elif len(free) == 3:
            v = v.rearrange("p (a b c) -> p a b c", a=free[0], b=free[1])
        return v

    def key(self, s):
        self.uid += 1
        return "%s#%d" % (s, self.uid)

    def ring(self, name, n, free, dt=F32):
        return [(self.alloc(free, dt), self.key(name)) for _ in range(n)]

    def dma(self, out, in_, r=(), w=(), q="sp", **kw):
        return self.p.add(q, lambda e: e.dma_start(out=out, in_=in_, **kw), r=r, w=w, dma=True)

    def mm(self, out, lhsT, rhs, start, stop, r, w):
        return self.p.add("pe", lambda e: e.matmul(out, lhsT=lhsT, rhs=rhs, start=start, stop=stop), r=r, w=w)

    def tr(self, out, in_, r, w):
        P = in_.shape[0]
        ident = self.ident_f[0:P, 0:P]
        return self.p.add("pe", lambda e: e.transpose(out=out, in_=in_, identity=ident), r=r, w=w)

    def act(self, out, in_, func, r, w, scale=None, bias=None):
        kw = {}
        if scale is not None:
            kw["scale"] = scale
        if bias is not None:
            kw["bias"] = bias
        return self.p.add("act", lambda e: e.activation(out=out, in_=in_, func=func, **kw), r=r, w=w)

    def tt(self, eng, out, in0, in1, op, r, w):
        return self.p.add(eng, lambda e: e.tensor_tensor(out=out, in0=in0, in1=in1, op=op), r=r, w=w)

    def ts(self, eng, out, in0, s1, s2, op0, op1, r, w):
        if op1 is None:
            return self.p.add(eng, lambda e: e.tensor_scalar(out=out, in0=in0, scalar1=s1, scalar2=None, op0=op0), r=r, w=w)
        return self.p.add(eng, lambda e: e.tensor_scalar(out=out, in0=in0, scalar1=s1, scalar2=s2, op0=op0, op1=op1), r=r, w=w)

    def stt(self, out, in0, scalar, in1, op0, op1, r, w):
        return self.p.add("dve", lambda e: e.scalar_tensor_tensor(out=out, in0=in0, scalar=scalar, in1=in1, op0=op0, op1=op1), r=r, w=w)

    def rsqrt(self, out, in_, eps, mul, r, w):
        kt = self.key("rs")
        self.ts("dve", out, in_, mul, eps, ALU.mult, ALU.add, r=r, w=[kt])
        self.p.add("act", lambda e: e.activation(out=out, in_=out, func=AF.Sqrt), r=[kt], w=[kt])
        self.p.add("dve", lambda e: e.reciprocal(out=out, in_=out), r=[kt], w=w)

    def bcast_row(self, dram_ap_row, n):
        return bass.AP(dram_ap_row.tensor, dram_ap_row.offset, [[0, 128], [1, n]])

    def build(self):
        nc, NB, TL, TC, TT = self.nc, self.NB, self.TL, self.TC, self.TT
        NLAY = 4
        i = {}
        i["x"] = self.dram_in("x", [NB, TL, D])
        i["c"] = self.dram_in("c", [NB, D])
        i["ctx"] = self.dram_in("ctx", [NB, TC, D])
        i["c_ctx"] = self.dram_in("c_ctx", [1, D])
        i["norm_g"] = self.dram_in("norm_g", [NLAY, D])
        i["mod_w"] = self.dram_in("mod_w", [NLAY, D, 3 * D])
        i["mod_b"] = self.dram_in("mod_b", [NLAY, 3 * D])
        i["conv_w_in"] = self.dram_in("conv_w_in", [2, D, 3 * D])
        i["conv_dw"] = self.dram_in("conv_dw", [2, CONV_W, D])
        i["conv_db"] = self.dram_in("conv_db", [2, D])
        i["conv_ln_g"] = self.dram_in("conv_ln_g", [2, D])
        i["conv_ln_b"] = self.dram_in("conv_ln_b", [2, D])
        i["conv_w_out"] = self.dram_in("conv_w_out", [2, D, D])
        i["rwkv_mu"] = self.dram_in("rwkv_mu", [1, 6, D])
        for nm in ("rwkv_w_r", "rwkv_w_k", "rwkv_w_v", "rwkv_w_g", "rwkv_w_o"):
            i[nm] = self.dram_in(nm, [1, D, D])
        i["rwkv_w0"] = self.dram_in("rwkv_w0", [1, 2, D])
        i["rwkv_w1"] = self.dram_in("rwkv_w1", [1, 2, D, 64])
        i["rwkv_w2"] = self.dram_in("rwkv_w2", [1, 2, 64, D])
        i["rwkv_a0"] = self.dram_in("rwkv_a0", [1, 2, D])
        i["rwkv_a1"] = self.dram_in("rwkv_a1", [1, 2, D, 64])
        i["rwkv_a2"] = self.dram_in("rwkv_a2", [1, 2, 64, D])
        i["rwkv_k_k"] = self.dram_in("rwkv_k_k", [1, D])
        i["rwkv_k_a"] = self.dram_in("rwkv_k_a", [1, D])
        i["rwkv_r_k"] = self.dram_in("rwkv_r_k", [1, D])
        i["rwkv_ln_g"] = self.dram_in("rwkv_ln_g", [1, D])
        i["rwkv_ln_b"] = self.dram_in("rwkv_ln_b", [1, D])
        i["attn_w_in"] = self.dram_in("attn_w_in", [1, D, 6 * D])
        i["attn_q_g"] = self.dram_in("attn_q_g", [1, 128])
        i["attn_k_g"] = self.dram_in("attn_k_g", [1, 128])
        i["attn_w_out"] = self.dram_in("attn_w_out", [1, 2 * D, D])
        i["final_g"] = self.dram_in("final_g", [1, D])
        i["k_ident"] = self.dram_in("k_ident", [128, 128])
        i["k_perm"] = self.dram_in("k_perm", [128, 128])
        i["k_cos"] = self.dram_in("k_cos", [128, TL])
        i["k_sin"] = self.dram_in("k_sin", [128, TL])
        i["k_tri"] = self.dram_in("k_tri", [2, 128, 128])
        i["k_ind"] = self.dram_in("k_ind", [128, 2])
        i["k_masks"] = self.dram_in("k_masks", [5, 64, 512])
        self.i = i
        self.out = nc.dram_tensor("out", [NB, TL, D], F32, kind="ExternalOutput").ap()
        self.xs = self.dram("xs", [NB, TL, D])
        if self.dbg:
            self.xcs = nc.dram_tensor("xcs", [NB, TC, D], F32, kind="ExternalOutput").ap()
        else:
            self.xcs = self.dram("xcs", [NB, TC, D])
        self.m_d = self.dram("m_d", [NLAY, self.NV, 3 * D])

        st = self.st
        with st:
            self.AREN = 106000
            self.arena = st.enter_context(nc.sbuf_tensor("arena", [128, self.AREN], BF16))
            self.top = 0
            self.ps = [st.enter_context(nc.psum_tensor("ps%d" % k, [128, 512], F32)) for k in range(8)]
            self.psk = ["ps%d" % k for k in range(8)]
            self.prologue()
            self.persist_top = self.top
            used = set(l[0] for l in self.layers)
            self.convert_weights(used)
            self.modulation()
            nl = len(self.layers)
            for li, (kind, j, ctx_in, ctx_out, midx) in enumerate(self.layers):
                self.p.fence()
                self.top = self.persist_top
                last = li == nl - 1
                if kind == 0:
                    self.conv_layer(li, j, ctx_out, midx, last)
                elif kind == 1:
                    self.rwkv_layer(li, j, ctx_out, midx, last)
                else:
                    self.attn_layer(li, j, ctx_out, midx, last)
            if nl == 0:
                self.p.fence()
                self.top = self.persist_top
                self.only_final()
            self.p.emit()
        return nc

    def prologue(self):
        i = self.i
        self.ident_f = self.alloc([128], F32)
        self.dma(self.ident_f, i["k_ident"], w=["ident_f"])
        self.ident_b = self.alloc([128], BF16)
        self.p.add("dve", lambda e: e.tensor_copy(out=self.ident_b, in_=self.ident_f), r=["ident_f"], w=["ident_b"])
        permf = self.alloc([128], F32)
        self.dma(permf, i["k_perm"], w=["permf"])
        self.perm_b = self.alloc([128], BF16)
        self.p.add("dve", lambda e: e.tensor_copy(out=self.perm_b, in_=permf), r=["permf"], w=["perm_b"])
        self.ones_d = self.alloc([128], BF16)
        self.ones_h = self.alloc([128], BF16)
        self.ones_1 = self.alloc([128], BF16)
        self.p.add("pool", lambda e: e.memset(self.ones_d, 1.0 / D), w=["ones_d"])
        self.p.add("pool", lambda e: e.memset(self.ones_h, 1.0 / 128), w=["ones_h"])
        self.p.add("pool", lambda e: e.memset(self.ones_1, 1.0), w=["ones_1"])
        self.fg_b = self.alloc([D], F32)
        self.dma(self.fg_b, self.bcast_row(i["final_g"][0, :], D), w=["fg_b"])
        self.constkeys = ["ident_f", "ident_b", "perm_b", "ones_d", "ones_h", "ones_1", "fg_b"]

    def after_fence_consts(self):
        return

    def convert_weights(self, used):
        i = self.i
        self.wb = {}

        def conv(name, src, rows, cols):
            dst = self.dram(name, [rows, cols], BF16)
            for r0 in range(0, rows, 256):
                rr = min(256, rows - r0)
                self.dma(dst[r0:r0 + rr, :], src[r0:r0 + rr, :], w=[name], q="pool")
            self.wb[name] = dst

        if 0 in used:
            for j in sorted(set(l[1] for l in self.layers if l[0] == 0)):
                conv("cwi%d" % j, i["conv_w_in"][j], D, 3 * D)
                conv("cwo%d" % j, i["conv_w_out"][j], D, D)
        if 1 in used:
            for nm in ("w_r", "w_k", "w_v", "w_g", "w_o"):
                conv("r" + nm, i["rwkv_" + nm][0], D, D)
            for dd in range(2):
                conv("rw1_%d" % dd, i["rwkv_w1"][0, dd], D, 64)
                conv("ra1_%d" % dd, i["rwkv_a1"][0, dd], D, 64)
                conv("rw2_%d" % dd, i["rwkv_w2"][0, dd], 64, D)
                conv("ra2_%d" % dd, i["rwkv_a2"][0, dd], 64, D)
        if 2 in used:
            conv("awi", i["attn_w_in"][0], D, 6 * D)
            conv("awo", i["attn_w_out"][0], 2 * D, D)

    def modulation(self):
        i, NV, NB = self.i, self.NV, self.NB
        top0 = self.top
        crow = self.alloc([D], F32)
        self.dma(crow[0:NB, :], i["c"], w=["crow"])
        self.dma(crow[NB:NV, :], i["c_ctx"], w=["crow"])
        self.act(crow[0:NV, :], crow[0:NV, :], AF.Silu, r=["crow"], w=["crow"])
        scT = self.alloc([KC, NV], F32)
        for kc in range(KC):
            self.tr(self.ps[0][:, kc * NV:(kc + 1) * NV], crow[0:NV, kc * 128:(kc + 1) * 128],
                    r=["crow", "ident_f"], w=["ps0"])
        self.p.add("dve", lambda e: e.tensor_copy(out=scT.rearrange("p a b -> p (a b)"), in_=self.ps[0][:, 0:KC * NV]),
                   r=["ps0"], w=["scT"])
        wring = self.ring("modw", 2, [KC, 512], F32)
        mrow = self.alloc([3 * D], F32)
        brow = self.alloc([3 * D], F32)
        used_mod = sorted(set(l[4] for l in self.layers))
        cnt = 0
        for l in used_mod:
            self.dma(brow[0:NV, :], bass.AP(i["mod_b"].tensor, i["mod_b"][l, :].offset, [[0, NV], [1, 3 * D]]),
                     r=[], w=["brow"])
            for pn in range(6):
                wt, wk = wring[cnt % 2]
                bank = 1 + cnt % 2
                cnt += 1
                self.dma(wt, i["mod_w"][l][:, pn * 512:(pn + 1) * 512].rearrange("(kc p) n -> p kc n", p=128), w=[wk])
                for kc in range(KC):
                    self.mm(self.ps[bank][0:NV, :], scT[:, kc, :], wt[:, kc, :], kc == 0, kc == KC - 1,
                            r=[wk, "scT"], w=[self.psk[bank]])
                self.tt("dve", mrow[0:NV, pn * 512:(pn + 1) * 512], self.ps[bank][0:NV, :],
                        brow[0:NV, pn * 512:(pn + 1) * 512], ALU.add, r=[self.psk[bank], "brow"], w=["mrow"])
            self.dma(self.m_d[l], mrow[0:NV, :], r=["mrow"], w=["m_d"], q="pool")
        self.top = top0

    def rows_to_cols(self, rows_aps, name):
        R = len(rows_aps)
        rt = self.alloc([D], F32)
        kr = self.key(name + "_rows")
        for r_, ap in enumerate(rows_aps):
            self.dma(rt[r_:r_ + 1, :], bass.AP(ap.tensor, ap.offset, [[0, 1], [1, D]]), r=["m_d"], w=[kr])
        colsT = self.alloc([KC, R], F32)
        kc_ = self.key(name + "_cols")
        assert KC * R <= 512
        for kc in range(KC):
            self.tr(self.ps[7][:, kc * R:(kc + 1) * R], rt[0:R, kc * 128:(kc + 1) * 128], r=[kr], w=["ps7"])
        self.p.add("dve", lambda e: e.tensor_copy(out=colsT.rearrange("p a b -> p (a b)"), in_=self.ps[7][:, 0:KC * R]),
                   r=["ps7"], w=[kc_])
        return colsT, kc_

    def layer_vectors(self, midx):
        i, NV = self.i, self.NV
        rows = [i["norm_g"][midx, :]]
        for v in range(NV):
            rows.append(self.m_d[midx, v, 0:D])
        for v in range(NV):
            rows.append(self.m_d[midx, v, D:2 * D])
        cols, ck = self.rows_to_cols(rows, "lv")
        gsT = self.alloc([KC, NV], F32)
        kg = self.key("gsT")
        for v in range(NV):
            self.p.add("dve", lambda e, v=v: e.scalar_tensor_tensor(
                out=gsT[:, :, v], in0=cols[:, :, 1 + NV + v], scalar=1.0, in1=cols[:, :, 0],
                op0=ALU.add, op1=ALU.mult), r=[ck], w=[kg])
        self.gsT, self.gsk = gsT, kg
        self.shT, self.shk = cols, ck
        self.midx = midx
        self.gate_tiles = {}

    def gate_b(self, v):
        gt = self.gate_tiles
        if v in gt:
            ent = gt.pop(v)
            gt[v] = ent
            return ent
        if len(gt) < 2:
            ent = (self.alloc([D], F32), self.key("gate_b"))
        else:
            old_v = next(iter(gt))
            ent = gt.pop(old_v)
        self.dma(ent[0], self.bcast_row(self.m_d[self.midx, v, 2 * D:3 * D], D), r=["m_d"], w=[ent[1]])
        gt[v] = ent
        return ent

    def stage1_setup(self, share=None):
        if share is None:
            self.xt_ring = self.ring("xt", 2, [D], F32)
            self.xn = self.alloc([D], F32)
            self.xnk = self.key("xn")
            self.junk = self.alloc([D], F32)
            self.junkk = self.key("junk")
        else:
            self.xt_ring = [share[0], share[1]]
            self.xn, self.xnk = share[2]
            self.junk, self.junkk = share[3]
        self.ss = self.alloc([4], F32)
        self.ssk = self.key("ss")
        self.xt_cnt = 0
        self.coff, self.loff = 0, self.TC

    def sumsq(self, x, ss, rk):
        junk = self.junk
        self.p.add("act", lambda e: e.activation(out=junk, in_=x, func=AF.Square, accum_out=ss),
                   r=rk, w=[self.junkk, self.ssk])

    def src_tile(self, li, b, is_ctx, t0, n=128):
        if li == 0:
            return (self.i["ctx"] if is_ctx else self.i["x"])[b, t0:t0 + n, :]
        return (self.xcs if is_ctx else self.xs)[b, t0:t0 + n, :]

    def xkey(self, b, is_ctx, t0):
        return "x_%d_%d_%d" % (b, int(is_ctx), t0 // 128)

    def stage1(self, li, b, do_ctx, hT, hk):
        TC, TL, NB = self.TC, self.TL, self.NB
        segs = []
        if do_ctx:
            segs += [(True, t0) for t0 in range(0, TC, 128)]
        segs += [(False, t0) for t0 in range(0, TL, 128)]
        for (is_ctx, t0) in segs:
            v = NB if is_ctx else b
            tok = self.coff + t0 if is_ctx else self.loff + t0
            xt, xk = self.xt_ring[self.xt_cnt % 2]
            self.xt_cnt += 1
            self.dma(xt, self.src_tile(li, b, is_ctx, t0), r=[self.xkey(b, is_ctx, t0)], w=[xk])
            ss = self.ss[:, 0:1]
            self.sumsq(xt, ss, [xk])
            self.rsqrt(ss, ss, NORM_EPS, 1.0 / D, r=[self.ssk], w=[self.ssk])
            xn = self.xn
            self.p.add("act", lambda e, xt=xt, ss=ss, xn=xn: e.activation(out=xn, in_=xt, func=AF.Identity, scale=ss),
                       r=[xk, self.ssk], w=[self.xnk])
            for g in range(2):
                bank = 6 + g
                for q in range(4):
                    kc = g * 4 + q
                    self.tr(self.ps[bank][:, q * 128:(q + 1) * 128], self.xn[:, kc * 128:(kc + 1) * 128],
                            r=[self.xnk], w=[self.psk[bank]])
                for q in range(4):
                    kc = g * 4 + q
                    self.ts("dve", hT[:, kc, tok:tok + 128], self.ps[bank][:, q * 128:(q + 1) * 128],
                            self.gsT[:, kc, v:v + 1], self.shT[:, kc, 1 + v:2 + v], ALU.mult, ALU.add,
                            r=[self.psk[bank], self.gsk, self.shk], w=[hk])

    def epi_setup(self):
        self.tmp = self.alloc([D], F32)
        self.tmpk = self.key("tmp")
        self.xnew_ring = self.ring("xnew", 2, [D], F32)
        self.epi_cnt = 0

    def epilogue_tile(self, li, b, is_ctx, t0, yT_fn, nkc, w_out, wok, ykeys, last, banks=(4, 5)):
        v = self.NB if is_ctx else b
        gb, gk = self.gate_b(v)
        xt, xk = self.xt_ring[self.xt_cnt % 2]
        self.xt_cnt += 1
        xkey = self.xkey(b, is_ctx, t0)
        self.dma(xt, self.src_tile(li, b, is_ctx, t0), r=[xkey], w=[xk])
        xn_, xnk_ = self.xnew_ring[self.epi_cnt % 2]
        self.epi_cnt += 1
        for half in range(2):
            bank = banks[half]
            for kc in range(nkc):
                self.mm(self.ps[bank][:, :], yT_fn(kc), w_out[:, kc, half * 512:(half + 1) * 512],
                        kc == 0, kc == nkc - 1, r=ykeys + [wok], w=[self.psk[bank]])
            hs = slice(half * 512, (half + 1) * 512)
            self.tt("dve", self.tmp[:, hs], self.ps[bank][:, :], gb[:, hs], ALU.mult,
                    r=[self.psk[bank], gk], w=[self.tmpk])
            self.tt("pool", xn_[:, hs], self.tmp[:, hs], xt[:, hs], ALU.add, r=[self.tmpk, xk], w=[xnk_])
        if last and not is_ctx:
            ss = self.ss[:, 1:2]
            self.sumsq(xn_, ss, [xnk_])
            self.rsqrt(ss, ss, NORM_EPS, 1.0 / D, r=[self.ssk], w=[self.ssk])
            self.stt(self.tmp, xn_, ss, self.fg_b, ALU.mult, ALU.mult, r=[xnk_, self.ssk], w=[self.tmpk])
            self.dma(self.out[b, t0:t0 + 128, :], self.tmp, r=[self.tmpk], w=["out"], q="pool")
        else:
            dst = (self.xcs if is_ctx else self.xs)[b, t0:t0 + 128, :]
            self.dma(dst, xn_, r=[xnk_], w=[xkey], q="pool")

    def only_final(self):
        self.stage1_setup()
        tmp = self.alloc([D], F32)
        for b in range(self.NB):
            for t0 in range(0, self.TL, 128):
                xt, xk = self.xt_ring[self.xt_cnt % 2]
                self.xt_cnt += 1
                self.dma(xt, self.i["x"][b, t0:t0 + 128, :], w=[xk])
                ss = self.ss[:, 1:2]
                self.sumsq(xt, ss, [xk])
                self.rsqrt(ss, ss, NORM_EPS, 1.0 / D, r=[self.ssk], w=[self.ssk])
                self.stt(tmp, xt, ss, self.fg_b, ALU.mult, ALU.mult, r=[xk, self.ssk], w=["tmpf"])
                self.dma(self.out[b, t0:t0 + 128, :], tmp, r=["tmpf"], w=["out"], q="pool")

    def conv_layer(self, li, j, ctx_out, midx, last):
        NB, TL, TC, TT = self.NB, self.TL, self.TC, self.TT
        i = self.i
        T2 = 256
        self.layer_vectors(midx)
        rows = [i["conv_dw"][j, t, :] for t in range(CONV_W)]
        rows += [i["conv_db"][j, :], i["conv_ln_g"][j, :], i["conv_ln_b"][j, :]]
        cv, cvk = self.rows_to_cols(rows, "cv")
        w_in = self.wb["cwi%d" % j]
        w_in_v = w_in.rearrange("(kc p) n -> p kc n", p=128)
        w_out = self.alloc([KC, D], BF16)
        wok = self.key("cwo")
        self.dma(w_out, self.wb["cwo%d" % j].rearrange("(kc p) n -> p kc n", p=128), r=["cwo%d" % j], w=[wok])
        w_g = self.alloc([KC, D], BF16)
        wgk = self.key("cwg")
        self.dma(w_g, w_in_v[:, :, 2 * D:3 * D], r=["cwi%d" % j], w=[wgk])
        hT = self.alloc([KC, TT], BF16)
        hk = self.key("hT")
        self.stage1_setup()
        self.epi_setup()
        seqs = []
        if ctx_out:
            seqs.append((True, 0, TC))
        seqs.append((False, TC, TL))
        GLW = sum(n + 2 * HALO for (_, _, n) in seqs)
        glu_d = self.dram("glu_d%d" % li, [KC, 128, GLW], BF16)
        wp_ring = self.ring("wp", 2, [KC, 2, 128], BF16)
        sig_ring = self.ring("sig", 2, [512], F32)
        gl_ring = self.ring("gl", 2, [GLW], BF16)
        for (t, k) in gl_ring:
            self.p.add("pool", lambda e, t=t: e.memset(t, 0.0), w=[k])
        glt_ring = self.ring("glt", 2, [KC, T2 + 2 * HALO], BF16)
        dg_ring = self.ring("dg", 2, [CONV_W, 128], BF16)
        sgt_ring = self.ring("sgt", 2, [T2], BF16)
        yb = self.alloc([KC, T2], BF16)
        ybk = self.key("yb")
        ysq = self.alloc([KC, T2], BF16)
        ysqk = self.key("ysq")
        y2_ring = self.ring("y2", 2, [KC, T2], BF16)
        mean = self.alloc([T2], F32)
        meank = self.key("mean")
        rstd = self.alloc([T2], F32)
        rstdk = self.key("rstd")
        t1_ring = self.ring("t1", 2, [T2], F32)
        s_ring = self.ring("s", 2, [T2], F32)
        cnt1 = 0
        cnt2 = 0
        dg_d = self.dram("dg_d%d" % li, [KC, 128, CONV_W, 128], BF16)
        for fc in range(KC):
            dg, dgk = dg_ring[fc % 2]
            for t in range(CONV_W):
                self.ts("dve", dg[:, t, :], self.ident_b, cv[:, fc, t:t + 1], None, ALU.mult, None, r=[cvk], w=[dgk])
            self.dma(dg_d[fc], dg, r=[dgk], w=["dg_d%d" % fc], q="pool")
        for b in range(NB):
            self.stage1(li, b, ctx_out, hT, hk)
            for fc in range(KC):
                wp, wpk = wp_ring[fc % 2]
                for ab in range(2):
                    self.dma(wp[:, :, ab, :], w_in_v[:, :, ab * D + fc * 128: ab * D + (fc + 1) * 128],
                             r=["cwi%d" % j], w=[wpk])
                gl, glk = gl_ring[fc % 2]
                col = 0
                for (is_ctx, tok0, n) in seqs:
                    for (o, m) in _tiles(n, 512):
                        bA = (cnt1 % 2) * 2
                        bB = bA + 1
                        sg, sgk = sig_ring[cnt1 % 2]
                        cnt1 += 1
                        for ab, bank in ((0, bA), (1, bB)):
                            for kc in range(KC):
                                self.mm(self.ps[bank][:, 0:m], wp[:, kc, ab, :], hT[:, kc, tok0 + o: tok0 + o + m],
                                        kc == 0, kc == KC - 1, r=[wpk, hk], w=[self.psk[bank]])
                        self.act(sg[:, 0:m], self.ps[bB][:, 0:m], AF.Sigmoid, r=[self.psk[bB]], w=[sgk])
                        c0 = col + HALO + o
                        self.tt("dve", gl[:, c0:c0 + m], self.ps[bA][:, 0:m], sg[:, 0:m], ALU.mult,
                                r=[self.psk[bA], sgk], w=[glk])
                    col += n + 2 * HALO
                self.dma(glu_d[fc], gl, r=[glk], w=["glu_d"], q="pool")
            col = 0
            for (is_ctx, tok0, n) in seqs:
                for (o, m) in _tiles(n, T2):
                    glt, gltk = glt_ring[cnt2 % 2]
                    y2, y2k = y2_ring[cnt2 % 2]
                    cnt2 += 1
                    c0 = col + o
                    self.dma(glt[:, :, 0:m + 2 * HALO], glu_d[:, :, c0:c0 + m + 2 * HALO].rearrange("f p c -> p f c"),
                             r=["glu_d"], w=[gltk])
                    for fc in range(KC):
                        dg, dgk = dg_ring[fc % 2]
                        self.dma(dg, dg_d[fc], r=["dg_d%d" % fc], w=[dgk])
                        bank = fc % 2
                        for t in range(CONV_W):
                            self.mm(self.ps[bank][:, 0:m], dg[:, t, :], glt[:, fc, t:t + m], t == 0, t == CONV_W - 1,
                                    r=[dgk, gltk], w=[self.psk[bank]])
                        self.act(yb[:, fc, 0:m], self.ps[bank][:, 0:m], AF.Identity, r=[self.psk[bank], cvk], w=[ybk],
                                 bias=cv[:, fc, CONV_W:CONV_W + 1])
                        self.act(ysq[:, fc, 0:m], self.ps[bank][:, 0:m], AF.Square, r=[self.psk[bank], cvk], w=[ysqk],
                                 bias=cv[:, fc, CONV_W:CONV_W + 1])
                    for fc in range(KC):
                        self.mm(self.ps[2][:, 0:m], self.ones_d, yb[:, fc, 0:m], fc == 0, fc == KC - 1,
                                r=[ybk], w=[self.psk[2]])
                    for fc in range(KC):
                        self.mm(self.ps[3][:, 0:m], self.ones_d, ysq[:, fc, 0:m], fc == 0, fc == KC - 1,
                                r=[ysqk], w=[self.psk[3]])
                    self.p.add("act", lambda e, m=m: e.copy(out=mean[:, 0:m], in_=self.ps[2][:, 0:m]),
                               r=[self.psk[2]], w=[meank])
                    self.tt("dve", rstd[:, 0:m], mean[:, 0:m], mean[:, 0:m], ALU.mult, r=[meank], w=[rstdk])
                    self.tt("dve", rstd[:, 0:m], self.ps[3][:, 0:m], rstd[:, 0:m], ALU.subtract,
                            r=[self.psk[3], rstdk], w=[rstdk])
                    self.rsqrt(rstd[:, 0:m], rstd[:, 0:m], LN_EPS, 1.0, r=[rstdk], w=[rstdk])
                    for fc in range(KC):
                        t1, t1k = t1_ring[fc % 2]
                        s_, sk = s_ring[fc % 2]
                        sgt, sgtk = sgt_ring[fc % 2]
                        bank = 6 + fc % 2
                        for kc in range(KC):
                            self.mm(self.ps[bank][:, 0:m], w_g[:, kc, fc * 128:(fc + 1) * 128],
                                    hT[:, kc, tok0 + o: tok0 + o + m], kc == 0, kc == KC - 1,
                                    r=[wgk, hk], w=[self.psk[bank]])
                        self.act(sgt[:, 0:m], self.ps[bank][:, 0:m], AF.Silu, r=[self.psk[bank]], w=[sgtk])
                        self.tt("dve", t1[:, 0:m], yb[:, fc, 0:m], mean[:, 0:m], ALU.subtract, r=[ybk, meank], w=[t1k])
                        self.tt("dve", t1[:, 0:m], t1[:, 0:m], rstd[:, 0:m], ALU.mult, r=[t1k, rstdk], w=[t1k])
                        self.act(s_[:, 0:m], t1[:, 0:m], AF.Silu, r=[t1k, cvk], w=[sk],
                                 scale=cv[:, fc, CONV_W + 1:CONV_W + 2], bias=cv[:, fc, CONV_W + 2:CONV_W + 3])
                        self.tt("pool", y2[:, fc, 0:m], s_[:, 0:m], sgt[:, 0:m], ALU.mult, r=[sk, sgtk], w=[y2k])
                    for (oo, mm_) in _tiles(m, 128):
                        self.epilogue_tile(li, b, is_ctx, o + oo,
                                           lambda kc, y2=y2, oo=oo: y2[:, kc, oo:oo + 128],
                                           KC, w_out, wok, [y2k], last)
                col += n + 2 * HALO

    def bc_free(self, ap2, pattern):
        a = list(ap2.ap)
        if pattern == "k":
            return bass.AP(ap2.tensor, ap2.offset, [list(a[0]), [0, 64], list(a[1])])
        return bass.AP(ap2.tensor, ap2.offset, [list(a[0]), list(a[1]), [0, 64]])

    def rwkv_layer(self, li, j, ctx_out, midx, last):
        NB, TL, TC, TT = self.NB, self.TL, self.TC, self.TT
        i = self.i
        names = ["r", "v", "kk", "sg", "dec0", "dec1", "kd0", "kd1", "bn0", "bn1", "o0", "o1"]
        A = {n: self.dram("rk_%s_%d" % (n, li), [NB, TT, D]) for n in names}
        if RWKV_CHUNKED:
            NCH = TT // 64
            F = {}
            for d in range(2):
                for nm in ("KT", "RT", "KS", "NB"):
                    F[nm, d] = self.dram("rf_%s%d_%d" % (nm, d, li), [NB, 16, 64, TT], BF16)
                for nm in ("KSt", "NBt"):
                    F[nm, d] = self.dram("rf_%s%d_%d" % (nm, d, li), [NB, TT, D], BF16)
                F["pc", d] = self.dram("rf_pc%d_%d" % (d, li), [NB, NCH, 64, 16])
            F["Vt"] = self.dram("rf_Vt_%d" % li, [NB, TT, D], BF16)
            self.rwkv_phase_a(li, j, midx, A, F)
            self.p.fence()
            self.top = self.persist_top
            self.rwkv_scan2(li, A, F)
        else:
            self.rwkv_phase_a(li, j, midx, A, None)
            self.p.fence()
            self.top = self.persist_top
            self.rwkv_scan(li, A)
        self.p.fence()
        self.top = self.persist_top
        self.rwkv_phase_c(li, j, ctx_out, midx, last, A)

    def rwkv_phase_a(self, li, j, midx, A, F):
        NB, TL, TC, TT = self.NB, self.TL, self.TC, self.TT
        i = self.i
        self.layer_vectors(midx)
        muT, muk = self.rows_to_cols([i["rwkv_mu"][j, n, :] for n in range(6)], "mu")
        W = {}
        for nm in ("w_r", "w_k", "w_v", "w_g"):
            t = self.alloc([KC, D], BF16)
            k = self.key(nm)
            self.dma(t, self.wb["r" + nm].rearrange("(kc p) n -> p kc n", p=128), w=[k])
            W[nm] = (t, k)
        w1cat = self.alloc([KC, 128], BF16)
        a1cat = self.alloc([KC, 128], BF16)
        w2cat = self.alloc([D], BF16)
        a2cat = self.alloc([D], BF16)
        lk = self.key("lora")
        for d in range(2):
            self.dma(w1cat[:, :, d * 64:(d + 1) * 64], self.wb["rw1_%d" % d].rearrange("(kc p) n -> p kc n", p=128), w=[lk])
            self.dma(a1cat[:, :, d * 64:(d + 1) * 64], self.wb["ra1_%d" % d].rearrange("(kc p) n -> p kc n", p=128), w=[lk])
            self.dma(w2cat[d * 64:(d + 1) * 64, :], self.wb["rw2_%d" % d], w=[lk])
            self.dma(a2cat[d * 64:(d + 1) * 64, :], self.wb["ra2_%d" % d], w=[lk])
        pk = self.key("params")
        kkb = self.alloc([D], F32)
        kab = self.alloc([D], F32)
        self.dma(kkb, self.bcast_row(i["rwkv_k_k"][j, :], D), w=[pk])
        self.dma(kab, self.bcast_row(i["rwkv_k_a"][j, :], D), w=[pk])
        w0b, a0b = [], []
        for d in range(2):
            t = self.alloc([D], F32)
            self.dma(t, self.bcast_row(i["rwkv_w0"][j, d, :], D), w=[pk])
            w0b.append(t)
            t = self.alloc([D], F32)
            self.dma(t, self.bcast_row(i["rwkv_a0"][j, d, :], D), w=[pk])
            a0b.append(t)
        HW = TT + 4
        hT = self.alloc([KC, HW], BF16)
        hk = self.key("hT")
        self.p.add("pool", lambda e: e.memset(hT, 0.0), w=[hk])
        T = [(self.alloc([D], F32), self.key("T%d" % n)) for n in range(4)]
        self.stage1_setup(share=T)
        self.coff, self.loff = 1, TC + 3
        xx = self.alloc([KC, 128], F32)
        xxk = self.key("xx")
        tl = self.alloc([KC, 128], F32)
        tlk = self.key("tl")
        lerp = [(self.alloc([KC, 128], BF16), self.key("lerp%d" % n)) for n in range(6)]
        thT = self.alloc([128], BF16)
        thk = self.key("thT")
        ahT = self.alloc([128], BF16)
        ahk = self.key("ahT")
        kraw = self.alloc([D], F32)
        krk = self.key("kraw")
        kk = self.alloc([D], F32)
        kkk = self.key("kk")
        ssq = self.alloc([16], F32)
        ssqk = self.key("ssq")
        bankc = [0]
        if F is not None:
            rt = self.alloc([D], F32)
            rtk = self.key("rt")
            tri = self.alloc([2, 128], F32)
            ind = self.alloc([2], F32)
            ck = self.key("rconst")
            for d in range(2):
                self.dma(tri[:, d, :], i["k_tri"][d], w=[ck])
            self.dma(ind, i["k_ind"], w=[ck])
            fm_ring = self.ring("fmt", 2, [16, 128], BF16)
            tm_ring = self.ring("tmt", 1, [D], BF16) * 2
            pct = self.alloc([32], F32)
            pctk = self.key("pct")
            fmc = [0]
            tmc = [0]

            def emit_fm(tile, tk, name, d, b, tok):
                fmt, fmk = fm_ring[fmc[0] % 2]
                fmc[0] += 1
                for q4 in range(4):
                    bank = nxt()
                    for jj in range(4):
                        h = q4 * 4 + jj
                        self.tr(self.ps[bank][0:64, jj * 128:(jj + 1) * 128], tile[:, h * 64:(h + 1) * 64],
                                r=[tk], w=[self.psk[bank]])
                    self.p.add("act", lambda e, bank=bank, q4=q4, fmt=fmt: e.copy(
                        out=fmt[0:64, q4 * 4:(q4 + 1) * 4, :].rearrange("p a b -> p (a b)"), in_=self.ps[bank][0:64, :]),
                        r=[self.psk[bank]], w=[fmk])
                self.dma(F[name, d][b, :, :, tok:tok + 128].rearrange("h f t -> f h t"), fmt[0:64], r=[fmk],
                         w=[self.key("fst")], q=RQ)

            def emit_tm(tile, tk, dst):
                tmt, tmk = tm_ring[tmc[0] % 2]
                tmc[0] += 1
                self.p.add("pool", lambda e, tmt=tmt, tile=tile: e.tensor_copy(out=tmt, in_=tile), r=[tk], w=[tmk])
                self.dma(dst, tmt, r=[tmk], w=[self.key("tst")], q=RQ)

        def nxt():
            b_ = bankc[0] % 6
            bankc[0] += 1
            return b_

        def proj(n_, wname, half):
            wt, wk = W[wname]
            lp, lpk = lerp[n_]
            bank = nxt()
            for kc in range(KC):
                self.mm(self.ps[bank][:, :], lp[:, kc, :], wt[:, kc, half * 512:(half + 1) * 512], kc == 0, kc == KC - 1,
                        r=[lpk, wk], w=[self.psk[bank]])
            return bank

        def store(name, b, tok, tile, tk):
            self.dma(A[name][b, tok:tok + 128, :], tile, r=[tk], w=[self.key("st")], q=(RQ if F is not None else "pool"))

        for b in range(NB):
            self.stage1(li, b, True, hT, hk)
            for (col0, n, tokbase) in ((self.coff, TC, 0), (self.loff, TL, TC)):
                for t0 in range(0, n, 128):
                    c = col0 + t0
                    tok = tokbase + t0
                    hc_ = hT[:, :, c:c + 128]
                    self.tt("pool", xx, hT[:, :, c - 1:c + 127], hT[:, :, c + 1:c + 129], ALU.add, r=[hk], w=[xxk])
                    self.stt(xx, xx, 0.5, hc_, ALU.mult, ALU.subtract, r=[xxk, hk], w=[xxk])
                    for n_ in range(6):
                        mb = bass.AP(muT.tensor, muT[:, :, n_].offset, [list(muT.ap[0]), [6, KC], [0, 128]])
                        self.tt("pool", tl, xx, mb, ALU.mult, r=[xxk, muk], w=[tlk])
                        lp, lpk = lerp[n_]
                        self.tt("dve", lp, tl, hc_, ALU.add, r=[tlk, hk], w=[lpk])
                    for half in range(2):
                        hs = slice(half * 512, (half + 1) * 512)
                        bk = proj(2, "w_k", half)
                        self.p.add("act", lambda e, bk=bk, hs=hs: e.copy(out=kraw[:, hs], in_=self.ps[bk][:, :]),
                                   r=[self.psk[bk]], w=[krk])
                    T1, T1k = T[0]
                    T2, T2k = T[1]
                    T3, T3k = T[2]
                    T4, T4k = T[3]
                    self.tt("dve", T1, kraw, kkb, ALU.mult, r=[krk, pk], w=[T1k])
                    self.tt("pool", T2, T1, T1, ALU.mult, r=[T1k], w=[T2k])
                    self.p.add("dve", lambda e: e.tensor_reduce(out=ssq, in_=T2.rearrange("p (h f) -> p h f", f=64),
                                                                axis=AX.X, op=ALU.add), r=[T2k], w=[ssqk])
                    self.rsqrt(ssq, ssq, 1e-12, 1.0, r=[ssqk], w=[ssqk])
                    self.tt("pool", kk.rearrange("p (h f) -> p h f", f=64), T1.rearrange("p (h f) -> p h f", f=64),
                            self.bc_free(ssq, "v"), ALU.mult, r=[T1k, ssqk], w=[kkk])
                    if F is None:
                        store("kk", b, tok, kk, kkk)
                    else:
                        for half in range(2):
                            hs = slice(half * 512, (half + 1) * 512)
                            bk = proj(0, "w_r", half)
                            self.p.add("act", lambda e, bk=bk, hs=hs: e.copy(out=rt[:, hs], in_=self.ps[bk][:, :]),
                                       r=[self.psk[bk]], w=[rtk])
                        store("r", b, tok, rt, rtk)
                    for kc in range(KC):
                        self.mm(self.ps[6][:, 0:128], w1cat[:, kc, :], lerp[1][0][:, kc, :], kc == 0, kc == KC - 1,
                                r=[lk, lerp[1][1]], w=[self.psk[6]])
                    self.act(thT, self.ps[6][:, 0:128], AF.Tanh, r=[self.psk[6]], w=[thk])
                    for kc in range(KC):
                        self.mm(self.ps[7][:, 0:128], a1cat[:, kc, :], lerp[4][0][:, kc, :], kc == 0, kc == KC - 1,
                                r=[lk, lerp[4][1]], w=[self.psk[7]])
                    self.p.add("act", lambda e: e.copy(out=ahT, in_=self.ps[7][:, 0:128]), r=[self.psk[7]], w=[ahk])
                    for d in range(2):
                        ds = slice(d * 64, (d + 1) * 64)
                        for half in range(2):
                            hs = slice(half * 512, (half + 1) * 512)
                            bank = nxt()
                            self.mm(self.ps[bank][:, :], ahT[ds, :], a2cat[ds, hs], True, True, r=[ahk, lk], w=[self.psk[bank]])
                            self.tt("dve", T1[:, hs], self.ps[bank][:, :], a0b[d][:, hs], ALU.add,
                                    r=[self.psk[bank], pk], w=[T1k])
                        self.act(T1, T1, AF.Sigmoid, r=[T1k], w=[T1k])
                        self.stt(T2, T1, -1.0, kab, ALU.add, ALU.mult, r=[T1k, pk], w=[T2k])
                        self.stt(T2, T2, 1.0, kraw, ALU.add, ALU.mult, r=[T2k, krk], w=[T2k])
                        store("kd%d" % d, b, tok, T2, T2k)
                        self.stt(T3, kk, -1.0, T1, ALU.mult, ALU.mult, r=[kkk, T1k], w=[T3k])
                        if F is None:
                            store("bn%d" % d, b, tok, T3, T3k)
                        for half in range(2):
                            hs = slice(half * 512, (half + 1) * 512)
                            bank = nxt()
                            self.mm(self.ps[bank][:, :], thT[ds, :], w2cat[ds, hs], True, True, r=[thk, lk], w=[self.psk[bank]])
                            self.tt("dve", T4[:, hs], self.ps[bank][:, :], w0b[d][:, hs], ALU.add,
                                    r=[self.psk[bank], pk], w=[T4k])
                        self.act(T4, T4, AF.Sigmoid, r=[T4k], w=[T4k])
                        if F is None:
                            self.act(T4, T4, AF.Exp, r=[T4k], w=[T4k], scale=-math.exp(-0.5))
                            store("dec%d" % d, b, tok, T4, T4k)
                            continue
                        self.ts("dve", T4, T4, -math.exp(-0.5), None, ALU.mult, None, r=[T4k], w=[T4k])
                        for h in range(16):
                            self.mm(self.ps[7][0:64, h:h + 17:16], T4[:, h * 64:(h + 1) * 64], ind, True, True,
                                    r=[T4k, ck], w=[self.psk[7]])
                        self.act(pct[0:64, :], self.ps[7][0:64, 0:32], AF.Exp, r=[self.psk[7]], w=[pctk])
                        for jj in range(2):
                            self.dma(F["pc", d][b, tok // 64 + jj], pct[0:64, jj * 16:(jj + 1) * 16], r=[pctk],
                                     w=[self.key("pst")], q=RQ)
                        cb = []
                        for half in range(2):
                            hs = slice(half * 512, (half + 1) * 512)
                            bank = 6 + half
                            cb.append(bank)
                            self.mm(self.ps[bank][:, :], tri[:, d, :], T4[:, hs], True, True, r=[T4k, ck], w=[self.psk[bank]])
                        for half in range(2):
                            hs = slice(half * 512, (half + 1) * 512)
                            self.tt("dve", T4[:, hs], self.ps[cb[half]][:, :], T4[:, hs], ALU.subtract,
                                    r=[self.psk[cb[half]], T4k], w=[T4k])
                        self.act(T4, T4, AF.Exp, r=[T4k], w=[T4k])
                        self.tt("dve", T1, kk, T4, ALU.mult, r=[kkk, T4k], w=[T1k])
                        emit_fm(T1, T1k, "KT", d, b, tok)
                        for half in range(2):
                            hs = slice(half * 512, (half + 1) * 512)
                            self.act(T4[:, hs], self.ps[cb[half]][:, :], AF.Exp, r=[self.psk[cb[half]], T1k], w=[T4k],
                                     scale=-1.0)
                        self.tt("dve", T2, T2, T4, ALU.mult, r=[T2k, T4k], w=[T2k])
                        emit_fm(T2, T2k, "KS", d, b, tok)
                        emit_tm(T2, T2k, F["KSt", d][b, tok:tok + 128, :])
                        self.tt("dve", T3, T3, T4, ALU.mult, r=[T3k, T4k], w=[T3k])
                        emit_fm(T3, T3k, "NB", d, b, tok)
                        emit_tm(T3, T3k, F["NBt", d][b, tok:tok + 128, :])
                        for half in range(2):
                            hs = slice(half * 512, (half + 1) * 512)
                            self.act(T4[:, hs], self.ps[cb[half]][:, :], AF.Exp, r=[self.psk[cb[half]], T2k, T3k], w=[T4k])
                        self.tt("dve", T1, rt, T4, ALU.mult, r=[rtk, T4k], w=[T1k])
                        emit_fm(T1, T1k, "RT", d, b, tok)
                    plist = ((3, "w_v", "v", T2, T2k, AF.Identity), (5, "w_g", "sg", T3, T3k, AF.Silu))
                    if F is None:
                        plist = ((0, "w_r", "r", T1, T1k, AF.Identity),) + plist
                    for (n_, wname, name, tile, tk, fn) in plist:
                        for half in range(2):
                            hs = slice(half * 512, (half + 1) * 512)
                            bk = proj(n_, wname, half)
                            self.act(tile[:, hs], self.ps[bk][:, :], fn, r=[self.psk[bk]], w=[tk])
                        store(name, b, tok, tile, tk)
                        if F is not None and name == "v":
                            emit_tm(tile, tk, F["Vt"][b, tok:tok + 128, :])

    def rwkv_scan(self, li, A):
        NB, TL, TC, TT = self.NB, self.TL, self.TC, self.TT
        P = 2 * NB * 16
        SB = 32
        S = self.alloc([64, 64], F32)
        Sk = self.key("S")
        tmp = self.alloc([64, 64], F32)
        tmpk = self.key("tmp")
        vk_ring = self.ring("vk", 2, [64, 64], F32)
        sa = self.alloc([64], F32)
        sak = self.key("sa")
        arrs = ["kk", "bn", "dec", "kd", "v", "r"]
        blk = [{a: self.alloc([SB, 64], F32) for a in arrs} for _ in range(2)]
        oblk = self.ring("oblk", 2, [SB, 64], F32)
        self.p.add("dve", lambda e: e.memset(S[0:P], 0.0), w=[Sk])
        cb = 0
        for (soff, n) in ((0, TC), (TC, TL)):
            for i0 in range(0, n, SB):
                slot = cb % 2
                cb += 1
                bkeys = []
                for d in range(2):
                    for b in range(NB):
                        p0 = d * NB * 16 + b * 16
                        if d == 0:
                            tok0, step = soff + i0, D
                        else:
                            tok0, step = soff + n - 1 - i0, -D
                        for a in arrs:
                            name = a + str(d) if a in ("bn", "dec", "kd") else a
                            src_t = A[name]
                            src = bass.AP(src_t.tensor, src_t[b, tok0, :].offset, [[64, 16], [step, SB], [1, 64]])
                            k = "blk_%d_%d_%d_%s" % (slot, d, b, a)
                            self.dma(blk[slot][a][p0:p0 + 16, :, :], src, w=[k])
                            bkeys.append(k)
                ob, obk = oblk[slot]
                for s_ in range(SB):
                    g = lambda a: blk[slot][a][0:P, s_, :]
                    vkt, vkk = vk_ring[s_ % 2]
                    Sv = S[0:P]
                    tv = tmp[0:P]
                    self.tt("pool", vkt[0:P], self.bc_free(g("v"), "v"), self.bc_free(g("kd"), "k"), ALU.mult,
                            r=bkeys, w=[vkk])
                    self.tt("dve", tv, Sv, self.bc_free(g("kk"), "k"), ALU.mult, r=[Sk] + bkeys, w=[tmpk])
                    self.p.add("dve", lambda e, tv=tv: e.tensor_reduce(out=sa[0:P], in_=tv, axis=AX.X, op=ALU.add),
                               r=[tmpk], w=[sak])
                    e3 = "pool" if SCAN_POOL >= 1 else "dve"
                    e7 = "pool" if SCAN_POOL >= 2 else "dve"
                    self.tt(e3, Sv, Sv, self.bc_free(g("dec"), "k"), ALU.mult, r=[Sk] + bkeys, w=[Sk])
                    if SCAN_POOL >= 2:
                        self.tt(e7, Sv, Sv, vkt[0:P], ALU.add, r=[Sk, vkk], w=[Sk])
                    self.tt("dve", tv, self.bc_free(sa[0:P], "v"), self.bc_free(g("bn"), "k"), ALU.mult,
                            r=[sak] + bkeys, w=[tmpk])
                    self.tt("dve", Sv, Sv, tv, ALU.add, r=[Sk, tmpk], w=[Sk])
                    if SCAN_POOL < 2:
                        self.tt(e7, Sv, Sv, vkt[0:P], ALU.add, r=[Sk, vkk], w=[Sk])
                    self.tt("dve", tv, Sv, self.bc_free(g("r"), "k"), ALU.mult, r=[Sk] + bkeys, w=[tmpk])
                    self.p.add("dve", lambda e, tv=tv, ob=ob, s_=s_: e.tensor_reduce(out=ob[0:P, s_, :], in_=tv, axis=AX.X, op=ALU.add),
                               r=[tmpk], w=[obk])
                for d in range(2):
                    for b in range(NB):
                        p0 = d * NB * 16 + b * 16
                        if d == 0:
                            tok0, step = soff + i0, D
                        else:
                            tok0, step = soff + n - 1 - i0, -D
                        dst_t = A["o%d" % d]
                        dst = bass.AP(dst_t.tensor, dst_t[b, tok0, :].offset, [[64, 16], [step, SB], [1, 64]])
                        self.dma(dst, ob[p0:p0 + 16, :, :], r=[obk], w=[self.key("ost")], q="pool")

    def rwkv_scan2(self, li, A, F):
        NB, TL, TC, TT = self.NB, self.TL, self.TC, self.TT
        i = self.i
        C = 64
        masks = self.alloc([5, 512], F32)
        mk = self.key("masks")
        self.dma(masks[0:64], i["k_masks"].rearrange("m p c -> p m c"), w=[mk])
        US, LS, UI, LI, EYE = (masks[0:64, m, :] for m in range(5))
        units = [(d, b, g) for d in range(2) for b in range(NB) for g in range(2)]
        UB = 4
        fm_names = ("KT", "RT", "KS", "NB")
        tm_names = ("KSt", "NBt", "Vt")

        def t512(dt):
            return self.alloc([512], dt)[0:64]

        slots = []
        for s_ in range(UB):
            sl = {"ST": t512(F32), "STk": self.key("ST"), "STb": t512(BF16), "STbk": self.key("STb")}
            sl["op"] = []
            for par in range(2):
                o = {}
                for nm in fm_names:
                    o[nm] = (self.alloc([8, 64], BF16)[0:64], self.key(nm))
                for nm in tm_names:
                    o[nm] = (t512(BF16), self.key(nm))
                o["pc"] = (self.alloc([8], F32)[0:64], self.key("pc"))
                sl["op"].append(o)
            for nm in ("X", "XT"):
                sl[nm] = [(t512(F32), self.key(nm)) for _ in range(2)]
            sl["Tb"] = [(t512(BF16), self.key("Tb"))] * 2
            sl["Tf"] = (t512(F32), self.key("Tf"))
            for nm in ("Akk", "Akr", "nAbr", "RHST", "UT"):
                sl[nm] = (t512(BF16), self.key(nm))
            sl["ot"] = [(t512(F32), self.key("ot")) for _ in range(2)]
            slots.append(sl)
        bankc = [0]

        def nxtb():
            b_ = bankc[0] % 8
            bankc[0] += 1
            return b_

        def hc(ap, h):
            return ap[:, h * 64:(h + 1) * 64]

        def mmg(terms, rkeys):
            bank = nxtb()
            n = len(terms)
            for h in range(8):
                for ti, (lf, rf) in enumerate(terms):
                    self.mm(self.ps[bank][0:64, h * 64:(h + 1) * 64], lf(h), rf(h), ti == 0, ti == n - 1,
                            r=rkeys, w=[self.psk[bank]])
            return bank

        def pcopy(eng, out, bank, rk, wk):
            if eng == "act":
                self.p.add("act", lambda e: e.copy(out=out, in_=self.ps[bank][0:64, :]), r=[self.psk[bank]] + rk, w=wk)
            else:
                self.p.add("dve", lambda e: e.tensor_copy(out=out, in_=self.ps[bank][0:64, :]), r=[self.psk[bank]] + rk, w=wk)

        seqs = ((0, TC), (TC, TL))
        for batch in range(0, len(units), UB):
            ub = units[batch:batch + UB]
            chunks = []
            for (d, b, g) in ub:
                lst = []
                for (soff, n) in seqs:
                    toks = list(range(soff, soff + n, C))
                    if d == 1:
                        toks = toks[::-1]
                    lst += toks
                chunks.append(lst)
            for s_, _u in enumerate(ub):
                sl = slots[s_]
                self.p.add("dve", lambda e, sl=sl: e.memset(sl["ST"], 0.0), w=[sl["STk"]])
                self.p.add("dve", lambda e, sl=sl: e.memset(sl["STb"], 0.0), w=[sl["STbk"]])
            NCH = len(chunks[0])
            for ci in range(NCH):
                par = ci % 2
                cur = []
                for s_, (d, b, g) in enumerate(ub):
                    sl = slots[s_]
                    o = sl["op"][par]
                    tok0 = chunks[s_][ci]
                    for nm in fm_names:
                        t, k = o[nm]
                        self.dma(t, F[nm, d][b, g * 8:(g + 1) * 8, :, tok0:tok0 + C].rearrange("h f t -> f h t"), w=[k])
                    for nm in tm_names:
                        t, k = o[nm]
                        src = F["Vt"] if nm == "Vt" else F[nm, d]
                        self.dma(t, src[b, tok0:tok0 + C, g * 512:(g + 1) * 512], w=[k])
                    t, k = o["pc"]
                    self.dma(t, F["pc", d][b, tok0 // C, :, g * 8:(g + 1) * 8], w=[k])
                    mX, mXT, mS, mI = (US, LS, US, UI) if d == 0 else (LS, US, LS, LI)
                    cur.append((sl, o, d, b, g, tok0, mX, mXT, mS, mI))
                for (sl, o, d, b, g, tok0, mX, mXT, mS, mI) in cur:
                    NBf, NBk = o["NB"]
                    KT, KTk = o["KT"]
                    bk = mmg([(lambda h: NBf[:, h, :], lambda h: KT[:, h, :])], [NBk, KTk])
                    X0, X0k = sl["X"][0]
                    self.tt("dve", X0, self.ps[bk][0:64, :], mX, ALU.mult, r=[self.psk[bk], mk], w=[X0k])
                    bk = mmg([(lambda h: KT[:, h, :], lambda h: NBf[:, h, :])], [NBk, KTk])
                    XT0, XT0k = sl["XT"][0]
                    self.tt("dve", XT0, self.ps[bk][0:64, :], mXT, ALU.mult, r=[self.psk[bk], mk], w=[XT0k])
                for (sl, o, d, b, g, tok0, mX, mXT, mS, mI) in cur:
                    X0, X0k = sl["X"][0]
                    Tf, Tfk = sl["Tf"]
                    self.tt("dve", Tf, X0, EYE, ALU.add, r=[X0k, mk], w=[Tfk])
                for j in range(1, 6):
                    pj, cj = (j - 1) % 2, j % 2
                    for (sl, o, d, b, g, tok0, mX, mXT, mS, mI) in cur:
                        Xp, Xpk = sl["X"][pj]
                        XTp, XTpk = sl["XT"][pj]
                        if j < 5:
                            bk = mmg([(lambda h: hc(XTp, h), lambda h: hc(Xp, h))], [Xpk, XTpk])
                            pcopy("act", sl["X"][cj][0], bk, [], [sl["X"][cj][1]])
                        bk = mmg([(lambda h: hc(Xp, h), lambda h: hc(XTp, h))], [Xpk, XTpk])
                        pcopy("dve", sl["XT"][cj][0], bk, [], [sl["XT"][cj][1]])
                    for (sl, o, d, b, g, tok0, mX, mXT, mS, mI) in cur:
                        XTc, XTck = sl["XT"][cj]
                        Tf, Tfk = sl["Tf"]
                        bk = mmg([(lambda h: hc(XTc, h), lambda h: hc(Tf, h))], [XTck, Tfk])
                        self.tt("dve", Tf, Tf, self.ps[bk][0:64, :], ALU.add, r=[Tfk, self.psk[bk]], w=[Tfk])
                        if j == 5:
                            Tbc, Tbck = sl["Tb"][0]
                            self.p.add("act", lambda e, Tbc=Tbc, Tf=Tf: e.copy(out=Tbc, in_=Tf), r=[Tfk], w=[Tbck])
                for (sl, o, d, b, g, tok0, mX, mXT, mS, mI) in cur:
                    KS, KSk = o["KS"]
                    KT, KTk = o["KT"]
                    RT, RTk = o["RT"]
                    NBf, NBk = o["NB"]
                    for (nm, lf, lk_, rf, rk_, msk) in (("Akk", KS, KSk, KT, KTk, mS), ("Akr", KS, KSk, RT, RTk, mI),
                                                         ("nAbr", NBf, NBk, RT, RTk, mI)):
                        bk = mmg([(lambda h, lf=lf: lf[:, h, :], lambda h, rf=rf: rf[:, h, :])], [lk_, rk_])
                        self.tt("dve", sl[nm][0], self.ps[bk][0:64, :], msk, ALU.mult, r=[self.psk[bk], mk], w=[sl[nm][1]])
                Tfin = 5 % 2
                for (sl, o, d, b, g, tok0, mX, mXT, mS, mI) in cur:
                    KT, KTk = o["KT"]
                    Vt, Vtk = o["Vt"]
                    Akk, Akkk = sl["Akk"]
                    STb, STbk = sl["STb"], sl["STbk"]
                    bk = mmg([(lambda h: KT[:, h, :], lambda h: hc(STb, h)), (lambda h: hc(Akk, h), lambda h: hc(Vt, h))],
                             [KTk, STbk, Akkk, Vtk])
                    pcopy("act", sl["RHST"][0], bk, [], [sl["RHST"][1]])
                for (sl, o, d, b, g, tok0, mX, mXT, mS, mI) in cur:
                    Tb, Tbk = sl["Tb"][Tfin]
                    RH, RHk = sl["RHST"]
                    bk = mmg([(lambda h: hc(Tb, h), lambda h: hc(RH, h))], [Tbk, RHk])
                    pcopy("act", sl["UT"][0], bk, [], [sl["UT"][1]])
                for (sl, o, d, b, g, tok0, mX, mXT, mS, mI) in cur:
                    RT, RTk = o["RT"]
                    Vt, Vtk = o["Vt"]
                    Akr, Akrk = sl["Akr"]
                    nAbr, nAbrk = sl["nAbr"]
                    UT, UTk = sl["UT"]
                    STb, STbk = sl["STb"], sl["STbk"]
                    bk = mmg([(lambda h: RT[:, h, :], lambda h: hc(STb, h)), (lambda h: hc(Akr, h), lambda h: hc(Vt, h)),
                              (lambda h: hc(nAbr, h), lambda h: hc(UT, h))], [RTk, STbk, Akrk, Vtk, nAbrk, UTk])
                    ot, otk = sl["ot"][ci % 2]
                    pcopy("act", ot, bk, [], [otk])
                    self.dma(A["o%d" % d][b, tok0:tok0 + C, g * 512:(g + 1) * 512], ot, r=[otk], w=[self.key("ost")], q=RQ)
                if self.dbg and batch == 0 and ci == 0:
                    sl = cur[0][0]
                    for nm, (t_, k_) in (("X0f", sl["Tf"]), ("Tb", sl["Tb"][Tfin]), ("Akk", sl["Akk"]), ("Akr", sl["Akr"]),
                                         ("nAbr", sl["nAbr"]), ("RHST", sl["RHST"]), ("UT", sl["UT"])):
                        dd = self.nc.dram_tensor("dbg_" + nm, [64, 512], t_.dtype, kind="ExternalOutput").ap()
                        self.dma(dd, t_, r=[k_], w=[self.key("dbg")], q="pool")
                for (sl, o, d, b, g, tok0, mX, mXT, mS, mI) in cur:
                    KSt, KStk = o["KSt"]
                    NBt, NBtk = o["NBt"]
                    Vt, Vtk = o["Vt"]
                    UT, UTk = sl["UT"]
                    pc, pck = o["pc"]
                    ST, STk = sl["ST"], sl["STk"]
                    STb, STbk = sl["STb"], sl["STbk"]
                    bk = mmg([(lambda h: hc(KSt, h), lambda h: hc(Vt, h)), (lambda h: hc(NBt, h), lambda h: hc(UT, h))],
                             [KStk, Vtk, NBtk, UTk])
                    self.tt("dve", ST, ST, self.ps[bk][0:64, :], ALU.add, r=[STk, self.psk[bk]], w=[STk])
                    st3 = ST.rearrange("p (h v) -> p h v", v=64)
                    self.tt("dve", st3, st3, self.bc_free(pc, "v"), ALU.mult, r=[STk, pck], w=[STk])
                    self.p.add("act", lambda e, STb=STb, ST=ST: e.copy(out=STb, in_=ST), r=[STk], w=[STbk])

    def rwkv_phase_c(self, li, j, ctx_out, midx, last, A):
        NB, TL, TC, TT = self.NB, self.TL, self.TC, self.TT
        i = self.i
        self.layer_vectors(midx)
        w_o = self.alloc([KC, D], BF16)
        wok = self.key("rwo")
        self.dma(w_o, self.wb["rw_o"].rearrange("(kc p) n -> p kc n", p=128), w=[wok])
        pk = self.key("cparams")
        lngb = self.alloc([D], F32)
        lnbb = self.alloc([D], F32)
        rkb = self.alloc([D], F32)
        self.dma(lngb, self.bcast_row(i["rwkv_ln_g"][j, :], D), w=[pk])
        self.dma(lnbb, self.bcast_row(i["rwkv_ln_b"][j, :], D), w=[pk])
        self.dma(rkb, self.bcast_row(i["rwkv_r_k"][j, :], D), w=[pk])
        self.stage1_setup()
        self.epi_setup()
        L = {n: self.ring("ld_" + n, 2, [D], F32) for n in ("o0", "o1", "r", "kd0", "kd1", "v", "sg")}
        o = self.alloc([D], F32)
        ok_ = self.key("o")
        sq = self.alloc([D], F32)
        sqk = self.key("sq")
        st = self.alloc([64], F32)
        stk = self.key("st")
        y2T_ring = self.ring("y2T", 2, [KC, 128], BF16)
        cnt = 0
        h3 = lambda t: t.rearrange("p (h f) -> p h f", f=64)
        for b in range(NB):
            segs = []
            if ctx_out:
                segs += [(True, t0, t0) for t0 in range(0, TC, 128)]
            segs += [(False, t0, TC + t0) for t0 in range(0, TL, 128)]
            for (is_ctx, t0, tok) in segs:
                ld = {}
                for n in L:
                    t, k = L[n][cnt % 2]
                    self.dma(t, A[n][b, tok:tok + 128, :], w=[k])
                    ld[n] = (t, k)
                y2T, y2Tk = y2T_ring[cnt % 2]
                cnt += 1
                self.tt("pool", o, ld["o0"][0], ld["o1"][0], ALU.add, r=[ld["o0"][1], ld["o1"][1]], w=[ok_])
                mean, ex2, bon, tmp16 = st[:, 0:16], st[:, 16:32], st[:, 32:48], st[:, 48:64]
                self.p.add("dve", lambda e: e.tensor_reduce(out=mean, in_=h3(o), axis=AX.X, op=ALU.add), r=[ok_], w=[stk])
                self.tt("pool", sq, o, o, ALU.mult, r=[ok_], w=[sqk])
                self.p.add("dve", lambda e: e.tensor_reduce(out=ex2, in_=h3(sq), axis=AX.X, op=ALU.add), r=[sqk], w=[stk])
                self.ts("dve", mean, mean, 1.0 / 64, None, ALU.mult, None, r=[stk], w=[stk])
                self.tt("dve", tmp16, mean, mean, ALU.mult, r=[stk], w=[stk])
                self.stt(ex2, ex2, 1.0 / 64, tmp16, ALU.mult, ALU.subtract, r=[stk], w=[stk])
                self.rsqrt(ex2, ex2, GN_EPS, 1.0, r=[stk], w=[stk])
                self.tt("dve", h3(o), h3(o), self.bc_free(mean, "v"), ALU.subtract, r=[ok_, stk], w=[ok_])
                self.tt("pool", h3(o), h3(o), self.bc_free(ex2, "v"), ALU.mult, r=[ok_, stk], w=[ok_])
                self.tt("dve", o, o, lngb, ALU.mult, r=[ok_, pk], w=[ok_])
                self.tt("pool", o, o, lnbb, ALU.add, r=[ok_, pk], w=[ok_])
                self.tt("pool", sq, ld["kd0"][0], ld["kd1"][0], ALU.add, r=[ld["kd0"][1], ld["kd1"][1]], w=[sqk])
                self.tt("dve", sq, sq, ld["r"][0], ALU.mult, r=[sqk, ld["r"][1]], w=[sqk])
                self.tt("pool", sq, sq, rkb, ALU.mult, r=[sqk, pk], w=[sqk])
                self.p.add("dve", lambda e: e.tensor_reduce(out=bon, in_=h3(sq), axis=AX.X, op=ALU.add), r=[sqk], w=[stk])
                self.tt("pool", h3(sq), h3(ld["v"][0]), self.bc_free(bon, "v"), ALU.mult, r=[ld["v"][1], stk, sqk], w=[sqk])
                self.tt("dve", o, o, sq, ALU.add, r=[ok_, sqk], w=[ok_])
                self.tt("pool", o, o, ld["sg"][0], ALU.mult, r=[ok_, ld["sg"][1]], w=[ok_])
                for g in range(2):
                    bank = 6 + g
                    for q in range(4):
                        kc = g * 4 + q
                        self.tr(self.ps[bank][:, q * 128:(q + 1) * 128], o[:, kc * 128:(kc + 1) * 128], r=[ok_], w=[self.psk[bank]])
                    self.p.add("act", lambda e, bank=bank, g=g, y2T=y2T: e.copy(
                        out=y2T[:, g * 4:(g + 1) * 4, :].rearrange("p a b -> p (a b)"), in_=self.ps[bank][:, :]),
                        r=[self.psk[bank]], w=[y2Tk])
                self.epilogue_tile(li, b, is_ctx, t0, lambda kc, y2T=y2T: y2T[:, kc, :], KC, w_o, wok, [y2Tk], last)

    def attn_layer(self, li, j, ctx_out, midx, last):
        assert not ctx_out
        NB, TL, TC, TT = self.NB, self.TL, self.TC, self.TT
        i = self.i
        NKT = TT // 128
        SC = 1.0 / math.sqrt(128.0)
        self.layer_vectors(midx)
        qg = self.alloc([1], F32)
        kg = self.alloc([1], F32)
        SK = globals().get("ATTN_SKIP", "")
        if "q" not in SK:
            self.dma(qg, bass.AP(i["attn_q_g"].tensor, i["attn_q_g"][j, :].offset, [[1, 128], [1, 1]]), w=["qg"])
            self.dma(kg, bass.AP(i["attn_k_g"].tensor, i["attn_k_g"][j, :].offset, [[1, 128], [1, 1]]), w=["kg"])
        cosT = self.alloc([TL], F32)
        sinT = self.alloc([TL], F32)
        if "c" not in SK:
            self.dma(cosT, i["k_cos"], w=["cosT"])
            self.dma(sinT, i["k_sin"], w=["sinT"])
        w_in_v = self.wb["awi"].rearrange("(kc p) n -> p kc n", p=128)
        w_out = self.alloc([16, D], BF16)
        wok = self.key("awo")
        if "w" not in SK:
            self.dma(w_out, self.wb["awo"].rearrange("(h p) n -> p h n", p=128), r=["awo"], w=[wok])
        og_d = self.dram("og_d%d" % li, [16, 128, TL], BF16)
        hT = self.alloc([KC, TT], BF16)
        hk = self.key("hT")
        self.stage1_setup()
        self.epi_setup()
        wring = self.ring("awp", 2, [KC, 768], BF16)
        kT = self.alloc([TT], BF16)
        kTk = self.key("kT")
        vT = self.alloc([NKT, 128], BF16)
        vTk = self.key("vT")
        qT = self.alloc([2, TL], BF16)
        qTk = self.key("qT")
        sgT = self.alloc([2, TL], BF16)
        sgTk = self.key("sgT")
        sq_ring = self.ring("sq", 1, [512], BF16) * 2
        rs_ring = self.ring("rs", 1, [512], F32) * 2
        kn_ring = self.ring("kn", 1, [512], BF16) * 2
        t1_ring = self.ring("rt1", 1, [512], F32) * 2
        t2_ring = self.ring("rt2", 1, [512], F32) * 2
        pT_ring = self.ring("pT", 4, [512], BF16)
        rz = self.alloc([512], F32)
        rzk = self.key("rz")
        o1 = self.alloc([512], F32)
        o1k = self.key("o1")
        og_ring = self.ring("og", 2, [512], BF16)
        ogt_ring = self.ring("ogt", 2, [16, 128], BF16)
        cn = [0]

        def normrope(ps_ap, psk, n, gvec, gk, dest, destk, pos0):
            c = cn[0]
            cn[0] += 1
            sq, sqk = sq_ring[c % 2]
            rs, rsk = rs_ring[c % 2]
            self.act(sq[:, 0:n], ps_ap, AF.Square, r=[psk], w=[sqk])
            self.mm(self.ps[0][:, 0:n], self.ones_h, sq[:, 0:n], True, True, r=[sqk], w=[self.psk[0]])
            self.rsqrt(rs[:, 0:n], self.ps[0][:, 0:n], NORM_EPS, 1.0, r=[self.psk[0]], w=[rsk])
            if pos0 is None:
                self.stt(dest, ps_ap, gvec, rs[:, 0:n], ALU.mult, ALU.mult, r=[psk, rsk, gk], w=[destk])
                return
            kn, knk = kn_ring[c % 2]
            t1, t1k = t1_ring[c % 2]
            t2, t2k = t2_ring[c % 2]
            self.stt(kn[:, 0:n], ps_ap, gvec, rs[:, 0:n], ALU.mult, ALU.mult, r=[psk, rsk, gk], w=[knk])
            self.mm(self.ps[1][:, 0:n], self.perm_b, kn[:, 0:n], True, True, r=[knk], w=[self.psk[1]])
            self.tt("dve", t1[:, 0:n], kn[:, 0:n], cosT[:, pos0:pos0 + n], ALU.mult, r=[knk, "cosT"], w=[t1k])
            self.tt("dve", t2[:, 0:n], self.ps[1][:, 0:n], sinT[:, pos0:pos0 + n], ALU.mult,
                    r=[self.psk[1], "sinT"], w=[t2k])
            self.tt("pool", dest, t1[:, 0:n], t2[:, 0:n], ALU.add, r=[t1k, t2k], w=[destk])

        cw = 0
        ca = 0
        STOP = globals().get("ATTN_STOP", 99)
        if STOP <= 2:
            return
        for b in range(NB):
            self.stage1(li, b, True, hT, hk)
            if STOP <= 3:
                return
            for g in range(8 if STOP > 4 else 1):
                wt, wtk = wring[cw % 2]
                cw += 1
                self.dma(wt[:, :, 0:256], w_in_v[:, :, 256 * g:256 * g + 256], r=["awi"], w=[wtk])
                self.dma(wt[:, :, 256:384], w_in_v[:, :, 2048 + 128 * g:2048 + 128 * g + 128], r=["awi"], w=[wtk])
                self.dma(wt[:, :, 384:512], w_in_v[:, :, 3072 + 128 * g:3072 + 128 * g + 128], r=["awi"], w=[wtk])
                self.dma(wt[:, :, 512:768], w_in_v[:, :, 4096 + 256 * g:4096 + 256 * g + 256], r=["awi"], w=[wtk])
                ktiles = [(o, m, None) for (o, m) in _tiles(TC, 512)] + [(TC + o, m, o) for (o, m) in _tiles(TL, 512)]
                for (tok0, n, pos0) in ktiles:
                    for kc in range(KC):
                        self.mm(self.ps[7][:, 0:n], wt[:, kc, 256:384], hT[:, kc, tok0:tok0 + n], kc == 0, kc == KC - 1,
                                r=[wtk, hk], w=[self.psk[7]])
                    normrope(self.ps[7][:, 0:n], self.psk[7], n, kg, "kg", kT[:, tok0:tok0 + n], kTk, pos0)
                for k0 in range(0, NKT, 4):
                    nk = min(4, NKT - k0)
                    for q in range(nk):
                        kt = k0 + q
                        for kc in range(KC):
                            self.mm(self.ps[2][:, q * 128:(q + 1) * 128], hT[:, kc, kt * 128:(kt + 1) * 128],
                                    wt[:, kc, 384:512], kc == 0, kc == KC - 1, r=[wtk, hk], w=[self.psk[2]])
                    self.p.add("act", lambda e, k0=k0, nk=nk: e.copy(
                        out=vT[:, k0:k0 + nk, :].rearrange("p a b -> p (a b)"), in_=self.ps[2][:, 0:nk * 128]),
                        r=[self.psk[2]], w=[vTk])
                for hq in range(2):
                    for (o, m) in _tiles(TL, 512):
                        for kc in range(KC):
                            self.mm(self.ps[7][:, 0:m], wt[:, kc, hq * 128:(hq + 1) * 128], hT[:, kc, TC + o:TC + o + m],
                                    kc == 0, kc == KC - 1, r=[wtk, hk], w=[self.psk[7]])
                        normrope(self.ps[7][:, 0:m], self.psk[7], m, qg, "qg", qT[:, hq, o:o + m], qTk, o)
                        for kc in range(KC):
                            self.mm(self.ps[2][:, 0:m], wt[:, kc, 512 + hq * 128:512 + (hq + 1) * 128],
                                    hT[:, kc, TC + o:TC + o + m], kc == 0, kc == KC - 1, r=[wtk, hk], w=[self.psk[2]])
                        self.act(sgT[:, hq, o:o + m], self.ps[2][:, 0:m], AF.Silu, r=[self.psk[2]], w=[sgTk])
                if STOP <= 5:
                    continue
                for hq in range(2):
                    for (o, m) in _tiles(TL, 512):
                        bO = 4 + 2 * (ca % 2)
                        bZ = bO + 1
                        og, ogk = og_ring[ca % 2]
                        ca += 1

                        def S(kt):
                            bank = kt % 4
                            self.mm(self.ps[bank][:, 0:m], kT[:, kt * 128:(kt + 1) * 128], qT[:, hq, o:o + m], True, True,
                                    r=[kTk, qTk], w=[self.psk[bank]])
                        for k_ in range(min(3, NKT)):
                            S(k_)
                        for kt in range(NKT):
                            bank = kt % 4
                            pT, pTk = pT_ring[kt % 4]
                            self.act(pT[:, 0:m], self.ps[bank][:, 0:m], AF.Exp, r=[self.psk[bank]], w=[pTk], scale=SC)
                            self.mm(self.ps[bO][:, 0:m], vT[:, kt, :], pT[:, 0:m], kt == 0, kt == NKT - 1,
                                    r=[vTk, pTk], w=[self.psk[bO]])
                            self.mm(self.ps[bZ][:, 0:m], self.ones_1, pT[:, 0:m], kt == 0, kt == NKT - 1,
                                    r=[pTk], w=[self.psk[bZ]])
                            if kt + 3 < NKT:
                                S(kt + 3)
                        self.p.add("dve", lambda e, bZ=bZ, m=m: e.reciprocal(out=rz[:, 0:m], in_=self.ps[bZ][:, 0:m]),
                                   r=[self.psk[bZ]], w=[rzk])
                        self.tt("dve", o1[:, 0:m], self.ps[bO][:, 0:m], rz[:, 0:m], ALU.mult, r=[self.psk[bO], rzk], w=[o1k])
                        self.tt("pool", og[:, 0:m], o1[:, 0:m], sgT[:, hq, o:o + m], ALU.mult, r=[o1k, sgTk], w=[ogk])
                        self.dma(og_d[2 * g + hq, :, o:o + m], og[:, 0:m], r=[ogk], w=["og_d"], q="pool")
            if STOP <= 6:
                return
            ce = 0
            for t0 in range(0, TL, 128):
                ogt, ogtk = ogt_ring[ce % 2]
                ce += 1
                self.dma(ogt, og_d[:, :, t0:t0 + 128].rearrange("h p t -> p h t"), r=["og_d"], w=[ogtk])
                self.epilogue_tile(li, b, False, t0, lambda kc, ogt=ogt: ogt[:, kc, :], 16, w_out, wok, [ogtk], last)


FULL_LAYERS = [(0, 0, True, True, 0), (1, 0, True, True, 1), (2, 0, True, False, 2), (0, 1, False, False, 3)]


def host_consts(TL):
    ident = np.eye(128, dtype=np.float32)
    perm = np.zeros((128, 128), np.float32)
    for k in range(128):
        blk = k // 32
        if blk % 2 == 0:
            perm[k, k + 32] = 1.0
        else:
            perm[k, k - 32] = -1.0
    rows = TL // GRID_W
    t = np.arange(TL)
    row = (t // GRID_W).astype(np.float32)
    col = (t % GRID_W).astype(np.float32)
    inv = (1.0 / (10000.0 ** (np.arange(0, 64, 2, dtype=np.float32) / np.float32(64)))).astype(np.float32)
    cosT = np.zeros((128, TL), np.float32)
    sinT = np.zeros((128, TL), np.float32)
    for p in range(128):
        pos = row if p < 64 else col
        ang = (pos * inv[p % 32]).astype(np.float32)
        cosT[p] = np.cos(ang)
        sinT[p] = np.sin(ang)
    idx = np.arange(128)
    same = (idx[:, None] // 64) == (idx[None, :] // 64)
    tri = np.zeros((2, 128, 128), np.float32)
    tri[0] = (same & (idx[:, None] <= idx[None, :])).astype(np.float32)
    tri[1] = (same & (idx[:, None] >= idx[None, :])).astype(np.float32)
    ind = np.zeros((128, 2), np.float32)
    ind[:64, 0] = 1.0
    ind[64:, 1] = 1.0
    i64 = np.arange(64)
    us = (i64[:, None] < i64[None, :]).astype(np.float32)
    ls = (i64[:, None] > i64[None, :]).astype(np.float32)
    ui = (i64[:, None] <= i64[None, :]).astype(np.float32)
    li_ = (i64[:, None] >= i64[None, :]).astype(np.float32)
    eye = np.eye(64, dtype=np.float32)
    masks = np.stack([np.tile(m, (1, 8)) for m in (us, ls, ui, li_, eye)]).astype(np.float32)
    return {"k_ident": ident, "k_perm": perm, "k_cos": cosT, "k_sin": sinT,
            "k_tri": tri, "k_ind": ind, "k_masks": masks}


def make_in_maps(inputs, n_cores, NB, TL):
    consts = host_consts(TL)
    maps = []
    for cid in range(n_cores):
        m = {}
        for k, v in inputs.items():
            v = np.asarray(v)
            if k in ("x", "c", "ctx"):
                m[k] = np.ascontiguousarray(v[cid * NB:(cid + 1) * NB])
            elif k in ("c_ctx", "final_g"):
                m[k] = np.ascontiguousarray(v.reshape(1, -1))
            elif k == "rwkv_r_k":
                m[k] = np.ascontiguousarray(v.reshape(v.shape[0], -1))
            else:
                m[k] = np.ascontiguousarray(v)
        m.update(consts)
        maps.append(m)
    return maps


_CACHE = {}


def run(inputs, layers, n_cores=8, trace=False, dbg=False):
    x = np.asarray(inputs["x"])
    B, TL, _ = x.shape
    TC = np.asarray(inputs["ctx"]).shape[1]
    NB = B // n_cores
    mk = MK(NB, TL, TC, layers, dbg=dbg)
    nc = mk.build()
    maps = make_in_maps(inputs, n_cores, NB, TL)
    res = run_bass_kernel_spmd(nc, maps, core_ids=list(range(n_cores)), trace=trace)
    out = np.concatenate([np.asarray(r["out"]) for r in res.results], axis=0)
    return out.astype(np.float32), res


def kernel(**inputs):
    out, _ = run(inputs, FULL_LAYERS, n_cores=8)
    return out
```

```python
import contextlib
import math
import numpy as np
import ml_dtypes
import concourse.bass as bass
import concourse.mybir as mybir
from concourse.bass_utils import run_bass_kernel_spmd
from concourse.alu_op_type import AluOpType as ALU

F32 = mybir.dt.float32
BF16 = mybir.dt.bfloat16
AF = mybir.ActivationFunctionType
AX = mybir.AxisListType

D = 1024
KC = 8
GRID_W = 64
CONV_W = 31
HALO = 15
NORM_EPS = 1e-6
LN_EPS = 1e-5
GN_EPS = 64e-5

ENG = ("pe", "act", "dve", "pool", "sp")
NDMASEM = 12
RWKV_CHUNKED = True
SCAN_POOL = 0
RQ = "sp"


class Op:
    __slots__ = ("eng", "fn", "dma", "deps", "needs_inc", "cnt", "slot", "val", "idx")


class Prog:
    def __init__(self, nc):
        self.nc = nc
        self.ops = []
        self.by_eng = {e: [] for e in ENG}
        self.last_w = {}
        self.readers = {}
        self.ndma = {e: 0 for e in ENG}
        self.fence_deps = []
        self.fence_pending = set()

    def fence(self):
        deps = []
        for e in ENG:
            lst = self.by_eng[e]
            for op in reversed(lst):
                if not op.dma:
                    deps.append(op.idx)
                    break
            cnt = 0
            for op in reversed(lst):
                if op.dma:
                    deps.append(op.idx)
                    cnt += 1
                    if cnt >= NDMASEM:
                        break
        self.fence_deps = deps
        self.fence_pending = set(ENG)
        self.last_w.clear()
        self.readers.clear()

    def add(self, eng, fn, r=(), w=(), dma=False):
        op = Op()
        op.eng, op.fn, op.dma = eng, fn, dma
        op.needs_inc = False
        op.cnt = op.slot = op.val = None
        op.idx = len(self.ops)
        deps = set()
        lw = self.last_w
        for k in r:
            y = lw.get(k)
            if y is not None:
                deps.add(y)
        for k in w:
            y = lw.get(k)
            if y is not None:
                deps.add(y)
            ys = self.readers.get(k)
            if ys:
                deps.update(ys)
        keep = []
        for yi in deps:
            y = self.ops[yi]
            if (not y.dma) and (not dma) and y.eng == eng:
                if eng == "pe":
                    continue
                raw = False
                for k in r:
                    if lw.get(k) == yi:
                        raw = True
                        break
                if not raw:
                    continue
            keep.append(yi)
        if eng in self.fence_pending:
            self.fence_pending.discard(eng)
            keep = list(set(keep) | set(self.fence_deps))
        op.deps = keep
        for yi in keep:
            self.ops[yi].needs_inc = True
        if dma:
            n = self.ndma[eng]
            self.ndma[eng] = n + 1
            op.slot = n % NDMASEM
            op.val = 16 * (n // NDMASEM + 1)
        for k in r:
            self.readers.setdefault(k, []).append(op.idx)
        for k in w:
            lw[k] = op.idx
            self.readers[k] = []
        self.ops.append(op)
        self.by_eng[eng].append(op)
        return op

    def emit(self):
        nc = self.nc
        for e in ENG:
            c = 0
            for op in self.by_eng[e]:
                if (not op.dma) and op.needs_inc:
                    c += 1
                    op.cnt = c
        with contextlib.ExitStack() as st:
            csem = {e: st.enter_context(nc.semaphore("c_" + e)) for e in ENG}
            dsem = {e: [st.enter_context(nc.semaphore("d_%s_%d" % (e, i))) for i in range(NDMASEM)]
                    for e in ENG if self.ndma[e] > 0}
            block = st.enter_context(nc.Block())
            handles = {"pe": block.tensor, "act": block.scalar, "dve": block.vector,
                       "pool": block.gpsimd, "sp": block.sync}
            ops = self.ops

            def make(e):
                def body(eng):
                    waited = {}
                    for op in self.by_eng[e]:
                        need = {}
                        for yi in op.deps:
                            y = ops[yi]
                            if y.dma:
                                key = ("d", y.eng, y.slot)
                                v = y.val
                            else:
                                key = ("c", y.eng)
                                v = y.cnt
                            if need.get(key, 0) < v:
                                need[key] = v
                        if op.dma and op.val > 16:
                            key = ("d", e, op.slot)
                            if need.get(key, 0) < op.val - 16:
                                need[key] = op.val - 16
                        for key, v in need.items():
                            if waited.get(key, 0) >= v:
                                continue
                            waited[key] = v
                            sem = csem[key[1]] if key[0] == "c" else dsem[key[1]][key[2]]
                            eng.wait_ge(sem, v)
                        ins = op.fn(eng)
                        if op.dma:
                            ins.then_inc(dsem[e][op.slot], 16)
                        elif op.needs_inc:
                            ins.then_inc(csem[e], 1)
                    if e == "sp":
                        for q in ENG:
                            n = self.ndma[q]
                            for s in range(min(n, NDMASEM)):
                                cnt = (n - 1 - s) // NDMASEM + 1
                                if waited.get(("d", q, s), 0) < 16 * cnt:
                                    eng.wait_ge(dsem[q][s], 16 * cnt)
                        for q in ENG:
                            last = None
                            for op in self.by_eng[q]:
                                if op.cnt is not None:
                                    last = op.cnt
                            if last is not None and q != "sp":
                                eng.wait_ge(csem[q], last)
                return body

            for e in ENG:
                if self.by_eng[e] or e == "sp":
                    handles[e](make(e))


def _tiles(n, t):
    out = []
    o = 0
    while o < n:
        m = min(t, n - o)
        out.append((o, m))
        o += m
    return out


class MK:
    def __init__(self, NB, TL, TC, layers, dbg=False):
        self.NB, self.TL, self.TC, self.layers = NB, TL, TC, layers
        self.TT = TL + TC
        self.NV = NB + 1
        self.nc = bass.Bass("TRN2", target_bir_lowering=False)
        self.p = Prog(self.nc)
        self.st = contextlib.ExitStack()
        self.uid = 0
        self.dbg = dbg

    def dram_in(self, name, shape, dt=F32):
        return self.nc.dram_tensor(name, list(shape), dt, kind="ExternalInput").ap()

    def dram(self, name, shape, dt=F32):
        if self.dbg:
            return self.nc.dram_tensor(name, list(shape), dt, kind="ExternalOutput").ap()
        return self.nc.dram_tensor(name, list(shape), dt).ap()

    def alloc(self, free, dt=F32):
        n = 1
        for f in free:
            n *= f
        units = n * 2 if dt == F32 else n
        self.top = (self.top + 15) // 16 * 16
        assert self.top + units <= self.AREN, ("arena overflow", self.top, units, self.AREN)
        v = self.arena[:, self.top:self.top + units]
        self.top += units
        if dt == F32:
            v = v.bitcast(F32)
        if len(free) == 2:
            v = v.rearrange("p (a b) -> p a b", a=free[0])
        elif len(free) == 3:
            v = v.rearrange("p (a b c) -> p a b c", a=free[0], b=free[1])
        return v

    def key(self, s):
        self.uid += 1
        return "%s#%d" % (s, self.uid)

    def ring(self, name, n, free, dt=F32):
        return [(self.alloc(free, dt), self.key(name)) for _ in range(n)]

    def dma(self, out, in_, r=(), w=(), q="sp", **kw):
        return self.p.add(q, lambda e: e.dma_start(out=out, in_=in_, **kw), r=r, w=w, dma=True)

    def mm(self, out, lhsT, rhs, start, stop, r, w):
        return self.p.add("pe", lambda e: e.matmul(out, lhsT=lhsT, rhs=rhs, start=start, stop=stop), r=r, w=w)

    def tr(self, out, in_, r, w):
        P = in_.shape[0]
        ident = self.ident_f[0:P, 0:P]
        return self.p.add("pe", lambda e: e.transpose(out=out, in_=in_, identity=ident), r=r, w=w)

    def act(self, out, in_, func, r, w, scale=None, bias=None):
        kw = {}
        if scale is not None:
            kw["scale"] = scale
        if bias is not None:
            kw["bias"] = bias
        return self.p.add("act", lambda e: e.activation(out=out, in_=in_, func=func, **kw), r=r, w=w)

    def tt(self, eng, out, in0, in1, op, r, w):
        return self.p.add(eng, lambda e: e.tensor_tensor(out=out, in0=in0, in1=in1, op=op), r=r, w=w)

    def ts(self, eng, out, in0, s1, s2, op0, op1, r, w):
        if op1 is None:
            return self.p.add(eng, lambda e: e.tensor_scalar(out=out, in0=in0, scalar1=s1, scalar2=None, op0=op0), r=r, w=w)
        return self.p.add(eng, lambda e: e.tensor_scalar(out=out, in0=in0, scalar1=s1, scalar2=s2, op0=op0, op1=op1), r=r, w=w)

    def stt(self, out, in0, scalar, in1, op0, op1, r, w):
        return self.p.add("dve", lambda e: e.scalar_tensor_tensor(out=out, in0=in0, scalar=scalar, in1=in1, op0=op0, op1=op1), r=r, w=w)

    def rsqrt(self, out, in_, eps, mul, r, w):
        kt = self.key("rs")
        self.ts("dve", out, in_, mul, eps, ALU.mult, ALU.add, r=r, w=[kt])
        self.p.add("act", lambda e: e.activation(out=out, in_=out, func=AF.Sqrt), r=[kt], w=[kt])
        self.p.add("dve", lambda e: e.reciprocal(out=out, in_=out), r=[kt], w=w)

    def bcast_row(self, dram_ap_row, n):
        return bass.AP(dram_ap_row.tensor, dram_ap_row.offset, [[0, 128], [1, n]])

    def build(self):
        nc, NB, TL, TC, TT = self.nc, self.NB, self.TL, self.TC, self.TT
        NLAY = 4
        i = {}
        i["x"] = self.dram_in("x", [NB, TL, D])
        i["c"] = self.dram_in("c", [NB, D])
        i["ctx"] = self.dram_in("ctx", [NB, TC, D])
        i["c_ctx"] = self.dram_in("c_ctx", [1, D])
        i["norm_g"] = self.dram_in("norm_g", [NLAY, D])
        i["mod_w"] = self.dram_in("mod_w", [NLAY, D, 3 * D])
        i["mod_b"] = self.dram_in("mod_b", [NLAY, 3 * D])
        i["conv_w_in"] = self.dram_in("conv_w_in", [2, D, 3 * D])
        i["conv_dw"] = self.dram_in("conv_dw", [2, CONV_W, D])
        i["conv_db"] = self.dram_in("conv_db", [2, D])
        i["conv_ln_g"] = self.dram_in("conv_ln_g", [2, D])
        i["conv_ln_b"] = self.dram_in("conv_ln_b", [2, D])
        i["conv_w_out"] = self.dram_in("conv_w_out", [2, D, D])
        i["rwkv_mu"] = self.dram_in("rwkv_mu", [1, 6, D])
        for nm in ("rwkv_w_r", "rwkv_w_k", "rwkv_w_v", "rwkv_w_g", "rwkv_w_o"):
            i[nm] = self.dram_in(nm, [1, D, D])
        i["rwkv_w0"] = self.dram_in("rwkv_w0", [1, 2, D])
        i["rwkv_w1"] = self.dram_in("rwkv_w1", [1, 2, D, 64])
        i["rwkv_w2"] = self.dram_in("rwkv_w2", [1, 2, 64, D])
        i["rwkv_a0"] = self.dram_in("rwkv_a0", [1, 2, D])
        i["rwkv_a1"] = self.dram_in("rwkv_a1", [1, 2, D, 64])
        i["rwkv_a2"] = self.dram_in("rwkv_a2", [1, 2, 64, D])
        i["rwkv_k_k"] = self.dram_in("rwkv_k_k", [1, D])
        i["rwkv_k_a"] = self.dram_in("rwkv_k_a", [1, D])
        i["rwkv_r_k"] = self.dram_in("rwkv_r_k", [1, D])
        i["rwkv_ln_g"] = self.dram_in("rwkv_ln_g", [1, D])
        i["rwkv_ln_b"] = self.dram_in("rwkv_ln_b", [1, D])
        i["attn_w_in"] = self.dram_in("attn_w_in", [1, D, 6 * D])
        i["attn_q_g"] = self.dram_in("attn_q_g", [1, 128])
        i["attn_k_g"] = self.dram_in("attn_k_g", [1, 128])
        i["attn_w_out"] = self.dram_in("attn_w_out", [1, 2 * D, D])
        i["final_g"] = self.dram_in("final_g", [1, D])
        i["k_ident"] = self.dram_in("k_ident", [128, 128])
        i["k_perm"] = self.dram_in("k_perm", [128, 128])
        i["k_cos"] = self.dram_in("k_cos", [128, TL])
        i["k_sin"] = self.dram_in("k_sin", [128, TL])
        i["k_tri"] = self.dram_in("k_tri", [2, 128, 128])
        i["k_ind"] = self.dram_in("k_ind", [128, 2])
        i["k_masks"] = self.dram_in("k_masks", [5, 64, 512])
        self.i = i
        self.out = nc.dram_tensor("out", [NB, TL, D], F32, kind="ExternalOutput").ap()
        self.xs = self.dram("xs", [NB, TL, D])
        if self.dbg:
            self.xcs = nc.dram_tensor("xcs", [NB, TC, D], F32, kind="ExternalOutput").ap()
        else:
            self.xcs = self.dram("xcs", [NB, TC, D])
        self.m_d = self.dram("m_d", [NLAY, self.NV, 3 * D])

        st = self.st
        with st:
            self.AREN = 106000
            self.arena = st.enter_context(nc.sbuf_tensor("arena", [128, self.AREN], BF16))
            self.top = 0
            self.ps = [st.enter_context(nc.psum_tensor("ps%d" % k, [128, 512], F32)) for k in range(8)]
            self.psk = ["ps%d" % k for k in range(8)]
            self.prologue()
            self.persist_top = self.top
            used = set(l[0] for l in self.layers)
            self.convert_weights(used)
            self.modulation()
            nl = len(self.layers)
            for li, (kind, j, ctx_in, ctx_out, midx) in enumerate(self.layers):
                self.p.fence()
                self.top = self.persist_top
                last = li == nl - 1
                if kind == 0:
                    self.conv_layer(li, j, ctx_out, midx, last)
                elif kind == 1:
                    self.rwkv_layer(li, j, ctx_out, midx, last)
                else:
                    self.attn_layer(li, j, ctx_out, midx, last)
            if nl == 0:
                self.p.fence()
                self.top = self.persist_top
                self.only_final()
            self.p.emit()
        return nc

    def prologue(self):
        i = self.i
        self.ident_f = self.alloc([128], F32)
        self.dma(self.ident_f, i["k_ident"], w=["ident_f"])
        self.ident_b = self.alloc([128], BF16)
        self.p.add("dve", lambda e: e.tensor_copy(out=self.ident_b, in_=self.ident_f), r=["ident_f"], w=["ident_b"])
        permf = self.alloc([128], F32)
        self.dma(permf, i["k_perm"], w=["permf"])
        self.perm_b = self.alloc([128], BF16)
        self.p.add("dve", lambda e: e.tensor_copy(out=self.perm_b, in_=permf), r=["permf"], w=["perm_b"])
        self.ones_d = self.alloc([128], BF16)
        self.ones_h = self.alloc([128], BF16)
        self.ones_1 = self.alloc([128], BF16)
        self.p.add("pool", lambda e: e.memset(self.ones_d, 1.0 / D), w=["ones_d"])
        self.p.add("pool", lambda e: e.memset(self.ones_h, 1.0 / 128), w=["ones_h"])
        self.p.add("pool", lambda e: e.memset(self.ones_1, 1.0), w=["ones_1"])
        self.fg_b = self.alloc([D], F32)
        self.dma(self.fg_b, self.bcast_row(i["final_g"][0, :], D), w=["fg_b"])
        self.constkeys = ["ident_f", "ident_b", "perm_b", "ones_d", "ones_h", "ones_1", "fg_b"]

    def after_fence_consts(self):
        return

    def convert_weights(self, used):
        i = self.i
        self.wb = {}

        def conv(name, src, rows, cols):
            dst = self.dram(name, [rows, cols], BF16)
            for r0 in range(0, rows, 256):
                rr = min(256, rows - r0)
                self.dma(dst[r0:r0 + rr, :], src[r0:r0 + rr, :], w=[name], q="pool")
            self.wb[name] = dst

        if 0 in used:
            for j in sorted(set(l[1] for l in self.layers if l[0] == 0)):
                conv("cwi%d" % j, i["conv_w_in"][j], D, 3 * D)
                conv("cwo%d" % j, i["conv_w_out"][j], D, D)
        if 1 in used:
            for nm in ("w_r", "w_k", "w_v", "w_g", "w_o"):
                conv("r" + nm, i["rwkv_" + nm][0], D, D)
            for dd in range(2):
                conv("rw1_%d" % dd, i["rwkv_w1"][0, dd], D, 64)
                conv("ra1_%d" % dd, i["rwkv_a1"][0, dd], D, 64)
                conv("rw2_%d" % dd, i["rwkv_w2"][0, dd], 64, D)
                conv("ra2_%d" % dd, i["rwkv_a2"][0, dd], 64, D)
        if 2 in used:
            conv("awi", i["attn_w_in"][0], D, 6 * D)
            conv("awo", i["attn_w_out"][0], 2 * D, D)

    def modulation(self):
        i, NV, NB = self.i, self.NV, self.NB
        top0 = self.top
        crow = self.alloc([D], F32)
        self.dma(crow[0:NB, :], i["c"], w=["crow"])
        self.dma(crow[NB:NV, :], i["c_ctx"], w=["crow"])
        self.act(crow[0:NV, :], crow[0:NV, :], AF.Silu, r=["crow"], w=["crow"])
        scT = self.alloc([KC, NV], F32)
        for kc in range(KC):
            self.tr(self.ps[0][:, kc * NV:(kc + 1) * NV], crow[0:NV, kc * 128:(kc + 1) * 128],
                    r=["crow", "ident_f"], w=["ps0"])
        self.p.add("dve", lambda e: e.tensor_copy(out=scT.rearrange("p a b -> p (a b)"), in_=self.ps[0][:, 0:KC * NV]),
                   r=["ps0"], w=["scT"])
        wring = self.ring("modw", 2, [KC, 512], F32)
        mrow = self.alloc([3 * D], F32)
        brow = self.alloc([3 * D], F32)
        used_mod = sorted(set(l[4] for l in self.layers))
        cnt = 0
        for l in used_mod:
            self.dma(brow[0:NV, :], bass.AP(i["mod_b"].tensor, i["mod_b"][l, :].offset, [[0, NV], [1, 3 * D]]),
                     r=[], w=["brow"])
            for pn in range(6):
                wt, wk = wring[cnt % 2]
                bank = 1 + cnt % 2
                cnt += 1
                self.dma(wt, i["mod_w"][l][:, pn * 512:(pn + 1) * 512].rearrange("(kc p) n -> p kc n", p=128), w=[wk])
                for kc in range(KC):
                    self.mm(self.ps[bank][0:NV, :], scT[:, kc, :], wt[:, kc, :], kc == 0, kc == KC - 1,
                            r=[wk, "scT"], w=[self.psk[bank]])
                self.tt("dve", mrow[0:NV, pn * 512:(pn + 1) * 512], self.ps[bank][0:NV, :],
                        brow[0:NV, pn * 512:(pn + 1) * 512], ALU.add, r=[self.psk[bank], "brow"], w=["mrow"])
            self.dma(self.m_d[l], mrow[0:NV, :], r=["mrow"], w=["m_d"], q="pool")
        self.top = top0

    def rows_to_cols(self, rows_aps, name):
        R = len(rows_aps)
        rt = self.alloc([D], F32)
        kr = self.key(name + "_rows")
        for r_, ap in enumerate(rows_aps):
            self.dma(rt[r_:r_ + 1, :], bass.AP(ap.tensor, ap.offset, [[0, 1], [1, D]]), r=["m_d"], w=[kr])
        colsT = self.alloc([KC, R], F32)
        kc_ = self.key(name + "_cols")
        assert KC * R <= 512
        for kc in range(KC):
            self.tr(self.ps[7][:, kc * R:(kc + 1) * R], rt[0:R, kc * 128:(kc + 1) * 128], r=[kr], w=["ps7"])
        self.p.add("dve", lambda e: e.tensor_copy(out=colsT.rearrange("p a b -> p (a b)"), in_=self.ps[7][:, 0:KC * R]),
                   r=["ps7"], w=[kc_])
        return colsT, kc_

    def layer_vectors(self, midx):
        i, NV = self.i, self.NV
        rows = [i["norm_g"][midx, :]]
        for v in range(NV):
            rows.append(self.m_d[midx, v, 0:D])
        for v in range(NV):
            rows.append(self.m_d[midx, v, D:2 * D])
        cols, ck = self.rows_to_cols(rows, "lv")
        gsT = self.alloc([KC, NV], F32)
        kg = self.key("gsT")
        for v in range(NV):
            self.p.add("dve", lambda e, v=v: e.scalar_tensor_tensor(
                out=gsT[:, :, v], in0=cols[:, :, 1 + NV + v], scalar=1.0, in1=cols[:, :, 0],
                op0=ALU.add, op1=ALU.mult), r=[ck], w=[kg])
        self.gsT, self.gsk = gsT, kg
        self.shT, self.shk = cols, ck
        self.midx = midx
        self.gate_tiles = {}

    def gate_b(self, v):
        gt = self.gate_tiles
        if v in gt:
            ent = gt.pop(v)
            gt[v] = ent
            return ent
        if len(gt) < 2:
            ent = (self.alloc([D], F32), self.key("gate_b"))
        else:
            old_v = next(iter(gt))
            ent = gt.pop(old_v)
        self.dma(ent[0], self.bcast_row(self.m_d[self.midx, v, 2 * D:3 * D], D), r=["m_d"], w=[ent[1]])
        gt[v] = ent
        return ent

    def stage1_setup(self, share=None):
        if share is None:
            self.xt_ring = self.ring("xt", 2, [D], F32)
            self.xn = self.alloc([D], F32)
            self.xnk = self.key("xn")
            self.junk = self.alloc([D], F32)
            self.junkk = self.key("junk")
        else:
            self.xt_ring = [share[0], share[1]]
            self.xn, self.xnk = share[2]
            self.junk, self.junkk = share[3]
        self.ss = self.alloc([4], F32)
        self.ssk = self.key("ss")
        self.xt_cnt = 0
        self.coff, self.loff = 0, self.TC

    def sumsq(self, x, ss, rk):
        junk = self.junk
        self.p.add("act", lambda e: e.activation(out=junk, in_=x, func=AF.Square, accum_out=ss),
                   r=rk, w=[self.junkk, self.ssk])

    def src_tile(self, li, b, is_ctx, t0, n=128):
        if li == 0:
            return (self.i["ctx"] if is_ctx else self.i["x"])[b, t0:t0 + n, :]
        return (self.xcs if is_ctx else self.xs)[b, t0:t0 + n, :]

    def xkey(self, b, is_ctx, t0):
        return "x_%d_%d_%d" % (b, int(is_ctx), t0 // 128)

    def stage1(self, li, b, do_ctx, hT, hk):
        TC, TL, NB = self.TC, self.TL, self.NB
        segs = []
        if do_ctx:
            segs += [(True, t0) for t0 in range(0, TC, 128)]
        segs += [(False, t0) for t0 in range(0, TL, 128)]
        for (is_ctx, t0) in segs:
            v = NB if is_ctx else b
            tok = self.coff + t0 if is_ctx else self.loff + t0
            xt, xk = self.xt_ring[self.xt_cnt % 2]
            self.xt_cnt += 1
            self.dma(xt, self.src_tile(li, b, is_ctx, t0), r=[self.xkey(b, is_ctx, t0)], w=[xk])
            ss = self.ss[:, 0:1]
            self.sumsq(xt, ss, [xk])
            self.rsqrt(ss, ss, NORM_EPS, 1.0 / D, r=[self.ssk], w=[self.ssk])
            xn = self.xn
            self.p.add("act", lambda e, xt=xt, ss=ss, xn=xn: e.activation(out=xn, in_=xt, func=AF.Identity, scale=ss),
                       r=[xk, self.ssk], w=[self.xnk])
            for g in range(2):
                bank = 6 + g
                for q in range(4):
                    kc = g * 4 + q
                    self.tr(self.ps[bank][:, q * 128:(q + 1) * 128], self.xn[:, kc * 128:(kc + 1) * 128],
                            r=[self.xnk], w=[self.psk[bank]])
                for q in range(4):
                    kc = g * 4 + q
                    self.ts("dve", hT[:, kc, tok:tok + 128], self.ps[bank][:, q * 128:(q + 1) * 128],
                            self.gsT[:, kc, v:v + 1], self.shT[:, kc, 1 + v:2 + v], ALU.mult, ALU.add,
                            r=[self.psk[bank], self.gsk, self.shk], w=[hk])

    def epi_setup(self):
        self.tmp = self.alloc([D], F32)
        self.tmpk = self.key("tmp")
        self.xnew_ring = self.ring("xnew", 2, [D], F32)
        self.epi_cnt = 0

    def epilogue_tile(self, li, b, is_ctx, t0, yT_fn, nkc, w_out, wok, ykeys, last, banks=(4, 5)):
        v = self.NB if is_ctx else b
        gb, gk = self.gate_b(v)
        xt, xk = self.xt_ring[self.xt_cnt % 2]
        self.xt_cnt += 1
        xkey = self.xkey(b, is_ctx, t0)
        self.dma(xt, self.src_tile(li, b, is_ctx, t0), r=[xkey], w=[xk])
        xn_, xnk_ = self.xnew_ring[self.epi_cnt % 2]
        self.epi_cnt += 1
        for half in range(2):
            bank = banks[half]
            for kc in range(nkc):
                self.mm(self.ps[bank][:, :], yT_fn(kc), w_out[:, kc, half * 512:(half + 1) * 512],
                        kc == 0, kc == nkc - 1, r=ykeys + [wok], w=[self.psk[bank]])
            hs = slice(half * 512, (half + 1) * 512)
            self.tt("dve", self.tmp[:, hs], self.ps[bank][:, :], gb[:, hs], ALU.mult,
                    r=[self.psk[bank], gk], w=[self.tmpk])
            self.tt("pool", xn_[:, hs], self.tmp[:, hs], xt[:, hs], ALU.add, r=[self.tmpk, xk], w=[xnk_])
        if last and not is_ctx:
            ss = self.ss[:, 1:2]
            self.sumsq(xn_, ss, [xnk_])
            self.rsqrt(ss, ss, NORM_EPS, 1.0 / D, r=[self.ssk], w=[self.ssk])
            self.stt(self.tmp, xn_, ss, self.fg_b, ALU.mult, ALU.mult, r=[xnk_, self.ssk], w=[self.tmpk])
            self.dma(self.out[b, t0:t0 + 128, :], self.tmp, r=[self.tmpk], w=["out"], q="pool")
        else:
            dst = (self.xcs if is_ctx else self.xs)[b, t0:t0 + 128, :]
            self.dma(dst, xn_, r=[xnk_], w=[xkey], q="pool")

    def only_final(self):
        self.stage1_setup()
        tmp = self.alloc([D], F32)
        for b in range(self.NB):
            for t0 in range(0, self.TL, 128):
                xt, xk = self.xt_ring[self.xt_cnt % 2]
                self.xt_cnt += 1
                self.dma(xt, self.i["x"][b, t0:t0 + 128, :], w=[xk])
                ss = self.ss[:, 1:2]
                self.sumsq(xt, ss, [xk])
                self.rsqrt(ss, ss, NORM_EPS, 1.0 / D, r=[self.ssk], w=[self.ssk])
                self.stt(tmp, xt, ss, self.fg_b, ALU.mult, ALU.mult, r=[xk, self.ssk], w=["tmpf"])
                self.dma(self.out[b, t0:t0 + 128, :], tmp, r=["tmpf"], w=["out"], q="pool")

    def conv_layer(self, li, j, ctx_out, midx, last):
        NB, TL, TC, TT = self.NB, self.TL, self.TC, self.TT
        i = self.i
        T2 = 256
        self.layer_vectors(midx)
        rows = [i["conv_dw"][j, t, :] for t in range(CONV_W)]
        rows += [i["conv_db"][j, :], i["conv_ln_g"][j, :], i["conv_ln_b"][j, :]]
        cv, cvk = self.rows_to_cols(rows, "cv")
        w_in = self.wb["cwi%d" % j]
        w_in_v = w_in.rearrange("(kc p) n -> p kc n", p=128)
        w_out = self.alloc([KC, D], BF16)
        wok = self.key("cwo")
        self.dma(w_out, self.wb["cwo%d" % j].rearrange("(kc p) n -> p kc n", p=128), r=["cwo%d" % j], w=[wok])
        w_g = self.alloc([KC, D], BF16)
        wgk = self.key("cwg")
        self.dma(w_g, w_in_v[:, :, 2 * D:3 * D], r=["cwi%d" % j], w=[wgk])
        hT = self.alloc([KC, TT], BF16)
        hk = self.key("hT")
        self.stage1_setup()
        self.epi_setup()
        seqs = []
        if ctx_out:
            seqs.append((True, 0, TC))
        seqs.append((False, TC, TL))
        GLW = sum(n + 2 * HALO for (_, _, n) in seqs)
        glu_d = self.dram("glu_d%d" % li, [KC, 128, GLW], BF16)
        wp_ring = self.ring("wp", 2, [KC, 2, 128], BF16)
        sig_ring = self.ring("sig", 2, [512], F32)
        gl_ring = self.ring("gl", 2, [GLW], BF16)
        for (t, k) in gl_ring:
            self.p.add("pool", lambda e, t=t: e.memset(t, 0.0), w=[k])
        glt_ring = self.ring("glt", 2, [KC, T2 + 2 * HALO], BF16)
        dg_ring = self.ring("dg", 2, [CONV_W, 128], BF16)
        sgt_ring = self.ring("sgt", 2, [T2], BF16)
        yb = self.alloc([KC, T2], BF16)
        ybk = self.key("yb")
        ysq = self.alloc([KC, T2], BF16)
        ysqk = self.key("ysq")
        y2_ring = self.ring("y2", 2, [KC, T2], BF16)
        mean = self.alloc([T2], F32)
        meank = self.key("mean")
        rstd = self.alloc([T2], F32)
        rstdk = self.key("rstd")
        t1_ring = self.ring("t1", 2, [T2], F32)
        s_ring = self.ring("s", 2, [T2], F32)
        cnt1 = 0
        cnt2 = 0
        dg_d = self.dram("dg_d%d" % li, [KC, 128, CONV_W, 128], BF16)
        for fc in range(KC):
            dg, dgk = dg_ring[fc % 2]
            for t in range(CONV_W):
                self.ts("dve", dg[:, t, :], self.ident_b, cv[:, fc, t:t + 1], None, ALU.mult, None, r=[cvk], w=[dgk])
            self.dma(dg_d[fc], dg, r=[dgk], w=["dg_d%d" % fc], q="pool")
        for b in range(NB):
            self.stage1(li, b, ctx_out, hT, hk)
            for fc in range(KC):
                wp, wpk = wp_ring[fc % 2]
                for ab in range(2):
                    self.dma(wp[:, :, ab, :], w_in_v[:, :, ab * D + fc * 128: ab * D + (fc + 1) * 128],
                             r=["cwi%d" % j], w=[wpk])
                gl, glk = gl_ring[fc % 2]
                col = 0
                for (is_ctx, tok0, n) in seqs:
                    for (o, m) in _tiles(n, 512):
                        bA = (cnt1 % 2) * 2
                        bB = bA + 1
                        sg, sgk = sig_ring[cnt1 % 2]
                        cnt1 += 1
                        for ab, bank in ((0, bA), (1, bB)):
                            for kc in range(KC):
                                self.mm(self.ps[bank][:, 0:m], wp[:, kc, ab, :], hT[:, kc, tok0 + o: tok0 + o + m],
                                        kc == 0, kc == KC - 1, r=[wpk, hk], w=[self.psk[bank]])
                        self.act(sg[:, 0:m], self.ps[bB][:, 0:m], AF.Sigmoid, r=[self.psk[bB]], w=[sgk])
                        c0 = col + HALO + o
                        self.tt("dve", gl[:, c0:c0 + m], self.ps[bA][:, 0:m], sg[:, 0:m], ALU.mult,
                                r=[self.psk[bA], sgk], w=[glk])
                    col += n + 2 * HALO
                self.dma(glu_d[fc], gl, r=[glk], w=["glu_d"], q="pool")
            col = 0
            for (is_ctx, tok0, n) in seqs:
                for (o, m) in _tiles(n, T2):
                    glt, gltk = glt_ring[cnt2 % 2]
                    y2, y2k = y2_ring[cnt2 % 2]
                    cnt2 += 1
                    c0 = col + o
                    self.dma(glt[:, :, 0:m + 2 * HALO], glu_d[:, :, c0:c0 + m + 2 * HALO].rearrange("f p c -> p f c"),
                             r=["glu_d"], w=[gltk])
                    for fc in range(KC):
                        dg, dgk = dg_ring[fc % 2]
                        self.dma(dg, dg_d[fc], r=["dg_d%d" % fc], w=[dgk])
                        bank = fc % 2
                        for t in range(CONV_W):
                            self.mm(self.ps[bank][:, 0:m], dg[:, t, :], glt[:, fc, t:t + m], t == 0, t == CONV_W - 1,
                                    r=[dgk, gltk], w=[self.psk[bank]])
                        self.act(yb[:, fc, 0:m], self.ps[bank][:, 0:m], AF.Identity, r=[self.psk[bank], cvk], w=[ybk],
                                 bias=cv[:, fc, CONV_W:CONV_W + 1])
                        self.act(ysq[:, fc, 0:m], self.ps[bank][:, 0:m], AF.Square, r=[self.psk[bank], cvk], w=[ysqk],
                                 bias=cv[:, fc, CONV_W:CONV_W + 1])
                    for fc in range(KC):
                        self.mm(self.ps[2][:, 0:m], self.ones_d, yb[:, fc, 0:m], fc == 0, fc == KC - 1,
                                r=[ybk], w=[self.psk[2]])
                    for fc in range(KC):
                        self.mm(self.ps[3][:, 0:m], self.ones_d, ysq[:, fc, 0:m], fc == 0, fc == KC - 1,
                                r=[ysqk], w=[self.psk[3]])
                    self.p.add("act", lambda e, m=m: e.copy(out=mean[:, 0:m], in_=self.ps[2][:, 0:m]),
                               r=[self.psk[2]], w=[meank])
                    self.tt("dve", rstd[:, 0:m], mean[:, 0:m], mean[:, 0:m], ALU.mult, r=[meank], w=[rstdk])
                    self.tt("dve", rstd[:, 0:m], self.ps[3][:, 0:m], rstd[:, 0:m], ALU.subtract,
                            r=[self.psk[3], rstdk], w=[rstdk])
                    self.rsqrt(rstd[:, 0:m], rstd[:, 0:m], LN_EPS, 1.0, r=[rstdk], w=[rstdk])
                    for fc in range(KC):
                        t1, t1k = t1_ring[fc % 2]
                        s_, sk = s_ring[fc % 2]
                        sgt, sgtk = sgt_ring[fc % 2]
                        bank = 6 + fc % 2
                        for kc in range(KC):
                            self.mm(self.ps[bank][:, 0:m], w_g[:, kc, fc * 128:(fc + 1) * 128],
                                    hT[:, kc, tok0 + o: tok0 + o + m], kc == 0, kc == KC - 1,
                                    r=[wgk, hk], w=[self.psk[bank]])
                        self.act(sgt[:, 0:m], self.ps[bank][:, 0:m], AF.Silu, r=[self.psk[bank]], w=[sgtk])
                        self.tt("dve", t1[:, 0:m], yb[:, fc, 0:m], mean[:, 0:m], ALU.subtract, r=[ybk, meank], w=[t1k])
                        self.tt("dve", t1[:, 0:m], t1[:, 0:m], rstd[:, 0:m], ALU.mult, r=[t1k, rstdk], w=[t1k])
                        self.act(s_[:, 0:m], t1[:, 0:m], AF.Silu, r=[t1k, cvk], w=[sk],
                                 scale=cv[:, fc, CONV_W + 1:CONV_W + 2], bias=cv[:, fc, CONV_W + 2:CONV_W + 3])
                        self.tt("pool", y2[:, fc, 0:m], s_[:, 0:m], sgt[:, 0:m], ALU.mult, r=[sk, sgtk], w=[y2k])
                    for (oo, mm_) in _tiles(m, 128):
                        self.epilogue_tile(li, b, is_ctx, o + oo,
                                           lambda kc, y2=y2, oo=oo: y2[:, kc, oo:oo + 128],
                                           KC, w_out, wok, [y2k], last)
                col += n + 2 * HALO

    def bc_free(self, ap2, pattern):
        a = list(ap2.ap)
        if pattern == "k":
            return bass.AP(ap2.tensor, ap2.offset, [list(a[0]), [0, 64], list(a[1])])
        return bass.AP(ap2.tensor, ap2.offset, [list(a[0]), list(a[1]), [0, 64]])

    def rwkv_layer(self, li, j, ctx_out, midx, last):
        NB, TL, TC, TT = self.NB, self.TL, self.TC, self.TT
        i = self.i
        names = ["r", "v", "kk", "sg", "dec0", "dec1", "kd0", "kd1", "bn0", "bn1", "o0", "o1"]
        A = {n: self.dram("rk_%s_%d" % (n, li), [NB, TT, D]) for n in names}
        if RWKV_CHUNKED:
            NCH = TT // 64
            F = {}
            for d in range(2):
                for nm in ("KT", "RT", "KS", "NB"):
                    F[nm, d] = self.dram("rf_%s%d_%d" % (nm, d, li), [NB, 16, 64, TT], BF16)
                for nm in ("KSt", "NBt"):
                    F[nm, d] = self.dram("rf_%s%d_%d" % (nm, d, li), [NB, TT, D], BF16)
                F["pc", d] = self.dram("rf_pc%d_%d" % (d, li), [NB, NCH, 64, 16])
            F["Vt"] = self.dram("rf_Vt_%d" % li, [NB, TT, D], BF16)
            self.rwkv_phase_a(li, j, midx, A, F)
            if globals().get("RWKV_STOP", 9) <= 1:
                return
            self.p.fence()
            self.top = self.persist_top
            self.rwkv_scan2(li, A, F)
            if globals().get("RWKV_STOP", 9) <= 2:
                return
        else:
            self.rwkv_phase_a(li, j, midx, A, None)
            self.p.fence()
            self.top = self.persist_top
            self.rwkv_scan(li, A)
        self.p.fence()
        self.top = self.persist_top
        self.rwkv_phase_c(li, j, ctx_out, midx, last, A)

    def rwkv_phase_a(self, li, j, midx, A, F):
        NB, TL, TC, TT = self.NB, self.TL, self.TC, self.TT
        i = self.i
        self.layer_vectors(midx)
        muT, muk = self.rows_to_cols([i["rwkv_mu"][j, n, :] for n in range(6)], "mu")
        W = {}
        for nm in ("w_r", "w_k", "w_v", "w_g"):
            t = self.alloc([KC, D], BF16)
            k = self.key(nm)
            self.dma(t, self.wb["r" + nm].rearrange("(kc p) n -> p kc n", p=128), w=[k])
            W[nm] = (t, k)
        w1cat = self.alloc([KC, 128], BF16)
        a1cat = self.alloc([KC, 128], BF16)
        w2cat = self.alloc([D], BF16)
        a2cat = self.alloc([D], BF16)
        lk = self.key("lora")
        for d in range(2):
            self.dma(w1cat[:, :, d * 64:(d + 1) * 64], self.wb["rw1_%d" % d].rearrange("(kc p) n -> p kc n", p=128), w=[lk])
            self.dma(a1cat[:, :, d * 64:(d + 1) * 64], self.wb["ra1_%d" % d].rearrange("(kc p) n -> p kc n", p=128), w=[lk])
            self.dma(w2cat[d * 64:(d + 1) * 64, :], self.wb["rw2_%d" % d], w=[lk])
            self.dma(a2cat[d * 64:(d + 1) * 64, :], self.wb["ra2_%d" % d], w=[lk])
        pk = self.key("params")
        kkb = self.alloc([D], F32)
        kab = self.alloc([D], F32)
        self.dma(kkb, self.bcast_row(i["rwkv_k_k"][j, :], D), w=[pk])
        self.dma(kab, self.bcast_row(i["rwkv_k_a"][j, :], D), w=[pk])
        w0b, a0b = [], []
        for d in range(2):
            t = self.alloc([D], F32)
            self.dma(t, self.bcast_row(i["rwkv_w0"][j, d, :], D), w=[pk])
            w0b.append(t)
            t = self.alloc([D], F32)
            self.dma(t, self.bcast_row(i["rwkv_a0"][j, d, :], D), w=[pk])
            a0b.append(t)
        HW = TT + 4
        hT = self.alloc([KC, HW], BF16)
        hk = self.key("hT")
        self.p.add("pool", lambda e: e.memset(hT, 0.0), w=[hk])
        T = [(self.alloc([D], F32), self.key("T%d" % n)) for n in range(4)]
        self.stage1_setup(share=T)
        self.coff, self.loff = 1, TC + 3
        xx = self.alloc([KC, 128], F32)
        xxk = self.key("xx")
        tl = self.alloc([KC, 128], F32)
        tlk = self.key("tl")
        lerp = [(self.alloc([KC, 128], BF16), self.key("lerp%d" % n)) for n in range(6)]
        thT = self.alloc([128], BF16)
        thk = self.key("thT")
        ahT = self.alloc([128], BF16)
        ahk = self.key("ahT")
        kraw = self.alloc([D], F32)
        krk = self.key("kraw")
        kk = self.alloc([D], F32)
        kkk = self.key("kk")
        ssq = self.alloc([16], F32)
        ssqk = self.key("ssq")
        bankc = [0]
        if F is not None:
            rt = self.alloc([D], F32)
            rtk = self.key("rt")
            tri = self.alloc([2, 128], F32)
            ind = self.alloc([2], F32)
            ck = self.key("rconst")
            for d in range(2):
                self.dma(tri[:, d, :], i["k_tri"][d], w=[ck])
            self.dma(ind, i["k_ind"], w=[ck])
            fm_ring = self.ring("fmt", 2, [16, 128], BF16)
            tm_ring = self.ring("tmt", 1, [D], BF16) * 2
            pct = self.alloc([32], F32)
            pctk = self.key("pct")
            fmc = [0]
            tmc = [0]

            def emit_fm(tile, tk, name, d, b, tok):
                fmt, fmk = fm_ring[fmc[0] % 2]
                fmc[0] += 1
                for q4 in range(4):
                    bank = nxt()
                    for jj in range(4):
                        h = q4 * 4 + jj
                        self.tr(self.ps[bank][0:64, jj * 128:(jj + 1) * 128], tile[:, h * 64:(h + 1) * 64],
                                r=[tk], w=[self.psk[bank]])
                    self.p.add("act", lambda e, bank=bank, q4=q4, fmt=fmt: e.copy(
                        out=fmt[0:64, q4 * 4:(q4 + 1) * 4, :].rearrange("p a b -> p (a b)"), in_=self.ps[bank][0:64, :]),
                        r=[self.psk[bank]], w=[fmk])
                self.dma(F[name, d][b, :, :, tok:tok + 128].rearrange("h f t -> f h t"), fmt[0:64], r=[fmk],
                         w=[self.key("fst")], q=RQ)

            def emit_tm(tile, tk, dst):
                tmt, tmk = tm_ring[tmc[0] % 2]
                tmc[0] += 1
                self.p.add("act", lambda e, tmt=tmt, tile=tile: e.copy(out=tmt, in_=tile), r=[tk], w=[tmk])
                self.dma(dst, tmt, r=[tmk], w=[self.key("tst")], q=RQ)

        def nxt():
            b_ = bankc[0] % 6
            bankc[0] += 1
            return b_

        def proj(n_, wname, half):
            wt, wk = W[wname]
            lp, lpk = lerp[n_]
            bank = nxt()
            for kc in range(KC):
                self.mm(self.ps[bank][:, :], lp[:, kc, :], wt[:, kc, half * 512:(half + 1) * 512], kc == 0, kc == KC - 1,
                        r=[lpk, wk], w=[self.psk[bank]])
            return bank

        def store(name, b, tok, tile, tk):
            self.dma(A[name][b, tok:tok + 128, :], tile, r=[tk], w=[self.key("st")], q=(RQ if F is not None else "pool"))

        for b in range(NB):
            self.stage1(li, b, True, hT, hk)
            for (col0, n, tokbase) in ((self.coff, TC, 0), (self.loff, TL, TC)):
                for t0 in range(0, n, 128):
                    c = col0 + t0
                    tok = tokbase + t0
                    hc_ = hT[:, :, c:c + 128]
                    self.tt("pool", xx, hT[:, :, c - 1:c + 127], hT[:, :, c + 1:c + 129], ALU.add, r=[hk], w=[xxk])
                    self.stt(xx, xx, 0.5, hc_, ALU.mult, ALU.subtract, r=[xxk, hk], w=[xxk])
                    for n_ in range(6):
                        lp, lpk = lerp[n_]
                        for kc in range(KC):
                            self.stt(lp[:, kc, :], xx[:, kc, :], muT[:, kc, n_:n_ + 1], hT[:, kc, c:c + 128],
                                     ALU.mult, ALU.add, r=[xxk, muk, hk], w=[lpk])
                    for half in range(2):
                        hs = slice(half * 512, (half + 1) * 512)
                        bk = proj(2, "w_k", half)
                        self.p.add("act", lambda e, bk=bk, hs=hs: e.copy(out=kraw[:, hs], in_=self.ps[bk][:, :]),
                                   r=[self.psk[bk]], w=[krk])
                    T1, T1k = T[0]
                    T2, T2k = T[1]
                    T3, T3k = T[2]
                    T4, T4k = T[3]
                    self.tt("dve", T1, kraw, kkb, ALU.mult, r=[krk, pk], w=[T1k])
                    self.tt("pool", T2, T1, T1, ALU.mult, r=[T1k], w=[T2k])
                    self.p.add("dve", lambda e: e.tensor_reduce(out=ssq, in_=T2.rearrange("p (h f) -> p h f", f=64),
                                                                axis=AX.X, op=ALU.add), r=[T2k], w=[ssqk])
                    self.rsqrt(ssq, ssq, 1e-12, 1.0, r=[ssqk], w=[ssqk])
                    self.tt("pool", kk.rearrange("p (h f) -> p h f", f=64), T1.rearrange("p (h f) -> p h f", f=64),
                            self.bc_free(ssq, "v"), ALU.mult, r=[T1k, ssqk], w=[kkk])
                    if F is None:
                        store("kk", b, tok, kk, kkk)
                    else:
                        for half in range(2):
                            hs = slice(half * 512, (half + 1) * 512)
                            bk = proj(0, "w_r", half)
                            self.p.add("act", lambda e, bk=bk, hs=hs: e.copy(out=rt[:, hs], in_=self.ps[bk][:, :]),
                                       r=[self.psk[bk]], w=[rtk])
                        store("r", b, tok, rt, rtk)
                    for kc in range(KC):
                        self.mm(self.ps[6][:, 0:128], w1cat[:, kc, :], lerp[1][0][:, kc, :], kc == 0, kc == KC - 1,
                                r=[lk, lerp[1][1]], w=[self.psk[6]])
                    self.act(thT, self.ps[6][:, 0:128], AF.Tanh, r=[self.psk[6]], w=[thk])
                    for kc in range(KC):
                        self.mm(self.ps[7][:, 0:128], a1cat[:, kc, :], lerp[4][0][:, kc, :], kc == 0, kc == KC - 1,
                                r=[lk, lerp[4][1]], w=[self.psk[7]])
                    self.p.add("act", lambda e: e.copy(out=ahT, in_=self.ps[7][:, 0:128]), r=[self.psk[7]], w=[ahk])
                    for d in range(2):
                        ds = slice(d * 64, (d + 1) * 64)
                        for half in range(2):
                            hs = slice(half * 512, (half + 1) * 512)
                            bank = nxt()
                            self.mm(self.ps[bank][:, :], ahT[ds, :], a2cat[ds, hs], True, True, r=[ahk, lk], w=[self.psk[bank]])
                            self.tt("dve", T1[:, hs], self.ps[bank][:, :], a0b[d][:, hs], ALU.add,
                                    r=[self.psk[bank], pk], w=[T1k])
                        self.act(T1, T1, AF.Sigmoid, r=[T1k], w=[T1k])
                        self.stt(T2, T1, -1.0, kab, ALU.add, ALU.mult, r=[T1k, pk], w=[T2k])
                        self.stt(T2, T2, 1.0, kraw, ALU.add, ALU.mult, r=[T2k, krk], w=[T2k])
                        store("kd%d" % d, b, tok, T2, T2k)
                        self.stt(T3, kk, -1.0, T1, ALU.mult, ALU.mult, r=[kkk, T1k], w=[T3k])
                        if F is None:
                            store("bn%d" % d, b, tok, T3, T3k)
                        for half in range(2):
                            hs = slice(half * 512, (half + 1) * 512)
                            bank = nxt()
                            self.mm(self.ps[bank][:, :], thT[ds, :], w2cat[ds, hs], True, True, r=[thk, lk], w=[self.psk[bank]])
                            self.tt("dve", T4[:, hs], self.ps[bank][:, :], w0b[d][:, hs], ALU.add,
                                    r=[self.psk[bank], pk], w=[T4k])
                        self.act(T4, T4, AF.Sigmoid, r=[T4k], w=[T4k])
                        if F is None:
                            self.act(T4, T4, AF.Exp, r=[T4k], w=[T4k], scale=-math.exp(-0.5))
                            store("dec%d" % d, b, tok, T4, T4k)
                            continue
                        self.ts("dve", T4, T4, -math.exp(-0.5), None, ALU.mult, None, r=[T4k], w=[T4k])
                        for h in range(16):
                            self.mm(self.ps[7][0:64, h:h + 17:16], T4[:, h * 64:(h + 1) * 64], ind, True, True,
                                    r=[T4k, ck], w=[self.psk[7]])
                        self.act(pct[0:64, :], self.ps[7][0:64, 0:32], AF.Exp, r=[self.psk[7]], w=[pctk])
                        for jj in range(2):
                            self.dma(F["pc", d][b, tok // 64 + jj], pct[0:64, jj * 16:(jj + 1) * 16], r=[pctk],
                                     w=[self.key("pst")], q=RQ)
                        cb = []
                        for half in range(2):
                            hs = slice(half * 512, (half + 1) * 512)
                            bank = 6 + half
                            cb.append(bank)
                            self.mm(self.ps[bank][:, :], tri[:, d, :], T4[:, hs], True, True, r=[T4k, ck], w=[self.psk[bank]])
                        for half in range(2):
                            hs = slice(half * 512, (half + 1) * 512)
                            self.tt("dve", T4[:, hs], self.ps[cb[half]][:, :], T4[:, hs], ALU.subtract,
                                    r=[self.psk[cb[half]], T4k], w=[T4k])
                        self.act(T4, T4, AF.Exp, r=[T4k], w=[T4k])
                        self.tt("dve", T1, kk, T4, ALU.mult, r=[kkk, T4k], w=[T1k])
                        emit_fm(T1, T1k, "KT", d, b, tok)
                        for half in range(2):
                            hs = slice(half * 512, (half + 1) * 512)
                            self.act(T4[:, hs], self.ps[cb[half]][:, :], AF.Exp, r=[self.psk[cb[half]], T1k], w=[T4k],
                                     scale=-1.0)
                        self.tt("dve", T2, T2, T4, ALU.mult, r=[T2k, T4k], w=[T2k])
                        emit_fm(T2, T2k, "KS", d, b, tok)
                        emit_tm(T2, T2k, F["KSt", d][b, tok:tok + 128, :])
                        self.tt("dve", T3, T3, T4, ALU.mult, r=[T3k, T4k], w=[T3k])
                        emit_fm(T3, T3k, "NB", d, b, tok)
                        emit_tm(T3, T3k, F["NBt", d][b, tok:tok + 128, :])
                        for half in range(2):
                            hs = slice(half * 512, (half + 1) * 512)
                            self.act(T4[:, hs], self.ps[cb[half]][:, :], AF.Exp, r=[self.psk[cb[half]], T2k, T3k], w=[T4k])
                        self.tt("dve", T1, rt, T4, ALU.mult, r=[rtk, T4k], w=[T1k])
                        emit_fm(T1, T1k, "RT", d, b, tok)
                    plist = ((3, "w_v", "v", T2, T2k, AF.Identity), (5, "w_g", "sg", T3, T3k, AF.Silu))
                    if F is None:
                        plist = ((0, "w_r", "r", T1, T1k, AF.Identity),) + plist
                    for (n_, wname, name, tile, tk, fn) in plist:
                        for half in range(2):
                            hs = slice(half * 512, (half + 1) * 512)
                            bk = proj(n_, wname, half)
                            self.act(tile[:, hs], self.ps[bk][:, :], fn, r=[self.psk[bk]], w=[tk])
                        store(name, b, tok, tile, tk)
                        if F is not None and name == "v":
                            emit_tm(tile, tk, F["Vt"][b, tok:tok + 128, :])

    def rwkv_scan(self, li, A):
        NB, TL, TC, TT = self.NB, self.TL, self.TC, self.TT
        P = 2 * NB * 16
        SB = 32
        S = self.alloc([64, 64], F32)
        Sk = self.key("S")
        tmp = self.alloc([64, 64], F32)
        tmpk = self.key("tmp")
        vk_ring = self.ring("vk", 2, [64, 64], F32)
        sa = self.alloc([64], F32)
        sak = self.key("sa")
        arrs = ["kk", "bn", "dec", "kd", "v", "r"]
        blk = [{a: self.alloc([SB, 64], F32) for a in arrs} for _ in range(2)]
        oblk = self.ring("oblk", 2, [SB, 64], F32)
        self.p.add("dve", lambda e: e.memset(S[0:P], 0.0), w=[Sk])
        cb = 0
        for (soff, n) in ((0, TC), (TC, TL)):
            for i0 in range(0, n, SB):
                slot = cb % 2
                cb += 1
                bkeys = []
                for d in range(2):
                    for b in range(NB):
                        p0 = d * NB * 16 + b * 16
                        if d == 0:
                            tok0, step = soff + i0, D
                        else:
                            tok0, step = soff + n - 1 - i0, -D
                        for a in arrs:
                            name = a + str(d) if a in ("bn", "dec", "kd") else a
                            src_t = A[name]
                            src = bass.AP(src_t.tensor, src_t[b, tok0, :].offset, [[64, 16], [step, SB], [1, 64]])
                            k = "blk_%d_%d_%d_%s" % (slot, d, b, a)
                            self.dma(blk[slot][a][p0:p0 + 16, :, :], src, w=[k])
                            bkeys.append(k)
                ob, obk = oblk[slot]
                for s_ in range(SB):
                    g = lambda a: blk[slot][a][0:P, s_, :]
                    vkt, vkk = vk_ring[s_ % 2]
                    Sv = S[0:P]
                    tv = tmp[0:P]
                    self.tt("pool", vkt[0:P], self.bc_free(g("v"), "v"), self.bc_free(g("kd"), "k"), ALU.mult,
                            r=bkeys, w=[vkk])
                    self.tt("dve", tv, Sv, self.bc_free(g("kk"), "k"), ALU.mult, r=[Sk] + bkeys, w=[tmpk])
                    self.p.add("dve", lambda e, tv=tv: e.tensor_reduce(out=sa[0:P], in_=tv, axis=AX.X, op=ALU.add),
                               r=[tmpk], w=[sak])
                    e3 = "pool" if SCAN_POOL >= 1 else "dve"
                    e7 = "pool" if SCAN_POOL >= 2 else "dve"
                    self.tt(e3, Sv, Sv, self.bc_free(g("dec"), "k"), ALU.mult, r=[Sk] + bkeys, w=[Sk])
                    if SCAN_POOL >= 2:
                        self.tt(e7, Sv, Sv, vkt[0:P], ALU.add, r=[Sk, vkk], w=[Sk])
                    self.tt("dve", tv, self.bc_free(sa[0:P], "v"), self.bc_free(g("bn"), "k"), ALU.mult,
                            r=[sak] + bkeys, w=[tmpk])
                    self.tt("dve", Sv, Sv, tv, ALU.add, r=[Sk, tmpk], w=[Sk])
                    if SCAN_POOL < 2:
                        self.tt(e7, Sv, Sv, vkt[0:P], ALU.add, r=[Sk, vkk], w=[Sk])
                    self.tt("dve", tv, Sv, self.bc_free(g("r"), "k"), ALU.mult, r=[Sk] + bkeys, w=[tmpk])
                    self.p.add("dve", lambda e, tv=tv, ob=ob, s_=s_: e.tensor_reduce(out=ob[0:P, s_, :], in_=tv, axis=AX.X, op=ALU.add),
                               r=[tmpk], w=[obk])
                for d in range(2):
                    for b in range(NB):
                        p0 = d * NB * 16 + b * 16
                        if d == 0:
                            tok0, step = soff + i0, D
                        else:
                            tok0, step = soff + n - 1 - i0, -D
                        dst_t = A["o%d" % d]
                        dst = bass.AP(dst_t.tensor, dst_t[b, tok0, :].offset, [[64, 16], [step, SB], [1, 64]])
                        self.dma(dst, ob[p0:p0 + 16, :, :], r=[obk], w=[self.key("ost")], q="pool")

    def rwkv_scan2(self, li, A, F):
        NB, TL, TC, TT = self.NB, self.TL, self.TC, self.TT
        i = self.i
        C = 64
        masks = self.alloc([5, 512], F32)
        mk = self.key("masks")
        self.dma(masks[0:64], i["k_masks"].rearrange("m p c -> p m c"), w=[mk])
        US, LS, UI, LI, EYE = (masks[0:64, m, :] for m in range(5))
        units = [(d, b, g) for d in range(2) for b in range(NB) for g in range(2)]
        UB = 4
        fm_names = ("KT", "RT", "KS", "NB")
        tm_names = ("KSt", "NBt", "Vt")

        def t512(dt):
            return self.alloc([512], dt)[0:64]

        slots = []
        for s_ in range(UB):
            sl = {"ST": t512(F32), "STk": self.key("ST"), "STb": t512(BF16), "STbk": self.key("STb")}
            sl["op"] = []
            for par in range(2):
                o = {}
                for nm in fm_names:
                    o[nm] = (self.alloc([8, 64], BF16)[0:64], self.key(nm))
                for nm in tm_names:
                    o[nm] = (t512(BF16), self.key(nm))
                o["pc"] = (self.alloc([8], F32)[0:64], self.key("pc"))
                sl["op"].append(o)
            for nm in ("X", "XT"):
                sl[nm] = [(t512(F32), self.key(nm)) for _ in range(2)]
            sl["Tb"] = [(t512(BF16), self.key("Tb"))] * 2
            sl["Tf"] = (t512(F32), self.key("Tf"))
            for nm in ("Akk", "Akr", "nAbr", "RHST", "UT"):
                sl[nm] = (t512(BF16), self.key(nm))
            sl["ot"] = [(t512(F32), self.key("ot")) for _ in range(2)]
            slots.append(sl)
        bankc = [0]

        def nxtb():
            b_ = bankc[0] % 8
            bankc[0] += 1
            return b_

        def hc(ap, h):
            return ap[:, h * 64:(h + 1) * 64]

        def mmg(terms, rkeys):
            bank = nxtb()
            n = len(terms)
            for h in range(8):
                for ti, (lf, rf) in enumerate(terms):
                    self.mm(self.ps[bank][0:64, h * 64:(h + 1) * 64], lf(h), rf(h), ti == 0, ti == n - 1,
                            r=rkeys, w=[self.psk[bank]])
            return bank

        def pcopy(eng, out, bank, rk, wk):
            if eng == "act":
                self.p.add("act", lambda e: e.copy(out=out, in_=self.ps[bank][0:64, :]), r=[self.psk[bank]] + rk, w=wk)
            else:
                self.p.add("dve", lambda e: e.tensor_copy(out=out, in_=self.ps[bank][0:64, :]), r=[self.psk[bank]] + rk, w=wk)

        seqs = ((0, TC), (TC, TL))
        for batch in range(0, len(units), UB):
            ub = units[batch:batch + UB]
            chunks = []
            for (d, b, g) in ub:
                lst = []
                for (soff, n) in seqs:
                    toks = list(range(soff, soff + n, C))
                    if d == 1:
                        toks = toks[::-1]
                    lst += toks
                chunks.append(lst)
            for s_, _u in enumerate(ub):
                sl = slots[s_]
                self.p.add("dve", lambda e, sl=sl: e.memset(sl["ST"], 0.0), w=[sl["STk"]])
                self.p.add("dve", lambda e, sl=sl: e.memset(sl["STb"], 0.0), w=[sl["STbk"]])
            NCH = len(chunks[0])
            for ci in range(NCH):
                par = ci % 2
                cur = []
                for s_, (d, b, g) in enumerate(ub):
                    sl = slots[s_]
                    o = sl["op"][par]
                    tok0 = chunks[s_][ci]
                    for nm in fm_names:
                        t, k = o[nm]
                        self.dma(t, F[nm, d][b, g * 8:(g + 1) * 8, :, tok0:tok0 + C].rearrange("h f t -> f h t"), w=[k])
                    for nm in tm_names:
                        t, k = o[nm]
                        src = F["Vt"] if nm == "Vt" else F[nm, d]
                        self.dma(t, src[b, tok0:tok0 + C, g * 512:(g + 1) * 512], w=[k])
                    t, k = o["pc"]
                    self.dma(t, F["pc", d][b, tok0 // C, :, g * 8:(g + 1) * 8], w=[k])
                    mX, mXT, mS, mI = (US, LS, US, UI) if d == 0 else (LS, US, LS, LI)
                    cur.append((sl, o, d, b, g, tok0, mX, mXT, mS, mI))
                for (sl, o, d, b, g, tok0, mX, mXT, mS, mI) in cur:
                    NBf, NBk = o["NB"]
                    KT, KTk = o["KT"]
                    bk = mmg([(lambda h: NBf[:, h, :], lambda h: KT[:, h, :])], [NBk, KTk])
                    X0, X0k = sl["X"][0]
                    self.tt("dve", X0, self.ps[bk][0:64, :], mX, ALU.mult, r=[self.psk[bk], mk], w=[X0k])
                    bk = mmg([(lambda h: KT[:, h, :], lambda h: NBf[:, h, :])], [NBk, KTk])
                    XT0, XT0k = sl["XT"][0]
                    self.tt("dve", XT0, self.ps[bk][0:64, :], mXT, ALU.mult, r=[self.psk[bk], mk], w=[XT0k])
                for (sl, o, d, b, g, tok0, mX, mXT, mS, mI) in cur:
                    X0, X0k = sl["X"][0]
                    Tf, Tfk = sl["Tf"]
                    self.tt("dve", Tf, X0, EYE, ALU.add, r=[X0k, mk], w=[Tfk])
                for j in range(1, 6):
                    pj, cj = (j - 1) % 2, j % 2
                    for (sl, o, d, b, g, tok0, mX, mXT, mS, mI) in cur:
                        Xp, Xpk = sl["X"][pj]
                        XTp, XTpk = sl["XT"][pj]
                        if j < 5:
                            bk = mmg([(lambda h: hc(XTp, h), lambda h: hc(Xp, h))], [Xpk, XTpk])
                            pcopy("act", sl["X"][cj][0], bk, [], [sl["X"][cj][1]])
                        bk = mmg([(lambda h: hc(Xp, h), lambda h: hc(XTp, h))], [Xpk, XTpk])
                        pcopy("dve", sl["XT"][cj][0], bk, [], [sl["XT"][cj][1]])
                    for (sl, o, d, b, g, tok0, mX, mXT, mS, mI) in cur:
                        XTc, XTck = sl["XT"][cj]
                        Tf, Tfk = sl["Tf"]
                        bk = mmg([(lambda h: hc(XTc, h), lambda h: hc(Tf, h))], [XTck, Tfk])
                        self.tt("dve", Tf, Tf, self.ps[bk][0:64, :], ALU.add, r=[Tfk, self.psk[bk]], w=[Tfk])
                        if j == 5:
                            Tbc, Tbck = sl["Tb"][0]
                            self.p.add("act", lambda e, Tbc=Tbc, Tf=Tf: e.copy(out=Tbc, in_=Tf), r=[Tfk], w=[Tbck])
                for (sl, o, d, b, g, tok0, mX, mXT, mS, mI) in cur:
                    KS, KSk = o["KS"]
                    KT, KTk = o["KT"]
                    RT, RTk = o["RT"]
                    NBf, NBk = o["NB"]
                    for (nm, lf, lk_, rf, rk_, msk) in (("Akk", KS, KSk, KT, KTk, mS), ("Akr", KS, KSk, RT, RTk, mI),
                                                         ("nAbr", NBf, NBk, RT, RTk, mI)):
                        bk = mmg([(lambda h, lf=lf: lf[:, h, :], lambda h, rf=rf: rf[:, h, :])], [lk_, rk_])
                        self.tt("dve", sl[nm][0], self.ps[bk][0:64, :], msk, ALU.mult, r=[self.psk[bk], mk], w=[sl[nm][1]])
                Tfin = 5 % 2
                for (sl, o, d, b, g, tok0, mX, mXT, mS, mI) in cur:
                    KT, KTk = o["KT"]
                    Vt, Vtk = o["Vt"]
                    Akk, Akkk = sl["Akk"]
                    STb, STbk = sl["STb"], sl["STbk"]
                    bk = mmg([(lambda h: KT[:, h, :], lambda h: hc(STb, h)), (lambda h: hc(Akk, h), lambda h: hc(Vt, h))],
                             [KTk, STbk, Akkk, Vtk])
                    pcopy("act", sl["RHST"][0], bk, [], [sl["RHST"][1]])
                for (sl, o, d, b, g, tok0, mX, mXT, mS, mI) in cur:
                    Tb, Tbk = sl["Tb"][Tfin]
                    RH, RHk = sl["RHST"]
                    bk = mmg([(lambda h: hc(Tb, h), lambda h: hc(RH, h))], [Tbk, RHk])
                    pcopy("act", sl["UT"][0], bk, [], [sl["UT"][1]])
                for (sl, o, d, b, g, tok0, mX, mXT, mS, mI) in cur:
                    RT, RTk = o["RT"]
                    Vt, Vtk = o["Vt"]
                    Akr, Akrk = sl["Akr"]
                    nAbr, nAbrk = sl["nAbr"]
                    UT, UTk = sl["UT"]
                    STb, STbk = sl["STb"], sl["STbk"]
                    bk = mmg([(lambda h: RT[:, h, :], lambda h: hc(STb, h)), (lambda h: hc(Akr, h), lambda h: hc(Vt, h)),
                              (lambda h: hc(nAbr, h), lambda h: hc(UT, h))], [RTk, STbk, Akrk, Vtk, nAbrk, UTk])
                    ot, otk = sl["ot"][ci % 2]
                    pcopy("act", ot, bk, [], [otk])
                    self.dma(A["o%d" % d][b, tok0:tok0 + C, g * 512:(g + 1) * 512], ot, r=[otk], w=[self.key("ost")], q=RQ)
                if self.dbg and batch == 0 and ci == 0:
                    sl = cur[0][0]
                    for nm, (t_, k_) in (("X0f", sl["Tf"]), ("Tb", sl["Tb"][Tfin]), ("Akk", sl["Akk"]), ("Akr", sl["Akr"]),
                                         ("nAbr", sl["nAbr"]), ("RHST", sl["RHST"]), ("UT", sl["UT"])):
                        dd = self.nc.dram_tensor("dbg_" + nm, [64, 512], t_.dtype, kind="ExternalOutput").ap()
                        self.dma(dd, t_, r=[k_], w=[self.key("dbg")], q="pool")
                for (sl, o, d, b, g, tok0, mX, mXT, mS, mI) in cur:
                    KSt, KStk = o["KSt"]
                    NBt, NBtk = o["NBt"]
                    Vt, Vtk = o["Vt"]
                    UT, UTk = sl["UT"]
                    pc, pck = o["pc"]
                    ST, STk = sl["ST"], sl["STk"]
                    STb, STbk = sl["STb"], sl["STbk"]
                    bk = mmg([(lambda h: hc(KSt, h), lambda h: hc(Vt, h)), (lambda h: hc(NBt, h), lambda h: hc(UT, h))],
                             [KStk, Vtk, NBtk, UTk])
                    self.tt("dve", ST, ST, self.ps[bk][0:64, :], ALU.add, r=[STk, self.psk[bk]], w=[STk])
                    st3 = ST.rearrange("p (h v) -> p h v", v=64)
                    self.tt("dve", st3, st3, self.bc_free(pc, "v"), ALU.mult, r=[STk, pck], w=[STk])
                    self.p.add("act", lambda e, STb=STb, ST=ST: e.copy(out=STb, in_=ST), r=[STk], w=[STbk])

    def rwkv_phase_c(self, li, j, ctx_out, midx, last, A):
        NB, TL, TC, TT = self.NB, self.TL, self.TC, self.TT
        i = self.i
        self.layer_vectors(midx)
        w_o = self.alloc([KC, D], BF16)
        wok = self.key("rwo")
        self.dma(w_o, self.wb["rw_o"].rearrange("(kc p) n -> p kc n", p=128), w=[wok])
        pk = self.key("cparams")
        lngb = self.alloc([D], F32)
        lnbb = self.alloc([D], F32)
        rkb = self.alloc([D], F32)
        self.dma(lngb, self.bcast_row(i["rwkv_ln_g"][j, :], D), w=[pk])
        self.dma(lnbb, self.bcast_row(i["rwkv_ln_b"][j, :], D), w=[pk])
        self.dma(rkb, self.bcast_row(i["rwkv_r_k"][j, :], D), w=[pk])
        self.stage1_setup()
        self.epi_setup()
        L = {n: self.ring("ld_" + n, 2, [D], F32) for n in ("o0", "o1", "r", "kd0", "kd1", "v", "sg")}
        o = self.alloc([D], F32)
        ok_ = self.key("o")
        sq = self.alloc([D], F32)
        sqk = self.key("sq")
        st = self.alloc([64], F32)
        stk = self.key("st")
        y2T_ring = self.ring("y2T", 2, [KC, 128], BF16)
        cnt = 0
        h3 = lambda t: t.rearrange("p (h f) -> p h f", f=64)
        for b in range(NB):
            segs = []
            if ctx_out:
                segs += [(True, t0, t0) for t0 in range(0, TC, 128)]
            segs += [(False, t0, TC + t0) for t0 in range(0, TL, 128)]
            for (is_ctx, t0, tok) in segs:
                ld = {}
                for n in L:
                    t, k = L[n][cnt % 2]
                    self.dma(t, A[n][b, tok:tok + 128, :], w=[k])
                    ld[n] = (t, k)
                y2T, y2Tk = y2T_ring[cnt % 2]
                cnt += 1
                self.tt("dve", o, ld["o0"][0], ld["o1"][0], ALU.add, r=[ld["o0"][1], ld["o1"][1]], w=[ok_])
                mean, ex2, bon, tmp16 = st[:, 0:16], st[:, 16:32], st[:, 32:48], st[:, 48:64]
                self.p.add("dve", lambda e: e.tensor_reduce(out=mean, in_=h3(o), axis=AX.X, op=ALU.add), r=[ok_], w=[stk])
                self.act(sq, o, AF.Square, r=[ok_], w=[sqk])
                self.p.add("dve", lambda e: e.tensor_reduce(out=ex2, in_=h3(sq), axis=AX.X, op=ALU.add), r=[sqk], w=[stk])
                self.ts("dve", mean, mean, 1.0 / 64, None, ALU.mult, None, r=[stk], w=[stk])
                self.tt("dve", tmp16, mean, mean, ALU.mult, r=[stk], w=[stk])
                self.stt(ex2, ex2, 1.0 / 64, tmp16, ALU.mult, ALU.subtract, r=[stk], w=[stk])
                self.rsqrt(ex2, ex2, GN_EPS, 1.0, r=[stk], w=[stk])
                self.tt("dve", h3(o), h3(o), self.bc_free(mean, "v"), ALU.subtract, r=[ok_, stk], w=[ok_])
                self.tt("pool", h3(o), h3(o), self.bc_free(ex2, "v"), ALU.mult, r=[ok_, stk], w=[ok_])
                self.tt("dve", o, o, lngb, ALU.mult, r=[ok_, pk], w=[ok_])
                self.tt("dve", o, o, lnbb, ALU.add, r=[ok_, pk], w=[ok_])
                self.tt("pool", sq, ld["kd0"][0], ld["kd1"][0], ALU.add, r=[ld["kd0"][1], ld["kd1"][1]], w=[sqk])
                self.tt("dve", sq, sq, ld["r"][0], ALU.mult, r=[sqk, ld["r"][1]], w=[sqk])
                self.tt("dve", sq, sq, rkb, ALU.mult, r=[sqk, pk], w=[sqk])
                self.p.add("dve", lambda e: e.tensor_reduce(out=bon, in_=h3(sq), axis=AX.X, op=ALU.add), r=[sqk], w=[stk])
                self.tt("pool", h3(sq), h3(ld["v"][0]), self.bc_free(bon, "v"), ALU.mult, r=[ld["v"][1], stk, sqk], w=[sqk])
                self.tt("dve", o, o, sq, ALU.add, r=[ok_, sqk], w=[ok_])
                self.tt("dve", o, o, ld["sg"][0], ALU.mult, r=[ok_, ld["sg"][1]], w=[ok_])
                for g in range(2):
                    bank = 6 + g
                    for q in range(4):
                        kc = g * 4 + q
                        self.tr(self.ps[bank][:, q * 128:(q + 1) * 128], o[:, kc * 128:(kc + 1) * 128], r=[ok_], w=[self.psk[bank]])
                    self.p.add("act", lambda e, bank=bank, g=g, y2T=y2T: e.copy(
                        out=y2T[:, g * 4:(g + 1) * 4, :].rearrange("p a b -> p (a b)"), in_=self.ps[bank][:, :]),
                        r=[self.psk[bank]], w=[y2Tk])
                self.epilogue_tile(li, b, is_ctx, t0, lambda kc, y2T=y2T: y2T[:, kc, :], KC, w_o, wok, [y2Tk], last)

    def attn_layer(self, li, j, ctx_out, midx, last):
        assert not ctx_out
        NB, TL, TC, TT = self.NB, self.TL, self.TC, self.TT
        i = self.i
        NKT = TT // 128
        SC = 1.0 / math.sqrt(128.0)
        self.layer_vectors(midx)
        qg = self.alloc([1], F32)
        kg = self.alloc([1], F32)
        SK = globals().get("ATTN_SKIP", "")
        if "q" not in SK:
            self.dma(qg, bass.AP(i["attn_q_g"].tensor, i["attn_q_g"][j, :].offset, [[1, 128], [1, 1]]), w=["qg"])
            self.dma(kg, bass.AP(i["attn_k_g"].tensor, i["attn_k_g"][j, :].offset, [[1, 128], [1, 1]]), w=["kg"])
        cosT = self.alloc([TL], F32)
        sinT = self.alloc([TL], F32)
        if "c" not in SK:
            self.dma(cosT, i["k_cos"], w=["cosT"])
            self.dma(sinT, i["k_sin"], w=["sinT"])
        w_in_v = self.wb["awi"].rearrange("(kc p) n -> p kc n", p=128)
        w_out = self.alloc([16, D], BF16)
        wok = self.key("awo")
        if "w" not in SK:
            self.dma(w_out, self.wb["awo"].rearrange("(h p) n -> p h n", p=128), r=["awo"], w=[wok])
        og_d = self.dram("og_d%d" % li, [16, 128, TL], BF16)
        hT = self.alloc([KC, TT], BF16)
        hk = self.key("hT")
        self.stage1_setup()
        self.epi_setup()
        wring = self.ring("awp", 2, [KC, 768], BF16)
        kT = self.alloc([TT], BF16)
        kTk = self.key("kT")
        vT = self.alloc([NKT, 128], BF16)
        vTk = self.key("vT")
        qT = self.alloc([2, TL], BF16)
        qTk = self.key("qT")
        sgT = self.alloc([2, TL], BF16)
        sgTk = self.key("sgT")
        sq_ring = self.ring("sq", 1, [512], BF16) * 2
        rs_ring = self.ring("rs", 1, [512], F32) * 2
        kn_ring = self.ring("kn", 1, [512], BF16) * 2
        t1_ring = self.ring("rt1", 1, [512], F32) * 2
        t2_ring = self.ring("rt2", 1, [512], F32) * 2
        pT_ring = self.ring("pT", 4, [512], BF16)
        rz = self.alloc([512], F32)
        rzk = self.key("rz")
        o1 = self.alloc([512], F32)
        o1k = self.key("o1")
        og_ring = self.ring("og", 2, [512], BF16)
        ogt_ring = self.ring("ogt", 2, [16, 128], BF16)
        cn = [0]

        def normrope(ps_ap, psk, n, gvec, gk, dest, destk, pos0):
            c = cn[0]
            cn[0] += 1
            sq, sqk = sq_ring[c % 2]
            rs, rsk = rs_ring[c % 2]
            self.act(sq[:, 0:n], ps_ap, AF.Square, r=[psk], w=[sqk])
            self.mm(self.ps[0][:, 0:n], self.ones_h, sq[:, 0:n], True, True, r=[sqk], w=[self.psk[0]])
            self.rsqrt(rs[:, 0:n], self.ps[0][:, 0:n], NORM_EPS, 1.0, r=[self.psk[0]], w=[rsk])
            if pos0 is None:
                self.stt(dest, ps_ap, gvec, rs[:, 0:n], ALU.mult, ALU.mult, r=[psk, rsk, gk], w=[destk])
                return
            kn, knk = kn_ring[c % 2]
            t1, t1k = t1_ring[c % 2]
            t2, t2k = t2_ring[c % 2]
            self.stt(kn[:, 0:n], ps_ap, gvec, rs[:, 0:n], ALU.mult, ALU.mult, r=[psk, rsk, gk], w=[knk])
            self.mm(self.ps[1][:, 0:n], self.perm_b, kn[:, 0:n], True, True, r=[knk], w=[self.psk[1]])
            self.tt("dve", t1[:, 0:n], kn[:, 0:n], cosT[:, pos0:pos0 + n], ALU.mult, r=[knk, "cosT"], w=[t1k])
            self.tt("dve", t2[:, 0:n], self.ps[1][:, 0:n], sinT[:, pos0:pos0 + n], ALU.mult,
                    r=[self.psk[1], "sinT"], w=[t2k])
            self.tt("pool", dest, t1[:, 0:n], t2[:, 0:n], ALU.add, r=[t1k, t2k], w=[destk])

        cw = 0
        ca = 0
        STOP = globals().get("ATTN_STOP", 99)
        if STOP <= 2:
            return
        for b in range(NB):
            self.stage1(li, b, True, hT, hk)
            if STOP <= 3:
                return
            for g in range(8 if STOP > 4 else 1):
                wt, wtk = wring[cw % 2]
                cw += 1
                self.dma(wt[:, :, 0:256], w_in_v[:, :, 256 * g:256 * g + 256], r=["awi"], w=[wtk])
                self.dma(wt[:, :, 256:384], w_in_v[:, :, 2048 + 128 * g:2048 + 128 * g + 128], r=["awi"], w=[wtk])
                self.dma(wt[:, :, 384:512], w_in_v[:, :, 3072 + 128 * g:3072 + 128 * g + 128], r=["awi"], w=[wtk])
                self.dma(wt[:, :, 512:768], w_in_v[:, :, 4096 + 256 * g:4096 + 256 * g + 256], r=["awi"], w=[wtk])
                ktiles = [(o, m, None) for (o, m) in _tiles(TC, 512)] + [(TC + o, m, o) for (o, m) in _tiles(TL, 512)]
                for (tok0, n, pos0) in ktiles:
                    for kc in range(KC):
                        self.mm(self.ps[7][:, 0:n], wt[:, kc, 256:384], hT[:, kc, tok0:tok0 + n], kc == 0, kc == KC - 1,
                                r=[wtk, hk], w=[self.psk[7]])
                    normrope(self.ps[7][:, 0:n], self.psk[7], n, kg, "kg", kT[:, tok0:tok0 + n], kTk, pos0)
                for k0 in range(0, NKT, 4):
                    nk = min(4, NKT - k0)
                    for q in range(nk):
                        kt = k0 + q
                        for kc in range(KC):
                            self.mm(self.ps[2][:, q * 128:(q + 1) * 128], hT[:, kc, kt * 128:(kt + 1) * 128],
                                    wt[:, kc, 384:512], kc == 0, kc == KC - 1, r=[wtk, hk], w=[self.psk[2]])
                    self.p.add("act", lambda e, k0=k0, nk=nk: e.copy(
                        out=vT[:, k0:k0 + nk, :].rearrange("p a b -> p (a b)"), in_=self.ps[2][:, 0:nk * 128]),
                        r=[self.psk[2]], w=[vTk])
                for hq in range(2):
                    for (o, m) in _tiles(TL, 512):
                        for kc in range(KC):
                            self.mm(self.ps[7][:, 0:m], wt[:, kc, hq * 128:(hq + 1) * 128], hT[:, kc, TC + o:TC + o + m],
                                    kc == 0, kc == KC - 1, r=[wtk, hk], w=[self.psk[7]])
                        normrope(self.ps[7][:, 0:m], self.psk[7], m, qg, "qg", qT[:, hq, o:o + m], qTk, o)
                        for kc in range(KC):
                            self.mm(self.ps[2][:, 0:m], wt[:, kc, 512 + hq * 128:512 + (hq + 1) * 128],
                                    hT[:, kc, TC + o:TC + o + m], kc == 0, kc == KC - 1, r=[wtk, hk], w=[self.psk[2]])
                        self.act(sgT[:, hq, o:o + m], self.ps[2][:, 0:m], AF.Silu, r=[self.psk[2]], w=[sgTk])
                if STOP <= 5:
                    continue
                for hq in range(2):
                    for (o, m) in _tiles(TL, 512):
                        bO = 4 + 2 * (ca % 2)
                        bZ = bO + 1
                        og, ogk = og_ring[ca % 2]
                        ca += 1

                        def S(kt):
                            bank = kt % 4
                            self.mm(self.ps[bank][:, 0:m], kT[:, kt * 128:(kt + 1) * 128], qT[:, hq, o:o + m], True, True,
                                    r=[kTk, qTk], w=[self.psk[bank]])
                        for k_ in range(min(3, NKT)):
                            S(k_)
                        for kt in range(NKT):
                            bank = kt % 4
                            pT, pTk = pT_ring[kt % 4]
                            self.act(pT[:, 0:m], self.ps[bank][:, 0:m], AF.Exp, r=[self.psk[bank]], w=[pTk], scale=SC)
                            self.mm(self.ps[bO][:, 0:m], vT[:, kt, :], pT[:, 0:m], kt == 0, kt == NKT - 1,
                                    r=[vTk, pTk], w=[self.psk[bO]])
                            self.mm(self.ps[bZ][:, 0:m], self.ones_1, pT[:, 0:m], kt == 0, kt == NKT - 1,
                                    r=[pTk], w=[self.psk[bZ]])
                            if kt + 3 < NKT:
                                S(kt + 3)
                        self.p.add("dve", lambda e, bZ=bZ, m=m: e.reciprocal(out=rz[:, 0:m], in_=self.ps[bZ][:, 0:m]),
                                   r=[self.psk[bZ]], w=[rzk])
                        self.tt("dve", o1[:, 0:m], self.ps[bO][:, 0:m], rz[:, 0:m], ALU.mult, r=[self.psk[bO], rzk], w=[o1k])
                        self.tt("pool", og[:, 0:m], o1[:, 0:m], sgT[:, hq, o:o + m], ALU.mult, r=[o1k, sgTk], w=[ogk])
                        self.dma(og_d[2 * g + hq, :, o:o + m], og[:, 0:m], r=[ogk], w=["og_d"], q="pool")
            if STOP <= 6:
                return
            ce = 0
            for t0 in range(0, TL, 128):
                ogt, ogtk = ogt_ring[ce % 2]
                ce += 1
                self.dma(ogt, og_d[:, :, t0:t0 + 128].rearrange("h p t -> p h t"), r=["og_d"], w=[ogtk])
                self.epilogue_tile(li, b, False, t0, lambda kc, ogt=ogt: ogt[:, kc, :], 16, w_out, wok, [ogtk], last)


FULL_LAYERS = [(0, 0, True, True, 0), (1, 0, True, True, 1), (2, 0, True, False, 2), (0, 1, False, False, 3)]


def host_consts(TL):
    ident = np.eye(128, dtype=np.float32)
    perm = np.zeros((128, 128), np.float32)
    for k in range(128):
        blk = k // 32
        if blk % 2 == 0:
            perm[k, k + 32] = 1.0
        else:
            perm[k, k - 32] = -1.0
    rows = TL // GRID_W
    t = np.arange(TL)
    row = (t // GRID_W).astype(np.float32)
    col = (t % GRID_W).astype(np.float32)
    inv = (1.0 / (10000.0 ** (np.arange(0, 64, 2, dtype=np.float32) / np.float32(64)))).astype(np.float32)
    cosT = np.zeros((128, TL), np.float32)
    sinT = np.zeros((128, TL), np.float32)
    for p in range(128):
        pos = row if p < 64 else col
        ang = (pos * inv[p % 32]).astype(np.float32)
        cosT[p] = np.cos(ang)
        sinT[p] = np.sin(ang)
    idx = np.arange(128)
    same = (idx[:, None] // 64) == (idx[None, :] // 64)
    tri = np.zeros((2, 128, 128), np.float32)
    tri[0] = (same & (idx[:, None] <= idx[None, :])).astype(np.float32)
    tri[1] = (same & (idx[:, None] >= idx[None, :])).astype(np.float32)
    ind = np.zeros((128, 2), np.float32)
    ind[:64, 0] = 1.0
    ind[64:, 1] = 1.0
    i64 = np.arange(64)
    us = (i64[:, None] < i64[None, :]).astype(np.float32)
    ls = (i64[:, None] > i64[None, :]).astype(np.float32)
    ui = (i64[:, None] <= i64[None, :]).astype(np.float32)
    li_ = (i64[:, None] >= i64[None, :]).astype(np.float32)
    eye = np.eye(64, dtype=np.float32)
    masks = np.stack([np.tile(m, (1, 8)) for m in (us, ls, ui, li_, eye)]).astype(np.float32)
    return {"k_ident": ident, "k_perm": perm, "k_cos": cosT, "k_sin": sinT,
            "k_tri": tri, "k_ind": ind, "k_masks": masks}


def make_in_maps(inputs, n_cores, NB, TL):
    consts = host_consts(TL)
    maps = []
    for cid in range(n_cores):
        m = {}
        for k, v in inputs.items():
            v = np.asarray(v)
            if k in ("x", "c", "ctx"):
                m[k] = np.ascontiguousarray(v[cid * NB:(cid + 1) * NB])
            elif k in ("c_ctx", "final_g"):
                m[k] = np.ascontiguousarray(v.reshape(1, -1))
            elif k == "rwkv_r_k":
                m[k] = np.ascontiguousarray(v.reshape(v.shape[0], -1))
            else:
                m[k] = np.ascontiguousarray(v)
        m.update(consts)
        maps.append(m)
    return maps


_CACHE = {}


def run(inputs, layers, n_cores=8, trace=False, dbg=False):
    x = np.asarray(inputs["x"])
    B, TL, _ = x.shape
    TC = np.asarray(inputs["ctx"]).shape[1]
    NB = B // n_cores
    mk = MK(NB, TL, TC, layers, dbg=dbg)
    nc = mk.build()
    maps = make_in_maps(inputs, n_cores, NB, TL)
    res = run_bass_kernel_spmd(nc, maps, core_ids=list(range(n_cores)), trace=trace)
    out = np.concatenate([np.asarray(r["out"]) for r in res.results], axis=0)
    return out.astype(np.float32), res


def kernel(**inputs):
    out, _ = run(inputs, FULL_LAYERS, n_cores=8)
    return out
```

```python
import contextlib
import math
import numpy as np
import ml_dtypes
import concourse.bass as bass
import concourse.mybir as mybir
from concourse.bass_utils import run_bass_kernel_spmd
from concourse.alu_op_type import AluOpType as ALU

F32 = mybir.dt.float32
BF16 = mybir.dt.bfloat16
AF = mybir.ActivationFunctionType
AX = mybir.AxisListType

D = 1024
KC = 8
GRID_W = 64
CONV_W = 31
HALO = 15
NORM_EPS = 1e-6
LN_EPS = 1e-5
GN_EPS = 64e-5

ENG = ("pe", "act", "dve", "pool", "sp")
NDMASEM = 12
RWKV_CHUNKED = True
SCAN_POOL = 0
CHAIN_TRANSPOSE = True
RQ = "sp"


class Op:
    __slots__ = ("eng", "fn", "dma", "deps", "needs_inc", "cnt", "slot", "val", "idx")


class Prog:
    def __init__(self, nc):
        self.nc = nc
        self.ops = []
        self.by_eng = {e: [] for e in ENG}
        self.last_w = {}
        self.readers = {}
        self.ndma = {e: 0 for e in ENG}
        self.fence_deps = []
        self.fence_pending = set()

    def fence(self):
        deps = []
        for e in ENG:
            lst = self.by_eng[e]
            for op in reversed(lst):
                if not op.dma:
                    deps.append(op.idx)
                    break
            cnt = 0
            for op in reversed(lst):
                if op.dma:
                    deps.append(op.idx)
                    cnt += 1
                    if cnt >= NDMASEM:
                        break
        self.fence_deps = deps
        self.fence_pending = set(ENG)
        self.last_w.clear()
        self.readers.clear()

    def add(self, eng, fn, r=(), w=(), dma=False):
        op = Op()
        op.eng, op.fn, op.dma = eng, fn, dma
        op.needs_inc = False
        op.cnt = op.slot = op.val = None
        op.idx = len(self.ops)
        deps = set()
        lw = self.last_w
        for k in r:
            y = lw.get(k)
            if y is not None:
                deps.add(y)
        for k in w:
            y = lw.get(k)
            if y is not None:
                deps.add(y)
            ys = self.readers.get(k)
            if ys:
                deps.update(ys)
        keep = []
        for yi in deps:
            y = self.ops[yi]
            if (not y.dma) and (not dma) and y.eng == eng:
                if eng == "pe":
                    continue
                raw = False
                for k in r:
                    if lw.get(k) == yi:
                        raw = True
                        break
                if not raw:
                    continue
            keep.append(yi)
        if eng in self.fence_pending:
            self.fence_pending.discard(eng)
            keep = list(set(keep) | set(self.fence_deps))
        op.deps = keep
        for yi in keep:
            self.ops[yi].needs_inc = True
        if dma:
            n = self.ndma[eng]
            self.ndma[eng] = n + 1
            op.slot = n % NDMASEM
            op.val = 16 * (n // NDMASEM + 1)
        for k in r:
            self.readers.setdefault(k, []).append(op.idx)
        for k in w:
            lw[k] = op.idx
            self.readers[k] = []
        self.ops.append(op)
        self.by_eng[eng].append(op)
        return op

    def emit(self):
        nc = self.nc
        for e in ENG:
            c = 0
            for op in self.by_eng[e]:
                if (not op.dma) and op.needs_inc:
                    c += 1
                    op.cnt = c
        with contextlib.ExitStack() as st:
            csem = {e: st.enter_context(nc.semaphore("c_" + e)) for e in ENG}
            dsem = {e: [st.enter_context(nc.semaphore("d_%s_%d" % (e, i))) for i in range(NDMASEM)]
                    for e in ENG if self.ndma[e] > 0}
            block = st.enter_context(nc.Block())
            handles = {"pe": block.tensor, "act": block.scalar, "dve": block.vector,
                       "pool": block.gpsimd, "sp": block.sync}
            ops = self.ops

            def make(e):
                def body(eng):
                    waited = {}
                    for op in self.by_eng[e]:
                        need = {}
                        for yi in op.deps:
                            y = ops[yi]
                            if y.dma:
                                key = ("d", y.eng, y.slot)
                                v = y.val
                            else:
                                key = ("c", y.eng)
                                v = y.cnt
                            if need.get(key, 0) < v:
                                need[key] = v
                        if op.dma and op.val > 16:
                            key = ("d", e, op.slot)
                            if need.get(key, 0) < op.val - 16:
                                need[key] = op.val - 16
                        for key, v in need.items():
                            if waited.get(key, 0) >= v:
                                continue
                            waited[key] = v
                            sem = csem[key[1]] if key[0] == "c" else dsem[key[1]][key[2]]
                            eng.wait_ge(sem, v)
                        ins = op.fn(eng)
                        if op.dma:
                            ins.then_inc(dsem[e][op.slot], 16)
                        elif op.needs_inc:
                            ins.then_inc(csem[e], 1)
                    if e == "sp":
                        for q in ENG:
                            n = self.ndma[q]
                            for s in range(min(n, NDMASEM)):
                                cnt = (n - 1 - s) // NDMASEM + 1
                                if waited.get(("d", q, s), 0) < 16 * cnt:
                                    eng.wait_ge(dsem[q][s], 16 * cnt)
                        for q in ENG:
                            last = None
                            for op in self.by_eng[q]:
                                if op.cnt is not None:
                                    last = op.cnt
                            if last is not None and q != "sp":
                                eng.wait_ge(csem[q], last)
                return body

            for e in ENG:
                if self.by_eng[e] or e == "sp":
                    handles[e](make(e))


def _tiles(n, t):
    out = []
    o = 0
    while o < n:
        m = min(t, n - o)
        out.append((o, m))
        o += m
    return out


class MK:
    def __init__(self, NB, TL, TC, layers, dbg=False):
        self.NB, self.TL, self.TC, self.layers = NB, TL, TC, layers
        self.TT = TL + TC
        self.NV = NB + 1
        self.nc = bass.Bass("TRN2", target_bir_lowering=False)
        self.p = Prog(self.nc)
        self.st = contextlib.ExitStack()
        self.uid = 0
        self.dbg = dbg

    def dram_in(self, name, shape, dt=F32):
        return self.nc.dram_tensor(name, list(shape), dt, kind="ExternalInput").ap()

    def dram(self, name, shape, dt=F32):
        if self.dbg:
            return self.nc.dram_tensor(name, list(shape), dt, kind="ExternalOutput").ap()
        return self.nc.dram_tensor(name, list(shape), dt).ap()

    def alloc(self, free, dt=F32):
        n = 1
        for f in free:
            n *= f
        units = n * 2 if dt == F32 else n
        self.top = (self.top + 15) // 16 * 16
        assert self.top + units <= self.AREN, ("arena overflow", self.top, units, self.AREN)
        v = self.arena[:, self.top:self.top + units]
        self.top += units
        if dt == F32:
            v = v.bitcast(F32)
        if len(free) == 2:
            v = v.rearrange("p (a b) -> p a b", a=free[0])
        elif len(free) == 3:
            v = v.rearrange("p (a b c) -> p a b c", a=free[0], b=free[1])
        return v

    def key(self, s):
        self.uid += 1
        return "%s#%d" % (s, self.uid)

    def ring(self, name, n, free, dt=F32):
        return [(self.alloc(free, dt), self.key(name)) for _ in range(n)]

    def dma(self, out, in_, r=(), w=(), q="sp", **kw):
        return self.p.add(q, lambda e: e.dma_start(out=out, in_=in_, **kw), r=r, w=w, dma=True)

    def mm(self, out, lhsT, rhs, start, stop, r, w):
        return self.p.add("pe", lambda e: e.matmul(out, lhsT=lhsT, rhs=rhs, start=start, stop=stop), r=r, w=w)

    def tr(self, out, in_, r, w):
        P = in_.shape[0]
        ident = self.ident_f[0:P, 0:P]
        return self.p.add("pe", lambda e: e.transpose(out=out, in_=in_, identity=ident), r=r, w=w)

    def act(self, out, in_, func, r, w, scale=None, bias=None):
        kw = {}
        if scale is not None:
            kw["scale"] = scale
        if bias is not None:
            kw["bias"] = bias
        return self.p.add("act", lambda e: e.activation(out=out, in_=in_, func=func, **kw), r=r, w=w)

    def tt(self, eng, out, in0, in1, op, r, w):
        return self.p.add(eng, lambda e: e.tensor_tensor(out=out, in0=in0, in1=in1, op=op), r=r, w=w)

    def ts(self, eng, out, in0, s1, s2, op0, op1, r, w):
        if op1 is None:
            return self.p.add(eng, lambda e: e.tensor_scalar(out=out, in0=in0, scalar1=s1, scalar2=None, op0=op0), r=r, w=w)
        return self.p.add(eng, lambda e: e.tensor_scalar(out=out, in0=in0, scalar1=s1, scalar2=s2, op0=op0, op1=op1), r=r, w=w)

    def stt(self, out, in0, scalar, in1, op0, op1, r, w):
        return self.p.add("dve", lambda e: e.scalar_tensor_tensor(out=out, in0=in0, scalar=scalar, in1=in1, op0=op0, op1=op1), r=r, w=w)

    def rsqrt(self, out, in_, eps, mul, r, w):
        kt = self.key("rs")
        self.ts("dve", out, in_, mul, eps, ALU.mult, ALU.add, r=r, w=[kt])
        self.p.add("act", lambda e: e.activation(out=out, in_=out, func=AF.Sqrt), r=[kt], w=[kt])
        self.p.add("dve", lambda e: e.reciprocal(out=out, in_=out), r=[kt], w=w)

    def bcast_row(self, dram_ap_row, n):
        return bass.AP(dram_ap_row.tensor, dram_ap_row.offset, [[0, 128], [1, n]])

    def build(self):
        nc, NB, TL, TC, TT = self.nc, self.NB, self.TL, self.TC, self.TT
        NLAY = 4
        i = {}
        i["x"] = self.dram_in("x", [NB, TL, D])
        i["c"] = self.dram_in("c", [NB, D])
        i["ctx"] = self.dram_in("ctx", [NB, TC, D])
        i["c_ctx"] = self.dram_in("c_ctx", [1, D])
        i["norm_g"] = self.dram_in("norm_g", [NLAY, D])
        i["mod_w"] = self.dram_in("mod_w", [NLAY, D, 3 * D])
        i["mod_b"] = self.dram_in("mod_b", [NLAY, 3 * D])
        i["conv_w_in"] = self.dram_in("conv_w_in", [2, D, 3 * D])
        i["conv_dw"] = self.dram_in("conv_dw", [2, CONV_W, D])
        i["conv_db"] = self.dram_in("conv_db", [2, D])
        i["conv_ln_g"] = self.dram_in("conv_ln_g", [2, D])
        i["conv_ln_b"] = self.dram_in("conv_ln_b", [2, D])
        i["conv_w_out"] = self.dram_in("conv_w_out", [2, D, D])
        i["rwkv_mu"] = self.dram_in("rwkv_mu", [1, 6, D])
        for nm in ("rwkv_w_r", "rwkv_w_k", "rwkv_w_v", "rwkv_w_g", "rwkv_w_o"):
            i[nm] = self.dram_in(nm, [1, D, D])
        i["rwkv_w0"] = self.dram_in("rwkv_w0", [1, 2, D])
        i["rwkv_w1"] = self.dram_in("rwkv_w1", [1, 2, D, 64])
        i["rwkv_w2"] = self.dram_in("rwkv_w2", [1, 2, 64, D])
        i["rwkv_a0"] = self.dram_in("rwkv_a0", [1, 2, D])
        i["rwkv_a1"] = self.dram_in("rwkv_a1", [1, 2, D, 64])
        i["rwkv_a2"] = self.dram_in("rwkv_a2", [1, 2, 64, D])
        i["rwkv_k_k"] = self.dram_in("rwkv_k_k", [1, D])
        i["rwkv_k_a"] = self.dram_in("rwkv_k_a", [1, D])
        i["rwkv_r_k"] = self.dram_in("rwkv_r_k", [1, D])
        i["rwkv_ln_g"] = self.dram_in("rwkv_ln_g", [1, D])
        i["rwkv_ln_b"] = self.dram_in("rwkv_ln_b", [1, D])
        i["attn_w_in"] = self.dram_in("attn_w_in", [1, D, 6 * D])
        i["attn_q_g"] = self.dram_in("attn_q_g", [1, 128])
        i["attn_k_g"] = self.dram_in("attn_k_g", [1, 128])
        i["attn_w_out"] = self.dram_in("attn_w_out", [1, 2 * D, D])
        i["final_g"] = self.dram_in("final_g", [1, D])
        i["k_ident"] = self.dram_in("k_ident", [128, 128])
        i["k_perm"] = self.dram_in("k_perm", [128, 128])
        i["k_cos"] = self.dram_in("k_cos", [128, TL])
        i["k_sin"] = self.dram_in("k_sin", [128, TL])
        i["k_tri"] = self.dram_in("k_tri", [2, 128, 128])
        i["k_ind"] = self.dram_in("k_ind", [128, 2])
        i["k_masks"] = self.dram_in("k_masks", [5, 64, 512])
        self.i = i
        self.out = nc.dram_tensor("out", [NB, TL, D], F32, kind="ExternalOutput").ap()
        self.xs = self.dram("xs", [NB, TL, D])
        if self.dbg:
            self.xcs = nc.dram_tensor("xcs", [NB, TC, D], F32, kind="ExternalOutput").ap()
        else:
            self.xcs = self.dram("xcs", [NB, TC, D])
        self.m_d = self.dram("m_d", [NLAY, self.NV, 3 * D])

        st = self.st
        with st:
            self.AREN = 106000
            self.arena = st.enter_context(nc.sbuf_tensor("arena", [128, self.AREN], BF16))
            self.top = 0
            self.ps = [st.enter_context(nc.psum_tensor("ps%d" % k, [128, 512], F32)) for k in range(8)]
            self.psk = ["ps%d" % k for k in range(8)]
            self.prologue()
            self.persist_top = self.top
            used = set(l[0] for l in self.layers)
            self.convert_weights(used)
            self.modulation()
            nl = len(self.layers)
            for li, (kind, j, ctx_in, ctx_out, midx) in enumerate(self.layers):
                self.p.fence()
                self.top = self.persist_top
                last = li == nl - 1
                if kind == 0:
                    self.conv_layer(li, j, ctx_out, midx, last)
                elif kind == 1:
                    self.rwkv_layer(li, j, ctx_out, midx, last)
                else:
                    self.attn_layer(li, j, ctx_out, midx, last)
            if nl == 0:
                self.p.fence()
                self.top = self.persist_top
                self.only_final()
            self.p.emit()
        return nc

    def prologue(self):
        i = self.i
        self.ident_f = self.alloc([128], F32)
        self.dma(self.ident_f, i["k_ident"], w=["ident_f"])
        self.ident_b = self.alloc([128], BF16)
        self.p.add("dve", lambda e: e.tensor_copy(out=self.ident_b, in_=self.ident_f), r=["ident_f"], w=["ident_b"])
        permf = self.alloc([128], F32)
        self.dma(permf, i["k_perm"], w=["permf"])
        self.perm_b = self.alloc([128], BF16)
        self.p.add("dve", lambda e: e.tensor_copy(out=self.perm_b, in_=permf), r=["permf"], w=["perm_b"])
        self.ones_d = self.alloc([128], BF16)
        self.ones_h = self.alloc([128], BF16)
        self.ones_1 = self.alloc([128], BF16)
        self.p.add("pool", lambda e: e.memset(self.ones_d, 1.0 / D), w=["ones_d"])
        self.p.add("pool", lambda e: e.memset(self.ones_h, 1.0 / 128), w=["ones_h"])
        self.p.add("pool", lambda e: e.memset(self.ones_1, 1.0), w=["ones_1"])
        self.fg_b = self.alloc([D], F32)
        self.dma(self.fg_b, self.bcast_row(i["final_g"][0, :], D), w=["fg_b"])
        self.constkeys = ["ident_f", "ident_b", "perm_b", "ones_d", "ones_h", "ones_1", "fg_b"]

    def after_fence_consts(self):
        return

    def convert_weights(self, used):
        i = self.i
        self.wb = {}

        def conv(name, src, rows, cols):
            dst = self.dram(name, [rows, cols], BF16)
            for r0 in range(0, rows, 256):
                rr = min(256, rows - r0)
                self.dma(dst[r0:r0 + rr, :], src[r0:r0 + rr, :], w=[name], q="pool")
            self.wb[name] = dst

        if 0 in used:
            for j in sorted(set(l[1] for l in self.layers if l[0] == 0)):
                conv("cwi%d" % j, i["conv_w_in"][j], D, 3 * D)
                conv("cwo%d" % j, i["conv_w_out"][j], D, D)
        if 1 in used:
            for nm in ("w_r", "w_k", "w_v", "w_g", "w_o"):
                conv("r" + nm, i["rwkv_" + nm][0], D, D)
            for dd in range(2):
                conv("rw1_%d" % dd, i["rwkv_w1"][0, dd], D, 64)
                conv("ra1_%d" % dd, i["rwkv_a1"][0, dd], D, 64)
                conv("rw2_%d" % dd, i["rwkv_w2"][0, dd], 64, D)
                conv("ra2_%d" % dd, i["rwkv_a2"][0, dd], 64, D)
        if 2 in used:
            conv("awi", i["attn_w_in"][0], D, 6 * D)
            conv("awo", i["attn_w_out"][0], 2 * D, D)

    def modulation(self):
        i, NV, NB = self.i, self.NV, self.NB
        top0 = self.top
        crow = self.alloc([D], F32)
        self.dma(crow[0:NB, :], i["c"], w=["crow"])
        self.dma(crow[NB:NV, :], i["c_ctx"], w=["crow"])
        self.act(crow[0:NV, :], crow[0:NV, :], AF.Silu, r=["crow"], w=["crow"])
        scT = self.alloc([KC, NV], F32)
        for kc in range(KC):
            self.tr(self.ps[0][:, kc * NV:(kc + 1) * NV], crow[0:NV, kc * 128:(kc + 1) * 128],
                    r=["crow", "ident_f"], w=["ps0"])
        self.p.add("dve", lambda e: e.tensor_copy(out=scT.rearrange("p a b -> p (a b)"), in_=self.ps[0][:, 0:KC * NV]),
                   r=["ps0"], w=["scT"])
        wring = self.ring("modw", 2, [KC, 512], F32)
        mrow = self.alloc([3 * D], F32)
        brow = self.alloc([3 * D], F32)
        used_mod = sorted(set(l[4] for l in self.layers))
        cnt = 0
        for l in used_mod:
            self.dma(brow[0:NV, :], bass.AP(i["mod_b"].tensor, i["mod_b"][l, :].offset, [[0, NV], [1, 3 * D]]),
                     r=[], w=["brow"])
            for pn in range(6):
                wt, wk = wring[cnt % 2]
                bank = 1 + cnt % 2
                cnt += 1
                self.dma(wt, i["mod_w"][l][:, pn * 512:(pn + 1) * 512].rearrange("(kc p) n -> p kc n", p=128), w=[wk])
                for kc in range(KC):
                    self.mm(self.ps[bank][0:NV, :], scT[:, kc, :], wt[:, kc, :], kc == 0, kc == KC - 1,
                            r=[wk, "scT"], w=[self.psk[bank]])
                self.tt("dve", mrow[0:NV, pn * 512:(pn + 1) * 512], self.ps[bank][0:NV, :],
                        brow[0:NV, pn * 512:(pn + 1) * 512], ALU.add, r=[self.psk[bank], "brow"], w=["mrow"])
            self.dma(self.m_d[l], mrow[0:NV, :], r=["mrow"], w=["m_d"], q="pool")
        self.top = top0

    def rows_to_cols(self, rows_aps, name):
        R = len(rows_aps)
        rt = self.alloc([D], F32)
        kr = self.key(name + "_rows")
        for r_, ap in enumerate(rows_aps):
            self.dma(rt[r_:r_ + 1, :], bass.AP(ap.tensor, ap.offset, [[0, 1], [1, D]]), r=["m_d"], w=[kr])
        colsT = self.alloc([KC, R], F32)
        kc_ = self.key(name + "_cols")
        assert KC * R <= 512
        for kc in range(KC):
            self.tr(self.ps[7][:, kc * R:(kc + 1) * R], rt[0:R, kc * 128:(kc + 1) * 128], r=[kr], w=["ps7"])
        self.p.add("dve", lambda e: e.tensor_copy(out=colsT.rearrange("p a b -> p (a b)"), in_=self.ps[7][:, 0:KC * R]),
                   r=["ps7"], w=[kc_])
        return colsT, kc_

    def layer_vectors(self, midx):
        i, NV = self.i, self.NV
        rows = [i["norm_g"][midx, :]]
        for v in range(NV):
            rows.append(self.m_d[midx, v, 0:D])
        for v in range(NV):
            rows.append(self.m_d[midx, v, D:2 * D])
        cols, ck = self.rows_to_cols(rows, "lv")
        gsT = self.alloc([KC, NV], F32)
        kg = self.key("gsT")
        for v in range(NV):
            self.p.add("dve", lambda e, v=v: e.scalar_tensor_tensor(
                out=gsT[:, :, v], in0=cols[:, :, 1 + NV + v], scalar=1.0, in1=cols[:, :, 0],
                op0=ALU.add, op1=ALU.mult), r=[ck], w=[kg])
        self.gsT, self.gsk = gsT, kg
        self.shT, self.shk = cols, ck
        self.midx = midx
        self.gate_tiles = {}

    def gate_b(self, v):
        gt = self.gate_tiles
        if v in gt:
            ent = gt.pop(v)
            gt[v] = ent
            return ent
        if len(gt) < 2:
            ent = (self.alloc([D], F32), self.key("gate_b"))
        else:
            old_v = next(iter(gt))
            ent = gt.pop(old_v)
        self.dma(ent[0], self.bcast_row(self.m_d[self.midx, v, 2 * D:3 * D], D), r=["m_d"], w=[ent[1]])
        gt[v] = ent
        return ent

    def stage1_setup(self, share=None):
        if share is None:
            self.xt_ring = self.ring("xt", 2, [D], F32)
            self.xn = self.alloc([D], F32)
            self.xnk = self.key("xn")
            self.junk = self.alloc([D], F32)
            self.junkk = self.key("junk")
        else:
            self.xt_ring = [share[0], share[1]]
            self.xn, self.xnk = share[2]
            self.junk, self.junkk = share[3]
        self.ss = self.alloc([4], F32)
        self.ssk = self.key("ss")
        self.xt_cnt = 0
        self.coff, self.loff = 0, self.TC

    def sumsq(self, x, ss, rk):
        junk = self.junk
        self.p.add("act", lambda e: e.activation(out=junk, in_=x, func=AF.Square, accum_out=ss),
                   r=rk, w=[self.junkk, self.ssk])

    def src_tile(self, li, b, is_ctx, t0, n=128):
        if li == 0:
            return (self.i["ctx"] if is_ctx else self.i["x"])[b, t0:t0 + n, :]
        return (self.xcs if is_ctx else self.xs)[b, t0:t0 + n, :]

    def xkey(self, b, is_ctx, t0):
        return "x_%d_%d_%d" % (b, int(is_ctx), t0 // 128)

    def stage1(self, li, b, do_ctx, hT, hk):
        TC, TL, NB = self.TC, self.TL, self.NB
        segs = []
        if do_ctx:
            segs += [(True, t0) for t0 in range(0, TC, 128)]
        segs += [(False, t0) for t0 in range(0, TL, 128)]
        for (is_ctx, t0) in segs:
            v = NB if is_ctx else b
            tok = self.coff + t0 if is_ctx else self.loff + t0
            xt, xk = self.xt_ring[self.xt_cnt % 2]
            self.xt_cnt += 1
            self.dma(xt, self.src_tile(li, b, is_ctx, t0), r=[self.xkey(b, is_ctx, t0)], w=[xk])
            ss = self.ss[:, 0:1]
            self.sumsq(xt, ss, [xk])
            self.rsqrt(ss, ss, NORM_EPS, 1.0 / D, r=[self.ssk], w=[self.ssk])
            xn = self.xn
            self.p.add("act", lambda e, xt=xt, ss=ss, xn=xn: e.activation(out=xn, in_=xt, func=AF.Identity, scale=ss),
                       r=[xk, self.ssk], w=[self.xnk])
            for g in range(2):
                bank = 6 + g
                for q in range(4):
                    kc = g * 4 + q
                    self.tr(self.ps[bank][:, q * 128:(q + 1) * 128], self.xn[:, kc * 128:(kc + 1) * 128],
                            r=[self.xnk], w=[self.psk[bank]])
                for q in range(4):
                    kc = g * 4 + q
                    self.ts("dve", hT[:, kc, tok:tok + 128], self.ps[bank][:, q * 128:(q + 1) * 128],
                            self.gsT[:, kc, v:v + 1], self.shT[:, kc, 1 + v:2 + v], ALU.mult, ALU.add,
                            r=[self.psk[bank], self.gsk, self.shk], w=[hk])

    def epi_setup(self):
        self.tmp = self.alloc([D], F32)
        self.tmpk = self.key("tmp")
        self.xnew_ring = self.ring("xnew", 2, [D], F32)
        self.epi_cnt = 0

    def epilogue_tile(self, li, b, is_ctx, t0, yT_fn, nkc, w_out, wok, ykeys, last, banks=(4, 5)):
        v = self.NB if is_ctx else b
        gb, gk = self.gate_b(v)
        xt, xk = self.xt_ring[self.xt_cnt % 2]
        self.xt_cnt += 1
        xkey = self.xkey(b, is_ctx, t0)
        self.dma(xt, self.src_tile(li, b, is_ctx, t0), r=[xkey], w=[xk])
        xn_, xnk_ = self.xnew_ring[self.epi_cnt % 2]
        self.epi_cnt += 1
        for half in range(2):
            bank = banks[half]
            for kc in range(nkc):
                self.mm(self.ps[bank][:, :], yT_fn(kc), w_out[:, kc, half * 512:(half + 1) * 512],
                        kc == 0, kc == nkc - 1, r=ykeys + [wok], w=[self.psk[bank]])
            hs = slice(half * 512, (half + 1) * 512)
            self.tt("dve", self.tmp[:, hs], self.ps[bank][:, :], gb[:, hs], ALU.mult,
                    r=[self.psk[bank], gk], w=[self.tmpk])
            self.tt("pool", xn_[:, hs], self.tmp[:, hs], xt[:, hs], ALU.add, r=[self.tmpk, xk], w=[xnk_])
        if last and not is_ctx:
            ss = self.ss[:, 1:2]
            self.sumsq(xn_, ss, [xnk_])
            self.rsqrt(ss, ss, NORM_EPS, 1.0 / D, r=[self.ssk], w=[self.ssk])
            self.stt(self.tmp, xn_, ss, self.fg_b, ALU.mult, ALU.mult, r=[xnk_, self.ssk], w=[self.tmpk])
            self.dma(self.out[b, t0:t0 + 128, :], self.tmp, r=[self.tmpk], w=["out"], q="pool")
        else:
            dst = (self.xcs if is_ctx else self.xs)[b, t0:t0 + 128, :]
            self.dma(dst, xn_, r=[xnk_], w=[xkey], q="pool")

    def only_final(self):
        self.stage1_setup()
        tmp = self.alloc([D], F32)
        for b in range(self.NB):
            for t0 in range(0, self.TL, 128):
                xt, xk = self.xt_ring[self.xt_cnt % 2]
                self.xt_cnt += 1
                self.dma(xt, self.i["x"][b, t0:t0 + 128, :], w=[xk])
                ss = self.ss[:, 1:2]
                self.sumsq(xt, ss, [xk])
                self.rsqrt(ss, ss, NORM_EPS, 1.0 / D, r=[self.ssk], w=[self.ssk])
                self.stt(tmp, xt, ss, self.fg_b, ALU.mult, ALU.mult, r=[xk, self.ssk], w=["tmpf"])
                self.dma(self.out[b, t0:t0 + 128, :], tmp, r=["tmpf"], w=["out"], q="pool")

    def conv_layer(self, li, j, ctx_out, midx, last):
        NB, TL, TC, TT = self.NB, self.TL, self.TC, self.TT
        i = self.i
        T2 = 256
        self.layer_vectors(midx)
        rows = [i["conv_dw"][j, t, :] for t in range(CONV_W)]
        rows += [i["conv_db"][j, :], i["conv_ln_g"][j, :], i["conv_ln_b"][j, :]]
        cv, cvk = self.rows_to_cols(rows, "cv")
        w_in = self.wb["cwi%d" % j]
        w_in_v = w_in.rearrange("(kc p) n -> p kc n", p=128)
        w_out = self.alloc([KC, D], BF16)
        wok = self.key("cwo")
        self.dma(w_out, self.wb["cwo%d" % j].rearrange("(kc p) n -> p kc n", p=128), r=["cwo%d" % j], w=[wok])
        w_g = self.alloc([KC, D], BF16)
        wgk = self.key("cwg")
        self.dma(w_g, w_in_v[:, :, 2 * D:3 * D], r=["cwi%d" % j], w=[wgk])
        hT = self.alloc([KC, TT], BF16)
        hk = self.key("hT")
        self.stage1_setup()
        self.epi_setup()
        seqs = []
        if ctx_out:
            seqs.append((True, 0, TC))
        seqs.append((False, TC, TL))
        GLW = sum(n + 2 * HALO for (_, _, n) in seqs)
        glu_d = self.dram("glu_d%d" % li, [KC, 128, GLW], BF16)
        wp_ring = self.ring("wp", 2, [KC, 2, 128], BF16)
        sig_ring = self.ring("sig", 2, [512], F32)
        gl_ring = self.ring("gl", 2, [GLW], BF16)
        for (t, k) in gl_ring:
            self.p.add("pool", lambda e, t=t: e.memset(t, 0.0), w=[k])
        glt_ring = self.ring("glt", 2, [KC, T2 + 2 * HALO], BF16)
        dg_ring = self.ring("dg", 2, [CONV_W, 128], BF16)
        sgt_ring = self.ring("sgt", 2, [T2], BF16)
        yb = self.alloc([KC, T2], BF16)
        ybk = self.key("yb")
        ysq = self.alloc([KC, T2], BF16)
        ysqk = self.key("ysq")
        y2_ring = self.ring("y2", 2, [KC, T2], BF16)
        mean = self.alloc([T2], F32)
        meank = self.key("mean")
        rstd = self.alloc([T2], F32)
        rstdk = self.key("rstd")
        t1_ring = self.ring("t1", 2, [T2], F32)
        s_ring = self.ring("s", 2, [T2], F32)
        cnt1 = 0
        cnt2 = 0
        dg_d = self.dram("dg_d%d" % li, [KC, 128, CONV_W, 128], BF16)
        for fc in range(KC):
            dg, dgk = dg_ring[fc % 2]
            for t in range(CONV_W):
                self.ts("dve", dg[:, t, :], self.ident_b, cv[:, fc, t:t + 1], None, ALU.mult, None, r=[cvk], w=[dgk])
            self.dma(dg_d[fc], dg, r=[dgk], w=["dg_d%d" % fc], q="pool")
        for b in range(NB):
            self.stage1(li, b, ctx_out, hT, hk)
            for fc in range(KC):
                wp, wpk = wp_ring[fc % 2]
                for ab in range(2):
                    self.dma(wp[:, :, ab, :], w_in_v[:, :, ab * D + fc * 128: ab * D + (fc + 1) * 128],
                             r=["cwi%d" % j], w=[wpk])
                gl, glk = gl_ring[fc % 2]
                col = 0
                for (is_ctx, tok0, n) in seqs:
                    for (o, m) in _tiles(n, 512):
                        bA = (cnt1 % 2) * 2
                        bB = bA + 1
                        sg, sgk = sig_ring[cnt1 % 2]
                        cnt1 += 1
                        for ab, bank in ((0, bA), (1, bB)):
                            for kc in range(KC):
                                self.mm(self.ps[bank][:, 0:m], wp[:, kc, ab, :], hT[:, kc, tok0 + o: tok0 + o + m],
                                        kc == 0, kc == KC - 1, r=[wpk, hk], w=[self.psk[bank]])
                        self.act(sg[:, 0:m], self.ps[bB][:, 0:m], AF.Sigmoid, r=[self.psk[bB]], w=[sgk])
                        c0 = col + HALO + o
                        self.tt("dve", gl[:, c0:c0 + m], self.ps[bA][:, 0:m], sg[:, 0:m], ALU.mult,
                                r=[self.psk[bA], sgk], w=[glk])
                    col += n + 2 * HALO
                self.dma(glu_d[fc], gl, r=[glk], w=["glu_d"], q="pool")
            col = 0
            for (is_ctx, tok0, n) in seqs:
                for (o, m) in _tiles(n, T2):
                    glt, gltk = glt_ring[cnt2 % 2]
                    y2, y2k = y2_ring[cnt2 % 2]
                    cnt2 += 1
                    c0 = col + o
                    self.dma(glt[:, :, 0:m + 2 * HALO], glu_d[:, :, c0:c0 + m + 2 * HALO].rearrange("f p c -> p f c"),
                             r=["glu_d"], w=[gltk])
                    for fc in range(KC):
                        dg, dgk = dg_ring[fc % 2]
                        self.dma(dg, dg_d[fc], r=["dg_d%d" % fc], w=[dgk])
                        bank = fc % 2
                        for t in range(CONV_W):
                            self.mm(self.ps[bank][:, 0:m], dg[:, t, :], glt[:, fc, t:t + m], t == 0, t == CONV_W - 1,
                                    r=[dgk, gltk], w=[self.psk[bank]])
                        self.act(yb[:, fc, 0:m], self.ps[bank][:, 0:m], AF.Identity, r=[self.psk[bank], cvk], w=[ybk],
                                 bias=cv[:, fc, CONV_W:CONV_W + 1])
                        self.act(ysq[:, fc, 0:m], self.ps[bank][:, 0:m], AF.Square, r=[self.psk[bank], cvk], w=[ysqk],
                                 bias=cv[:, fc, CONV_W:CONV_W + 1])
                    for fc in range(KC):
                        self.mm(self.ps[2][:, 0:m], self.ones_d, yb[:, fc, 0:m], fc == 0, fc == KC - 1,
                                r=[ybk], w=[self.psk[2]])
                    for fc in range(KC):
                        self.mm(self.ps[3][:, 0:m], self.ones_d, ysq[:, fc, 0:m], fc == 0, fc == KC - 1,
                                r=[ysqk], w=[self.psk[3]])
                    self.p.add("act", lambda e, m=m: e.copy(out=mean[:, 0:m], in_=self.ps[2][:, 0:m]),
                               r=[self.psk[2]], w=[meank])
                    self.tt("dve", rstd[:, 0:m], mean[:, 0:m], mean[:, 0:m], ALU.mult, r=[meank], w=[rstdk])
                    self.tt("dve", rstd[:, 0:m], self.ps[3][:, 0:m], rstd[:, 0:m], ALU.subtract,
                            r=[self.psk[3], rstdk], w=[rstdk])
                    self.rsqrt(rstd[:, 0:m], rstd[:, 0:m], LN_EPS, 1.0, r=[rstdk], w=[rstdk])
                    for fc in range(KC):
                        t1, t1k = t1_ring[fc % 2]
                        s_, sk = s_ring[fc % 2]
                        sgt, sgtk = sgt_ring[fc % 2]
                        bank = 6 + fc % 2
                        for kc in range(KC):
                            self.mm(self.ps[bank][:, 0:m], w_g[:, kc, fc * 128:(fc + 1) * 128],
                                    hT[:, kc, tok0 + o: tok0 + o + m], kc == 0, kc == KC - 1,
                                    r=[wgk, hk], w=[self.psk[bank]])
                        self.act(sgt[:, 0:m], self.ps[bank][:, 0:m], AF.Silu, r=[self.psk[bank]], w=[sgtk])
                        self.tt("dve", t1[:, 0:m], yb[:, fc, 0:m], mean[:, 0:m], ALU.subtract, r=[ybk, meank], w=[t1k])
                        self.tt("dve", t1[:, 0:m], t1[:, 0:m], rstd[:, 0:m], ALU.mult, r=[t1k, rstdk], w=[t1k])
                        self.act(s_[:, 0:m], t1[:, 0:m], AF.Silu, r=[t1k, cvk], w=[sk],
                                 scale=cv[:, fc, CONV_W + 1:CONV_W + 2], bias=cv[:, fc, CONV_W + 2:CONV_W + 3])
                        self.tt("pool", y2[:, fc, 0:m], s_[:, 0:m], sgt[:, 0:m], ALU.mult, r=[sk, sgtk], w=[y2k])
                    for (oo, mm_) in _tiles(m, 128):
                        self.epilogue_tile(li, b, is_ctx, o + oo,
                                           lambda kc, y2=y2, oo=oo: y2[:, kc, oo:oo + 128],
                                           KC, w_out, wok, [y2k], last)
                col += n + 2 * HALO

    def bc_free(self, ap2, pattern):
        a = list(ap2.ap)
        if pattern == "k":
            return bass.AP(ap2.tensor, ap2.offset, [list(a[0]), [0, 64], list(a[1])])
        return bass.AP(ap2.tensor, ap2.offset, [list(a[0]), list(a[1]), [0, 64]])

    def rwkv_layer(self, li, j, ctx_out, midx, last):
        NB, TL, TC, TT = self.NB, self.TL, self.TC, self.TT
        i = self.i
        names = ["r", "v", "kk", "sg", "dec0", "dec1", "kd0", "kd1", "bn0", "bn1", "o0", "o1"]
        A = {n: self.dram("rk_%s_%d" % (n, li), [NB, TT, D]) for n in names}
        if RWKV_CHUNKED:
            NCH = TT // 64
            F = {}
            for d in range(2):
                for nm in ("KT", "RT", "KS", "NB"):
                    F[nm, d] = self.dram("rf_%s%d_%d" % (nm, d, li), [NB, 16, 64, TT], BF16)
                for nm in ("KSt", "NBt"):
                    F[nm, d] = self.dram("rf_%s%d_%d" % (nm, d, li), [NB, TT, D], BF16)
                F["pc", d] = self.dram("rf_pc%d_%d" % (d, li), [NB, NCH, 64, 16])
            F["Vt"] = self.dram("rf_Vt_%d" % li, [NB, TT, D], BF16)
            self.rwkv_phase_a(li, j, midx, A, F)
            if globals().get("RWKV_STOP", 9) <= 1:
                return
            self.p.fence()
            self.top = self.persist_top
            self.rwkv_scan2(li, A, F)
            if globals().get("RWKV_STOP", 9) <= 2:
                return
        else:
            self.rwkv_phase_a(li, j, midx, A, None)
            self.p.fence()
            self.top = self.persist_top
            self.rwkv_scan(li, A)
        self.p.fence()
        self.top = self.persist_top
        self.rwkv_phase_c(li, j, ctx_out, midx, last, A)

    def rwkv_phase_a(self, li, j, midx, A, F):
        NB, TL, TC, TT = self.NB, self.TL, self.TC, self.TT
        i = self.i
        self.layer_vectors(midx)
        muT, muk = self.rows_to_cols([i["rwkv_mu"][j, n, :] for n in range(6)], "mu")
        W = {}
        for nm in ("w_r", "w_k", "w_v", "w_g"):
            t = self.alloc([KC, D], BF16)
            k = self.key(nm)
            self.dma(t, self.wb["r" + nm].rearrange("(kc p) n -> p kc n", p=128), w=[k])
            W[nm] = (t, k)
        w1cat = self.alloc([KC, 128], BF16)
        a1cat = self.alloc([KC, 128], BF16)
        w2cat = self.alloc([D], BF16)
        a2cat = self.alloc([D], BF16)
        lk = self.key("lora")
        for d in range(2):
            self.dma(w1cat[:, :, d * 64:(d + 1) * 64], self.wb["rw1_%d" % d].rearrange("(kc p) n -> p kc n", p=128), w=[lk])
            self.dma(a1cat[:, :, d * 64:(d + 1) * 64], self.wb["ra1_%d" % d].rearrange("(kc p) n -> p kc n", p=128), w=[lk])
            self.dma(w2cat[d * 64:(d + 1) * 64, :], self.wb["rw2_%d" % d], w=[lk])
            self.dma(a2cat[d * 64:(d + 1) * 64, :], self.wb["ra2_%d" % d], w=[lk])
        pk = self.key("params")
        kkb = self.alloc([D], F32)
        kab = self.alloc([D], F32)
        self.dma(kkb, self.bcast_row(i["rwkv_k_k"][j, :], D), w=[pk])
        self.dma(kab, self.bcast_row(i["rwkv_k_a"][j, :], D), w=[pk])
        w0b, a0b = [], []
        for d in range(2):
            t = self.alloc([D], F32)
            self.dma(t, self.bcast_row(i["rwkv_w0"][j, d, :], D), w=[pk])
            w0b.append(t)
            t = self.alloc([D], F32)
            self.dma(t, self.bcast_row(i["rwkv_a0"][j, d, :], D), w=[pk])
            a0b.append(t)
        HW = TT + 4
        hT = self.alloc([KC, HW], BF16)
        hk = self.key("hT")
        self.p.add("pool", lambda e: e.memset(hT, 0.0), w=[hk])
        T = [(self.alloc([D], F32), self.key("T%d" % n)) for n in range(4)]
        self.stage1_setup(share=T)
        self.coff, self.loff = 1, TC + 3
        xx = self.alloc([KC, 128], F32)
        xxk = self.key("xx")
        tl = self.alloc([KC, 128], F32)
        tlk = self.key("tl")
        lerp = [(self.alloc([KC, 128], BF16), self.key("lerp%d" % n)) for n in range(6)]
        thT = self.alloc([128], BF16)
        thk = self.key("thT")
        ahT = self.alloc([128], BF16)
        ahk = self.key("ahT")
        kraw = self.alloc([D], F32)
        krk = self.key("kraw")
        kk = self.alloc([D], F32)
        kkk = self.key("kk")
        ssq = self.alloc([16], F32)
        ssqk = self.key("ssq")
        bankc = [0]
        if F is not None:
            rt = self.alloc([D], F32)
            rtk = self.key("rt")
            tri = self.alloc([2, 128], F32)
            ind = self.alloc([2], F32)
            ck = self.key("rconst")
            for d in range(2):
                self.dma(tri[:, d, :], i["k_tri"][d], w=[ck])
            self.dma(ind, i["k_ind"], w=[ck])
            fm_ring = self.ring("fmt", 2, [16, 128], BF16)
            tm_ring = self.ring("tmt", 1, [D], BF16) * 2
            pct = self.alloc([32], F32)
            pctk = self.key("pct")
            fmc = [0]
            tmc = [0]

            def emit_fm(tile, tk, name, d, b, tok):
                fmt, fmk = fm_ring[fmc[0] % 2]
                fmc[0] += 1
                for q4 in range(4):
                    bank = nxt()
                    for jj in range(4):
                        h = q4 * 4 + jj
                        self.tr(self.ps[bank][0:64, jj * 128:(jj + 1) * 128], tile[:, h * 64:(h + 1) * 64],
                                r=[tk], w=[self.psk[bank]])
                    self.p.add("act", lambda e, bank=bank, q4=q4, fmt=fmt: e.copy(
                        out=fmt[0:64, q4 * 4:(q4 + 1) * 4, :].rearrange("p a b -> p (a b)"), in_=self.ps[bank][0:64, :]),
                        r=[self.psk[bank]], w=[fmk])
                self.dma(F[name, d][b, :, :, tok:tok + 128].rearrange("h f t -> f h t"), fmt[0:64], r=[fmk],
                         w=[self.key("fst")], q=RQ)

            def emit_tm(tile, tk, dst):
                tmt, tmk = tm_ring[tmc[0] % 2]
                tmc[0] += 1
                self.p.add("act", lambda e, tmt=tmt, tile=tile: e.copy(out=tmt, in_=tile), r=[tk], w=[tmk])
                self.dma(dst, tmt, r=[tmk], w=[self.key("tst")], q=RQ)

        def nxt():
            b_ = bankc[0] % 6
            bankc[0] += 1
            return b_

        def proj(n_, wname, half):
            wt, wk = W[wname]
            lp, lpk = lerp[n_]
            bank = nxt()
            for kc in range(KC):
                self.mm(self.ps[bank][:, :], lp[:, kc, :], wt[:, kc, half * 512:(half + 1) * 512], kc == 0, kc == KC - 1,
                        r=[lpk, wk], w=[self.psk[bank]])
            return bank

        def store(name, b, tok, tile, tk):
            self.dma(A[name][b, tok:tok + 128, :], tile, r=[tk], w=[self.key("st")], q=(RQ if F is not None else "pool"))

        for b in range(NB):
            self.stage1(li, b, True, hT, hk)
            for (col0, n, tokbase) in ((self.coff, TC, 0), (self.loff, TL, TC)):
                for t0 in range(0, n, 128):
                    c = col0 + t0
                    tok = tokbase + t0
                    hc_ = hT[:, :, c:c + 128]
                    self.tt("pool", xx, hT[:, :, c - 1:c + 127], hT[:, :, c + 1:c + 129], ALU.add, r=[hk], w=[xxk])
                    self.stt(xx, xx, 0.5, hc_, ALU.mult, ALU.subtract, r=[xxk, hk], w=[xxk])
                    for n_ in range(6):
                        lp, lpk = lerp[n_]
                        for kc in range(KC):
                            self.stt(lp[:, kc, :], xx[:, kc, :], muT[:, kc, n_:n_ + 1], hT[:, kc, c:c + 128],
                                     ALU.mult, ALU.add, r=[xxk, muk, hk], w=[lpk])
                    for half in range(2):
                        hs = slice(half * 512, (half + 1) * 512)
                        bk = proj(2, "w_k", half)
                        self.p.add("act", lambda e, bk=bk, hs=hs: e.copy(out=kraw[:, hs], in_=self.ps[bk][:, :]),
                                   r=[self.psk[bk]], w=[krk])
                    T1, T1k = T[0]
                    T2, T2k = T[1]
                    T3, T3k = T[2]
                    T4, T4k = T[3]
                    self.tt("dve", T1, kraw, kkb, ALU.mult, r=[krk, pk], w=[T1k])
                    self.tt("pool", T2, T1, T1, ALU.mult, r=[T1k], w=[T2k])
                    self.p.add("dve", lambda e: e.tensor_reduce(out=ssq, in_=T2.rearrange("p (h f) -> p h f", f=64),
                                                                axis=AX.X, op=ALU.add), r=[T2k], w=[ssqk])
                    self.rsqrt(ssq, ssq, 1e-12, 1.0, r=[ssqk], w=[ssqk])
                    self.tt("pool", kk.rearrange("p (h f) -> p h f", f=64), T1.rearrange("p (h f) -> p h f", f=64),
                            self.bc_free(ssq, "v"), ALU.mult, r=[T1k, ssqk], w=[kkk])
                    if F is None:
                        store("kk", b, tok, kk, kkk)
                    else:
                        for half in range(2):
                            hs = slice(half * 512, (half + 1) * 512)
                            bk = proj(0, "w_r", half)
                            self.p.add("act", lambda e, bk=bk, hs=hs: e.copy(out=rt[:, hs], in_=self.ps[bk][:, :]),
                                       r=[self.psk[bk]], w=[rtk])
                        store("r", b, tok, rt, rtk)
                    for kc in range(KC):
                        self.mm(self.ps[6][:, 0:128], w1cat[:, kc, :], lerp[1][0][:, kc, :], kc == 0, kc == KC - 1,
                                r=[lk, lerp[1][1]], w=[self.psk[6]])
                    self.act(thT, self.ps[6][:, 0:128], AF.Tanh, r=[self.psk[6]], w=[thk])
                    for kc in range(KC):
                        self.mm(self.ps[7][:, 0:128], a1cat[:, kc, :], lerp[4][0][:, kc, :], kc == 0, kc == KC - 1,
                                r=[lk, lerp[4][1]], w=[self.psk[7]])
                    self.p.add("act", lambda e: e.copy(out=ahT, in_=self.ps[7][:, 0:128]), r=[self.psk[7]], w=[ahk])
                    for d in range(2):
                        ds = slice(d * 64, (d + 1) * 64)
                        for half in range(2):
                            hs = slice(half * 512, (half + 1) * 512)
                            bank = nxt()
                            self.mm(self.ps[bank][:, :], ahT[ds, :], a2cat[ds, hs], True, True, r=[ahk, lk], w=[self.psk[bank]])
                            self.tt("dve", T1[:, hs], self.ps[bank][:, :], a0b[d][:, hs], ALU.add,
                                    r=[self.psk[bank], pk], w=[T1k])
                        self.act(T1, T1, AF.Sigmoid, r=[T1k], w=[T1k])
                        self.stt(T2, T1, -1.0, kab, ALU.add, ALU.mult, r=[T1k, pk], w=[T2k])
                        self.stt(T2, T2, 1.0, kraw, ALU.add, ALU.mult, r=[T2k, krk], w=[T2k])
                        store("kd%d" % d, b, tok, T2, T2k)
                        self.stt(T3, kk, -1.0, T1, ALU.mult, ALU.mult, r=[kkk, T1k], w=[T3k])
                        if F is None:
                            store("bn%d" % d, b, tok, T3, T3k)
                        for half in range(2):
                            hs = slice(half * 512, (half + 1) * 512)
                            bank = nxt()
                            self.mm(self.ps[bank][:, :], thT[ds, :], w2cat[ds, hs], True, True, r=[thk, lk], w=[self.psk[bank]])
                            self.tt("dve", T4[:, hs], self.ps[bank][:, :], w0b[d][:, hs], ALU.add,
                                    r=[self.psk[bank], pk], w=[T4k])
                        self.act(T4, T4, AF.Sigmoid, r=[T4k], w=[T4k])
                        if F is None:
                            self.act(T4, T4, AF.Exp, r=[T4k], w=[T4k], scale=-math.exp(-0.5))
                            store("dec%d" % d, b, tok, T4, T4k)
                            continue
                        self.ts("dve", T4, T4, -math.exp(-0.5), None, ALU.mult, None, r=[T4k], w=[T4k])
                        for h in range(16):
                            self.mm(self.ps[7][0:64, h:h + 17:16], T4[:, h * 64:(h + 1) * 64], ind, True, True,
                                    r=[T4k, ck], w=[self.psk[7]])
                        self.act(pct[0:64, :], self.ps[7][0:64, 0:32], AF.Exp, r=[self.psk[7]], w=[pctk])
                        for jj in range(2):
                            self.dma(F["pc", d][b, tok // 64 + jj], pct[0:64, jj * 16:(jj + 1) * 16], r=[pctk],
                                     w=[self.key("pst")], q=RQ)
                        cb = []
                        for half in range(2):
                            hs = slice(half * 512, (half + 1) * 512)
                            bank = 6 + half
                            cb.append(bank)
                            self.mm(self.ps[bank][:, :], tri[:, d, :], T4[:, hs], True, True, r=[T4k, ck], w=[self.psk[bank]])
                        for half in range(2):
                            hs = slice(half * 512, (half + 1) * 512)
                            self.tt("dve", T4[:, hs], self.ps[cb[half]][:, :], T4[:, hs], ALU.subtract,
                                    r=[self.psk[cb[half]], T4k], w=[T4k])
                        self.act(T4, T4, AF.Exp, r=[T4k], w=[T4k])
                        self.tt("dve", T1, kk, T4, ALU.mult, r=[kkk, T4k], w=[T1k])
                        emit_fm(T1, T1k, "KT", d, b, tok)
                        for half in range(2):
                            hs = slice(half * 512, (half + 1) * 512)
                            self.act(T4[:, hs], self.ps[cb[half]][:, :], AF.Exp, r=[self.psk[cb[half]], T1k], w=[T4k],
                                     scale=-1.0)
                        self.tt("dve", T2, T2, T4, ALU.mult, r=[T2k, T4k], w=[T2k])
                        emit_fm(T2, T2k, "KS", d, b, tok)
                        emit_tm(T2, T2k, F["KSt", d][b, tok:tok + 128, :])
                        self.tt("dve", T3, T3, T4, ALU.mult, r=[T3k, T4k], w=[T3k])
                        emit_fm(T3, T3k, "NB", d, b, tok)
                        emit_tm(T3, T3k, F["NBt", d][b, tok:tok + 128, :])
                        for half in range(2):
                            hs = slice(half * 512, (half + 1) * 512)
                            self.act(T4[:, hs], self.ps[cb[half]][:, :], AF.Exp, r=[self.psk[cb[half]], T2k, T3k], w=[T4k])
                        self.tt("dve", T1, rt, T4, ALU.mult, r=[rtk, T4k], w=[T1k])
                        emit_fm(T1, T1k, "RT", d, b, tok)
                    plist = ((3, "w_v", "v", T2, T2k, AF.Identity), (5, "w_g", "sg", T3, T3k, AF.Silu))
                    if F is None:
                        plist = ((0, "w_r", "r", T1, T1k, AF.Identity),) + plist
                    for (n_, wname, name, tile, tk, fn) in plist:
                        for half in range(2):
                            hs = slice(half * 512, (half + 1) * 512)
                            bk = proj(n_, wname, half)
                            self.act(tile[:, hs], self.ps[bk][:, :], fn, r=[self.psk[bk]], w=[tk])
                        store(name, b, tok, tile, tk)
                        if F is not None and name == "v":
                            emit_tm(tile, tk, F["Vt"][b, tok:tok + 128, :])

    def rwkv_scan(self, li, A):
        NB, TL, TC, TT = self.NB, self.TL, self.TC, self.TT
        P = 2 * NB * 16
        SB = 32
        S = self.alloc([64, 64], F32)
        Sk = self.key("S")
        tmp = self.alloc([64, 64], F32)
        tmpk = self.key("tmp")
        vk_ring = self.ring("vk", 2, [64, 64], F32)
        sa = self.alloc([64], F32)
        sak = self.key("sa")
        arrs = ["kk", "bn", "dec", "kd", "v", "r"]
        blk = [{a: self.alloc([SB, 64], F32) for a in arrs} for _ in range(2)]
        oblk = self.ring("oblk", 2, [SB, 64], F32)
        self.p.add("dve", lambda e: e.memset(S[0:P], 0.0), w=[Sk])
        cb = 0
        for (soff, n) in ((0, TC), (TC, TL)):
            for i0 in range(0, n, SB):
                slot = cb % 2
                cb += 1
                bkeys = []
                for d in range(2):
                    for b in range(NB):
                        p0 = d * NB * 16 + b * 16
                        if d == 0:
                            tok0, step = soff + i0, D
                        else:
                            tok0, step = soff + n - 1 - i0, -D
                        for a in arrs:
                            name = a + str(d) if a in ("bn", "dec", "kd") else a
                            src_t = A[name]
                            src = bass.AP(src_t.tensor, src_t[b, tok0, :].offset, [[64, 16], [step, SB], [1, 64]])
                            k = "blk_%d_%d_%d_%s" % (slot, d, b, a)
                            self.dma(blk[slot][a][p0:p0 + 16, :, :], src, w=[k])
                            bkeys.append(k)
                ob, obk = oblk[slot]
                for s_ in range(SB):
                    g = lambda a: blk[slot][a][0:P, s_, :]
                    vkt, vkk = vk_ring[s_ % 2]
                    Sv = S[0:P]
                    tv = tmp[0:P]
                    self.tt("pool", vkt[0:P], self.bc_free(g("v"), "v"), self.bc_free(g("kd"), "k"), ALU.mult,
                            r=bkeys, w=[vkk])
                    self.tt("dve", tv, Sv, self.bc_free(g("kk"), "k"), ALU.mult, r=[Sk] + bkeys, w=[tmpk])
                    self.p.add("dve", lambda e, tv=tv: e.tensor_reduce(out=sa[0:P], in_=tv, axis=AX.X, op=ALU.add),
                               r=[tmpk], w=[sak])
                    e3 = "pool" if SCAN_POOL >= 1 else "dve"
                    e7 = "pool" if SCAN_POOL >= 2 else "dve"
                    self.tt(e3, Sv, Sv, self.bc_free(g("dec"), "k"), ALU.mult, r=[Sk] + bkeys, w=[Sk])
                    if SCAN_POOL >= 2:
                        self.tt(e7, Sv, Sv, vkt[0:P], ALU.add, r=[Sk, vkk], w=[Sk])
                    self.tt("dve", tv, self.bc_free(sa[0:P], "v"), self.bc_free(g("bn"), "k"), ALU.mult,
                            r=[sak] + bkeys, w=[tmpk])
                    self.tt("dve", Sv, Sv, tv, ALU.add, r=[Sk, tmpk], w=[Sk])
                    if SCAN_POOL < 2:
                        self.tt(e7, Sv, Sv, vkt[0:P], ALU.add, r=[Sk, vkk], w=[Sk])
                    self.tt("dve", tv, Sv, self.bc_free(g("r"), "k"), ALU.mult, r=[Sk] + bkeys, w=[tmpk])
                    self.p.add("dve", lambda e, tv=tv, ob=ob, s_=s_: e.tensor_reduce(out=ob[0:P, s_, :], in_=tv, axis=AX.X, op=ALU.add),
                               r=[tmpk], w=[obk])
                for d in range(2):
                    for b in range(NB):
                        p0 = d * NB * 16 + b * 16
                        if d == 0:
                            tok0, step = soff + i0, D
                        else:
                            tok0, step = soff + n - 1 - i0, -D
                        dst_t = A["o%d" % d]
                        dst = bass.AP(dst_t.tensor, dst_t[b, tok0, :].offset, [[64, 16], [step, SB], [1, 64]])
                        self.dma(dst, ob[p0:p0 + 16, :, :], r=[obk], w=[self.key("ost")], q="pool")

    def rwkv_scan2(self, li, A, F):
        NB, TL, TC, TT = self.NB, self.TL, self.TC, self.TT
        i = self.i
        C = 64
        masks = self.alloc([5, 512], F32)
        mk = self.key("masks")
        self.dma(masks[0:64], i["k_masks"].rearrange("m p c -> p m c"), w=[mk])
        US, LS, UI, LI, EYE = (masks[0:64, m, :] for m in range(5))
        units = [(d, b, g) for d in range(2) for b in range(NB) for g in range(2)]
        UB = 4
        fm_names = ("KT", "RT", "KS", "NB")
        tm_names = ("KSt", "NBt", "Vt")

        def t512(dt):
            return self.alloc([512], dt)[0:64]

        slots = []
        for s_ in range(UB):
            sl = {"ST": t512(F32), "STk": self.key("ST"), "STb": t512(BF16), "STbk": self.key("STb")}
            sl["op"] = []
            for par in range(2):
                o = {}
                for nm in fm_names:
                    o[nm] = (self.alloc([8, 64], BF16)[0:64], self.key(nm))
                for nm in tm_names:
                    o[nm] = (t512(BF16), self.key(nm))
                o["pc"] = (self.alloc([8], F32)[0:64], self.key("pc"))
                sl["op"].append(o)
            for nm in ("X", "XT"):
                sl[nm] = [(t512(F32), self.key(nm)) for _ in range(2)]
            sl["Tb"] = [(t512(BF16), self.key("Tb"))] * 2
            sl["Tf"] = (t512(F32), self.key("Tf"))
            for nm in ("Akk", "Akr", "nAbr", "RHST", "UT"):
                sl[nm] = (t512(BF16), self.key(nm))
            sl["ot"] = [(t512(F32), self.key("ot")) for _ in range(2)]
            slots.append(sl)
        bankc = [0]

        def nxtb():
            b_ = bankc[0] % 8
            bankc[0] += 1
            return b_

        def hc(ap, h):
            return ap[:, h * 64:(h + 1) * 64]

        def mmg(terms, rkeys):
            bank = nxtb()
            n = len(terms)
            for h in range(8):
                for ti, (lf, rf) in enumerate(terms):
                    self.mm(self.ps[bank][0:64, h * 64:(h + 1) * 64], lf(h), rf(h), ti == 0, ti == n - 1,
                            r=rkeys, w=[self.psk[bank]])
            return bank

        def pcopy(eng, out, bank, rk, wk):
            if eng == "act":
                self.p.add("act", lambda e: e.copy(out=out, in_=self.ps[bank][0:64, :]), r=[self.psk[bank]] + rk, w=wk)
            else:
                self.p.add("dve", lambda e: e.tensor_copy(out=out, in_=self.ps[bank][0:64, :]), r=[self.psk[bank]] + rk, w=wk)

        seqs = ((0, TC), (TC, TL))
        for batch in range(0, len(units), UB):
            ub = units[batch:batch + UB]
            chunks = []
            for (d, b, g) in ub:
                lst = []
                for (soff, n) in seqs:
                    toks = list(range(soff, soff + n, C))
                    if d == 1:
                        toks = toks[::-1]
                    lst += toks
                chunks.append(lst)
            for s_, _u in enumerate(ub):
                sl = slots[s_]
                self.p.add("dve", lambda e, sl=sl: e.memset(sl["ST"], 0.0), w=[sl["STk"]])
                self.p.add("dve", lambda e, sl=sl: e.memset(sl["STb"], 0.0), w=[sl["STbk"]])
            NCH = len(chunks[0])
            for ci in range(NCH):
                par = ci % 2
                cur = []
                for s_, (d, b, g) in enumerate(ub):
                    sl = slots[s_]
                    o = sl["op"][par]
                    tok0 = chunks[s_][ci]
                    for nm in fm_names:
                        t, k = o[nm]
                        self.dma(t, F[nm, d][b, g * 8:(g + 1) * 8, :, tok0:tok0 + C].rearrange("h f t -> f h t"), w=[k])
                    for nm in tm_names:
                        t, k = o[nm]
                        src = F["Vt"] if nm == "Vt" else F[nm, d]
                        self.dma(t, src[b, tok0:tok0 + C, g * 512:(g + 1) * 512], w=[k])
                    t, k = o["pc"]
                    self.dma(t, F["pc", d][b, tok0 // C, :, g * 8:(g + 1) * 8], w=[k])
                    mX, mXT, mS, mI = (US, LS, US, UI) if d == 0 else (LS, US, LS, LI)
                    cur.append((sl, o, d, b, g, tok0, mX, mXT, mS, mI))
                for (sl, o, d, b, g, tok0, mX, mXT, mS, mI) in cur:
                    NBf, NBk = o["NB"]
                    KT, KTk = o["KT"]
                    bk = mmg([(lambda h: NBf[:, h, :], lambda h: KT[:, h, :])], [NBk, KTk])
                    X0, X0k = sl["X"][0]
                    self.tt("dve", X0, self.ps[bk][0:64, :], mX, ALU.mult, r=[self.psk[bk], mk], w=[X0k])
                    bk = mmg([(lambda h: KT[:, h, :], lambda h: NBf[:, h, :])], [NBk, KTk])
                    XT0, XT0k = sl["XT"][0]
                    self.tt("dve", XT0, self.ps[bk][0:64, :], mXT, ALU.mult, r=[self.psk[bk], mk], w=[XT0k])
                for (sl, o, d, b, g, tok0, mX, mXT, mS, mI) in cur:
                    X0, X0k = sl["X"][0]
                    Tf, Tfk = sl["Tf"]
                    self.tt("dve", Tf, X0, EYE, ALU.add, r=[X0k, mk], w=[Tfk])
                for j in range(1, 6):
                    pj, cj = (j - 1) % 2, j % 2
                    for (sl, o, d, b, g, tok0, mX, mXT, mS, mI) in cur:
                        Xp, Xpk = sl["X"][pj]
                        XTp, XTpk = sl["XT"][pj]
                        if j < 5:
                            bk = mmg([(lambda h: hc(XTp, h), lambda h: hc(Xp, h))], [Xpk, XTpk])
                            pcopy("act", sl["X"][cj][0], bk, [], [sl["X"][cj][1]])
                        if j == 5 or not CHAIN_TRANSPOSE:
                            bk = mmg([(lambda h: hc(Xp, h), lambda h: hc(XTp, h))], [Xpk, XTpk])
                            pcopy("dve", sl["XT"][cj][0], bk, [], [sl["XT"][cj][1]])
                    if j < 5 and CHAIN_TRANSPOSE:
                        for (sl, o, d, b, g, tok0, mX, mXT, mS, mI) in cur:
                            Xc, Xck = sl["X"][cj]
                            bk = nxtb()
                            for h in range(8):
                                self.tr(self.ps[bk][0:64, h * 64:(h + 1) * 64], hc(Xc, h), r=[Xck], w=[self.psk[bk]])
                            pcopy("dve", sl["XT"][cj][0], bk, [], [sl["XT"][cj][1]])
                    for (sl, o, d, b, g, tok0, mX, mXT, mS, mI) in cur:
                        XTc, XTck = sl["XT"][cj]
                        Tf, Tfk = sl["Tf"]
                        bk = mmg([(lambda h: hc(XTc, h), lambda h: hc(Tf, h))], [XTck, Tfk])
                        self.tt("dve", Tf, Tf, self.ps[bk][0:64, :], ALU.add, r=[Tfk, self.psk[bk]], w=[Tfk])
                        if j == 5:
                            Tbc, Tbck = sl["Tb"][0]
                            self.p.add("act", lambda e, Tbc=Tbc, Tf=Tf: e.copy(out=Tbc, in_=Tf), r=[Tfk], w=[Tbck])
                for (sl, o, d, b, g, tok0, mX, mXT, mS, mI) in cur:
                    KS, KSk = o["KS"]
                    KT, KTk = o["KT"]
                    RT, RTk = o["RT"]
                    NBf, NBk = o["NB"]
                    for (nm, lf, lk_, rf, rk_, msk) in (("Akk", KS, KSk, KT, KTk, mS), ("Akr", KS, KSk, RT, RTk, mI),
                                                         ("nAbr", NBf, NBk, RT, RTk, mI)):
                        bk = mmg([(lambda h, lf=lf: lf[:, h, :], lambda h, rf=rf: rf[:, h, :])], [lk_, rk_])
                        self.tt("dve", sl[nm][0], self.ps[bk][0:64, :], msk, ALU.mult, r=[self.psk[bk], mk], w=[sl[nm][1]])
                Tfin = 5 % 2
                for (sl, o, d, b, g, tok0, mX, mXT, mS, mI) in cur:
                    KT, KTk = o["KT"]
                    Vt, Vtk = o["Vt"]
                    Akk, Akkk = sl["Akk"]
                    STb, STbk = sl["STb"], sl["STbk"]
                    bk = mmg([(lambda h: KT[:, h, :], lambda h: hc(STb, h)), (lambda h: hc(Akk, h), lambda h: hc(Vt, h))],
                             [KTk, STbk, Akkk, Vtk])
                    pcopy("act", sl["RHST"][0], bk, [], [sl["RHST"][1]])
                for (sl, o, d, b, g, tok0, mX, mXT, mS, mI) in cur:
                    Tb, Tbk = sl["Tb"][Tfin]
                    RH, RHk = sl["RHST"]
                    bk = mmg([(lambda h: hc(Tb, h), lambda h: hc(RH, h))], [Tbk, RHk])
                    pcopy("act", sl["UT"][0], bk, [], [sl["UT"][1]])
                for (sl, o, d, b, g, tok0, mX, mXT, mS, mI) in cur:
                    RT, RTk = o["RT"]
                    Vt, Vtk = o["Vt"]
                    Akr, Akrk = sl["Akr"]
                    nAbr, nAbrk = sl["nAbr"]
                    UT, UTk = sl["UT"]
                    STb, STbk = sl["STb"], sl["STbk"]
                    bk = mmg([(lambda h: RT[:, h, :], lambda h: hc(STb, h)), (lambda h: hc(Akr, h), lambda h: hc(Vt, h)),
                              (lambda h: hc(nAbr, h), lambda h: hc(UT, h))], [RTk, STbk, Akrk, Vtk, nAbrk, UTk])
                    ot, otk = sl["ot"][ci % 2]
                    pcopy("act", ot, bk, [], [otk])
                    self.dma(A["o%d" % d][b, tok0:tok0 + C, g * 512:(g + 1) * 512], ot, r=[otk], w=[self.key("ost")], q=RQ)
                if self.dbg and batch == 0 and ci == 0:
                    sl = cur[0][0]
                    for nm, (t_, k_) in (("X0f", sl["Tf"]), ("Tb", sl["Tb"][Tfin]), ("Akk", sl["Akk"]), ("Akr", sl["Akr"]),
                                         ("nAbr", sl["nAbr"]), ("RHST", sl["RHST"]), ("UT", sl["UT"])):
                        dd = self.nc.dram_tensor("dbg_" + nm, [64, 512], t_.dtype, kind="ExternalOutput").ap()
                        self.dma(dd, t_, r=[k_], w=[self.key("dbg")], q="pool")
                for (sl, o, d, b, g, tok0, mX, mXT, mS, mI) in cur:
                    KSt, KStk = o["KSt"]
                    NBt, NBtk = o["NBt"]
                    Vt, Vtk = o["Vt"]
                    UT, UTk = sl["UT"]
                    pc, pck = o["pc"]
                    ST, STk = sl["ST"], sl["STk"]
                    STb, STbk = sl["STb"], sl["STbk"]
                    bk = mmg([(lambda h: hc(KSt, h), lambda h: hc(Vt, h)), (lambda h: hc(NBt, h), lambda h: hc(UT, h))],
                             [KStk, Vtk, NBtk, UTk])
                    self.tt("dve", ST, ST, self.ps[bk][0:64, :], ALU.add, r=[STk, self.psk[bk]], w=[STk])
                    st3 = ST.rearrange("p (h v) -> p h v", v=64)
                    self.tt("dve", st3, st3, self.bc_free(pc, "v"), ALU.mult, r=[STk, pck], w=[STk])
                    self.p.add("act", lambda e, STb=STb, ST=ST: e.copy(out=STb, in_=ST), r=[STk], w=[STbk])

    def rwkv_phase_c(self, li, j, ctx_out, midx, last, A):
        NB, TL, TC, TT = self.NB, self.TL, self.TC, self.TT
        i = self.i
        self.layer_vectors(midx)
        w_o = self.alloc([KC, D], BF16)
        wok = self.key("rwo")
        self.dma(w_o, self.wb["rw_o"].rearrange("(kc p) n -> p kc n", p=128), w=[wok])
        pk = self.key("cparams")
        lngb = self.alloc([D], F32)
        lnbb = self.alloc([D], F32)
        rkb = self.alloc([D], F32)
        self.dma(lngb, self.bcast_row(i["rwkv_ln_g"][j, :], D), w=[pk])
        self.dma(lnbb, self.bcast_row(i["rwkv_ln_b"][j, :], D), w=[pk])
        self.dma(rkb, self.bcast_row(i["rwkv_r_k"][j, :], D), w=[pk])
        self.stage1_setup()
        self.epi_setup()
        L = {n: self.ring("ld_" + n, 2, [D], F32) for n in ("o0", "o1", "r", "kd0", "kd1", "v", "sg")}
        o = self.alloc([D], F32)
        ok_ = self.key("o")
        sq = self.alloc([D], F32)
        sqk = self.key("sq")
        st = self.alloc([64], F32)
        stk = self.key("st")
        y2T_ring = self.ring("y2T", 2, [KC, 128], BF16)
        cnt = 0
        h3 = lambda t: t.rearrange("p (h f) -> p h f", f=64)
        for b in range(NB):
            segs = []
            if ctx_out:
                segs += [(True, t0, t0) for t0 in range(0, TC, 128)]
            segs += [(False, t0, TC + t0) for t0 in range(0, TL, 128)]
            for (is_ctx, t0, tok) in segs:
                ld = {}
                for n in L:
                    t, k = L[n][cnt % 2]
                    self.dma(t, A[n][b, tok:tok + 128, :], w=[k])
                    ld[n] = (t, k)
                y2T, y2Tk = y2T_ring[cnt % 2]
                cnt += 1
                self.tt("dve", o, ld["o0"][0], ld["o1"][0], ALU.add, r=[ld["o0"][1], ld["o1"][1]], w=[ok_])
                mean, ex2, bon, tmp16 = st[:, 0:16], st[:, 16:32], st[:, 32:48], st[:, 48:64]
                self.p.add("dve", lambda e: e.tensor_reduce(out=mean, in_=h3(o), axis=AX.X, op=ALU.add), r=[ok_], w=[stk])
                self.act(sq, o, AF.Square, r=[ok_], w=[sqk])
                self.p.add("dve", lambda e: e.tensor_reduce(out=ex2, in_=h3(sq), axis=AX.X, op=ALU.add), r=[sqk], w=[stk])
                self.ts("dve", mean, mean, 1.0 / 64, None, ALU.mult, None, r=[stk], w=[stk])
                self.tt("dve", tmp16, mean, mean, ALU.mult, r=[stk], w=[stk])
                self.stt(ex2, ex2, 1.0 / 64, tmp16, ALU.mult, ALU.subtract, r=[stk], w=[stk])
                self.rsqrt(ex2, ex2, GN_EPS, 1.0, r=[stk], w=[stk])
                self.tt("dve", h3(o), h3(o), self.bc_free(mean, "v"), ALU.subtract, r=[ok_, stk], w=[ok_])
                self.tt("pool", h3(o), h3(o), self.bc_free(ex2, "v"), ALU.mult, r=[ok_, stk], w=[ok_])
                self.tt("dve", o, o, lngb, ALU.mult, r=[ok_, pk], w=[ok_])
                self.tt("dve", o, o, lnbb, ALU.add, r=[ok_, pk], w=[ok_])
                self.tt("pool", sq, ld["kd0"][0], ld["kd1"][0], ALU.add, r=[ld["kd0"][1], ld["kd1"][1]], w=[sqk])
                self.tt("dve", sq, sq, ld["r"][0], ALU.mult, r=[sqk, ld["r"][1]], w=[sqk])
                self.tt("dve", sq, sq, rkb, ALU.mult, r=[sqk, pk], w=[sqk])
                self.p.add("dve", lambda e: e.tensor_reduce(out=bon, in_=h3(sq), axis=AX.X, op=ALU.add), r=[sqk], w=[stk])
                self.tt("pool", h3(sq), h3(ld["v"][0]), self.bc_free(bon, "v"), ALU.mult, r=[ld["v"][1], stk, sqk], w=[sqk])
                self.tt("dve", o, o, sq, ALU.add, r=[ok_, sqk], w=[ok_])
                self.tt("dve", o, o, ld["sg"][0], ALU.mult, r=[ok_, ld["sg"][1]], w=[ok_])
                for g in range(2):
                    bank = 6 + g
                    for q in range(4):
                        kc = g * 4 + q
                        self.tr(self.ps[bank][:, q * 128:(q + 1) * 128], o[:, kc * 128:(kc + 1) * 128], r=[ok_], w=[self.psk[bank]])
                    self.p.add("act", lambda e, bank=bank, g=g, y2T=y2T: e.copy(
                        out=y2T[:, g * 4:(g + 1) * 4, :].rearrange("p a b -> p (a b)"), in_=self.ps[bank][:, :]),
                        r=[self.psk[bank]], w=[y2Tk])
                self.epilogue_tile(li, b, is_ctx, t0, lambda kc, y2T=y2T: y2T[:, kc, :], KC, w_o, wok, [y2Tk], last)

    def attn_layer(self, li, j, ctx_out, midx, last):
        assert not ctx_out
        NB, TL, TC, TT = self.NB, self.TL, self.TC, self.TT
        i = self.i
        NKT = TT // 128
        SC = 1.0 / math.sqrt(128.0)
        self.layer_vectors(midx)
        qg = self.alloc([1], F32)
        kg = self.alloc([1], F32)
        SK = globals().get("ATTN_SKIP", "")
        if "q" not in SK:
            self.dma(qg, bass.AP(i["attn_q_g"].tensor, i["attn_q_g"][j, :].offset, [[1, 128], [1, 1]]), w=["qg"])
            self.dma(kg, bass.AP(i["attn_k_g"].tensor, i["attn_k_g"][j, :].offset, [[1, 128], [1, 1]]), w=["kg"])
        cosT = self.alloc([TL], F32)
        sinT = self.alloc([TL], F32)
        if "c" not in SK:
            self.dma(cosT, i["k_cos"], w=["cosT"])
            self.dma(sinT, i["k_sin"], w=["sinT"])
        w_in_v = self.wb["awi"].rearrange("(kc p) n -> p kc n", p=128)
        w_out = self.alloc([16, D], BF16)
        wok = self.key("awo")
        if "w" not in SK:
            self.dma(w_out, self.wb["awo"].rearrange("(h p) n -> p h n", p=128), r=["awo"], w=[wok])
        og_d = self.dram("og_d%d" % li, [16, 128, TL], BF16)
        hT = self.alloc([KC, TT], BF16)
        hk = self.key("hT")
        self.stage1_setup()
        self.epi_setup()
        wring = self.ring("awp", 2, [KC, 768], BF16)
        kT = self.alloc([TT], BF16)
        kTk = self.key("kT")
        vT = self.alloc([NKT, 128], BF16)
        vTk = self.key("vT")
        qT = self.alloc([2, TL], BF16)
        qTk = self.key("qT")
        sgT = self.alloc([2, TL], BF16)
        sgTk = self.key("sgT")
        sq_ring = self.ring("sq", 1, [512], BF16) * 2
        rs_ring = self.ring("rs", 1, [512], F32) * 2
        kn_ring = self.ring("kn", 1, [512], BF16) * 2
        t1_ring = self.ring("rt1", 1, [512], F32) * 2
        t2_ring = self.ring("rt2", 1, [512], F32) * 2
        pT_ring = self.ring("pT", 4, [512], BF16)
        rz = self.alloc([512], F32)
        rzk = self.key("rz")
        o1 = self.alloc([512], F32)
        o1k = self.key("o1")
        og_ring = self.ring("og", 2, [512], BF16)
        ogt_ring = self.ring("ogt", 2, [16, 128], BF16)
        cn = [0]

        def normrope(ps_ap, psk, n, gvec, gk, dest, destk, pos0):
            c = cn[0]
            cn[0] += 1
            sq, sqk = sq_ring[c % 2]
            rs, rsk = rs_ring[c % 2]
            self.act(sq[:, 0:n], ps_ap, AF.Square, r=[psk], w=[sqk])
            self.mm(self.ps[0][:, 0:n], self.ones_h, sq[:, 0:n], True, True, r=[sqk], w=[self.psk[0]])
            self.rsqrt(rs[:, 0:n], self.ps[0][:, 0:n], NORM_EPS, 1.0, r=[self.psk[0]], w=[rsk])
            if pos0 is None:
                self.stt(dest, ps_ap, gvec, rs[:, 0:n], ALU.mult, ALU.mult, r=[psk, rsk, gk], w=[destk])
                return
            kn, knk = kn_ring[c % 2]
            t1, t1k = t1_ring[c % 2]
            t2, t2k = t2_ring[c % 2]
            self.stt(kn[:, 0:n], ps_ap, gvec, rs[:, 0:n], ALU.mult, ALU.mult, r=[psk, rsk, gk], w=[knk])
            self.mm(self.ps[1][:, 0:n], self.perm_b, kn[:, 0:n], True, True, r=[knk], w=[self.psk[1]])
            self.tt("dve", t1[:, 0:n], kn[:, 0:n], cosT[:, pos0:pos0 + n], ALU.mult, r=[knk, "cosT"], w=[t1k])
            self.tt("dve", t2[:, 0:n], self.ps[1][:, 0:n], sinT[:, pos0:pos0 + n], ALU.mult,
                    r=[self.psk[1], "sinT"], w=[t2k])
            self.tt("pool", dest, t1[:, 0:n], t2[:, 0:n], ALU.add, r=[t1k, t2k], w=[destk])

        cw = 0
        ca = 0
        STOP = globals().get("ATTN_STOP", 99)
        if STOP <= 2:
            return
        for b in range(NB):
            self.stage1(li, b, True, hT, hk)
            if STOP <= 3:
                return
            for g in range(8 if STOP > 4 else 1):
                wt, wtk = wring[cw % 2]
                cw += 1
                self.dma(wt[:, :, 0:256], w_in_v[:, :, 256 * g:256 * g + 256], r=["awi"], w=[wtk])
                self.dma(wt[:, :, 256:384], w_in_v[:, :, 2048 + 128 * g:2048 + 128 * g + 128], r=["awi"], w=[wtk])
                self.dma(wt[:, :, 384:512], w_in_v[:, :, 3072 + 128 * g:3072 + 128 * g + 128], r=["awi"], w=[wtk])
                self.dma(wt[:, :, 512:768], w_in_v[:, :, 4096 + 256 * g:4096 + 256 * g + 256], r=["awi"], w=[wtk])
                ktiles = [(o, m, None) for (o, m) in _tiles(TC, 512)] + [(TC + o, m, o) for (o, m) in _tiles(TL, 512)]
                for (tok0, n, pos0) in ktiles:
                    for kc in range(KC):
                        self.mm(self.ps[7][:, 0:n], wt[:, kc, 256:384], hT[:, kc, tok0:tok0 + n], kc == 0, kc == KC - 1,
                                r=[wtk, hk], w=[self.psk[7]])
                    normrope(self.ps[7][:, 0:n], self.psk[7], n, kg, "kg", kT[:, tok0:tok0 + n], kTk, pos0)
                for k0 in range(0, NKT, 4):
                    nk = min(4, NKT - k0)
                    for q in range(nk):
                        kt = k0 + q
                        for kc in range(KC):
                            self.mm(self.ps[2][:, q * 128:(q + 1) * 128], hT[:, kc, kt * 128:(kt + 1) * 128],
                                    wt[:, kc, 384:512], kc == 0, kc == KC - 1, r=[wtk, hk], w=[self.psk[2]])
                    self.p.add("act", lambda e, k0=k0, nk=nk: e.copy(
                        out=vT[:, k0:k0 + nk, :].rearrange("p a b -> p (a b)"), in_=self.ps[2][:, 0:nk * 128]),
                        r=[self.psk[2]], w=[vTk])
                for hq in range(2):
                    for (o, m) in _tiles(TL, 512):
                        for kc in range(KC):
                            self.mm(self.ps[7][:, 0:m], wt[:, kc, hq * 128:(hq + 1) * 128], hT[:, kc, TC + o:TC + o + m],
                                    kc == 0, kc == KC - 1, r=[wtk, hk], w=[self.psk[7]])
                        normrope(self.ps[7][:, 0:m], self.psk[7], m, qg, "qg", qT[:, hq, o:o + m], qTk, o)
                        for kc in range(KC):
                            self.mm(self.ps[2][:, 0:m], wt[:, kc, 512 + hq * 128:512 + (hq + 1) * 128],
                                    hT[:, kc, TC + o:TC + o + m], kc == 0, kc == KC - 1, r=[wtk, hk], w=[self.psk[2]])
                        self.act(sgT[:, hq, o:o + m], self.ps[2][:, 0:m], AF.Silu, r=[self.psk[2]], w=[sgTk])
                if STOP <= 5:
                    continue
                for hq in range(2):
                    for (o, m) in _tiles(TL, 512):
                        bO = 4 + 2 * (ca % 2)
                        bZ = bO + 1
                        og, ogk = og_ring[ca % 2]
                        ca += 1

                        def S(kt):
                            bank = kt % 4
                            self.mm(self.ps[bank][:, 0:m], kT[:, kt * 128:(kt + 1) * 128], qT[:, hq, o:o + m], True, True,
                                    r=[kTk, qTk], w=[self.psk[bank]])
                        for k_ in range(min(3, NKT)):
                            S(k_)
                        for kt in range(NKT):
                            bank = kt % 4
                            pT, pTk = pT_ring[kt % 4]
                            self.act(pT[:, 0:m], self.ps[bank][:, 0:m], AF.Exp, r=[self.psk[bank]], w=[pTk], scale=SC)
                            self.mm(self.ps[bO][:, 0:m], vT[:, kt, :], pT[:, 0:m], kt == 0, kt == NKT - 1,
                                    r=[vTk, pTk], w=[self.psk[bO]])
                            self.mm(self.ps[bZ][:, 0:m], self.ones_1, pT[:, 0:m], kt == 0, kt == NKT - 1,
                                    r=[pTk], w=[self.psk[bZ]])
                            if kt + 3 < NKT:
                                S(kt + 3)
                        self.p.add("dve", lambda e, bZ=bZ, m=m: e.reciprocal(out=rz[:, 0:m], in_=self.ps[bZ][:, 0:m]),
                                   r=[self.psk[bZ]], w=[rzk])
                        self.tt("dve", o1[:, 0:m], self.ps[bO][:, 0:m], rz[:, 0:m], ALU.mult, r=[self.psk[bO], rzk], w=[o1k])
                        self.tt("pool", og[:, 0:m], o1[:, 0:m], sgT[:, hq, o:o + m], ALU.mult, r=[o1k, sgTk], w=[ogk])
                        self.dma(og_d[2 * g + hq, :, o:o + m], og[:, 0:m], r=[ogk], w=["og_d"], q="pool")
            if STOP <= 6:
                return
            ce = 0
            for t0 in range(0, TL, 128):
                ogt, ogtk = ogt_ring[ce % 2]
                ce += 1
                self.dma(ogt, og_d[:, :, t0:t0 + 128].rearrange("h p t -> p h t"), r=["og_d"], w=[ogtk])
                self.epilogue_tile(li, b, False, t0, lambda kc, ogt=ogt: ogt[:, kc, :], 16, w_out, wok, [ogtk], last)


FULL_LAYERS = [(0, 0, True, True, 0), (1, 0, True, True, 1), (2, 0, True, False, 2), (0, 1, False, False, 3)]


def host_consts(TL):
    ident = np.eye(128, dtype=np.float32)
    perm = np.zeros((128, 128), np.float32)
    for k in range(128):
        blk = k // 32
        if blk % 2 == 0:
            perm[k, k + 32] = 1.0
        else:
            perm[k, k - 32] = -1.0
    rows = TL // GRID_W
    t = np.arange(TL)
    row = (t // GRID_W).astype(np.float32)
    col = (t % GRID_W).astype(np.float32)
    inv = (1.0 / (10000.0 ** (np.arange(0, 64, 2, dtype=np.float32) / np.float32(64)))).astype(np.float32)
    cosT = np.zeros((128, TL), np.float32)
    sinT = np.zeros((128, TL), np.float32)
    for p in range(128):
        pos = row if p < 64 else col
        ang = (pos * inv[p % 32]).astype(np.float32)
        cosT[p] = np.cos(ang)
        sinT[p] = np.sin(ang)
    idx = np.arange(128)
    same = (idx[:, None] // 64) == (idx[None, :] // 64)
    tri = np.zeros((2, 128, 128), np.float32)
    tri[0] = (same & (idx[:, None] <= idx[None, :])).astype(np.float32)
    tri[1] = (same & (idx[:, None] >= idx[None, :])).astype(np.float32)
    ind = np.zeros((128, 2), np.float32)
    ind[:64, 0] = 1.0
    ind[64:, 1] = 1.0
    i64 = np.arange(64)
    us = (i64[:, None] < i64[None, :]).astype(np.float32)
    ls = (i64[:, None] > i64[None, :]).astype(np.float32)
    ui = (i64[:, None] <= i64[None, :]).astype(np.float32)
    li_ = (i64[:, None] >= i64[None, :]).astype(np.float32)
    eye = np.eye(64, dtype=np.float32)
    masks = np.stack([np.tile(m, (1, 8)) for m in (us, ls, ui, li_, eye)]).astype(np.float32)
    return {"k_ident": ident, "k_perm": perm, "k_cos": cosT, "k_sin": sinT,
            "k_tri": tri, "k_ind": ind, "k_masks": masks}


def make_in_maps(inputs, n_cores, NB, TL):
    consts = host_consts(TL)
    maps = []
    for cid in range(n_cores):
        m = {}
        for k, v in inputs.items():
            v = np.asarray(v)
            if k in ("x", "c", "ctx"):
                m[k] = np.ascontiguousarray(v[cid * NB:(cid + 1) * NB])
            elif k in ("c_ctx", "final_g"):
                m[k] = np.ascontiguousarray(v.reshape(1, -1))
            elif k == "rwkv_r_k":
                m[k] = np.ascontiguousarray(v.reshape(v.shape[0], -1))
            else:
                m[k] = np.ascontiguousarray(v)
        m.update(consts)
        maps.append(m)
    return maps


_CACHE = {}


def run(inputs, layers, n_cores=8, trace=False, dbg=False):
    x = np.asarray(inputs["x"])
    B, TL, _ = x.shape
    TC = np.asarray(inputs["ctx"]).shape[1]
    NB = B // n_cores
    mk = MK(NB, TL, TC, layers, dbg=dbg)
    nc = mk.build()
    maps = make_in_maps(inputs, n_cores, NB, TL)
    res = run_bass_kernel_spmd(nc, maps, core_ids=list(range(n_cores)), trace=trace)
    out = np.concatenate([np.asarray(r["out"]) for r in res.results], axis=0)
    return out.astype(np.float32), res


def kernel(**inputs):
    out, _ = run(inputs, FULL_LAYERS, n_cores=8)
    return out
```
